# Optimizing a Trainium2 kernel written in Bass

```python
import math
import jax
import jax.numpy as jnp
from jax import lax
import numpy as np

D_MODEL = 1024
BATCH = 8
SEQ = 2048
DEPTH = 2

GRID_W = 64
CTX_LEN = 256
EPS = 1e-6
ROPE_BASE = 10000.0
N_MOD = 9
N_MOD_CTX_LAST = 5
N_BRANCH = 4
BRANCH_W = 512
FFN_HIDDEN = 2816

MLA_HEADS = 4
MLA_NOPE = 128
MLA_ROPE = 64
MLA_V = 128
MLA_Q_RANK = 256
MLA_KV_RANK = 256
MLA_BLOCK = 128
HY_W = 512
HY_EMB = 33
HY_FILT = 64
HY_SHORT = 3
HY_FAST_PCT = 0.3
HY_SLOW_PCT = 1.5
HY_TARGET = 1e-2
S5_W = 512
S5_H = 16
S5_G = S5_W // S5_H
S5_P = 64
NA_HEADS = 8
NA_DIM = 64
NA_WIN_R = 8
NA_WIN_C = 16

IN_SPLITS = (MLA_KV_RANK, MLA_ROPE, NA_HEADS * NA_DIM, NA_HEADS * NA_DIM, S5_W,
             MLA_Q_RANK, NA_HEADS * NA_DIM, 3 * HY_W, N_BRANCH * D_MODEL)
N_KV_SIDE = 5
N_IN = sum(IN_SPLITS)
N_CTX_COLS = sum(IN_SPLITS[:N_KV_SIDE])

kernel_name = 'hybrid_mla_hyena_s5_natten_block'


def rms_norm(x, g):
    xf = x.astype(jnp.float32)
    y = xf * lax.rsqrt(jnp.mean(xf * xf, axis=-1, keepdims=True) + EPS)
    return (y * g.astype(jnp.float32)).astype(x.dtype)


def split_cols(z, sizes):
    return jnp.split(z, np.cumsum(sizes)[:-1].tolist(), axis=-1)


def modulated_norm(x, g, shift, scale):
    return rms_norm(x, g) * (1 + scale) + shift


def swiglu(h, w_in, w_out):
    a, b = jnp.split(h @ w_in, 2, axis=-1)
    return (jax.nn.silu(a) * b) @ w_out


def ffn_sublayer(x, mods, base, g_pre, g_post, w_in, w_out):
    h = modulated_norm(x, g_pre, mods[base], mods[base + 1])
    return x + 0.5 * mods[base + 2] * rms_norm(swiglu(h, w_in, w_out), g_post)


def axial_rope(x):
    n_tok, r = x.shape[1], x.shape[-1]
    q = r // 4
    t = jnp.arange(n_tok)
    pos = jnp.stack([t // GRID_W, t % GRID_W], axis=-1).astype(jnp.float32)
    inv = ROPE_BASE ** (-jnp.arange(q, dtype=jnp.float32) / q)
    ang = pos[:, :, None] * inv
    cos, sin = jnp.cos(ang)[:, None], jnp.sin(ang)[:, None]
    xr = x.astype(jnp.float32).reshape(x.shape[:-1] + (2, 2, q))
    x1, x2 = xr[..., 0, :], xr[..., 1, :]
    out = jnp.stack([x1 * cos - x2 * sin, x2 * cos + x1 * sin], axis=-2)
    return out.reshape(x.shape).astype(x.dtype)


def attend(q, k, v, scale):
    s = jnp.einsum('bqhe,bkhe->bhqk', q, k, preferred_element_type=jnp.float32) * scale
    p = jax.nn.softmax(s, axis=-1).astype(v.dtype)
    return jnp.einsum('bhqk,bkhe->bqhe', p, v)


def blocked_attention(q, k, v, scale):
    b, n_tok, h, e = q.shape
    nb = n_tok // MLA_BLOCK
    qb = q.reshape(b, nb, MLA_BLOCK, h, e).transpose(1, 0, 2, 3, 4)
    ob = lax.map(lambda qq: attend(qq, k, v, scale), qb)
    return ob.transpose(1, 0, 2, 3, 4).reshape(b, n_tok, h * v.shape[-1])


def mla_q(c_q, g_q, w_uq, rope):
    q = jnp.einsum('blr,rhe->blhe', rms_norm(c_q, g_q), w_uq)
    if not rope:
        return q
    return jnp.concatenate([q[..., :MLA_NOPE], axial_rope(q[..., MLA_NOPE:])], axis=-1)


def mla_kv(c_kv, k_rope, g_kv, w_ukv, rope):
    kv = jnp.einsum('blr,rhe->blhe', rms_norm(c_kv, g_kv), w_ukv)
    kr = k_rope[:, :, None, :]
    if rope:
        kr = axial_rope(kr)
    kr = jnp.broadcast_to(kr, kv.shape[:3] + (MLA_ROPE,))
    return jnp.concatenate([kv[..., :MLA_NOPE], kr], axis=-1), kv[..., MLA_NOPE:]


def neighbourhood_attention(q, k, v, k_c, v_c, rpb):
    b, n_tok, h, e = q.shape
    rows = n_tok // GRID_W
    kr = min(NA_WIN_R, rows)
    r = jnp.arange(rows)
    row_idx = jnp.clip(r - kr // 2, 0, rows - kr)[:, None] + jnp.arange(kr)
    col = jnp.arange(GRID_W)
    c0 = jnp.clip(col - NA_WIN_C // 2, 0, GRID_W - NA_WIN_C)
    in_win = (col[None, :] >= c0[:, None]) & (col[None, :] < c0[:, None] + NA_WIN_C)
    dr = row_idx - r[:, None] + NA_WIN_R - 1
    dc = jnp.clip(col[None, :] - col[:, None] + NA_WIN_C - 1, 0, 2 * NA_WIN_C - 2)
    bias = rpb[:, dr[:, None, :, None], dc[None, :, None, :]].astype(jnp.float32)
    bias = jnp.where(in_win[None, None, :, None, :], bias, -jnp.inf).reshape(h, rows, GRID_W, kr * GRID_W)
    qg = q.reshape(b, rows, GRID_W, h, e)
    kg = k.reshape(b, rows, GRID_W, h, e)[:, row_idx].reshape(b, rows, kr * GRID_W, h, e)
    vg = v.reshape(b, rows, GRID_W, h, e)[:, row_idx].reshape(b, rows, kr * GRID_W, h, e)
    scale = e ** -0.5
    s_ctx = jnp.einsum('brqhe,bkhe->bhrqk', qg, k_c, preferred_element_type=jnp.float32) * scale
    s_loc = jnp.einsum('brqhe,brkhe->bhrqk', qg, kg, preferred_element_type=jnp.float32) * scale + bias
    p = jax.nn.softmax(jnp.concatenate([s_ctx, s_loc], axis=-1), axis=-1).astype(v.dtype)
    n_c = k_c.shape[1]
    o = (jnp.einsum('bhrqk,bkhe->brqhe', p[..., :n_c], v_c)
         + jnp.einsum('bhrqk,brkhe->brqhe', p[..., n_c:], vg))
    return o.reshape(b, n_tok, h * e)


def hyena_filters(n_tok, w1, b1, f1, w2, b2, f2, w3):
    bands = (HY_EMB - 1) // 2
    t = jnp.arange(n_tok, dtype=jnp.float32)
    t01 = jnp.linspace(0.0, 1.0, n_tok, dtype=jnp.float32)[:, None]
    ang = (2.0 * math.pi * t / n_tok)[:, None] * jnp.linspace(1e-4, bands - 1, bands, dtype=jnp.float32)
    z = jnp.concatenate([t01, jnp.cos(ang), -jnp.sin(ang)], axis=-1)
    hf = jnp.sin(f1 * (z @ w1 + b1))
    hf = jnp.sin(f2 * (hf @ w2 + b2))
    hf = (hf @ w3).astype(jnp.float32)
    max_decay = math.log(HY_TARGET) / HY_FAST_PCT
    min_decay = math.log(HY_TARGET) / HY_SLOW_PCT
    deltas = jnp.abs(jnp.linspace(min_decay, max_decay, HY_W, dtype=jnp.float32))
    decay = jnp.exp(-t01 * deltas)
    return hf * jnp.concatenate([decay, decay], axis=-1)


def hyena_stream(z, conv_w, conv_b, bias_d, filt):
    b, n_tok, ch = z.shape
    pad = HY_SHORT // 2
    u = lax.conv_general_dilated(z, conv_w[:, None, :], (1,), [(pad, HY_SHORT - 1 - pad)],
                                 dimension_numbers=('NWC', 'WIO', 'NWC'), feature_group_count=ch) + conv_b
    v, x1, x0 = jnp.split(u, 3, axis=-1)
    s = (v * x1).astype(jnp.float32)
    n_fft = 2 * n_tok
    spec_h = (jnp.fft.rfft(filt[:, :HY_W], n=n_fft, axis=0)
              + jnp.conj(jnp.fft.rfft(filt[:, HY_W:], n=n_fft, axis=0)))
    y = jnp.fft.irfft(jnp.fft.rfft(s, n=n_fft, axis=1) * spec_h, n=n_fft, axis=1)[:, :n_tok]
    y = y + s * bias_d.astype(jnp.float32)
    return (x0.astype(jnp.float32) * y).astype(z.dtype)


def s5_discretize(lam_re, lam_im, log_dt, b_re, b_im):
    lam = lax.complex(jnp.minimum(lam_re.astype(jnp.float32), -1e-4), lam_im.astype(jnp.float32))
    lam_dt = lam * jnp.exp(log_dt.astype(jnp.float32))[..., None]
    lam_bar = jnp.exp(lam_dt)
    b_bar = ((lam_bar - 1.0) / lam)[..., None] * lax.complex(b_re.astype(jnp.float32), b_im.astype(jnp.float32))
    return lam_dt, lam_bar, b_bar


def _lin_combine(e1, e2):
    a1, b1 = e1
    a2, b2 = e2
    return a1 * a2, a2 * b1 + b2


def s5_scan(u, lam_dt, lam_bar, b_bar, s0, reverse):
    bu = jnp.einsum('blgh,gph->blgp', u.astype(jnp.float32).astype(jnp.complex64), b_bar)
    a = jnp.broadcast_to(lam_bar, bu.shape)
    _, xs = lax.associative_scan(_lin_combine, (a, bu), reverse=reverse, axis=1)
    if s0 is not None:
        n_tok = u.shape[1]
        steps = (n_tok - jnp.arange(n_tok) if reverse else jnp.arange(n_tok) + 1).astype(jnp.float32)
        xs = xs + jnp.exp(lam_dt * steps[:, None, None])[None] * s0[:, None]
    return xs


def s5_readout(u, xf, xb, c_re, c_im, d, w_glu, b_glu):
    cm = lax.complex(c_re.astype(jnp.float32), c_im.astype(jnp.float32))
    y = jnp.real(jnp.einsum('blgp,ghp->blgh', xf, cm[0]) + jnp.einsum('blgp,ghp->blgh', xb, cm[1]))
    b, n_tok = u.shape[:2]
    y = y.reshape(b, n_tok, S5_W) + d.astype(jnp.float32) * u.reshape(b, n_tok, S5_W).astype(jnp.float32)
    g = jax.nn.gelu(y).astype(u.dtype)
    ga, gb = jnp.split(g @ w_glu + b_glu, 2, axis=-1)
    return ga * jax.nn.sigmoid(gb)


def merge_branches(branches, gate_pre, w_branch, w_out):
    b, n_tok, _ = gate_pre.shape
    g = jax.nn.sigmoid(gate_pre.reshape(b, n_tok, N_BRANCH, D_MODEL))
    proj = jnp.einsum('blnw,nwd->blnd', jnp.stack(branches, axis=2), w_branch)
    return jnp.sum(g * proj, axis=2) @ w_out


def token_mixer(hc, hl, p, ctx_out):
    b, n_lat, _ = hl.shape
    n_ctx = hc.shape[1]
    zl = split_cols(hl @ p['w_in'], IN_SPLITS)
    if ctx_out:
        zc = split_cols(hc @ p['w_in'], IN_SPLITS)
    else:
        zc = split_cols(hc @ p['w_in'][:, :N_CTX_COLS], IN_SPLITS[:N_KV_SIDE])
    ckv_l, kr_l, nk_l, nv_l, u_l, cq_l, nq_l, hy_l, gt_l = zl
    ckv_c, kr_c, nk_c, nv_c, u_c = zc[:N_KV_SIDE]
    heads = lambda t, n: t.reshape(t.shape[:2] + (n, -1))

    mla_scale = (MLA_NOPE + MLA_ROPE) ** -0.5
    k_c, v_c = mla_kv(ckv_c, kr_c, p['mla_g_kv'], p['mla_w_ukv'], False)
    k_l, v_l = mla_kv(ckv_l, kr_l, p['mla_g_kv'], p['mla_w_ukv'], True)
    q_l = mla_q(cq_l, p['mla_g_q'], p['mla_w_uq'], True)
    a_l = blocked_attention(q_l, jnp.concatenate([k_c, k_l], axis=1),
                            jnp.concatenate([v_c, v_l], axis=1), mla_scale)

    hyp = (p['hy_w1'], p['hy_b1'], p['hy_freq1'], p['hy_w2'], p['hy_b2'], p['hy_freq2'], p['hy_w3'])
    b_l = hyena_stream(hy_l, p['hy_conv_w'], p['hy_conv_b'], p['hy_bias'], hyena_filters(n_lat, *hyp))

    lam_dt, lam_bar, b_bar = s5_discretize(p['s5_lam_re'], p['s5_lam_im'], p['s5_log_dt'],
                                           p['s5_b_re'], p['s5_b_im'])
    uc = u_c.reshape(b, n_ctx, S5_G, S5_H)
    ul = u_l.reshape(b, n_lat, S5_G, S5_H)
    xf_c = s5_scan(uc, lam_dt[0], lam_bar[0], b_bar[0], None, False)
    xb_c = s5_scan(uc, lam_dt[1], lam_bar[1], b_bar[1], None, True)
    xf_l = s5_scan(ul, lam_dt[0], lam_bar[0], b_bar[0], xf_c[:, -1], False)
    xb_l = s5_scan(ul, lam_dt[1], lam_bar[1], b_bar[1], xb_c[:, 0], True)
    s5p = (p['s5_c_re'], p['s5_c_im'], p['s5_d'], p['s5_w_glu'], p['s5_b_glu'])
    c_l = s5_readout(ul, xf_l, xb_l, *s5p)

    nk_c, nv_c = heads(nk_c, NA_HEADS), heads(nv_c, NA_HEADS)
    d_l = neighbourhood_attention(heads(nq_l, NA_HEADS), heads(nk_l, NA_HEADS), heads(nv_l, NA_HEADS),
                                  nk_c, nv_c, p['na_rpb'])

    y_l = merge_branches([a_l, b_l, c_l, d_l], gt_l, p['w_branch'], p['w_out'])
    if not ctx_out:
        return None, y_l

    cq_c, nq_c, hy_c, gt_c = zc[N_KV_SIDE:]
    a_c = attend(mla_q(cq_c, p['mla_g_q'], p['mla_w_uq'], False), k_c, v_c, mla_scale).reshape(b, n_ctx, -1)
    b_c = hyena_stream(hy_c, p['hy_conv_w'], p['hy_conv_b'], p['hy_bias'], hyena_filters(n_ctx, *hyp))
    c_c = s5_readout(uc, xf_c, xb_c, *s5p)
    d_c = attend(heads(nq_c, NA_HEADS), nk_c, nv_c, NA_DIM ** -0.5).reshape(b, n_ctx, -1)
    y_c = merge_branches([a_c, b_c, c_c, d_c], gt_c, p['w_branch'], p['w_out'])
    return y_c, y_l


def hybrid_layer(xc, xl, mod_c, mod_l, p, ctx_out):
    g = p['norm_g']
    mc = jnp.split(mod_c, mod_c.shape[-1] // D_MODEL, axis=-1)
    ml = jnp.split(mod_l, N_MOD, axis=-1)
    fi, fo = p['ffn_w_in'], p['ffn_w_out']
    xl = ffn_sublayer(xl, ml, 0, g[0], g[1], fi[0], fo[0])
    xc = ffn_sublayer(xc, mc, 0, g[0], g[1], fi[0], fo[0])
    hl = modulated_norm(xl, g[2], ml[3], ml[4])
    hc = modulated_norm(xc, g[2], mc[3], mc[4])
    yc, yl = token_mixer(hc, hl, p, ctx_out)
    xl = xl + ml[5] * rms_norm(yl, g[3])
    xl = ffn_sublayer(xl, ml, 6, g[4], g[5], fi[1], fo[1])
    if not ctx_out:
        return None, xl
    xc = xc + mc[5] * rms_norm(yc, g[3])
    xc = ffn_sublayer(xc, mc, 6, g[4], g[5], fi[1], fo[1])
    return xc, xl


def setup_inputs(seed: int = 0) -> dict:
    key = jax.random.key(seed)
    ks = iter(jax.random.split(key, 48))
    nrm = lambda shape, scale: jax.random.normal(next(ks), shape, jnp.float32) * scale
    gain = lambda shape, s=0.05: 1.0 + nrm(shape, s)
    d, nl, f = D_MODEL, DEPTH, FFN_HIDDEN
    n_idx = jnp.arange(S5_P, dtype=jnp.float32)
    return {
        'x': nrm((BATCH, SEQ, d), 1.0),
        'c': nrm((BATCH, d), 1.0),
        'ctx': nrm((BATCH, CTX_LEN, d), 1.0),
        'c_ctx': nrm((d,), 1.0),
        'w_mod': nrm((nl, d, N_MOD * d), 0.5 * d ** -0.5),
        'b_mod': nrm((nl, N_MOD * d), 0.01),
        'norm_g': gain((nl, 6, d)),
        'ffn_w_in': nrm((nl, 2, d, 2 * f), d ** -0.5),
        'ffn_w_out': nrm((nl, 2, f, d), f ** -0.5),
        'w_in': nrm((nl, d, N_IN), d ** -0.5),
        'mla_g_q': gain((nl, MLA_Q_RANK)),
        'mla_g_kv': gain((nl, MLA_KV_RANK)),
        'mla_w_uq': nrm((nl, MLA_Q_RANK, MLA_HEADS, MLA_NOPE + MLA_ROPE), MLA_Q_RANK ** -0.5),
        'mla_w_ukv': nrm((nl, MLA_KV_RANK, MLA_HEADS, MLA_NOPE + MLA_V), MLA_KV_RANK ** -0.5),
        'na_rpb': nrm((nl, NA_HEADS, 2 * NA_WIN_R - 1, 2 * NA_WIN_C - 1), 0.2),
        'hy_conv_w': nrm((nl, HY_SHORT, 3 * HY_W), HY_SHORT ** -0.5),
        'hy_conv_b': nrm((nl, 3 * HY_W), 0.01),
        'hy_bias': nrm((nl, HY_W), 0.5),
        'hy_w1': nrm((nl, HY_EMB, HY_FILT), HY_EMB ** -0.5),
        'hy_b1': nrm((nl, HY_FILT), 0.1),
        'hy_freq1': gain((nl, HY_FILT), 0.1),
        'hy_w2': nrm((nl, HY_FILT, HY_FILT), HY_FILT ** -0.5),
        'hy_b2': nrm((nl, HY_FILT), 0.1),
        'hy_freq2': gain((nl, HY_FILT), 0.1),
        'hy_w3': nrm((nl, HY_FILT, 2 * HY_W), 0.1 * HY_FILT ** -0.5),
        's5_lam_re': -0.5 + nrm((nl, 2, S5_G, S5_P), 0.01),
        's5_lam_im': math.pi * n_idx + nrm((nl, 2, S5_G, S5_P), 0.01),
        's5_log_dt': jax.random.uniform(next(ks), (nl, 2, S5_G), jnp.float32, math.log(1e-3), math.log(1e-1)),
        's5_b_re': nrm((nl, 2, S5_G, S5_P, S5_H), (2 * S5_H) ** -0.5),
        's5_b_im': nrm((nl, 2, S5_G, S5_P, S5_H), (2 * S5_H) ** -0.5),
        's5_c_re': nrm((nl, 2, S5_G, S5_H, S5_P), 2.0 * S5_P ** -0.5),
        's5_c_im': nrm((nl, 2, S5_G, S5_H, S5_P), 2.0 * S5_P ** -0.5),
        's5_d': nrm((nl, S5_W), 0.5),
        's5_w_glu': nrm((nl, S5_W, 2 * S5_W), S5_W ** -0.5),
        's5_b_glu': nrm((nl, 2 * S5_W), 0.01),
        'w_branch': nrm((nl, N_BRANCH, BRANCH_W, d), BRANCH_W ** -0.5),
        'w_out': nrm((nl, d, d), d ** -0.5),
    }


def reference(x, c, ctx, c_ctx, w_mod, b_mod, norm_g, ffn_w_in, ffn_w_out, w_in,
              mla_g_q, mla_g_kv, mla_w_uq, mla_w_ukv, na_rpb,
              hy_conv_w, hy_conv_b, hy_bias, hy_w1, hy_b1, hy_freq1, hy_w2, hy_b2, hy_freq2, hy_w3,
              s5_lam_re, s5_lam_im, s5_log_dt, s5_b_re, s5_b_im, s5_c_re, s5_c_im, s5_d, s5_w_glu, s5_b_glu,
              w_branch, w_out):
    act_l = jax.nn.silu(c)
    act_c = jax.nn.silu(c_ctx)
    xc, xl = ctx, x
    for i in range(DEPTH):
        ctx_out = i < DEPTH - 1
        n_c = (N_MOD if ctx_out else N_MOD_CTX_LAST) * D_MODEL
        mod_l = (act_l @ w_mod[i] + b_mod[i])[:, None, :]
        mod_c = (act_c @ w_mod[i][:, :n_c] + b_mod[i][:n_c])[None, None, :]
        p = {
            'norm_g': norm_g[i], 'ffn_w_in': ffn_w_in[i], 'ffn_w_out': ffn_w_out[i], 'w_in': w_in[i],
            'mla_g_q': mla_g_q[i], 'mla_g_kv': mla_g_kv[i], 'mla_w_uq': mla_w_uq[i], 'mla_w_ukv': mla_w_ukv[i],
            'na_rpb': na_rpb[i],
            'hy_conv_w': hy_conv_w[i], 'hy_conv_b': hy_conv_b[i], 'hy_bias': hy_bias[i],
            'hy_w1': hy_w1[i], 'hy_b1': hy_b1[i], 'hy_freq1': hy_freq1[i],
            'hy_w2': hy_w2[i], 'hy_b2': hy_b2[i], 'hy_freq2': hy_freq2[i], 'hy_w3': hy_w3[i],
            's5_lam_re': s5_lam_re[i], 's5_lam_im': s5_lam_im[i], 's5_log_dt': s5_log_dt[i],
            's5_b_re': s5_b_re[i], 's5_b_im': s5_b_im[i], 's5_c_re': s5_c_re[i], 's5_c_im': s5_c_im[i],
            's5_d': s5_d[i], 's5_w_glu': s5_w_glu[i], 's5_b_glu': s5_b_glu[i],
            'w_branch': w_branch[i], 'w_out': w_out[i],
        }
        xc, xl = hybrid_layer(xc, xl, mod_c, mod_l, p, ctx_out)
    return xl
```

```python
import numpy as np
import concourse.bass as bass
import concourse.mybir as mybir
from concourse.bass_utils import run_bass_kernel_spmd

F32 = mybir.dt.float32
F32R = mybir.dt.float32r
BF16 = mybir.dt.bfloat16
AF = mybir.ActivationFunctionType
ALU = mybir.AluOpType
AX = mybir.AxisListType

SEM_ROLL = 30000


class Prog:
    def __init__(self, n_dma_sems=16):
        self.nc = bass.Bass("TRN2", target_bir_lowering=False)
        nc = self.nc
        self.eng = {"pe": nc.tensor, "act": nc.scalar, "dve": nc.vector,
                    "pool": nc.gpsimd, "sp": nc.sync}
        self._ctx = []
        self._scopes = []
        self._in_scope_alloc = False
        self._uid = 0
        self.sem = {}
        self.cnt = {}
        self.nsem = 0
        for e in self.eng:
            self._new_eng_sem(e)
        self.dma_sems = {}
        self.dma_rr = {}
        for q in ("sp", "pool", "act"):
            self.dma_sems[q] = []
            for i in range(n_dma_sems if q != "act" else 4):
                s = self._enter(nc.semaphore("dq_%s%d" % (q, i)))
                self.dma_sems[q].append([s, 0])
            self.dma_rr[q] = 0
        self.waited = {e: {} for e in self.eng}
        self.regions = {}
        self.n_inst = 0
        self.n_wait = 0

    def _enter(self, cm):
        v = cm.__enter__()
        if self._scopes and self._in_scope_alloc:
            self._scopes[-1].append((cm, v))
        else:
            self._ctx.append(cm)
        return v

    class _Scope:
        def __init__(self, P):
            self.P = P

        def __enter__(self):
            self.P._scopes.append([])
            return self

        def __exit__(self, *a):
            P = self.P
            items = P._scopes.pop()
            toks = []
            for cm, v in items:
                nm = v.name if hasattr(v, "name") else None
                for r in P.regions.pop(nm, []):
                    toks.append((r[5], r[6]))
            if toks:
                for e in P.eng:
                    P._emit_waits(e, toks)
            for cm, v in reversed(items):
                cm.__exit__(None, None, None)
            return False

    def scope(self):
        return Prog._Scope(self)

    def _new_eng_sem(self, e):
        s = self._enter(self.nc.semaphore("s_%s_%d" % (e, self.nsem)))
        self.nsem += 1
        self.sem[e] = s
        self.cnt[e] = 0

    def close(self):
        for cm in reversed(self._ctx):
            cm.__exit__(None, None, None)
        self._ctx = []

    def sbuf(self, name, shape, dtype=F32):
        self._uid += 1
        self._in_scope_alloc = True
        try:
            return self._enter(self.nc.sbuf_tensor("%s_%d" % (name, self._uid), list(shape), dtype))
        finally:
            self._in_scope_alloc = False

    def psum(self, name, shape=(128, 512), dtype=F32):
        return self._enter(self.nc.psum_tensor(name, list(shape), dtype))

    def dram(self, name, shape, dtype=F32, kind="Internal"):
        return self.nc.dram_tensor(name, list(shape), dtype, kind=kind)

    @staticmethod
    def _region(ap):
        space = str(ap.space)
        name = ap.name
        aps = ap.ap
        off = int(ap.offset)
        if "DRAM" in space:
            ext = sum((c - 1) * abs(s) for s, c in aps)
            neg = sum((c - 1) * s for s, c in aps if s < 0)
            lo = off + neg
            return name, 0, 1, lo, lo + ext + 1, False
        pstep, pcnt = aps[0]
        if pstep == 0:
            pstep = 1 << 40
        p0 = off // pstep if pstep < (1 << 39) else 0
        lo = off - p0 * pstep if pstep < (1 << 39) else off
        ext = sum((c - 1) * abs(s) for s, c in aps[1:])
        is_psum = "PSUM" in space
        if is_psum:
            return name, 0, 128, 0, 1 << 30, True
        return name, p0, p0 + pcnt, lo, lo + ext + 1, False

    def _deps(self, reads, writes):
        toks = []
        info = []
        for ap, is_w in [(a, False) for a in reads] + [(a, True) for a in writes]:
            name, p0, p1, lo, hi, excl = self._region(ap)
            w = is_w or excl
            lst = self.regions.setdefault(name, [])
            for r in lst:
                if r[1] <= p0 or p1 <= r[0] or r[3] <= lo or hi <= r[2]:
                    continue
                if w or r[4]:
                    toks.append((r[5], r[6]))
            info.append((name, p0, p1, lo, hi, w))
        return toks, info

    def _record(self, info, sem, val):
        for name, p0, p1, lo, hi, w in info:
            lst = self.regions[name]
            if w:
                lst[:] = [r for r in lst if not (p0 <= r[0] and r[1] <= p1 and lo <= r[2] and r[3] <= hi)]
                lst.append([p0, p1, lo, hi, True, sem, val])
            else:
                for r in lst:
                    if (not r[4]) and r[0] == p0 and r[1] == p1 and r[2] == lo and r[3] == hi and r[5] is sem:
                        r[6] = max(r[6], val)
                        break
                else:
                    lst.append([p0, p1, lo, hi, False, sem, val])

    def _emit_waits(self, e, toks):
        best = {}
        for s, v in toks:
            k = id(s)
            if k not in best or best[k][1] < v:
                best[k] = (s, v)
        wd = self.waited[e]
        for k, (s, v) in best.items():
            if wd.get(k, 0) >= v:
                continue
            self.eng[e].wait_ge(s, v)
            wd[k] = v
            self.n_wait += 1

    def op(self, e, fn, reads, writes):
        toks, info = self._deps(reads, writes)
        if e == "pe":
            toks = [t for t in toks if t[0] is not self.sem["pe"]]
        self._emit_waits(e, toks)
        ins = fn()
        if self.cnt[e] >= SEM_ROLL:
            self._new_eng_sem(e)
        self.cnt[e] += 1
        ins.then_inc(self.sem[e], 1)
        self._record(info, self.sem[e], self.cnt[e])
        self.n_inst += 1
        return ins

    def dma(self, out, in_, q="sp", **kw):
        toks, info = self._deps([in_], [out])
        ent = self.dma_sems[q][self.dma_rr[q]]
        self.dma_rr[q] = (self.dma_rr[q] + 1) % len(self.dma_sems[q])
        s = ent[0]
        if ent[1] > 0:
            toks.append((s, ent[1]))
        self._emit_waits(q, toks)
        ent[1] += 16
        ins = self.eng[q].dma_start(out=out, in_=in_, **kw)
        ins.then_inc(s, 16)
        self._record(info, s, ent[1])
        self.n_inst += 1
        return ins

    def finish(self, e="sp"):
        toks = []
        for lst in self.regions.values():
            for r in lst:
                toks.append((r[5], r[6]))
        self._emit_waits(e, toks)

    def mm(self, out, lhsT, rhs, start=True, stop=True, **kw):
        return self.op("pe", lambda: self.nc.tensor.matmul(out, lhsT, rhs, start=start, stop=stop, **kw),
                       [lhsT, rhs], [out])

    def transpose(self, out, in_, ident):
        return self.op("pe", lambda: self.nc.tensor.transpose(out, in_, ident), [in_, ident], [out])

    def act(self, out, in_, func, bias=None, scale=1.0, e="act", **kw):
        reads = [in_]
        if bias is not None and not isinstance(bias, (int, float)):
            reads.append(bias)
        if not isinstance(scale, (int, float)):
            reads.append(scale)
        kw2 = dict(kw)
        if bias is not None:
            kw2["bias"] = bias
        writes = [out]
        if "accum_out" in kw2:
            writes.append(kw2["accum_out"])
        return self.op(e, lambda: self.nc.scalar.activation(out=out, in_=in_, func=func, scale=scale, **kw2),
                       reads, writes)

    def _veng(self, e):
        return self.nc.vector if e == "dve" else self.nc.gpsimd

    def tt(self, out, in0, in1, op, e="dve"):
        return self.op(e, lambda: self._veng(e).tensor_tensor(out=out, in0=in0, in1=in1, op=op), [in0, in1], [out])

    def ts(self, out, in0, s1, s2=None, op0=ALU.mult, op1=None, e="dve", **kw):
        reads = [in0] + [s for s in (s1, s2) if s is not None and not isinstance(s, (int, float))]
        writes = [out] + ([kw["accum_out"]] if "accum_out" in kw else [])
        if op1 is None:
            return self.op(e, lambda: self._veng(e).tensor_scalar(out=out, in0=in0, scalar1=s1, scalar2=None, op0=op0, **kw),
                           reads, writes)
        return self.op(e, lambda: self._veng(e).tensor_scalar(out=out, in0=in0, scalar1=s1, scalar2=s2, op0=op0, op1=op1, **kw),
                       reads, writes)

    def stt(self, out, in0, scalar, in1, op0, op1, e="dve"):
        reads = [in0, in1] + ([] if isinstance(scalar, (int, float)) else [scalar])
        return self.op(e, lambda: self.nc.vector.scalar_tensor_tensor(out=out, in0=in0, scalar=scalar, in1=in1, op0=op0, op1=op1),
                       reads, [out])

    def copy(self, out, in_, e="dve"):
        if e == "act":
            return self.op("act", lambda: self.nc.scalar.copy(out=out, in_=in_), [in_], [out])
        return self.op(e, lambda: self._veng(e).tensor_copy(out=out, in_=in_), [in_], [out])

    def memset(self, ap, val, e="dve"):
        return self.op(e, lambda: self._veng(e).memset(ap, val), [], [ap])

    def recip(self, out, in_):
        return self.op("dve", lambda: self.nc.vector.reciprocal(out=out, in_=in_), [in_], [out])

    def scan(self, out, d0, d1, initial, op0=ALU.mult, op1=ALU.add):
        reads = [d0, d1] + ([] if isinstance(initial, (int, float)) else [initial])
        return self.op("dve", lambda: self.nc.vector.tensor_tensor_scan(out=out, data0=d0, data1=d1, initial=initial, op0=op0, op1=op1),
                       reads, [out])


import numpy as np
def rope_tables(n=2048, grid_w=64, base=10000.0):
    q = 16
    t = np.arange(n)
    pos = np.stack([t // grid_w, t % grid_w], -1).astype(np.float32)
    inv = (base ** (-np.arange(q, dtype=np.float32) / q)).astype(np.float32)
    ang = pos[:, :, None] * inv
    C = np.zeros((64, n), np.float32); S = np.zeros((64, n), np.float32)
    for a in range(2):
        for hf in range(2):
            for j in range(q):
                f = a * 32 + hf * 16 + j
                C[f] = np.cos(ang[:, a, j])
                S[f] = (-1.0 if hf == 0 else 1.0) * np.sin(ang[:, a, j])
    return C, S

NEG = -30000.0
def na_geometry():
    rows = 32
    start = lambda r: min(max(r - 4, 0), rows - 8)
    geo = []
    for i in range(16):
        rs = [2 * i, 2 * i + 1]
        lo = min(start(r) for r in rs); hi = max(start(r) + 7 for r in rs)
        lst = []
        for j in range(lo // 2, hi // 2 + 1):
            codes = []
            for r in rs:
                v0 = start(r) <= 2 * j <= start(r) + 7
                v1 = start(r) <= 2 * j + 1 <= start(r) + 7
                code = {(True, True): 0, (False, True): 1, (True, False): 2, (False, False): 3}[(v0, v1)]
                codes.append(code)
            dr0 = 2 * (j - i)
            idxp = 7 - dr0
            assert 0 <= idxp <= 14, (i, j, idxp)
            lst.append((j, idxp, codes[0] * 4 + codes[1]))
        geo.append(lst)
    return geo

def na_const_tables():
    mv = np.zeros((2, 16, 128), np.float32)
    vecs = [np.zeros(128), np.r_[np.full(64, NEG), np.zeros(64)], np.r_[np.zeros(64), np.full(64, NEG)], np.full(128, NEG)]
    for c0 in range(4):
        for c1 in range(4):
            mv[0, c0 * 4 + c1] = vecs[c0]; mv[1, c0 * 4 + c1] = vecs[c1]
    sel = np.zeros((2, 128), np.float32); sel[0, :64] = 1; sel[1, 64:] = 1
    col = np.arange(64)
    c0 = np.clip(col - 8, 0, 48)
    inwin = (col[None, :] >= c0[:, None]) & (col[None, :] < c0[:, None] + 16)
    cm = np.where(inwin.T, 0.0, NEG).astype(np.float32)
    cm = np.concatenate([cm, cm], 0)
    return mv, sel, cm

def na_bias_gather(rpb):
    kc = np.arange(64)[:, None]; qc = np.arange(64)[None, :]
    dc = np.clip(kc - qc + 15, 0, 30)
    G = np.zeros((128, 8, 16, 64), np.float32)
    for idxp in range(16):
        for krl in range(2):
            dr = 7 - idxp + krl
            row = dr + 7
            if not (0 <= row <= 14):
                row = 0
            G[krl * 64:(krl + 1) * 64, :, idxp, :] = np.transpose(rpb[:, row][:, dc], (1, 0, 2))
    return G

def hyena_consts(N):
    t = np.arange(N, dtype=np.float64)[:, None]; f = np.arange(N, dtype=np.float64)[None, :]
    ang = 2.0 * np.pi * (f + 0.5) * t / (2.0 * N)
    Cm = np.cos(ang).astype(np.float32); Sm = np.sin(ang).astype(np.float32)
    bands = 16
    tt = np.arange(N, dtype=np.float32)
    t01 = np.linspace(0.0, 1.0, N, dtype=np.float32)[:, None]
    a2 = (np.float32(2.0 * np.pi) * tt / np.float32(N))[:, None] * np.linspace(1e-4, bands - 1, bands, dtype=np.float32)
    z = np.concatenate([t01, np.cos(a2), -np.sin(a2)], -1).astype(np.float32)
    max_decay = np.log(1e-2) / 0.3; min_decay = np.log(1e-2) / 1.5
    deltas = np.abs(np.linspace(min_decay, max_decay, 512, dtype=np.float32))
    decay = np.exp(-t01 * deltas).astype(np.float32)
    return dict(c=Cm, s=Sm, ct=np.ascontiguousarray(Cm.T), st=np.ascontiguousarray(Sm.T),
                zT=np.ascontiguousarray(z.T), decay=decay)

def s5_masks():
    m = np.zeros((2, 2, 128, 256), np.float32)
    for a in range(2):
        for il in range(8):
            i = a * 8 + il
            for j in range(16):
                if j >= i:
                    m[0, a, il * 16:(il + 1) * 16, j * 16:(j + 1) * 16] = 1.0
                if j <= i:
                    m[1, a, il * 16:(il + 1) * 16, j * 16:(j + 1) * 16] = 1.0
    return m

import math
import numpy as np

D = 1024
KC = 8
NCTX = 256
NLAT = 2048
T = NCTX + NLAT
FH = 2816
FHC = FH // 128
EPS = 1e-6
N_IN = 8256


class K:
    pass


def declare_inputs(P, nl):
    nc = P.nc
    I = {}

    def inp(name, shape):
        I[name] = nc.dram_tensor(name, list(shape), F32, kind="ExternalInput").ap()

    inp("xT", [D, T])
    inp("cvec", [D, 2])
    inp("ident", [128, 128])
    inp("w_mod", [nl, D, 9 * D])
    inp("b_mod", [nl, 9 * D])
    inp("norm_g", [nl, 6, D])
    inp("ffn_w_in", [nl, 2, D, 2 * FH])
    inp("ffn_w_out", [nl, 2, FH, D])
    return I


def setup_common(P, k):
    k.ident = P.sbuf("ident", [128, 128], F32)
    P.dma(k.ident[:], k.I["ident"])
    k.ident_bf = P.sbuf("ident_bf", [128, 128], BF16)
    P.copy(k.ident_bf[:], k.ident[:])
    k.ones_bf = P.sbuf("ones_bf", [128, 128], BF16)
    P.memset(k.ones_bf[:], 1.0)
    k.eps_col = P.sbuf("eps_col", [128, 1], F32)
    P.memset(k.eps_col[:], EPS)
    k.xres = P.nc.dram_tensor("xres", [128, KC, T], F32, kind="Internal").ap()
    k.ps = [P.psum("psb%d" % i) for i in range(8)]
    cv = P.sbuf("cv", [128, KC, 2], F32)
    P.dma(cv[:], k.I["cvec"].rearrange("(c p) n -> p c n", p=128))
    k.actv = P.sbuf("actv", [128, KC, 2], BF16)
    P.act(k.actv[:], cv[:], AF.Silu)
    k.modT = P.sbuf("modT", [128, 72, 2], F32)
    k.normg = P.sbuf("normg", [128, 48], F32)
    k.Asc = P.sbuf("Asc", [128, 3, KC, 2], F32)
    k.Bsh = P.sbuf("Bsh", [128, 3, KC, 2], F32)
    k.Gg = P.sbuf("Gg", [128, 3, KC, 2], F32)


def layer_mods(P, k, l):
    nc = P.nc
    wm = k.I["w_mod"][l].rearrange("(c p) n -> p c n", p=128)
    bm_t = P.sbuf("bm_t", [72, 128], F32)
    P.dma(bm_t[:], k.I["b_mod"][l].rearrange("(m f) -> m f", f=128))
    ng_t = P.sbuf("ng_t", [48, 128], F32)
    P.dma(ng_t[:], k.I["norm_g"][l].rearrange("g (c f) -> (g c) f", f=128))
    ps_m = k.ps[0]
    ps_t = k.ps[1]
    wt = [P.sbuf("wmod%d" % i, [128, KC, 512], BF16) for i in range(2)]
    for j in range(18):
        w = wt[j % 2]
        P.dma(w[:], wm[:, :, j * 512:(j + 1) * 512], q="pool")
        for mm in range(4):
            m = j * 4 + mm
            for c in range(KC):
                P.mm(ps_m[:, 2 * m:2 * m + 2], w[:, c, mm * 128:(mm + 1) * 128], k.actv[:, c, :],
                     start=(c == 0), stop=(c == KC - 1))
    P.transpose(ps_t[:, 0:72], bm_t[:], k.ident[0:72, 0:72])
    bmT = P.sbuf("bmT", [128, 72], F32)
    P.copy(bmT[:], ps_t[:, 0:72])
    P.tt(k.modT[:], ps_m[:, 0:144].rearrange("p (m s) -> p m s", s=2),
         bmT[:].unsqueeze(2).broadcast_to([128, 72, 2]), ALU.add)
    P.transpose(ps_t[:, 128:176], ng_t[:], k.ident[0:48, 0:48])
    P.copy(k.normg[:], ps_t[:, 128:176])
    for s in range(3):
        base = 3 * s
        gpre = k.normg[:, (2 * s) * 8:(2 * s + 1) * 8].unsqueeze(2).broadcast_to([128, KC, 2])
        gpost = k.normg[:, (2 * s + 1) * 8:(2 * s + 2) * 8].unsqueeze(2).broadcast_to([128, KC, 2])
        P.stt(k.Asc[:, s], k.modT[:, (base + 1) * 8:(base + 2) * 8, :], 1.0, gpre, ALU.add, ALU.mult)
        P.copy(k.Bsh[:, s], k.modT[:, base * 8:(base + 1) * 8, :])
        P.stt(k.Gg[:, s], k.modT[:, (base + 2) * 8:(base + 3) * 8, :], (1.0 if s == 1 else 0.5), gpost, ALU.mult, ALU.mult)


def sumsq_rstd(P, k, src_fn, nchunks, subs, rstd, ps_ss, sq_tiles, inv_n):
    for (o, w) in subs:
        for c in range(nchunks):
            sq = sq_tiles[c % len(sq_tiles)]
            P.act(sq[:, :w], src_fn(c, o, w), AF.Square)
            P.mm(ps_ss[:, :w], k.ones_bf[:], sq[:, :w], start=(c == 0), stop=(c == nchunks - 1))
        P.act(rstd[:, o:o + w], ps_ss[:, :w], AF.Sqrt, bias=k.eps_col[:], scale=inv_n)
        P.recip(rstd[:, o:o + w], rstd[:, o:o + w])


def ffn_sublayer(P, k, l, s, blocks):
    fi = s // 2
    w_in = k.I["ffn_w_in"][l, fi].rearrange("(c p) n -> p c n", p=128)
    w_out = k.I["ffn_w_out"][l, fi].rearrange("(j p) n -> p j n", p=128)
    maxw = max(sum(w for (_, w, _) in b) for b in blocks)
    hb = P.sbuf("ffn_h", [128, KC, maxw], BF16)
    xb = P.sbuf("ffn_x", [128, KC, maxw], F32)
    gb = P.sbuf("ffn_g", [128, FHC, maxw], BF16)
    ob = P.sbuf("ffn_o", [128, KC, maxw], BF16)
    rstd = P.sbuf("ffn_rstd", [128, maxw], F32)
    tmp = [P.sbuf("ffn_tmp%d" % i, [128, 512], F32) for i in range(2)]
    sq = [P.sbuf("ffn_sq%d" % i, [128, 512], BF16) for i in range(2)]
    sl = [P.sbuf("ffn_sl%d" % i, [128, 512], F32) for i in range(2)]
    wi = [P.sbuf("ffn_wi%d" % i, [128, KC, 512], BF16) for i in range(2)]
    wo = [P.sbuf("ffn_wo%d" % i, [128, FHC, 128], BF16) for i in range(2)]
    ps_ss = k.ps[0]
    ps_a = [k.ps[1], k.ps[2]]
    ps_b = [k.ps[3], k.ps[4]]
    ps_o = [k.ps[5], k.ps[6]]
    cnt = 0
    for blk in blocks:
        subs = []
        o = 0
        for (c0, w, st) in blk:
            subs.append((o, c0, w, st))
            o += w
        for (o, c0, w, st) in subs:
            P.dma(xb[:, :, o:o + w], k.xres[:, :, c0:c0 + w])
        for (o, c0, w, st) in subs:
            sumsq_rstd(P, k, lambda c, oo, ww: xb[:, c, oo:oo + ww], KC,
                       [(o, w)], rstd, ps_ss, sq, 1.0 / D)
            for c in range(KC):
                t = tmp[c % 2]
                P.tt(t[:, :w], xb[:, c, o:o + w], rstd[:, o:o + w], ALU.mult, e="pool")
                P.ts(hb[:, c, o:o + w], t[:, :w], k.Asc[:, s, c, st:st + 1], k.Bsh[:, s, c, st:st + 1],
                     op0=ALU.mult, op1=ALU.add)
        for j2 in range(FHC // 2):
            w = wi[j2 % 2]
            P.dma(w[:, :, 0:256], w_in[:, :, j2 * 256:(j2 + 1) * 256], q="pool")
            P.dma(w[:, :, 256:512], w_in[:, :, FH + j2 * 256:FH + (j2 + 1) * 256], q="pool")
            for jj in range(2):
                j = 2 * j2 + jj
                for (o, c0, ww, st) in subs:
                    pa = ps_a[cnt % 2]
                    pb = ps_b[cnt % 2]
                    slt = sl[cnt % 2]
                    cnt += 1
                    for c in range(KC):
                        P.mm(pa[:, :ww], w[:, c, jj * 128:(jj + 1) * 128], hb[:, c, o:o + ww],
                             start=(c == 0), stop=(c == KC - 1))
                    for c in range(KC):
                        P.mm(pb[:, :ww], w[:, c, 256 + jj * 128:256 + (jj + 1) * 128], hb[:, c, o:o + ww],
                             start=(c == 0), stop=(c == KC - 1))
                    P.act(slt[:, :ww], pa[:, :ww], AF.Silu)
                    P.tt(gb[:, j, o:o + ww], slt[:, :ww], pb[:, :ww], ALU.mult)
        for c in range(KC):
            w = wo[c % 2]
            P.dma(w[:], w_out[:, :, c * 128:(c + 1) * 128], q="pool")
            for (o, c0, ww, st) in subs:
                po = ps_o[cnt % 2]
                cnt += 1
                for j in range(FHC):
                    P.mm(po[:, :ww], w[:, j, :], gb[:, j, o:o + ww], start=(j == 0), stop=(j == FHC - 1))
                P.copy(ob[:, c, o:o + ww], po[:, :ww], e="act")
        for (o, c0, w, st) in subs:
            sumsq_rstd(P, k, lambda c, oo, ww: ob[:, c, oo:oo + ww], KC, [(o, w)], rstd, ps_ss, sq, 1.0 / D)
            for c in range(KC):
                t = tmp[c % 2]
                P.stt(t[:, :w], ob[:, c, o:o + w], k.Gg[:, s, c, st:st + 1], rstd[:, o:o + w], ALU.mult, ALU.mult)
                P.tt(xb[:, c, o:o + w], xb[:, c, o:o + w], t[:, :w], ALU.add, e="pool")
            P.dma(k.xres[:, :, c0:c0 + w], xb[:, :, o:o + w], q="sp")


FULL_BLOCKS = [
    [(0, 256, 1), (256, 512, 0)],
    [(768, 384, 0), (1152, 384, 0)],
    [(1536, 384, 0), (1920, 384, 0)],
]
LAT_BLOCKS = [
    [(256, 384, 0), (640, 384, 0)],
    [(1024, 384, 0), (1408, 384, 0)],
    [(1792, 512, 0)],
]


CT0 = 2
LT0 = 260
HW = 2310
ALLSUBS = [(0, 256, 1), (256, 512, 0), (768, 512, 0), (1280, 512, 0), (1792, 512, 0)]
LATSUBS = ALLSUBS[1:]
C_CKV, C_KR, C_NK, C_NV, C_U, C_CQ, C_NQ, C_HY, C_GT = 0, 256, 320, 832, 1344, 1856, 2112, 2624, 4160


def hcol(xc):
    return xc + CT0 if xc < NCTX else xc - NCTX + LT0


def declare_mixer_inputs(P, I, nl):
    nc = P.nc

    def inp(name, shape):
        I[name] = nc.dram_tensor(name, list(shape), F32, kind="ExternalInput").ap()
    inp("w_in", [nl, D, N_IN])
    inp("mla_g_q", [nl, 256]); inp("mla_g_kv", [nl, 256])
    inp("mla_w_uq", [nl, 256, 4, 192]); inp("mla_w_ukv", [nl, 256, 4, 256])
    inp("w_branch", [nl, 4, 512, D]); inp("w_out", [nl, D, D])
    inp("rope_c", [64, NLAT]); inp("rope_s", [64, NLAT])


def mixer_setup(P, k):
    nc = P.nc
    k.br = [nc.dram_tensor("br%d" % n, [512, T], BF16, kind="Internal").ap() for n in range(4)]
    k.hmx = nc.dram_tensor("hmx", [128, KC, HW], BF16, kind="Internal").ap()


def alloc_hmix(P, k):
    k.hmix = P.sbuf("hmix", [128, KC, HW], BF16)
    for c0 in (0, 258, 2308):
        P.memset(k.hmix[:, :, c0:c0 + 2], 0.0)


def mixer_modnorm(P, k):
    with P.scope():
        rstd = P.sbuf("mn_rstd", [128, 512], F32)
        tmp = [P.sbuf("mn_tmp%d" % i, [128, 512], F32) for i in range(2)]
        sq = [P.sbuf("mn_sq%d" % i, [128, 512], BF16) for i in range(2)]
        xt = [P.sbuf("mn_x%d" % i, [128, KC, 512], F32) for i in range(2)]
        for si, (c0, w, st) in enumerate(ALLSUBS):
            xb = xt[si % 2]
            P.dma(xb[:, :, :w], k.xres[:, :, c0:c0 + w])
            sumsq_rstd(P, k, lambda c, oo, ww, xb=xb: xb[:, c, oo:oo + ww], KC, [(0, w)], rstd, k.ps[0], sq, 1.0 / D)
            h0 = hcol(c0)
            for c in range(KC):
                t = tmp[c % 2]
                P.tt(t[:, :w], xb[:, c, 0:w], rstd[:, 0:w], ALU.mult, e="pool")
                P.ts(k.hmix[:, c, h0:h0 + w], t[:, :w], k.Asc[:, 1, c, st:st + 1], k.Bsh[:, 1, c, st:st + 1],
                     op0=ALU.mult, op1=ALU.add)
    P.dma(k.hmx, k.hmix[:], q="sp")


def load_col_vec(P, dst, src_1d, nchunk):
    P.dma(dst, src_1d.rearrange("(c p) -> p c", p=128), allow_slow_non_contiguous=True)


def mla_branch(P, k, l, ctx_out):
    nc = P.nc
    I = k.I
    w_in = I["w_in"][l].rearrange("(c p) n -> p c n", p=128)
    SC = 192.0 ** -0.5
    subs = ALLSUBS
    qsubs = ALLSUBS if ctx_out else LATSUBS
    with P.scope():
        wckv = P.sbuf("wckv", [128, KC, 256], BF16)
        P.dma(wckv[:], w_in[:, :, C_CKV:C_CKV + 256], q="pool")
        wkr = P.sbuf("wkr", [128, KC, 128], BF16)
        P.dma(wkr[:, :, 0:64], w_in[:, :, C_KR:C_KR + 64], q="pool")
        for a in range(2):
            for hf in range(2):
                P.dma(wkr[:, :, 64 + a * 32 + hf * 16:64 + a * 32 + hf * 16 + 16],
                      w_in[:, :, C_KR + a * 32 + (1 - hf) * 16:C_KR + a * 32 + (1 - hf) * 16 + 16], q="pool")
        wcq = P.sbuf("wcq", [128, KC, 256], BF16)
        P.dma(wcq[:], w_in[:, :, C_CQ:C_CQ + 256], q="pool")
        wukv = P.sbuf("wukv", [128, 2, 4, 256], BF16)
        P.dma(wukv[:], I["mla_w_ukv"][l].rearrange("(c p) h e -> p c h e", p=128), q="pool")
        wuq = P.sbuf("wuq", [128, 2, 4, 192], BF16)
        P.dma(wuq[:], I["mla_w_uq"][l].rearrange("(c p) h e -> p c h e", p=128), q="pool")
        wuqs = P.sbuf("wuqs", [128, 2, 4, 64], BF16)
        uq_r = I["mla_w_uq"][l].rearrange("(c p) h e -> p c h e", p=128)
        for a in range(2):
            for hf in range(2):
                for c in range(2):
                    P.dma(wuqs[:, c, :, a * 32 + hf * 16:a * 32 + hf * 16 + 16],
                          uq_r[:, c, :, 128 + a * 32 + (1 - hf) * 16:128 + a * 32 + (1 - hf) * 16 + 16], q="pool")
        gkv = P.sbuf("gkv", [128, 2], F32)
        load_col_vec(P, gkv[:], I["mla_g_kv"][l], 2)
        gq = P.sbuf("gq", [128, 2], F32)
        load_col_vec(P, gq[:], I["mla_g_q"][l], 2)
        ropc_t = [P.sbuf("ropc%d" % i, [64, 512], F32) for i in range(2)]
        rops_t = [P.sbuf("rops%d" % i, [64, 512], F32) for i in range(2)]
        rcnt = [0]

        def rope_tabs(l0, w):
            i = rcnt[0] % 2
            rcnt[0] += 1
            P.dma(ropc_t[i][:, :w], I["rope_c"][:, l0:l0 + w])
            P.dma(rops_t[i][:, :w], I["rope_s"][:, l0:l0 + w])
            return ropc_t[i], rops_t[i]
        nkv = P.sbuf("nkv", [128, 2, T], BF16)
        nq = P.sbuf("nq", [128, 2, T], BF16)
        krope = P.sbuf("krope", [64, T], BF16)
        vall = P.sbuf("vall", [128, 18, 128], BF16)
        aT = P.sbuf("aT", [128, T], BF16)
        raw = P.sbuf("raw", [128, 2, 512], F32)
        rstd = P.sbuf("rstd", [128, 512], F32)
        sq = [P.sbuf("sq%d" % i, [128, 512], BF16) for i in range(2)]
        t1 = P.sbuf("t1", [128, 512], F32)
        t2 = P.sbuf("t2", [128, 512], F32)
        ps = k.ps

        def lowrank_norm(wt, gvec, dst):
            for (c0, w, st) in (subs if dst is nkv else qsubs):
                h0 = hcol(c0)
                for m in range(2):
                    for c in range(KC):
                        P.mm(ps[1 + m][:, :w], wt[:, c, m * 128:(m + 1) * 128], k.hmix[:, c, h0:h0 + w],
                             start=(c == 0), stop=(c == KC - 1))
                    P.copy(raw[:, m, :w], ps[1 + m][:, :w], e="act")
                sumsq_rstd(P, k, lambda c, oo, ww: raw[:, c, oo:oo + ww], 2, [(0, w)], rstd, ps[0], sq, 1.0 / 256)
                for m in range(2):
                    P.stt(dst[:, m, c0:c0 + w], raw[:, m, :w], gvec[:, m:m + 1], rstd[:, :w], ALU.mult, ALU.mult)

        lowrank_norm(wckv, gkv, nkv)
        lowrank_norm(wcq, gq, nq)
        for (c0, w, st) in subs:
            h0 = hcol(c0)
            for hh in range(2):
                for c in range(KC):
                    P.mm(ps[1 + hh][0:64, :w], wkr[:, c, hh * 64:(hh + 1) * 64], k.hmix[:, c, h0:h0 + w],
                         start=(c == 0), stop=(c == KC - 1))
            if st == 1:
                P.copy(krope[:, c0:c0 + w], ps[1][0:64, :w], e="act")
            else:
                l0 = c0 - NCTX
                rc, rs = rope_tabs(l0, w)
                P.tt(t1[0:64, :w], ps[1][0:64, :w], rc[:, :w], ALU.mult)
                P.tt(t2[0:64, :w], ps[2][0:64, :w], rs[:, :w], ALU.mult)
                P.tt(krope[:, c0:c0 + w], t1[0:64, :w], t2[0:64, :w], ALU.add, e="pool")
        knT = P.sbuf("knT", [128, T], BF16)
        qnT = P.sbuf("qnT", [128, T], BF16)
        qrope = P.sbuf("qrope", [64, T], BF16)
        pT = [P.sbuf("pT%d" % i, [128, 512], BF16) for i in range(2)]
        rden = P.sbuf("rden", [128, 512], F32)
        for hd in range(4):
            for tc in range(18):
                pv = ps[1 + tc % 2]
                for c in range(2):
                    P.mm(pv[:, 0:128], nkv[:, c, tc * 128:(tc + 1) * 128], wukv[:, c, hd, 128:256], start=(c == 0), stop=(c == 1))
                P.copy(vall[:, tc, :], pv[:, 0:128], e=("act" if tc % 2 else "dve"))
            for (c0, w, st) in subs:
                for c in range(2):
                    P.mm(ps[1][:, :w], wukv[:, c, hd, 0:128], nkv[:, c, c0:c0 + w], start=(c == 0), stop=(c == 1))
                P.copy(knT[:, c0:c0 + w], ps[1][:, :w], e="act")
            for (c0, w, st) in qsubs:
                for c in range(2):
                    P.mm(ps[1][:, :w], wuq[:, c, hd, 0:128], nq[:, c, c0:c0 + w], start=(c == 0), stop=(c == 1))
                P.copy(qnT[:, c0:c0 + w], ps[1][:, :w], e="act")
                for c in range(2):
                    P.mm(ps[2][0:64, :w], wuq[:, c, hd, 128:192], nq[:, c, c0:c0 + w], start=(c == 0), stop=(c == 1))
                if st == 1:
                    P.copy(qrope[:, c0:c0 + w], ps[2][0:64, :w], e="dve")
                else:
                    for c in range(2):
                        P.mm(ps[3][0:64, :w], wuqs[:, c, hd, :], nq[:, c, c0:c0 + w], start=(c == 0), stop=(c == 1))
                    l0 = c0 - NCTX
                    rc, rs = rope_tabs(l0, w)
                    P.tt(t1[0:64, :w], ps[2][0:64, :w], rc[:, :w], ALU.mult)
                    P.tt(t2[0:64, :w], ps[3][0:64, :w], rs[:, :w], ALU.mult)
                    P.tt(qrope[:, c0:c0 + w], t1[0:64, :w], t2[0:64, :w], ALU.add, e="pool")
            cnt = 0
            for (c0, w, st) in qsubs:
                kcs = list(range(2)) if st == 1 else list(range(18))
                for i, kc in enumerate(kcs):
                    pss = ps[4 + cnt % 2]
                    p = pT[cnt % 2]
                    cnt += 1
                    P.mm(pss[:, :w], knT[:, kc * 128:(kc + 1) * 128], qnT[:, c0:c0 + w], start=True, stop=False)
                    P.mm(pss[:, :w], krope[:, kc * 128:(kc + 1) * 128], qrope[:, c0:c0 + w], start=False, stop=True)
                    P.act(p[:, :w], pss[:, :w], AF.Exp, scale=SC)
                    P.mm(ps[6][:, :w], vall[:, kc, :], p[:, :w], start=(i == 0), stop=(i == len(kcs) - 1))
                    P.mm(ps[7][:, :w], k.ones_bf[:], p[:, :w], start=(i == 0), stop=(i == len(kcs) - 1))
                P.recip(rden[:, :w], ps[7][:, :w])
                P.tt(aT[:, c0:c0 + w], ps[6][:, :w], rden[:, :w], ALU.mult)
            cols0 = 0 if ctx_out else NCTX
            P.dma(k.br[0][hd * 128:(hd + 1) * 128, cols0:T], aT[:, cols0:T], q="sp")


def declare_na_inputs(P, I, nl):
    nc = P.nc

    def inp(name, shape):
        I[name] = nc.dram_tensor(name, list(shape), F32, kind="ExternalInput").ap()
    inp("na_G", [nl, 128, 8 * 16 * 64])
    inp("na_mv", [2, 16 * 128]); inp("na_sel", [2, 128]); inp("na_cm", [128, 64])


def na_branch(P, k, l, ctx_out, geo):
    nc = P.nc
    I = k.I
    w_in = I["w_in"][l].rearrange("(c p) n -> p c n", p=128)
    ps = k.ps
    with P.scope():
        BP = P.sbuf("na_BP", [128, 8, 16, 64], BF16)
        cm = P.sbuf("na_cm", [128, 64], F32)
        P.dma(cm[:], I["na_cm"])
        gt = [P.sbuf("na_gt%d" % i, [128, 16, 64], F32) for i in range(2)]
        Gr = I["na_G"][l].rearrange("p (h i q) -> p h i q", h=8, i=16)
        for h in range(8):
            P.dma(gt[h % 2][:], Gr[:, h])
            P.tt(BP[:, h], gt[h % 2][:], cm[:].unsqueeze(1).broadcast_to([128, 16, 64]), ALU.add)
        mvf = P.sbuf("na_mvf", [2, 16 * 128], F32)
        P.dma(mvf[:], I["na_mv"])
        mv = P.sbuf("na_mv", [2, 16 * 128], BF16)
        P.copy(mv[:], mvf[:])
        self_f = P.sbuf("na_self", [2, 128], F32)
        P.dma(self_f[:], I["na_sel"])
        sel = P.sbuf("na_sel", [2, 128], BF16)
        P.copy(sel[:], self_f[:])
        wk = P.sbuf("na_wk", [128, KC, 128], BF16)
        wq = P.sbuf("na_wq", [128, KC, 128], BF16)
        wv = P.sbuf("na_wv", [128, KC, 128], BF16)
        KT = P.sbuf("na_KT", [128, T], BF16)
        QT = P.sbuf("na_QT", [128, T], BF16)
        V = P.sbuf("na_V", [128, 18, 128], BF16)
        dT = P.sbuf("na_dT", [128, T], BF16)
        pT = [P.sbuf("na_pT%d" % i, [128, 128], BF16) for i in range(3)]
        rden = P.sbuf("na_rden", [128, 128], F32)
        qsubs = ALLSUBS if ctx_out else LATSUBS
        cnt = 0
        for hp in range(4):
            P.dma(wk[:], w_in[:, :, C_NK + hp * 128:C_NK + (hp + 1) * 128], q="pool")
            P.dma(wq[:], w_in[:, :, C_NQ + hp * 128:C_NQ + (hp + 1) * 128], q="pool")
            P.dma(wv[:], w_in[:, :, C_NV + hp * 128:C_NV + (hp + 1) * 128], q="pool")
            for (c0, w, st) in ALLSUBS:
                h0 = hcol(c0)
                for c in range(KC):
                    P.mm(ps[1][:, :w], wk[:, c, :], k.hmix[:, c, h0:h0 + w], start=(c == 0), stop=(c == KC - 1))
                P.copy(KT[:, c0:c0 + w], ps[1][:, :w], e="act")
            for (c0, w, st) in qsubs:
                h0 = hcol(c0)
                for c in range(KC):
                    P.mm(ps[2][:, :w], wq[:, c, :], k.hmix[:, c, h0:h0 + w], start=(c == 0), stop=(c == KC - 1))
                P.ts(QT[:, c0:c0 + w], ps[2][:, :w], 0.125, None, op0=ALU.mult)
            for tc in range(18):
                h0 = hcol(tc * 128)
                pv = ps[1 + tc % 2]
                for c in range(KC):
                    P.mm(pv[:, 0:128], k.hmix[:, c, h0:h0 + 128], wv[:, c, :], start=(c == 0), stop=(c == KC - 1))
                P.copy(V[:, tc, :], pv[:, 0:128], e=("act" if tc % 2 else "dve"))
            qblocks = []
            if ctx_out:
                qblocks += [(0, []), (128, [])]
            for i in range(16):
                qblocks.append((NCTX + i * 128, geo[i]))
            for (q0, loc) in qblocks:
                chunks = [(0, None, None), (1, None, None)] + [(2 + j, idxp, combo) for (j, idxp, combo) in loc]
                for hh in range(2):
                    h = 2 * hp + hh
                    pr = slice(hh * 64, (hh + 1) * 64)
                    for ci, (kc, idxp, combo) in enumerate(chunks):
                        pss = ps[4 + cnt % 2]
                        p = pT[cnt % 3]
                        cnt += 1
                        last_s = (idxp is None)
                        P.mm(pss[:, 0:128], KT[pr, kc * 128:(kc + 1) * 128], QT[pr, q0:q0 + 128], start=True, stop=last_s)
                        if idxp is not None:
                            need_mask = combo != 0
                            P.mm(pss[:, 0:128], k.ident_bf[:], BP[:, h, idxp:idxp + 2, :], start=False, stop=not need_mask)
                            if need_mask:
                                P.mm(pss[:, 0:128], mv[0:2, combo * 128:(combo + 1) * 128], sel[0:2, :], start=False, stop=True)
                        P.act(p[:], pss[:, 0:128], AF.Exp)
                        P.mm(ps[6][pr, 0:128], V[:, kc, pr], p[:], start=(ci == 0), stop=(ci == len(chunks) - 1))
                        P.mm(ps[7][pr, 0:128], k.ones_bf[:, 0:64], p[:], start=(ci == 0), stop=(ci == len(chunks) - 1))
                P.recip(rden[:], ps[7][:, 0:128])
                P.tt(dT[:, q0:q0 + 128], ps[6][:, 0:128], rden[:], ALU.mult)
            cols0 = 0 if ctx_out else NCTX
            P.dma(k.br[3][hp * 128:(hp + 1) * 128, cols0:T], dT[:, cols0:T], q="sp")


def post_norm_residual(P, k, ob, s, subs_local, rstd, ps_ss, sq, tmp, xb):
    for (o, c0, w, st) in subs_local:
        P.dma(xb[:, :, o:o + w], k.xres[:, :, c0:c0 + w])
        sumsq_rstd(P, k, lambda c, oo, ww: ob[:, c, oo:oo + ww], KC, [(o, w)], rstd, ps_ss, sq, 1.0 / D)
        for c in range(KC):
            t = tmp[c % 2]
            P.stt(t[:, :w], ob[:, c, o:o + w], k.Gg[:, s, c, st:st + 1], rstd[:, o:o + w], ALU.mult, ALU.mult)
            P.tt(xb[:, c, o:o + w], xb[:, c, o:o + w], t[:, :w], ALU.add, e="pool")
        P.dma(k.xres[:, :, c0:c0 + w], xb[:, :, o:o + w], q="sp")


def merge_phase(P, k, l, ctx_out):
    I = k.I
    w_in = I["w_in"][l].rearrange("(c p) n -> p c n", p=128)
    ps = k.ps
    subs = ALLSUBS if ctx_out else LATSUBS
    with P.scope():
        brt = [P.sbuf("mg_br%d" % n, [128, 4, 512], BF16) for n in range(4)]
        mt = P.sbuf("mg_mt", [128, KC, 512], BF16)
        ob = P.sbuf("mg_ob", [128, KC, 512], BF16)
        wg = [P.sbuf("mg_wg%d" % i, [128, KC, 4, 128], BF16) for i in range(2)]
        wb = [P.sbuf("mg_wb%d" % i, [128, 4, 4, 128], BF16) for i in range(2)]
        wo = [P.sbuf("mg_wo%d" % i, [128, KC, 128], BF16) for i in range(2)]
        sg = [P.sbuf("mg_sg%d" % i, [128, 512], F32) for i in range(2)]
        acc = P.sbuf("mg_acc", [128, 512], F32)
        tm = [P.sbuf("mg_tm%d" % i, [128, 512], F32) for i in range(2)]
        rstd = P.sbuf("mg_rstd", [128, 512], F32)
        sq = [P.sbuf("mg_sq%d" % i, [128, 512], BF16) for i in range(2)]
        xbm = P.sbuf("mg_x", [128, KC, 512], F32)
        hbt = [P.sbuf("mg_h%d" % i, [128, KC, 512], BF16) for i in range(2)]
        cnt = 0
        wcnt = 0
        for si, (c0, w, st) in enumerate(subs):
            h0 = hcol(c0)
            hb_ = hbt[si % 2]
            P.dma(hb_[:, :, :w], k.hmx[:, :, h0:h0 + w])
            for n in range(4):
                P.dma(brt[n][:, :, :w], k.br[n].rearrange("(c p) t -> p c t", p=128)[:, :, c0:c0 + w])
            for dc in range(KC):
                g = wg[wcnt % 2]
                b = wb[wcnt % 2]
                wcnt += 1
                for n in range(4):
                    P.dma(g[:, :, n, :], w_in[:, :, C_GT + n * D + dc * 128:C_GT + n * D + (dc + 1) * 128], q="pool")
                    P.dma(b[:, n, :, :], I["w_branch"][l, n].rearrange("(kk p) d -> p kk d", p=128)[:, :, dc * 128:(dc + 1) * 128], q="pool")
                for n in range(4):
                    pg = ps[1 + cnt % 2]
                    pp = ps[3 + cnt % 2]
                    s_ = sg[cnt % 2]
                    cnt += 1
                    for c in range(KC):
                        P.mm(pg[:, :w], g[:, c, n, :], hb_[:, c, :w], start=(c == 0), stop=(c == KC - 1))
                    for kk in range(4):
                        P.mm(pp[:, :w], b[:, n, kk, :], brt[n][:, kk, :w], start=(kk == 0), stop=(kk == 3))
                    P.act(s_[:, :w], pg[:, :w], AF.Sigmoid)
                    if n == 0:
                        P.tt(acc[:, :w], s_[:, :w], pp[:, :w], ALU.mult)
                    else:
                        t = tm[n % 2]
                        P.tt(t[:, :w], s_[:, :w], pp[:, :w], ALU.mult)
                        if n < 3:
                            P.tt(acc[:, :w], acc[:, :w], t[:, :w], ALU.add, e="pool")
                        else:
                            P.tt(mt[:, dc, :w], acc[:, :w], t[:, :w], ALU.add, e="pool")
            for dc in range(KC):
                wo_ = wo[dc % 2]
                P.dma(wo_[:], I["w_out"][l].rearrange("(kk p) d -> p kk d", p=128)[:, :, dc * 128:(dc + 1) * 128], q="pool")
                po = ps[5 + dc % 2]
                for kk in range(KC):
                    P.mm(po[:, :w], wo_[:, kk, :], mt[:, kk, :w], start=(kk == 0), stop=(kk == KC - 1))
                P.copy(ob[:, dc, :w], po[:, :w], e="act")
            post_norm_residual(P, k, ob, 1, [(0, c0, w, st)], rstd, ps[0], sq, tm, xbm)


def declare_hyena_inputs(P, I, nl, with_ctx=True):
    nc = P.nc

    def inp(name, shape):
        I[name] = nc.dram_tensor(name, list(shape), F32, kind="ExternalInput").ap()
    inp("hy_conv_w", [nl, 3, 1536]); inp("hy_conv_b", [nl, 1536]); inp("hy_bias", [nl, 512])
    inp("hy_w1", [nl, 33, 64]); inp("hy_b1", [nl, 64]); inp("hy_freq1", [nl, 64])
    inp("hy_w2", [nl, 64, 64]); inp("hy_b2", [nl, 64]); inp("hy_freq2", [nl, 64]); inp("hy_w3", [nl, 64, 1024])
    for nm, N in (("lat", NLAT), ("ctx", NCTX)):
        if nm == "ctx" and not with_ctx:
            continue
        for t in ("c", "s", "ct", "st"):
            inp("dft_%s_%s" % (t, nm), [N, N])
        inp("hy_zT_%s" % nm, [33, N]); inp("hy_decay_%s" % nm, [N, 512])


PI = math.pi


def sin_reduced(P, dst, src, shp, tmp):
    P.ts(tmp, src, PI, -2.0 * PI, op0=ALU.is_gt, op1=ALU.mult)
    P.tt(src, src, tmp, ALU.add)
    P.ts(tmp, src, -PI, 2.0 * PI, op0=ALU.is_lt, op1=ALU.mult)
    P.tt(src, src, tmp, ALU.add)
    P.act(dst, src, AF.Sin)


def hyena_branch(P, k, l, seq):
    nc = P.nc
    I = k.I
    nm, N, hbase, xbase = seq
    NT = N // 128
    NF = NT
    CW = min(512, N)
    w_in = I["w_in"][l].rearrange("(c p) n -> p c n", p=128)
    ps = k.ps
    dft_c = I["dft_c_%s" % nm].rearrange("(tc p) f -> p tc f", p=128)
    dft_s = I["dft_s_%s" % nm].rearrange("(tc p) f -> p tc f", p=128)
    dft_ct = I["dft_ct_%s" % nm].rearrange("(fc p) t -> p fc t", p=128)
    dft_st = I["dft_st_%s" % nm].rearrange("(fc p) t -> p fc t", p=128)
    with P.scope():
        Kre = P.sbuf("hy_Kre", [128, NF, 512], BF16)
        Kim = P.sbuf("hy_Kim", [128, NF, 512], BF16)
        Ct = [P.sbuf("hy_Ct%d" % i, [128, NT, 128], BF16) for i in range(2)]
        St = [P.sbuf("hy_St%d" % i, [128, NT, 128], BF16) for i in range(2)]
        tA = P.sbuf("hy_tA", [128, 512], F32)
        tB = P.sbuf("hy_tB", [128, 512], F32)
        tC = P.sbuf("hy_tC", [128, 512], F32)
        tD = P.sbuf("hy_tD", [128, 512], F32)
        with P.scope():
            w1 = P.sbuf("hy_w1", [33, 64], F32); P.dma(w1[:], I["hy_w1"][l])
            w2 = P.sbuf("hy_w2", [64, 64], F32); P.dma(w2[:], I["hy_w2"][l])
            w3 = P.sbuf("hy_w3", [64, 1024], F32); P.dma(w3[:], I["hy_w3"][l])
            cols = P.sbuf("hy_cols", [64, 6], F32)
            for j, nm_ in enumerate(["hy_b1", "hy_freq1", "hy_b2", "hy_freq2"]):
                P.dma(cols[:, j:j + 1], I[nm_][l].rearrange("(p o) -> p o", o=1))
            P.tt(cols[:, 4:5], cols[:, 0:1], cols[:, 1:2], ALU.mult)
            P.tt(cols[:, 5:6], cols[:, 2:3], cols[:, 3:4], ALU.mult)
            zT = P.sbuf("hy_zT", [33, N], F32); P.dma(zT[:], I["hy_zT_%s" % nm])
            h1 = P.sbuf("hy_h1", [64, N], F32)
            h2 = P.sbuf("hy_h2", [64, N], F32)
            filt = P.sbuf("hy_filt", [128, NT, 1024], BF16)
            dec = [P.sbuf("hy_dec%d" % i, [128, 512], F32) for i in range(2)]
            for cb in range(N // CW):
                cs = slice(cb * CW, (cb + 1) * CW)
                P.mm(ps[1][0:64, :CW], w1[:, :], zT[:, cs], start=True, stop=True)
                P.act(tA[0:64, :CW], ps[1][0:64, :CW], AF.Identity, bias=cols[:, 4:5], scale=cols[:, 1:2])
                sin_reduced(P, h1[:, cs], tA[0:64, :CW], None, tB[0:64, :CW])
            for cb in range(N // CW):
                cs = slice(cb * CW, (cb + 1) * CW)
                P.mm(ps[1][0:64, :CW], w2[:, :], h1[:, cs], start=True, stop=True)
                P.act(tA[0:64, :CW], ps[1][0:64, :CW], AF.Identity, bias=cols[:, 5:6], scale=cols[:, 3:4])
                sin_reduced(P, h2[:, cs], tA[0:64, :CW], None, tB[0:64, :CW])
            for tc in range(NT):
                d = dec[tc % 2]
                P.dma(d[:], I["hy_decay_%s" % nm][tc * 128:(tc + 1) * 128, :])
                for hf in range(2):
                    pp = ps[1 + hf]
                    P.mm(pp[:, :], h2[:, tc * 128:(tc + 1) * 128], w3[:, hf * 512:(hf + 1) * 512], start=True, stop=True)
                    P.tt(filt[:, tc, hf * 512:(hf + 1) * 512], pp[:, :], d[:], ALU.mult)
            for fc in range(NF):
                c_ = Ct[fc % 2]; s_ = St[fc % 2]
                P.dma(c_[:], dft_c[:, :, fc * 128:(fc + 1) * 128], q="pool")
                P.dma(s_[:], dft_s[:, :, fc * 128:(fc + 1) * 128], q="pool")
                for bi, (mat, half) in enumerate([(c_, 0), (s_, 0), (c_, 1), (s_, 1)]):
                    for tc in range(NT):
                        P.mm(ps[1 + bi][:, :], mat[:, tc, :], filt[:, tc, half * 512:(half + 1) * 512],
                             start=(tc == 0), stop=(tc == NT - 1))
                P.copy(tA[:], ps[1][:, :], e="act")
                P.tt(Kre[:, fc, :], tA[:], ps[3][:, :], ALU.add)
                P.copy(tB[:], ps[2][:, :], e="act")
                P.tt(Kim[:, fc, :], tB[:], ps[4][:, :], ALU.subtract)
        s_bf = P.sbuf("hy_s", [128, NT, 512], BF16)
        x0_bf = P.sbuf("hy_x0", [128, NT, 512], BF16)
        with P.scope():
            wraw = P.sbuf("hy_wraw", [128, KC, 512], BF16)
            Wk = [P.sbuf("hy_Wk%d" % i, [128, KC, 512], BF16) for i in range(3)]
            cw = P.sbuf("hy_cw", [128, 512], F32)
            cbf = P.sbuf("hy_cbf", [1, 1536], F32)
            P.dma(cbf[:], I["hy_conv_b"][l].rearrange("(o n) -> o n", o=1))
            cbb = P.sbuf("hy_cbb", [1, 1536], BF16)
            P.copy(cbb[:], cbf[:])
            for blk in range(3):
                P.dma(wraw[:], w_in[:, :, C_HY + blk * 512:C_HY + (blk + 1) * 512], q="pool")
                for kk in range(3):
                    P.dma(cw[:], I["hy_conv_w"][l, kk, blk * 512:(blk + 1) * 512].partition_broadcast(128))
                    P.tt(Wk[kk][:], wraw[:], cw[:].unsqueeze(1).broadcast_to([128, KC, 512]), ALU.mult)
                for tc in range(NT):
                    pp = ps[1 + tc % 2]
                    for kk in range(3):
                        c0 = hbase + tc * 128 + kk - 1
                        for c in range(KC):
                            P.mm(pp[:, :], k.hmix[:, c, c0:c0 + 128], Wk[kk][:, c, :], start=(kk == 0 and c == 0), stop=False)
                    P.mm(pp[:, :], k.ones_bf[0:1, 0:128], cbb[0:1, blk * 512:(blk + 1) * 512], start=False, stop=True)
                    if blk == 0:
                        P.copy(s_bf[:, tc, :], pp[:, :], e="act")
                    elif blk == 1:
                        P.tt(s_bf[:, tc, :], s_bf[:, tc, :], pp[:, :], ALU.mult)
                    else:
                        P.copy(x0_bf[:, tc, :], pp[:, :], e="act")
        Yre = P.sbuf("hy_Yre", [128, NF, 512], BF16)
        Yim = P.sbuf("hy_Yim", [128, NF, 512], BF16)
        for fc in range(NF):
            c_ = Ct[fc % 2]; s_ = St[fc % 2]
            P.dma(c_[:], dft_c[:, :, fc * 128:(fc + 1) * 128], q="pool")
            P.dma(s_[:], dft_s[:, :, fc * 128:(fc + 1) * 128], q="pool")
            pa = ps[1 + 2 * (fc % 2)]; pb = ps[2 + 2 * (fc % 2)]
            for tc in range(NT):
                P.mm(pa[:, :], c_[:, tc, :], s_bf[:, tc, :], start=(tc == 0), stop=(tc == NT - 1))
            for tc in range(NT):
                P.mm(pb[:, :], s_[:, tc, :], s_bf[:, tc, :], start=(tc == 0), stop=(tc == NT - 1))
            P.copy(tA[:], pa[:, :], e="act")
            P.copy(tB[:], pb[:, :], e="act")
            P.tt(tC[:], tA[:], Kre[:, fc, :], ALU.mult)
            P.tt(tD[:], tB[:], Kim[:, fc, :], ALU.mult, e="pool")
            P.tt(Yre[:, fc, :], tC[:], tD[:], ALU.subtract)
            P.tt(tC[:], tA[:], Kim[:, fc, :], ALU.mult, e="pool")
            P.tt(tD[:], tB[:], Kre[:, fc, :], ALU.mult)
            P.tt(Yim[:, fc, :], tC[:], tD[:], ALU.add, e="pool")
        bd = P.sbuf("hy_bd", [128, 512], F32)
        P.dma(bd[:], I["hy_bias"][l].partition_broadcast(128))
        bT = P.sbuf("hy_bT", [128, 4, N], BF16)
        otm = [P.sbuf("hy_otm%d" % i, [128, 512], BF16) for i in range(2)]
        for tc in range(NT):
            c_ = Ct[tc % 2]; s_ = St[tc % 2]
            P.dma(c_[:], dft_ct[:, :, tc * 128:(tc + 1) * 128], q="pool")
            P.dma(s_[:], dft_st[:, :, tc * 128:(tc + 1) * 128], q="pool")
            pp = ps[1 + tc % 2]
            for fc in range(NF):
                P.mm(pp[:, :], c_[:, fc, :], Yre[:, fc, :], start=(fc == 0), stop=False)
            for fc in range(NF):
                P.mm(pp[:, :], s_[:, fc, :], Yim[:, fc, :], start=False, stop=(fc == NF - 1))
            P.tt(tA[:], s_bf[:, tc, :], bd[:], ALU.mult)
            P.stt(tB[:], pp[:, :], 1.0 / N, tA[:], ALU.mult, ALU.add)
            o = otm[tc % 2]
            P.tt(o[:], tB[:], x0_bf[:, tc, :], ALU.mult, e="pool")
            pt = ps[5 + tc % 2][:].bitcast(BF16)
            for j in range(4):
                P.transpose(pt[:, j * 128:(j + 1) * 128], o[:, j * 128:(j + 1) * 128], k.ident_bf[:])
            P.copy(bT[:, :, tc * 128:(tc + 1) * 128], pt[:, 0:512].rearrange("p (j t) -> p j t", j=4), e="act")
        P.dma(k.br[1].rearrange("(c p) t -> p c t", p=128)[:, :, xbase:xbase + N], bT[:], q="sp")


HY_LAT = ("lat", NLAT, LT0, NCTX)
HY_CTX = ("ctx", NCTX, CT0, 0)


NCH = 144


def declare_s5_inputs(P, I, nl):
    nc = P.nc

    def inp(name, shape):
        I[name] = nc.dram_tensor(name, list(shape), F32, kind="ExternalInput").ap()
    inp("s5_lam_re", [nl, 2, 2048]); inp("s5_lam_im", [nl, 2, 2048]); inp("s5_log_dt", [nl, 2, 32])
    inp("s5_b_re", [nl, 2, 2048, 16]); inp("s5_b_im", [nl, 2, 2048, 16])
    inp("s5_c_re", [nl, 2, 32, 16, 64]); inp("s5_c_im", [nl, 2, 32, 16, 64])
    inp("s5_d", [nl, 512]); inp("s5_w_glu", [nl, 512, 1024]); inp("s5_b_glu", [nl, 1024])
    inp("s5_mask", [2, 2, 128, 256])


def s5_setup(P, k):
    nc = P.nc
    k.u_tm = nc.dram_tensor("s5_u_tm", [T, 512], F32, kind="Internal").ap()
    k.y_tm = nc.dram_tensor("s5_y_tm", [T, 512], F32, kind="Internal").ap()


def s5_part1(P, k, l):
    I = k.I
    w_in = I["w_in"][l].rearrange("(c p) n -> p c n", p=128)
    ps = k.ps
    with P.scope():
        wu = P.sbuf("s5_wu", [128, KC, 512], BF16)
        P.dma(wu[:], w_in[:, :, C_U:C_U + 512], q="pool")
        ut = [P.sbuf("s5_ut%d" % i, [128, 512], F32) for i in range(2)]
        for tc in range(18):
            h0 = hcol(tc * 128)
            pp = ps[1 + tc % 2]
            for c in range(KC):
                P.mm(pp[:, :], k.hmix[:, c, h0:h0 + 128], wu[:, c, :], start=(c == 0), stop=(c == KC - 1))
            P.copy(ut[tc % 2][:], pp[:, :], e=("act" if tc % 2 else "dve"))
            P.dma(k.u_tm[tc * 128:(tc + 1) * 128, :], ut[tc % 2][:], q="sp")


def cmul(P, o_re, o_im, a_re, a_im, b_re, b_im, t1, t2, neg_im=False):
    P.tt(t1, a_re, b_re, ALU.mult)
    P.tt(t2, a_im, b_im, ALU.mult, e="pool")
    P.tt(o_re, t1, t2, ALU.subtract)
    P.tt(t1, a_re, b_im, ALU.mult)
    P.tt(t2, a_im, b_re, ALU.mult, e="pool")
    if neg_im:
        P.stt(o_im, t1, -1.0, t2, ALU.mult, ALU.subtract)
    else:
        P.tt(o_im, t1, t2, ALU.add)


def s5_part2(P, k, l, ctx_out):
    nc = P.nc
    I = k.I
    ps = k.ps
    with P.scope():
        U = P.sbuf("s5_U", [128, 32, 2, NCH], BF16)
        M = P.sbuf("s5_M", [128, 32, 2, 256], BF16)
        Qre = [P.sbuf("s5_Qre%d" % d, [128, 16, 256], BF16) for d in range(2)]
        nQim = [P.sbuf("s5_nQim%d" % d, [128, 16, 256], BF16) for d in range(2)]
        Xre = [P.sbuf("s5_Xre%d" % d, [128, 16, NCH], BF16) for d in range(2)]
        Xim = [P.sbuf("s5_Xim%d" % d, [128, 16, NCH], BF16) for d in range(2)]
        with P.scope():
            uc = P.sbuf("s5_uc", [128, 16 * 512], F32)
            ucv = uc[:].rearrange("p (g j h) -> p g j h", g=32, j=16)
            for (part, np_, c_lo) in (("lat", 128, 16), ("ctx", 16, 0)):
                rows = k.u_tm[NCTX:T, :] if part == "lat" else k.u_tm[0:NCTX, :]
                src = rows.rearrange("(c j) (g h) -> c g j h", j=16, g=32)
                for g in range(32):
                    P.dma(ucv[0:np_, g], src[:, g])
                cnt = 0
                for g in range(32):
                    for a in range(2):
                        pp = ps[1 + cnt % 4]
                        cnt += 1
                        P.transpose(pp[:, 0:np_], uc[0:np_, g * 256 + a * 128:g * 256 + (a + 1) * 128], k.ident[0:np_, 0:np_])
                        P.copy(U[:, g, a, c_lo:c_lo + np_], pp[:, 0:np_], e=("act" if cnt % 2 else "dve"))
        for d in range(2):
            with P.scope():
                sm = P.sbuf("s5_sm", [128, 40, 16], F32)
                slot = [0]

                def S():
                    i = slot[0]
                    slot[0] += 1
                    return sm[:, i, :]
                lre = S(); lim = S(); dt = S()
                P.dma(lre, I["s5_lam_re"][l, d].rearrange("(pr q) -> q pr", q=128), allow_slow_non_contiguous=True)
                P.dma(lim, I["s5_lam_im"][l, d].rearrange("(pr q) -> q pr", q=128), allow_slow_non_contiguous=True)
                ldt = I["s5_log_dt"][l, d].rearrange("(pr g2) -> g2 pr", g2=2)
                for g2 in range(2):
                    P.dma(sm[g2 * 64:(g2 + 1) * 64, 2, :], ldt[g2].partition_broadcast(64), allow_slow_non_contiguous=True)
                P.act(dt, dt, AF.Exp)
                P.ts(lre, lre, -1e-4, None, op0=ALU.min)
                a_ = S(); th = S(); t1 = S(); t2 = S()
                P.tt(a_, lre, dt, ALU.mult)
                P.tt(th, lim, dt, ALU.mult)
                mag = S(); imag = S()
                P.act(mag, a_, AF.Exp)
                P.act(imag, a_, AF.Exp, scale=-1.0)
                for _ in range(4):
                    P.ts(t1, th, PI, -2.0 * PI, op0=ALU.is_gt, op1=ALU.mult)
                    P.tt(th, th, t1, ALU.add)
                thc = S()
                P.ts(thc, th, PI / 2, None, op0=ALU.add)
                P.ts(t1, thc, PI, -2.0 * PI, op0=ALU.is_gt, op1=ALU.mult)
                P.tt(thc, thc, t1, ALU.add)
                sn = S(); cs = S()
                P.act(sn, th, AF.Sin)
                P.act(cs, thc, AF.Sin)
                lbr = S(); lbi = S(); lir = S(); lii = S()
                P.tt(lbr, mag, cs, ALU.mult); P.tt(lbi, mag, sn, ALU.mult)
                P.tt(lir, imag, cs, ALU.mult); P.stt(lii, imag, -1.0, sn, ALU.mult, ALU.mult)
                den = S(); icr = S(); ici = S()
                P.tt(den, lre, lre, ALU.mult); P.tt(t1, lim, lim, ALU.mult); P.tt(den, den, t1, ALU.add)
                P.recip(den, den)
                P.tt(icr, lre, den, ALU.mult); P.stt(ici, lim, -1.0, den, ALU.mult, ALU.mult)
                lm1 = S(); cfr = S(); cfi = S()
                P.ts(lm1, lbr, -1.0, None, op0=ALU.add)
                cmul(P, cfr, cfi, lm1, lbi, icr, ici, t1, t2)
                Ppr = P.sbuf("s5_Ppr", [128, 16, 17], F32); Ppi = P.sbuf("s5_Ppi", [128, 16, 17], F32)
                Pnr = P.sbuf("s5_Pnr", [128, 16, 17], F32); Pni = P.sbuf("s5_Pni", [128, 16, 17], F32)
                tw1 = P.sbuf("s5_tw1", [128, 16, 8], F32); tw2 = P.sbuf("s5_tw2", [128, 16, 8], F32)
                for (tr, ti, br_, bi_) in ((Ppr, Ppi, lbr, lbi), (Pnr, Pni, lir, lii)):
                    P.memset(tr[:, :, 0:1], 1.0); P.memset(ti[:, :, 0:1], 0.0)
                    P.copy(tr[:, :, 1], br_); P.copy(ti[:, :, 1], bi_)
                    n = 2
                    while n <= 16:
                        sqr = S() if False else None
                        h = n // 2
                        cmul(P, tr[:, :, n], ti[:, :, n], tr[:, :, h], ti[:, :, h], tr[:, :, h], ti[:, :, h], tw1[:, :, 0], tw2[:, :, 0])
                        cnt_ = min(n, 17 - n) - 1
                        if cnt_ > 0:
                            bre = tr[:, :, n:n + 1].broadcast_to([128, 16, cnt_]) if False else None
                            cmul(P, tr[:, :, n + 1:n + 1 + cnt_], ti[:, :, n + 1:n + 1 + cnt_],
                                 tr[:, :, 1:1 + cnt_], ti[:, :, 1:1 + cnt_],
                                 tr[:, :, n:n + 1].to_broadcast([128, 16, cnt_]), ti[:, :, n:n + 1].to_broadcast([128, 16, cnt_]),
                                 tw1[:, :, 0:cnt_], tw2[:, :, 0:cnt_])
                        n *= 2
                Ar = P.sbuf("s5_Ar", [128, 8, 16], F32); Ai = P.sbuf("s5_Ai", [128, 8, 16], F32); nAi = P.sbuf("s5_nAi", [128, 8, 16], F32)
                P.copy(Ar[:, 0, :], Ppr[:, :, 16]); P.copy(Ai[:, 0, :], Ppi[:, :, 16])
                for kk in range(1, 8):
                    cmul(P, Ar[:, kk, :], Ai[:, kk, :], Ar[:, kk - 1, :], Ai[:, kk - 1, :], Ar[:, kk - 1, :], Ai[:, kk - 1, :], t1, t2)
                P.ts(nAi[:], Ai[:], -1.0, None, op0=ALU.mult)
                Bre = P.sbuf("s5_Bre", [128, 16, 16], F32); Bim = P.sbuf("s5_Bim", [128, 16, 16], F32)
                P.dma(Bre[:], I["s5_b_re"][l, d].rearrange("(pr q) h -> q pr h", q=128))
                P.dma(Bim[:], I["s5_b_im"][l, d].rearrange("(pr q) h -> q pr h", q=128))
                Cre = P.sbuf("s5_Cre", [128, 16, 16], F32); Cim = P.sbuf("s5_Cim", [128, 16, 16], F32)
                for g2 in range(2):
                    for (dst, nm_) in ((Cre, "s5_c_re"), (Cim, "s5_c_im")):
                        src = I[nm_][l, d].rearrange("(pr g2) h p -> g2 pr p h", g2=2)[g2]
                        for pr_ in range(16):
                            P.dma(dst[g2 * 64:(g2 + 1) * 64, pr_, :], src[pr_], allow_slow_non_contiguous=True)
                tb1 = P.sbuf("s5_tb1", [128, 16, 16], F32); tb2 = P.sbuf("s5_tb2", [128, 16, 16], F32)
                bbr = P.sbuf("s5_bbr", [128, 16, 16], F32); bbi = P.sbuf("s5_bbi", [128, 16, 16], F32)
                bc = lambda x: x.unsqueeze(2).to_broadcast([128, 16, 16])
                cmul(P, bbr[:], bbi[:], Bre[:], Bim[:], bc(cfr), bc(cfi), tb1[:], tb2[:])
                if d == 1:
                    cmul(P, Bre[:], Bim[:], bbr[:], bbi[:], bc(Pnr[:, :, 15]), bc(Pni[:, :, 15]), tb1[:], tb2[:])
                    vbr, vbi = Bre, Bim
                    cr2 = P.sbuf("s5_cr2", [128, 16, 16], F32); ci2 = P.sbuf("s5_ci2", [128, 16, 16], F32)
                    cmul(P, cr2[:], ci2[:], Cre[:], Cim[:], bc(Ppr[:, :, 15]), bc(Ppi[:, :, 15]), tb1[:], tb2[:])
                    vcr, vci = cr2, ci2
                    tabP, tabQ = (Ppr, Ppi), (Pnr, Pni)
                else:
                    vbr, vbi = bbr, bbi
                    vcr, vci = Cre, Cim
                    tabP, tabQ = (Pnr, Pni), (Ppr, Ppi)
                Pre = P.sbuf("s5_Pre", [128, 16, 256], BF16); Pim = P.sbuf("s5_Pim", [128, 16, 256], BF16)
                to1 = P.sbuf("s5_to1", [128, 4, 256], F32); to2 = P.sbuf("s5_to2", [128, 4, 256], F32)
                v4 = lambda x: x.rearrange("p a (j h) -> p a j h", j=16)
                for pg in range(4):
                    sl = slice(pg * 4, pg * 4 + 4)
                    tj = lambda tab: tab[:, sl, 0:16].unsqueeze(3).to_broadcast([128, 4, 16, 16])
                    vh = lambda v: v[:, sl, :].unsqueeze(2).to_broadcast([128, 4, 16, 16])
                    cmul(P, v4(Pre[:, sl, :]), v4(Pim[:, sl, :]), tj(tabP[0]), tj(tabP[1]), vh(vbr), vh(vbi), v4(to1[:]), v4(to2[:]))
                    cmul(P, v4(Qre[d][:, sl, :]), v4(nQim[d][:, sl, :]), tj(tabQ[0]), tj(tabQ[1]), vh(vcr), vh(vci), v4(to1[:]), v4(to2[:]),
                         neg_im=True)
                msk = [P.sbuf("s5_msk%d" % a, [128, 256], F32) for a in range(2)]
                for a in range(2):
                    P.dma(msk[a][:], I["s5_mask"][d, a])
                tm = [P.sbuf("s5_tm%d" % i, [128, 256], BF16) for i in range(2)]
                cnt = 0
                for g in range(32):
                    pr_, g2 = g // 2, g % 2
                    rows = slice(g2 * 64, (g2 + 1) * 64)
                    for a in range(2):
                        pp = ps[1 + cnt % 4]
                        cnt += 1
                        P.mm(pp[:, 0:256], Pre[rows, pr_, a * 128:(a + 1) * 128], Qre[d][rows, pr_, :], start=True, stop=False)
                        P.mm(pp[:, 0:256], Pim[rows, pr_, a * 128:(a + 1) * 128], nQim[d][rows, pr_, :], start=False, stop=True)
                        if d == 0:
                            P.tt(M[:, g, a, :], pp[:, 0:256], msk[a][:], ALU.mult)
                        else:
                            t = tm[cnt % 2]
                            P.tt(t[:], pp[:, 0:256], msk[a][:], ALU.mult)
                            P.tt(M[:, g, a, :], M[:, g, a, :], t[:], ALU.add, e="pool")
                PTre = P.sbuf("s5_PTre", [128, 16, 2, 128], BF16); PTim = P.sbuf("s5_PTim", [128, 16, 2, 128], BF16)
                cnt = 0
                for pr_ in range(16):
                    for a in range(2):
                        for (src, dst) in ((Pre, PTre), (Pim, PTim)):
                            pt = ps[5 + cnt % 2][:].bitcast(BF16)
                            cnt += 1
                            P.transpose(pt[:, 0:128], src[:, pr_, a * 128:(a + 1) * 128], k.ident_bf[:])
                            P.copy(dst[:, pr_, a, :], pt[:, 0:128], e=("act" if cnt % 2 else "dve"))
                SA = [P.sbuf("s5_SAre", [128, 16, NCH], F32), P.sbuf("s5_SAim", [128, 16, NCH], F32)]
                SB = [P.sbuf("s5_SBre", [128, 16, NCH], F32), P.sbuf("s5_SBim", [128, 16, NCH], F32)]
                ts_ = P.sbuf("s5_ts", [128, NCH], F32); ts2 = P.sbuf("s5_ts2", [128, NCH], F32)
                for pr_ in range(16):
                    pre_, pim_ = ps[1 + 2 * (pr_ % 2)], ps[2 + 2 * (pr_ % 2)]
                    for (pt_, PT) in ((pre_, PTre), (pim_, PTim)):
                        for g2 in range(2):
                            g = 2 * pr_ + g2
                            rows = slice(g2 * 64, (g2 + 1) * 64)
                            if d == 0:
                                for a in range(2):
                                    P.mm(pt_[rows, 0:NCH], PT[:, pr_, a, rows], U[:, g, a, :], start=(a == 0), stop=(a == 1))
                            else:
                                for a in range(2):
                                    P.mm(pt_[rows, 0:128], PT[:, pr_, a, rows], U[:, g, a, 16:NCH], start=(a == 0), stop=(a == 1))
                                for a in range(2):
                                    P.mm(pt_[rows, 128:NCH], PT[:, pr_, a, rows], U[:, g, a, 0:16], start=(a == 0), stop=(a == 1))
                    P.ts(ts_[:], pre_[:, 0:NCH], Ar[:, 0, pr_:pr_ + 1], None, op0=ALU.mult)
                    P.stt(SA[0][:, pr_, :], pim_[:, 0:NCH], nAi[:, 0, pr_:pr_ + 1], ts_[:], ALU.mult, ALU.add)
                    P.ts(ts2[:], pim_[:, 0:NCH], Ar[:, 0, pr_:pr_ + 1], None, op0=ALU.mult)
                    P.stt(SA[1][:, pr_, :], pre_[:, 0:NCH], Ai[:, 0, pr_:pr_ + 1], ts2[:], ALU.mult, ALU.add)
                cur, nxt = SA, SB
                for kk in range(8):
                    sh = 1 << kk
                    n_ = NCH - sh
                    for pr_ in range(16):
                        if d == 0:
                            dst_r, dst_i = nxt[0][:, pr_, sh:NCH], nxt[1][:, pr_, sh:NCH]
                            src_r, src_i = cur[0][:, pr_, 0:n_], cur[1][:, pr_, 0:n_]
                            own_r, own_i = cur[0][:, pr_, sh:NCH], cur[1][:, pr_, sh:NCH]
                        else:
                            dst_r, dst_i = nxt[0][:, pr_, 0:n_], nxt[1][:, pr_, 0:n_]
                            src_r, src_i = cur[0][:, pr_, sh:NCH], cur[1][:, pr_, sh:NCH]
                            own_r, own_i = cur[0][:, pr_, 0:n_], cur[1][:, pr_, 0:n_]
                        P.stt(ts_[:, 0:n_], src_r, Ar[:, kk, pr_:pr_ + 1], own_r, ALU.mult, ALU.add)
                        P.stt(dst_r, src_i, nAi[:, kk, pr_:pr_ + 1], ts_[:, 0:n_], ALU.mult, ALU.add)
                        P.stt(ts2[:, 0:n_], src_i, Ar[:, kk, pr_:pr_ + 1], own_i, ALU.mult, ALU.add)
                        P.stt(dst_i, src_r, Ai[:, kk, pr_:pr_ + 1], ts2[:, 0:n_], ALU.mult, ALU.add)
                    for ri in range(2):
                        if d == 0:
                            P.copy(nxt[ri][:, :, 0:sh], cur[ri][:, :, 0:sh], e="pool")
                        else:
                            P.copy(nxt[ri][:, :, n_:NCH], cur[ri][:, :, n_:NCH], e="pool")
                    cur, nxt = nxt, cur
                for ri, X in ((0, Xre[d]), (1, Xim[d])):
                    W = cur[ri]
                    if d == 0:
                        P.memset(X[:, :, 0:1], 0.0)
                        P.copy(X[:, :, 1:NCH], W[:, :, 0:NCH - 1], e=("act" if ri else "dve"))
                    else:
                        P.copy(X[:, :, 0:15], W[:, :, 129:144], e="act")
                        P.memset(X[:, :, 15:16], 0.0)
                        P.copy(X[:, :, 16:143], W[:, :, 1:128], e="dve")
                        P.copy(X[:, :, 143:144], W[:, :, 128:129], e="act")
        with P.scope():
            ycl = P.sbuf("s5_ycl", [128, 16 * 512], F32)
            ycc = P.sbuf("s5_ycc", [16, 16 * 512], F32)
            yclv = ycl[:].rearrange("p (j g h) -> p j g h", j=16, g=32)
            yccv = ycc[:].rearrange("p (j g h) -> p j g h", j=16, g=32)
            ysb = [P.sbuf("s5_ysb%d" % i, [128, NCH], F32) for i in range(2)]
            cnt = 0
            for g in range(32):
                pr_, g2 = g // 2, g % 2
                rows = slice(g2 * 64, (g2 + 1) * 64)
                for b in range(2):
                    pp = ps[1 + cnt % 2]
                    ys = ysb[cnt % 2]
                    pt1 = ps[3 + cnt % 2]
                    pt2 = ps[5 + cnt % 2]
                    cnt += 1
                    cols = slice(b * 128, (b + 1) * 128)
                    P.mm(pp[:, 0:NCH], M[:, g, 0, cols], U[:, g, 0, :], start=True, stop=False)
                    P.mm(pp[:, 0:NCH], M[:, g, 1, cols], U[:, g, 1, :], start=False, stop=False)
                    for d in range(2):
                        P.mm(pp[:, 0:NCH], Qre[d][rows, pr_, cols], Xre[d][rows, pr_, :], start=False, stop=False)
                        P.mm(pp[:, 0:NCH], nQim[d][rows, pr_, cols], Xim[d][rows, pr_, :], start=False, stop=(d == 1))
                    P.copy(ys[:], pp[:, 0:NCH], e="act")
                    P.transpose(pt1[:, 0:128], ys[:, 16:NCH], k.ident[:])
                    P.copy(yclv[:, b * 8:(b + 1) * 8, g, :], pt1[:, 0:128].rearrange("p (j h) -> p j h", j=8), e="dve")
                    P.transpose(pt2[0:16, 0:128], ys[:, 0:16], k.ident[:])
                    P.copy(yccv[:, b * 8:(b + 1) * 8, g, :], pt2[0:16, 0:128].rearrange("p (j h) -> p j h", j=8), e="pool" if False else "dve")
            P.dma(k.y_tm[NCTX:T, :].rearrange("(c j) n -> c (j n)", j=16), ycl[:], q="sp")
            P.dma(k.y_tm[0:NCTX, :].rearrange("(c j) n -> c (j n)", j=16), ycc[:], q="sp")
        with P.scope():
            dbc = P.sbuf("s5_dbc", [128, 512], F32)
            P.dma(dbc[:], I["s5_d"][l].partition_broadcast(128))
            wgl = P.sbuf("s5_wgl", [128, 4, 1024], BF16)
            P.dma(wgl[:], I["s5_w_glu"][l].rearrange("(kk p) n -> p kk n", p=128), q="pool")
            bgl = P.sbuf("s5_bgl", [128, 8], F32)
            load_col_vec(P, bgl[:], I["s5_b_glu"][l], 8)
            gT = P.sbuf("s5_gT", [128, 4, T], BF16)
            yt = [P.sbuf("s5_yt%d" % i, [128, 512], F32) for i in range(2)]
            ut = [P.sbuf("s5_ut%d" % i, [128, 512], F32) for i in range(2)]
            w1_ = P.sbuf("s5_w1", [128, 512], F32); w2_ = P.sbuf("s5_w2", [128, 512], F32)
            gtm = [P.sbuf("s5_gtm%d" % i, [128, 512], BF16) for i in range(2)]
            tcs = list(range(18)) if ctx_out else list(range(2, 18))
            for tc in tcs:
                y = yt[tc % 2]; u = ut[tc % 2]
                P.dma(y[:], k.y_tm[tc * 128:(tc + 1) * 128, :])
                P.dma(u[:], k.u_tm[tc * 128:(tc + 1) * 128, :])
                P.tt(u[:], u[:], dbc[:], ALU.mult, e="pool")
                P.tt(y[:], y[:], u[:], ALU.add)
                P.tt(w1_[:], y[:], y[:], ALU.mult, e="pool")
                P.ts(w1_[:], w1_[:], 0.044715, 1.0, op0=ALU.mult, op1=ALU.add)
                P.tt(w1_[:], w1_[:], y[:], ALU.mult)
                P.act(w2_[:], w1_[:], AF.Sigmoid, scale=1.5957691216057308)
                gt_ = gtm[tc % 2]
                P.tt(gt_[:], y[:], w2_[:], ALU.mult, e="pool")
                pt = ps[5 + tc % 2][:].bitcast(BF16)
                for j in range(4):
                    P.transpose(pt[:, j * 128:(j + 1) * 128], gt_[:, j * 128:(j + 1) * 128], k.ident_bf[:])
                P.copy(gT[:, :, tc * 128:(tc + 1) * 128], pt[:, 0:512].rearrange("p (j t) -> p j t", j=4), e="act")
            cT = P.sbuf("s5_cT", [128, T], BF16)
            sg = [P.sbuf("s5_sg%d" % i, [128, 512], F32) for i in range(2)]
            subs = ALLSUBS if ctx_out else LATSUBS
            cnt = 0
            for m in range(4):
                for (c0, w, st) in subs:
                    pa = ps[1 + cnt % 2]; pb = ps[3 + cnt % 2]; s_ = sg[cnt % 2]
                    cnt += 1
                    for kk in range(4):
                        P.mm(pa[:, :w], wgl[:, kk, m * 128:(m + 1) * 128], gT[:, kk, c0:c0 + w], start=(kk == 0), stop=(kk == 3))
                    for kk in range(4):
                        P.mm(pb[:, :w], wgl[:, kk, 512 + m * 128:512 + (m + 1) * 128], gT[:, kk, c0:c0 + w], start=(kk == 0), stop=(kk == 3))
                    P.act(s_[:, :w], pb[:, :w], AF.Sigmoid, bias=bgl[:, 4 + m:5 + m])
                    P.stt(cT[:, c0:c0 + w], pa[:, :w], bgl[:, m:m + 1], s_[:, :w], ALU.add, ALU.mult)
                cols0 = 0 if ctx_out else NCTX
                P.dma(k.br[2][m * 128:(m + 1) * 128, cols0:T], cT[:, cols0:T], q="sp")


def build_full(nl=2, upto=None):
    P = Prog()
    nc = P.nc
    k = K()
    k.I = declare_inputs(P, nl)
    declare_mixer_inputs(P, k.I, nl)
    declare_na_inputs(P, k.I, nl)
    declare_hyena_inputs(P, k.I, nl)
    declare_s5_inputs(P, k.I, nl)
    yT = nc.dram_tensor("yT", [D, NLAT], F32, kind="ExternalOutput").ap()
    setup_common(P, k)
    mixer_setup(P, k)
    s5_setup(P, k)
    geo = na_geometry()
    P.dma(k.xres, k.I["xT"].rearrange("(c p) t -> p c t", p=128))
    for l in range(nl):
        ctx_out = l < nl - 1
        with P.scope():
            layer_mods(P, k, l)
        with P.scope():
            ffn_sublayer(P, k, l, 0, FULL_BLOCKS)
        with P.scope():
            alloc_hmix(P, k)
            mixer_modnorm(P, k)
            mla_branch(P, k, l, ctx_out)
            na_branch(P, k, l, ctx_out, geo)
            hyena_branch(P, k, l, HY_LAT)
            if ctx_out:
                hyena_branch(P, k, l, HY_CTX)
            s5_part1(P, k, l)
        s5_part2(P, k, l, ctx_out)
        merge_phase(P, k, l, ctx_out)
        with P.scope():
            ffn_sublayer(P, k, l, 2, FULL_BLOCKS if ctx_out else LAT_BLOCKS)
    P.dma(yT.rearrange("(c p) t -> p c t", p=128), k.xres[:, :, NCTX:T], q="sp")
    P.finish("sp")
    P.close()
    return P, k


_CACHE = {}


def _host_constants():
    if "c" in _CACHE:
        return _CACHE["c"]
    C, S = rope_tables()
    mv, sel, cm = na_const_tables()
    m = {"ident": np.eye(128, dtype=np.float32), "rope_c": C, "rope_s": S,
         "na_mv": np.ascontiguousarray(mv.reshape(2, -1)), "na_sel": sel, "na_cm": cm, "s5_mask": s5_masks()}
    for nm_, N in (("lat", NLAT), ("ctx", NCTX)):
        hc = hyena_consts(N)
        for t in ("c", "s", "ct", "st"):
            m["dft_%s_%s" % (t, nm_)] = hc[t]
        m["hy_zT_%s" % nm_] = hc["zT"]
        m["hy_decay_%s" % nm_] = hc["decay"]
    _CACHE["c"] = m
    return m


def kernel(**inputs):
    nl = 2
    if "prog" not in _CACHE:
        _CACHE["prog"] = build_full(nl)
    P, k = _CACHE["prog"]
    shared = dict(_host_constants())
    shared["na_G"] = np.ascontiguousarray(
        np.stack([na_bias_gather(np.asarray(inputs["na_rpb"][l], np.float32)).reshape(128, -1) for l in range(nl)], 0))
    for nm, ap in k.I.items():
        if nm in shared or nm in ("xT", "cvec"):
            continue
        shared[nm] = np.ascontiguousarray(np.asarray(inputs[nm], np.float32).reshape(ap.shape))
    x = np.asarray(inputs["x"], np.float32)
    ctx = np.asarray(inputs["ctx"], np.float32)
    c = np.asarray(inputs["c"], np.float32)
    c_ctx = np.asarray(inputs["c_ctx"], np.float32)
    B = x.shape[0]
    in_maps = []
    for b in range(B):
        m = dict(shared)
        m["xT"] = np.ascontiguousarray(np.concatenate([ctx[b], x[b]], 0).T)
        m["cvec"] = np.ascontiguousarray(np.stack([c[b], c_ctx], 1))
        in_maps.append(m)
    res = run_bass_kernel_spmd(P.nc, in_maps, core_ids=list(range(B)))
    out = np.stack([np.asarray(res.results[b]["yT"], np.float32).T for b in range(B)], 0)
    return np.ascontiguousarray(out)
```

```python
import numpy as np
import concourse.bass as bass
import concourse.mybir as mybir
from concourse.bass_utils import run_bass_kernel_spmd

F32 = mybir.dt.float32
F32R = mybir.dt.float32r
BF16 = mybir.dt.bfloat16
AF = mybir.ActivationFunctionType
ALU = mybir.AluOpType
AX = mybir.AxisListType

SEM_ROLL = 30000


class Prog:
    def __init__(self, n_dma_sems=16):
        self.nc = bass.Bass("TRN2", target_bir_lowering=False)
        nc = self.nc
        self.eng = {"pe": nc.tensor, "act": nc.scalar, "dve": nc.vector,
                    "pool": nc.gpsimd, "sp": nc.sync}
        self._ctx = []
        self._scopes = []
        self._in_scope_alloc = False
        self._uid = 0
        self.sem = {}
        self.cnt = {}
        self.nsem = 0
        for e in self.eng:
            self._new_eng_sem(e)
        self.dma_sems = {}
        self.dma_rr = {}
        for q in ("sp", "pool", "act"):
            self.dma_sems[q] = []
            for i in range(n_dma_sems if q != "act" else 4):
                s = self._enter(nc.semaphore("dq_%s%d" % (q, i)))
                self.dma_sems[q].append([s, 0])
            self.dma_rr[q] = 0
        self.waited = {e: {} for e in self.eng}
        self.regions = {}
        self.n_inst = 0
        self.n_wait = 0

    def _enter(self, cm):
        v = cm.__enter__()
        if self._scopes and self._in_scope_alloc:
            self._scopes[-1].append((cm, v))
        else:
            self._ctx.append(cm)
        return v

    class _Scope:
        def __init__(self, P):
            self.P = P

        def __enter__(self):
            self.P._scopes.append([])
            return self

        def __exit__(self, *a):
            P = self.P
            items = P._scopes.pop()
            toks = []
            for cm, v in items:
                nm = v.name if hasattr(v, "name") else None
                for r in P.regions.pop(nm, []):
                    toks.append((r[5], r[6]))
            if toks:
                for e in P.eng:
                    P._emit_waits(e, toks)
            for cm, v in reversed(items):
                cm.__exit__(None, None, None)
            return False

    def scope(self):
        return Prog._Scope(self)

    def _new_eng_sem(self, e):
        s = self._enter(self.nc.semaphore("s_%s_%d" % (e, self.nsem)))
        self.nsem += 1
        self.sem[e] = s
        self.cnt[e] = 0

    def close(self):
        for cm in reversed(self._ctx):
            cm.__exit__(None, None, None)
        self._ctx = []

    def sbuf(self, name, shape, dtype=F32):
        self._uid += 1
        self._in_scope_alloc = True
        try:
            return self._enter(self.nc.sbuf_tensor("%s_%d" % (name, self._uid), list(shape), dtype))
        finally:
            self._in_scope_alloc = False

    def psum(self, name, shape=(128, 512), dtype=F32):
        return self._enter(self.nc.psum_tensor(name, list(shape), dtype))

    def dram(self, name, shape, dtype=F32, kind="Internal"):
        return self.nc.dram_tensor(name, list(shape), dtype, kind=kind)

    @staticmethod
    def _region(ap):
        space = str(ap.space)
        name = ap.name
        aps = ap.ap
        off = int(ap.offset)
        if "DRAM" in space:
            ext = sum((c - 1) * abs(s) for s, c in aps)
            neg = sum((c - 1) * s for s, c in aps if s < 0)
            lo = off + neg
            return name, 0, 1, lo, lo + ext + 1, False
        pstep, pcnt = aps[0]
        if pstep == 0:
            pstep = 1 << 40
        p0 = off // pstep if pstep < (1 << 39) else 0
        lo = off - p0 * pstep if pstep < (1 << 39) else off
        ext = sum((c - 1) * abs(s) for s, c in aps[1:])
        is_psum = "PSUM" in space
        if is_psum:
            return name, 0, 128, 0, 1 << 30, True
        return name, p0, p0 + pcnt, lo, lo + ext + 1, False

    def _deps(self, reads, writes):
        toks = []
        info = []
        for ap, is_w in [(a, False) for a in reads] + [(a, True) for a in writes]:
            name, p0, p1, lo, hi, excl = self._region(ap)
            w = is_w or excl
            lst = self.regions.setdefault(name, [])
            for r in lst:
                if r[1] <= p0 or p1 <= r[0] or r[3] <= lo or hi <= r[2]:
                    continue
                if w or r[4]:
                    toks.append((r[5], r[6]))
            info.append((name, p0, p1, lo, hi, w))
        return toks, info

    def _record(self, info, sem, val):
        for name, p0, p1, lo, hi, w in info:
            lst = self.regions[name]
            if w:
                lst[:] = [r for r in lst if not (p0 <= r[0] and r[1] <= p1 and lo <= r[2] and r[3] <= hi)]
                lst.append([p0, p1, lo, hi, True, sem, val])
            else:
                for r in lst:
                    if (not r[4]) and r[0] == p0 and r[1] == p1 and r[2] == lo and r[3] == hi and r[5] is sem:
                        r[6] = max(r[6], val)
                        break
                else:
                    lst.append([p0, p1, lo, hi, False, sem, val])

    def _emit_waits(self, e, toks):
        best = {}
        for s, v in toks:
            k = id(s)
            if k not in best or best[k][1] < v:
                best[k] = (s, v)
        wd = self.waited[e]
        for k, (s, v) in best.items():
            if wd.get(k, 0) >= v:
                continue
            self.eng[e].wait_ge(s, v)
            wd[k] = v
            self.n_wait += 1

    def op(self, e, fn, reads, writes):
        toks, info = self._deps(reads, writes)
        if e == "pe":
            toks = [t for t in toks if t[0] is not self.sem["pe"]]
        self._emit_waits(e, toks)
        ins = fn()
        if self.cnt[e] >= SEM_ROLL:
            self._new_eng_sem(e)
        self.cnt[e] += 1
        ins.then_inc(self.sem[e], 1)
        self._record(info, self.sem[e], self.cnt[e])
        self.n_inst += 1
        return ins

    def dma(self, out, in_, q="sp", **kw):
        toks, info = self._deps([in_], [out])
        ent = self.dma_sems[q][self.dma_rr[q]]
        self.dma_rr[q] = (self.dma_rr[q] + 1) % len(self.dma_sems[q])
        s = ent[0]
        if ent[1] > 0:
            toks.append((s, ent[1]))
        self._emit_waits(q, toks)
        ent[1] += 16
        ins = self.eng[q].dma_start(out=out, in_=in_, **kw)
        ins.then_inc(s, 16)
        self._record(info, s, ent[1])
        self.n_inst += 1
        return ins

    def finish(self, e="sp"):
        toks = []
        for lst in self.regions.values():
            for r in lst:
                toks.append((r[5], r[6]))
        self._emit_waits(e, toks)

    def mm(self, out, lhsT, rhs, start=True, stop=True, **kw):
        return self.op("pe", lambda: self.nc.tensor.matmul(out, lhsT, rhs, start=start, stop=stop, **kw),
                       [lhsT, rhs], [out])

    def transpose(self, out, in_, ident):
        return self.op("pe", lambda: self.nc.tensor.transpose(out, in_, ident), [in_, ident], [out])

    def act(self, out, in_, func, bias=None, scale=1.0, e="act", **kw):
        reads = [in_]
        if bias is not None and not isinstance(bias, (int, float)):
            reads.append(bias)
        if not isinstance(scale, (int, float)):
            reads.append(scale)
        kw2 = dict(kw)
        if bias is not None:
            kw2["bias"] = bias
        writes = [out]
        if "accum_out" in kw2:
            writes.append(kw2["accum_out"])
        return self.op(e, lambda: self.nc.scalar.activation(out=out, in_=in_, func=func, scale=scale, **kw2),
                       reads, writes)

    def _veng(self, e):
        return self.nc.vector if e == "dve" else self.nc.gpsimd

    def tt(self, out, in0, in1, op, e="dve"):
        return self.op(e, lambda: self._veng(e).tensor_tensor(out=out, in0=in0, in1=in1, op=op), [in0, in1], [out])

    def ts(self, out, in0, s1, s2=None, op0=ALU.mult, op1=None, e="dve", **kw):
        reads = [in0] + [s for s in (s1, s2) if s is not None and not isinstance(s, (int, float))]
        writes = [out] + ([kw["accum_out"]] if "accum_out" in kw else [])
        if op1 is None:
            return self.op(e, lambda: self._veng(e).tensor_scalar(out=out, in0=in0, scalar1=s1, scalar2=None, op0=op0, **kw),
                           reads, writes)
        return self.op(e, lambda: self._veng(e).tensor_scalar(out=out, in0=in0, scalar1=s1, scalar2=s2, op0=op0, op1=op1, **kw),
                       reads, writes)

    def stt(self, out, in0, scalar, in1, op0, op1, e="dve"):
        reads = [in0, in1] + ([] if isinstance(scalar, (int, float)) else [scalar])
        return self.op(e, lambda: self.nc.vector.scalar_tensor_tensor(out=out, in0=in0, scalar=scalar, in1=in1, op0=op0, op1=op1),
                       reads, [out])

    def copy(self, out, in_, e="dve"):
        if e == "act":
            return self.op("act", lambda: self.nc.scalar.copy(out=out, in_=in_), [in_], [out])
        return self.op(e, lambda: self._veng(e).tensor_copy(out=out, in_=in_), [in_], [out])

    def memset(self, ap, val, e="dve"):
        return self.op(e, lambda: self._veng(e).memset(ap, val), [], [ap])

    def recip(self, out, in_):
        return self.op("dve", lambda: self.nc.vector.reciprocal(out=out, in_=in_), [in_], [out])

    def scan(self, out, d0, d1, initial, op0=ALU.mult, op1=ALU.add):
        reads = [d0, d1] + ([] if isinstance(initial, (int, float)) else [initial])
        return self.op("dve", lambda: self.nc.vector.tensor_tensor_scan(out=out, data0=d0, data1=d1, initial=initial, op0=op0, op1=op1),
                       reads, [out])


import numpy as np
def rope_tables(n=2048, grid_w=64, base=10000.0):
    q = 16
    t = np.arange(n)
    pos = np.stack([t // grid_w, t % grid_w], -1).astype(np.float32)
    inv = (base ** (-np.arange(q, dtype=np.float32) / q)).astype(np.float32)
    ang = pos[:, :, None] * inv
    C = np.zeros((64, n), np.float32); S = np.zeros((64, n), np.float32)
    for a in range(2):
        for hf in range(2):
            for j in range(q):
                f = a * 32 + hf * 16 + j
                C[f] = np.cos(ang[:, a, j])
                S[f] = (-1.0 if hf == 0 else 1.0) * np.sin(ang[:, a, j])
    return C, S

NEG = -30000.0
def na_geometry():
    rows = 32
    start = lambda r: min(max(r - 4, 0), rows - 8)
    geo = []
    for i in range(16):
        rs = [2 * i, 2 * i + 1]
        lo = min(start(r) for r in rs); hi = max(start(r) + 7 for r in rs)
        lst = []
        for j in range(lo // 2, hi // 2 + 1):
            codes = []
            for r in rs:
                v0 = start(r) <= 2 * j <= start(r) + 7
                v1 = start(r) <= 2 * j + 1 <= start(r) + 7
                code = {(True, True): 0, (False, True): 1, (True, False): 2, (False, False): 3}[(v0, v1)]
                codes.append(code)
            dr0 = 2 * (j - i)
            idxp = 7 - dr0
            assert 0 <= idxp <= 14, (i, j, idxp)
            lst.append((j, idxp, codes[0] * 4 + codes[1]))
        geo.append(lst)
    return geo

def na_const_tables():
    mv = np.zeros((2, 16, 128), np.float32)
    vecs = [np.zeros(128), np.r_[np.full(64, NEG), np.zeros(64)], np.r_[np.zeros(64), np.full(64, NEG)], np.full(128, NEG)]
    for c0 in range(4):
        for c1 in range(4):
            mv[0, c0 * 4 + c1] = vecs[c0]; mv[1, c0 * 4 + c1] = vecs[c1]
    sel = np.zeros((2, 128), np.float32); sel[0, :64] = 1; sel[1, 64:] = 1
    col = np.arange(64)
    c0 = np.clip(col - 8, 0, 48)
    inwin = (col[None, :] >= c0[:, None]) & (col[None, :] < c0[:, None] + 16)
    cm = np.where(inwin.T, 0.0, NEG).astype(np.float32)
    cm = np.concatenate([cm, cm], 0)
    return mv, sel, cm

def na_bias_gather(rpb):
    kc = np.arange(64)[:, None]; qc = np.arange(64)[None, :]
    dc = np.clip(kc - qc + 15, 0, 30)
    G = np.zeros((128, 8, 16, 64), np.float32)
    for idxp in range(16):
        for krl in range(2):
            dr = 7 - idxp + krl
            row = dr + 7
            if not (0 <= row <= 14):
                row = 0
            G[krl * 64:(krl + 1) * 64, :, idxp, :] = np.transpose(rpb[:, row][:, dc], (1, 0, 2))
    return G

def hyena_consts(N):
    t = np.arange(N, dtype=np.float64)[:, None]; f = np.arange(N, dtype=np.float64)[None, :]
    ang = 2.0 * np.pi * (f + 0.5) * t / (2.0 * N)
    Cm = np.cos(ang).astype(np.float32); Sm = np.sin(ang).astype(np.float32)
    bands = 16
    tt = np.arange(N, dtype=np.float32)
    t01 = np.linspace(0.0, 1.0, N, dtype=np.float32)[:, None]
    a2 = (np.float32(2.0 * np.pi) * tt / np.float32(N))[:, None] * np.linspace(1e-4, bands - 1, bands, dtype=np.float32)
    z = np.concatenate([t01, np.cos(a2), -np.sin(a2)], -1).astype(np.float32)
    max_decay = np.log(1e-2) / 0.3; min_decay = np.log(1e-2) / 1.5
    deltas = np.abs(np.linspace(min_decay, max_decay, 512, dtype=np.float32))
    decay = np.exp(-t01 * deltas).astype(np.float32)
    return dict(c=Cm, s=Sm, ct=np.ascontiguousarray(Cm.T), st=np.ascontiguousarray(Sm.T),
                zT=np.ascontiguousarray(z.T), decay=decay)

def s5_masks():
    m = np.zeros((2, 2, 128, 256), np.float32)
    for a in range(2):
        for il in range(8):
            i = a * 8 + il
            for j in range(16):
                if j >= i:
                    m[0, a, il * 16:(il + 1) * 16, j * 16:(j + 1) * 16] = 1.0
                if j <= i:
                    m[1, a, il * 16:(il + 1) * 16, j * 16:(j + 1) * 16] = 1.0
    return m

import math
import numpy as np

D = 1024
KC = 8
NCTX = 256
NLAT = 2048
T = NCTX + NLAT
FH = 2816
FHC = FH // 128
EPS = 1e-6
N_IN = 8256


class K:
    pass


def declare_inputs(P, nl):
    nc = P.nc
    I = {}

    def inp(name, shape):
        I[name] = nc.dram_tensor(name, list(shape), F32, kind="ExternalInput").ap()

    inp("xT", [D, T])
    inp("cvec", [D, 2])
    inp("ident", [128, 128])
    inp("w_mod", [nl, D, 9 * D])
    inp("b_mod", [nl, 9 * D])
    inp("norm_g", [nl, 6, D])
    inp("ffn_w_in", [nl, 2, D, 2 * FH])
    inp("ffn_w_out", [nl, 2, FH, D])
    return I


def setup_common(P, k):
    k.ident = P.sbuf("ident", [128, 128], F32)
    P.dma(k.ident[:], k.I["ident"])
    k.ident_bf = P.sbuf("ident_bf", [128, 128], BF16)
    P.copy(k.ident_bf[:], k.ident[:])
    k.ones_bf = P.sbuf("ones_bf", [128, 128], BF16)
    P.memset(k.ones_bf[:], 1.0)
    k.eps_col = P.sbuf("eps_col", [128, 1], F32)
    P.memset(k.eps_col[:], EPS)
    k.xres = P.nc.dram_tensor("xres", [128, KC, T], F32, kind="Internal").ap()
    k.ps = [P.psum("psb%d" % i) for i in range(8)]
    cv = P.sbuf("cv", [128, KC, 2], F32)
    P.dma(cv[:], k.I["cvec"].rearrange("(c p) n -> p c n", p=128))
    k.actv = P.sbuf("actv", [128, KC, 2], BF16)
    P.act(k.actv[:], cv[:], AF.Silu)
    k.modT = P.sbuf("modT", [128, 72, 2], F32)
    k.normg = P.sbuf("normg", [128, 48], F32)
    k.Asc = P.sbuf("Asc", [128, 3, KC, 2], F32)
    k.Bsh = P.sbuf("Bsh", [128, 3, KC, 2], F32)
    k.Gg = P.sbuf("Gg", [128, 3, KC, 2], F32)


def layer_mods(P, k, l):
    nc = P.nc
    wm = k.I["w_mod"][l].rearrange("(c p) n -> p c n", p=128)
    bm_t = P.sbuf("bm_t", [72, 128], F32)
    P.dma(bm_t[:], k.I["b_mod"][l].rearrange("(m f) -> m f", f=128))
    ng_t = P.sbuf("ng_t", [48, 128], F32)
    P.dma(ng_t[:], k.I["norm_g"][l].rearrange("g (c f) -> (g c) f", f=128))
    ps_m = k.ps[0]
    ps_t = k.ps[1]
    wt = [P.sbuf("wmod%d" % i, [128, KC, 512], BF16) for i in range(2)]
    for j in range(18):
        w = wt[j % 2]
        P.dma(w[:], wm[:, :, j * 512:(j + 1) * 512], q="pool")
        for mm in range(4):
            m = j * 4 + mm
            for c in range(KC):
                P.mm(ps_m[:, 2 * m:2 * m + 2], w[:, c, mm * 128:(mm + 1) * 128], k.actv[:, c, :],
                     start=(c == 0), stop=(c == KC - 1))
    P.transpose(ps_t[:, 0:72], bm_t[:], k.ident[0:72, 0:72])
    bmT = P.sbuf("bmT", [128, 72], F32)
    P.copy(bmT[:], ps_t[:, 0:72])
    P.tt(k.modT[:], ps_m[:, 0:144].rearrange("p (m s) -> p m s", s=2),
         bmT[:].unsqueeze(2).broadcast_to([128, 72, 2]), ALU.add)
    P.transpose(ps_t[:, 128:176], ng_t[:], k.ident[0:48, 0:48])
    P.copy(k.normg[:], ps_t[:, 128:176])
    for s in range(3):
        base = 3 * s
        gpre = k.normg[:, (2 * s) * 8:(2 * s + 1) * 8].unsqueeze(2).broadcast_to([128, KC, 2])
        gpost = k.normg[:, (2 * s + 1) * 8:(2 * s + 2) * 8].unsqueeze(2).broadcast_to([128, KC, 2])
        P.stt(k.Asc[:, s], k.modT[:, (base + 1) * 8:(base + 2) * 8, :], 1.0, gpre, ALU.add, ALU.mult)
        P.copy(k.Bsh[:, s], k.modT[:, base * 8:(base + 1) * 8, :])
        P.stt(k.Gg[:, s], k.modT[:, (base + 2) * 8:(base + 3) * 8, :], (1.0 if s == 1 else 0.5), gpost, ALU.mult, ALU.mult)


def sumsq_rstd(P, k, src_fn, nchunks, subs, rstd, ps_ss, sq_tiles, inv_n):
    for (o, w) in subs:
        for c in range(nchunks):
            sq = sq_tiles[c % len(sq_tiles)]
            P.act(sq[:, :w], src_fn(c, o, w), AF.Square)
            P.mm(ps_ss[:, :w], k.ones_bf[:], sq[:, :w], start=(c == 0), stop=(c == nchunks - 1))
        P.act(rstd[:, o:o + w], ps_ss[:, :w], AF.Sqrt, bias=k.eps_col[:], scale=inv_n)
        P.recip(rstd[:, o:o + w], rstd[:, o:o + w])


def ffn_sublayer(P, k, l, s, blocks):
    fi = s // 2
    w_in = k.I["ffn_w_in"][l, fi].rearrange("(c p) n -> p c n", p=128)
    w_out = k.I["ffn_w_out"][l, fi].rearrange("(j p) n -> p j n", p=128)
    maxw = max(sum(w for (_, w, _) in b) for b in blocks)
    hb = P.sbuf("ffn_h", [128, KC, maxw], BF16)
    xb = P.sbuf("ffn_x", [128, KC, maxw], F32)
    gb = P.sbuf("ffn_g", [128, FHC, maxw], BF16)
    ob = P.sbuf("ffn_o", [128, KC, maxw], BF16)
    rstd = P.sbuf("ffn_rstd", [128, maxw], F32)
    tmp = [P.sbuf("ffn_tmp%d" % i, [128, 512], F32) for i in range(2)]
    sq = [P.sbuf("ffn_sq%d" % i, [128, 512], BF16) for i in range(2)]
    sl = [P.sbuf("ffn_sl%d" % i, [128, 512], F32) for i in range(2)]
    wi = [P.sbuf("ffn_wi%d" % i, [128, KC, 512], BF16) for i in range(2)]
    wo = [P.sbuf("ffn_wo%d" % i, [128, FHC, 128], BF16) for i in range(2)]
    ps_ss = k.ps[0]
    ps_a = [k.ps[1], k.ps[2]]
    ps_b = [k.ps[3], k.ps[4]]
    ps_o = [k.ps[5], k.ps[6]]
    cnt = 0
    for blk in blocks:
        subs = []
        o = 0
        for (c0, w, st) in blk:
            subs.append((o, c0, w, st))
            o += w
        for (o, c0, w, st) in subs:
            P.dma(xb[:, :, o:o + w], k.xres[:, :, c0:c0 + w])
        for (o, c0, w, st) in subs:
            sumsq_rstd(P, k, lambda c, oo, ww: xb[:, c, oo:oo + ww], KC,
                       [(o, w)], rstd, ps_ss, sq, 1.0 / D)
            for c in range(KC):
                t = tmp[c % 2]
                P.tt(t[:, :w], xb[:, c, o:o + w], rstd[:, o:o + w], ALU.mult, e="pool")
                P.ts(hb[:, c, o:o + w], t[:, :w], k.Asc[:, s, c, st:st + 1], k.Bsh[:, s, c, st:st + 1],
                     op0=ALU.mult, op1=ALU.add)
        for j2 in range(FHC // 2):
            w = wi[j2 % 2]
            P.dma(w[:, :, 0:256], w_in[:, :, j2 * 256:(j2 + 1) * 256], q="pool")
            P.dma(w[:, :, 256:512], w_in[:, :, FH + j2 * 256:FH + (j2 + 1) * 256], q="pool")
            for jj in range(2):
                j = 2 * j2 + jj
                for (o, c0, ww, st) in subs:
                    pa = ps_a[cnt % 2]
                    pb = ps_b[cnt % 2]
                    slt = sl[cnt % 2]
                    cnt += 1
                    for c in range(KC):
                        P.mm(pa[:, :ww], w[:, c, jj * 128:(jj + 1) * 128], hb[:, c, o:o + ww],
                             start=(c == 0), stop=(c == KC - 1))
                    for c in range(KC):
                        P.mm(pb[:, :ww], w[:, c, 256 + jj * 128:256 + (jj + 1) * 128], hb[:, c, o:o + ww],
                             start=(c == 0), stop=(c == KC - 1))
                    P.act(slt[:, :ww], pa[:, :ww], AF.Silu)
                    P.tt(gb[:, j, o:o + ww], slt[:, :ww], pb[:, :ww], ALU.mult)
        for c in range(KC):
            w = wo[c % 2]
            P.dma(w[:], w_out[:, :, c * 128:(c + 1) * 128], q="pool")
            for (o, c0, ww, st) in subs:
                po = ps_o[cnt % 2]
                cnt += 1
                for j in range(FHC):
                    P.mm(po[:, :ww], w[:, j, :], gb[:, j, o:o + ww], start=(j == 0), stop=(j == FHC - 1))
                P.copy(ob[:, c, o:o + ww], po[:, :ww], e="act")
        for (o, c0, w, st) in subs:
            sumsq_rstd(P, k, lambda c, oo, ww: ob[:, c, oo:oo + ww], KC, [(o, w)], rstd, ps_ss, sq, 1.0 / D)
            for c in range(KC):
                t = tmp[c % 2]
                P.stt(t[:, :w], ob[:, c, o:o + w], k.Gg[:, s, c, st:st + 1], rstd[:, o:o + w], ALU.mult, ALU.mult)
                P.tt(xb[:, c, o:o + w], xb[:, c, o:o + w], t[:, :w], ALU.add, e="pool")
            P.dma(k.xres[:, :, c0:c0 + w], xb[:, :, o:o + w], q="sp")


FULL_BLOCKS = [
    [(0, 256, 1), (256, 512, 0)],
    [(768, 384, 0), (1152, 384, 0)],
    [(1536, 384, 0), (1920, 384, 0)],
]
LAT_BLOCKS = [
    [(256, 384, 0), (640, 384, 0)],
    [(1024, 384, 0), (1408, 384, 0)],
    [(1792, 512, 0)],
]


CT0 = 2
LT0 = 260
HW = 2310
ALLSUBS = [(0, 256, 1), (256, 512, 0), (768, 512, 0), (1280, 512, 0), (1792, 512, 0)]
LATSUBS = ALLSUBS[1:]
C_CKV, C_KR, C_NK, C_NV, C_U, C_CQ, C_NQ, C_HY, C_GT = 0, 256, 320, 832, 1344, 1856, 2112, 2624, 4160


def hcol(xc):
    return xc + CT0 if xc < NCTX else xc - NCTX + LT0


def declare_mixer_inputs(P, I, nl):
    nc = P.nc

    def inp(name, shape):
        I[name] = nc.dram_tensor(name, list(shape), F32, kind="ExternalInput").ap()
    inp("w_in", [nl, D, N_IN])
    inp("mla_g_q", [nl, 256]); inp("mla_g_kv", [nl, 256])
    inp("mla_w_uq", [nl, 256, 4, 192]); inp("mla_w_ukv", [nl, 256, 4, 256])
    inp("w_branch", [nl, 4, 512, D]); inp("w_out", [nl, D, D])
    inp("rope_c", [64, NLAT]); inp("rope_s", [64, NLAT])


def mixer_setup(P, k):
    nc = P.nc
    k.br = [nc.dram_tensor("br%d" % n, [512, T], BF16, kind="Internal").ap() for n in range(4)]
    k.hmx = nc.dram_tensor("hmx", [128, KC, HW], BF16, kind="Internal").ap()


def alloc_hmix(P, k):
    k.hmix = P.sbuf("hmix", [128, KC, HW], BF16)
    for c0 in (0, 258, 2308):
        P.memset(k.hmix[:, :, c0:c0 + 2], 0.0)


def mixer_modnorm(P, k):
    with P.scope():
        rstd = P.sbuf("mn_rstd", [128, 512], F32)
        tmp = [P.sbuf("mn_tmp%d" % i, [128, 512], F32) for i in range(2)]
        sq = [P.sbuf("mn_sq%d" % i, [128, 512], BF16) for i in range(2)]
        xt = [P.sbuf("mn_x%d" % i, [128, KC, 512], F32) for i in range(2)]
        for si, (c0, w, st) in enumerate(ALLSUBS):
            xb = xt[si % 2]
            P.dma(xb[:, :, :w], k.xres[:, :, c0:c0 + w])
            sumsq_rstd(P, k, lambda c, oo, ww, xb=xb: xb[:, c, oo:oo + ww], KC, [(0, w)], rstd, k.ps[0], sq, 1.0 / D)
            h0 = hcol(c0)
            for c in range(KC):
                t = tmp[c % 2]
                P.tt(t[:, :w], xb[:, c, 0:w], rstd[:, 0:w], ALU.mult, e="pool")
                P.ts(k.hmix[:, c, h0:h0 + w], t[:, :w], k.Asc[:, 1, c, st:st + 1], k.Bsh[:, 1, c, st:st + 1],
                     op0=ALU.mult, op1=ALU.add)
    P.dma(k.hmx, k.hmix[:], q="sp")


def load_col_vec(P, dst, src_1d, nchunk):
    P.dma(dst, src_1d.rearrange("(c p) -> p c", p=128), allow_slow_non_contiguous=True)


def mla_branch(P, k, l, ctx_out):
    nc = P.nc
    I = k.I
    w_in = I["w_in"][l].rearrange("(c p) n -> p c n", p=128)
    SC = 192.0 ** -0.5
    subs = ALLSUBS
    qsubs = ALLSUBS if ctx_out else LATSUBS
    with P.scope():
        wckv = P.sbuf("wckv", [128, KC, 256], BF16)
        P.dma(wckv[:], w_in[:, :, C_CKV:C_CKV + 256], q="pool")
        wkr = P.sbuf("wkr", [128, KC, 128], BF16)
        P.dma(wkr[:, :, 0:64], w_in[:, :, C_KR:C_KR + 64], q="pool")
        for a in range(2):
            for hf in range(2):
                P.dma(wkr[:, :, 64 + a * 32 + hf * 16:64 + a * 32 + hf * 16 + 16],
                      w_in[:, :, C_KR + a * 32 + (1 - hf) * 16:C_KR + a * 32 + (1 - hf) * 16 + 16], q="pool")
        wcq = P.sbuf("wcq", [128, KC, 256], BF16)
        P.dma(wcq[:], w_in[:, :, C_CQ:C_CQ + 256], q="pool")
        wukv = P.sbuf("wukv", [128, 2, 4, 256], BF16)
        P.dma(wukv[:], I["mla_w_ukv"][l].rearrange("(c p) h e -> p c h e", p=128), q="pool")
        wuq = P.sbuf("wuq", [128, 2, 4, 192], BF16)
        P.dma(wuq[:], I["mla_w_uq"][l].rearrange("(c p) h e -> p c h e", p=128), q="pool")
        wuqs = P.sbuf("wuqs", [128, 2, 4, 64], BF16)
        uq_r = I["mla_w_uq"][l].rearrange("(c p) h e -> p c h e", p=128)
        for a in range(2):
            for hf in range(2):
                for c in range(2):
                    P.dma(wuqs[:, c, :, a * 32 + hf * 16:a * 32 + hf * 16 + 16],
                          uq_r[:, c, :, 128 + a * 32 + (1 - hf) * 16:128 + a * 32 + (1 - hf) * 16 + 16], q="pool")
        gkv = P.sbuf("gkv", [128, 2], F32)
        load_col_vec(P, gkv[:], I["mla_g_kv"][l], 2)
        gq = P.sbuf("gq", [128, 2], F32)
        load_col_vec(P, gq[:], I["mla_g_q"][l], 2)
        ropc_t = [P.sbuf("ropc%d" % i, [64, 512], F32) for i in range(2)]
        rops_t = [P.sbuf("rops%d" % i, [64, 512], F32) for i in range(2)]
        rcnt = [0]

        def rope_tabs(l0, w):
            i = rcnt[0] % 2
            rcnt[0] += 1
            P.dma(ropc_t[i][:, :w], I["rope_c"][:, l0:l0 + w])
            P.dma(rops_t[i][:, :w], I["rope_s"][:, l0:l0 + w])
            return ropc_t[i], rops_t[i]
        nkv = P.sbuf("nkv", [128, 2, T], BF16)
        nq = P.sbuf("nq", [128, 2, T], BF16)
        krope = P.sbuf("krope", [64, T], BF16)
        vall = P.sbuf("vall", [128, 18, 128], BF16)
        aT = P.sbuf("aT", [128, T], BF16)
        raw = P.sbuf("raw", [128, 2, 512], F32)
        rstd = P.sbuf("rstd", [128, 512], F32)
        sq = [P.sbuf("sq%d" % i, [128, 512], BF16) for i in range(2)]
        t1 = P.sbuf("t1", [128, 512], F32)
        t2 = P.sbuf("t2", [128, 512], F32)
        ps = k.ps

        def lowrank_norm(wt, gvec, dst):
            for (c0, w, st) in (subs if dst is nkv else qsubs):
                h0 = hcol(c0)
                for m in range(2):
                    for c in range(KC):
                        P.mm(ps[1 + m][:, :w], wt[:, c, m * 128:(m + 1) * 128], k.hmix[:, c, h0:h0 + w],
                             start=(c == 0), stop=(c == KC - 1))
                    P.copy(raw[:, m, :w], ps[1 + m][:, :w], e="act")
                sumsq_rstd(P, k, lambda c, oo, ww: raw[:, c, oo:oo + ww], 2, [(0, w)], rstd, ps[0], sq, 1.0 / 256)
                for m in range(2):
                    P.stt(dst[:, m, c0:c0 + w], raw[:, m, :w], gvec[:, m:m + 1], rstd[:, :w], ALU.mult, ALU.mult)

        lowrank_norm(wckv, gkv, nkv)
        lowrank_norm(wcq, gq, nq)
        for (c0, w, st) in subs:
            h0 = hcol(c0)
            for hh in range(2):
                for c in range(KC):
                    P.mm(ps[1 + hh][0:64, :w], wkr[:, c, hh * 64:(hh + 1) * 64], k.hmix[:, c, h0:h0 + w],
                         start=(c == 0), stop=(c == KC - 1))
            if st == 1:
                P.copy(krope[:, c0:c0 + w], ps[1][0:64, :w], e="act")
            else:
                l0 = c0 - NCTX
                rc, rs = rope_tabs(l0, w)
                P.tt(t1[0:64, :w], ps[1][0:64, :w], rc[:, :w], ALU.mult)
                P.tt(t2[0:64, :w], ps[2][0:64, :w], rs[:, :w], ALU.mult)
                P.tt(krope[:, c0:c0 + w], t1[0:64, :w], t2[0:64, :w], ALU.add, e="pool")
        knT = P.sbuf("knT", [128, T], BF16)
        qnT = P.sbuf("qnT", [128, T], BF16)
        qrope = P.sbuf("qrope", [64, T], BF16)
        pT = [P.sbuf("pT%d" % i, [128, 512], BF16) for i in range(3)]
        rden = P.sbuf("rden", [128, 512], F32)
        for hd in range(4):
            for tc in range(18):
                pv = ps[1 + tc % 2]
                for c in range(2):
                    P.mm(pv[:, 0:128], nkv[:, c, tc * 128:(tc + 1) * 128], wukv[:, c, hd, 128:256], start=(c == 0), stop=(c == 1))
                P.copy(vall[:, tc, :], pv[:, 0:128], e=("act" if tc % 2 else "dve"))
            for (c0, w, st) in subs:
                for c in range(2):
                    P.mm(ps[1][:, :w], wukv[:, c, hd, 0:128], nkv[:, c, c0:c0 + w], start=(c == 0), stop=(c == 1))
                P.copy(knT[:, c0:c0 + w], ps[1][:, :w], e="act")
            for (c0, w, st) in qsubs:
                for c in range(2):
                    P.mm(ps[1][:, :w], wuq[:, c, hd, 0:128], nq[:, c, c0:c0 + w], start=(c == 0), stop=(c == 1))
                P.copy(qnT[:, c0:c0 + w], ps[1][:, :w], e="act")
                for c in range(2):
                    P.mm(ps[2][0:64, :w], wuq[:, c, hd, 128:192], nq[:, c, c0:c0 + w], start=(c == 0), stop=(c == 1))
                if st == 1:
                    P.copy(qrope[:, c0:c0 + w], ps[2][0:64, :w], e="dve")
                else:
                    for c in range(2):
                        P.mm(ps[3][0:64, :w], wuqs[:, c, hd, :], nq[:, c, c0:c0 + w], start=(c == 0), stop=(c == 1))
                    l0 = c0 - NCTX
                    rc, rs = rope_tabs(l0, w)
                    P.tt(t1[0:64, :w], ps[2][0:64, :w], rc[:, :w], ALU.mult)
                    P.tt(t2[0:64, :w], ps[3][0:64, :w], rs[:, :w], ALU.mult)
                    P.tt(qrope[:, c0:c0 + w], t1[0:64, :w], t2[0:64, :w], ALU.add, e="pool")
            cnt = 0
            for qi, (c0, w, st) in enumerate(qsubs):
                kcs = list(range(2)) if st == 1 else list(range(18))
                pso, psd = (ps[6], ps[7]) if qi % 2 == 0 else (ps[2], ps[3])
                base = cnt
                cnt += len(kcs)

                def emit_s(i):
                    kc = kcs[i]
                    pss = ps[4 + (base + i) % 2]
                    P.mm(pss[:, :w], knT[:, kc * 128:(kc + 1) * 128], qnT[:, c0:c0 + w], start=True, stop=False)
                    P.mm(pss[:, :w], krope[:, kc * 128:(kc + 1) * 128], qrope[:, c0:c0 + w], start=False, stop=True)
                emit_s(0)
                for i, kc in enumerate(kcs):
                    if i + 1 < len(kcs):
                        emit_s(i + 1)
                    pss = ps[4 + (base + i) % 2]
                    p = pT[(base + i) % 3]
                    P.act(p[:, :w], pss[:, :w], AF.Exp, scale=SC)
                    P.mm(pso[:, :w], vall[:, kc, :], p[:, :w], start=(i == 0), stop=(i == len(kcs) - 1))
                    P.mm(psd[:, :w], k.ones_bf[:], p[:, :w], start=(i == 0), stop=(i == len(kcs) - 1))
                P.recip(rden[:, :w], psd[:, :w])
                P.tt(aT[:, c0:c0 + w], pso[:, :w], rden[:, :w], ALU.mult)
            cols0 = 0 if ctx_out else NCTX
            P.dma(k.br[0][hd * 128:(hd + 1) * 128, cols0:T], aT[:, cols0:T], q="sp")


def declare_na_inputs(P, I, nl):
    nc = P.nc

    def inp(name, shape):
        I[name] = nc.dram_tensor(name, list(shape), F32, kind="ExternalInput").ap()
    inp("na_G", [nl, 128, 8 * 16 * 64])
    inp("na_mv", [2, 16 * 128]); inp("na_sel", [2, 128]); inp("na_cm", [128, 64])


def na_branch(P, k, l, ctx_out, geo):
    nc = P.nc
    I = k.I
    w_in = I["w_in"][l].rearrange("(c p) n -> p c n", p=128)
    ps = k.ps
    with P.scope():
        BP = P.sbuf("na_BP", [128, 8, 16, 64], BF16)
        cm = P.sbuf("na_cm", [128, 64], F32)
        P.dma(cm[:], I["na_cm"])
        gt = [P.sbuf("na_gt%d" % i, [128, 16, 64], F32) for i in range(2)]
        Gr = I["na_G"][l].rearrange("p (h i q) -> p h i q", h=8, i=16)
        for h in range(8):
            P.dma(gt[h % 2][:], Gr[:, h])
            P.tt(BP[:, h], gt[h % 2][:], cm[:].unsqueeze(1).broadcast_to([128, 16, 64]), ALU.add)
        mvf = P.sbuf("na_mvf", [2, 16 * 128], F32)
        P.dma(mvf[:], I["na_mv"])
        mv = P.sbuf("na_mv", [2, 16 * 128], BF16)
        P.copy(mv[:], mvf[:])
        self_f = P.sbuf("na_self", [2, 128], F32)
        P.dma(self_f[:], I["na_sel"])
        sel = P.sbuf("na_sel", [2, 128], BF16)
        P.copy(sel[:], self_f[:])
        wk = P.sbuf("na_wk", [128, KC, 128], BF16)
        wq = P.sbuf("na_wq", [128, KC, 128], BF16)
        wv = P.sbuf("na_wv", [128, KC, 128], BF16)
        KT = P.sbuf("na_KT", [128, T], BF16)
        QT = P.sbuf("na_QT", [128, T], BF16)
        V = P.sbuf("na_V", [128, 18, 128], BF16)
        dT = P.sbuf("na_dT", [128, T], BF16)
        pT = [P.sbuf("na_pT%d" % i, [128, 128], BF16) for i in range(3)]
        rden = P.sbuf("na_rden", [128, 128], F32)
        qsubs = ALLSUBS if ctx_out else LATSUBS
        cnt = 0
        for hp in range(4):
            P.dma(wk[:], w_in[:, :, C_NK + hp * 128:C_NK + (hp + 1) * 128], q="pool")
            P.dma(wq[:], w_in[:, :, C_NQ + hp * 128:C_NQ + (hp + 1) * 128], q="pool")
            P.dma(wv[:], w_in[:, :, C_NV + hp * 128:C_NV + (hp + 1) * 128], q="pool")
            for (c0, w, st) in ALLSUBS:
                h0 = hcol(c0)
                for c in range(KC):
                    P.mm(ps[1][:, :w], wk[:, c, :], k.hmix[:, c, h0:h0 + w], start=(c == 0), stop=(c == KC - 1))
                P.copy(KT[:, c0:c0 + w], ps[1][:, :w], e="act")
            for (c0, w, st) in qsubs:
                h0 = hcol(c0)
                for c in range(KC):
                    P.mm(ps[2][:, :w], wq[:, c, :], k.hmix[:, c, h0:h0 + w], start=(c == 0), stop=(c == KC - 1))
                P.ts(QT[:, c0:c0 + w], ps[2][:, :w], 0.125, None, op0=ALU.mult)
            for tc in range(18):
                h0 = hcol(tc * 128)
                pv = ps[1 + tc % 2]
                for c in range(KC):
                    P.mm(pv[:, 0:128], k.hmix[:, c, h0:h0 + 128], wv[:, c, :], start=(c == 0), stop=(c == KC - 1))
                P.copy(V[:, tc, :], pv[:, 0:128], e=("act" if tc % 2 else "dve"))
            qblocks = []
            if ctx_out:
                qblocks += [(0, []), (128, [])]
            for i in range(16):
                qblocks.append((NCTX + i * 128, geo[i]))
            for qi, (q0, loc) in enumerate(qblocks):
                chunks = [(0, None, None), (1, None, None)] + [(2 + j, idxp, combo) for (j, idxp, combo) in loc]
                pso, psd = (ps[6], ps[7]) if qi % 2 == 0 else (ps[2], ps[3])
                work = [(hh, ci) for hh in range(2) for ci in range(len(chunks))]
                base = cnt
                cnt += len(work)

                def emit_s(wi):
                    hh, ci = work[wi]
                    kc, idxp, combo = chunks[ci]
                    h = 2 * hp + hh
                    pr = slice(hh * 64, (hh + 1) * 64)
                    pss = ps[4 + (base + wi) % 2]
                    last_s = (idxp is None)
                    P.mm(pss[:, 0:128], KT[pr, kc * 128:(kc + 1) * 128], QT[pr, q0:q0 + 128], start=True, stop=last_s)
                    if idxp is not None:
                        need_mask = combo != 0
                        P.mm(pss[:, 0:128], k.ident_bf[:], BP[:, h, idxp:idxp + 2, :], start=False, stop=not need_mask)
                        if need_mask:
                            P.mm(pss[:, 0:128], mv[0:2, combo * 128:(combo + 1) * 128], sel[0:2, :], start=False, stop=True)
                emit_s(0)
                for wi, (hh, ci) in enumerate(work):
                    if wi + 1 < len(work):
                        emit_s(wi + 1)
                    kc = chunks[ci][0]
                    pr = slice(hh * 64, (hh + 1) * 64)
                    pss = ps[4 + (base + wi) % 2]
                    p = pT[(base + wi) % 3]
                    P.act(p[:], pss[:, 0:128], AF.Exp)
                    P.mm(pso[pr, 0:128], V[:, kc, pr], p[:], start=(ci == 0), stop=(ci == len(chunks) - 1))
                    P.mm(psd[pr, 0:128], k.ones_bf[:, 0:64], p[:], start=(ci == 0), stop=(ci == len(chunks) - 1))
                P.recip(rden[:], psd[:, 0:128])
                P.tt(dT[:, q0:q0 + 128], pso[:, 0:128], rden[:], ALU.mult)
            cols0 = 0 if ctx_out else NCTX
            P.dma(k.br[3][hp * 128:(hp + 1) * 128, cols0:T], dT[:, cols0:T], q="sp")


def post_norm_residual(P, k, ob, s, subs_local, rstd, ps_ss, sq, tmp, xb):
    for (o, c0, w, st) in subs_local:
        P.dma(xb[:, :, o:o + w], k.xres[:, :, c0:c0 + w])
        sumsq_rstd(P, k, lambda c, oo, ww: ob[:, c, oo:oo + ww], KC, [(o, w)], rstd, ps_ss, sq, 1.0 / D)
        for c in range(KC):
            t = tmp[c % 2]
            P.stt(t[:, :w], ob[:, c, o:o + w], k.Gg[:, s, c, st:st + 1], rstd[:, o:o + w], ALU.mult, ALU.mult)
            P.tt(xb[:, c, o:o + w], xb[:, c, o:o + w], t[:, :w], ALU.add, e="pool")
        P.dma(k.xres[:, :, c0:c0 + w], xb[:, :, o:o + w], q="sp")


def merge_phase(P, k, l, ctx_out):
    I = k.I
    w_in = I["w_in"][l].rearrange("(c p) n -> p c n", p=128)
    ps = k.ps
    subs = ALLSUBS if ctx_out else LATSUBS
    with P.scope():
        wg = P.sbuf("mg_wg", [128, KC, 4 * D], BF16)
        for j in range(8):
            P.dma(wg[:, :, j * 512:(j + 1) * 512], w_in[:, :, C_GT + j * 512:C_GT + (j + 1) * 512], q="pool")
        wb = P.sbuf("mg_wb", [128, 4, 4, D], BF16)
        for n in range(4):
            P.dma(wb[:, n], I["w_branch"][l, n].rearrange("(kk p) d -> p kk d", p=128), q="pool")
        wo = P.sbuf("mg_wo", [128, KC, D], BF16)
        for j in range(2):
            P.dma(wo[:, :, j * 512:(j + 1) * 512], I["w_out"][l].rearrange("(kk p) d -> p kk d", p=128)[:, :, j * 512:(j + 1) * 512], q="pool")
        brt = [[P.sbuf("mg_br%d_%d" % (n, i), [128, 4, 512], BF16) for n in range(4)] for i in range(1)]
        mt = P.sbuf("mg_mt", [128, KC, 512], BF16)
        ob = P.sbuf("mg_ob", [128, KC, 512], BF16)
        sg = [P.sbuf("mg_sg%d" % i, [128, 512], F32) for i in range(2)]
        acc = P.sbuf("mg_acc", [128, 512], F32)
        tm = [P.sbuf("mg_tm%d" % i, [128, 512], F32) for i in range(2)]
        rstd = P.sbuf("mg_rstd", [128, 512], F32)
        sq = [P.sbuf("mg_sq%d" % i, [128, 512], BF16) for i in range(2)]
        xbm = P.sbuf("mg_x", [128, KC, 512], F32)
        hbt = [P.sbuf("mg_h%d" % i, [128, KC, 512], BF16) for i in range(2)]
        cnt = 0
        for si, (c0, w, st) in enumerate(subs):
            h0 = hcol(c0)
            hb_ = hbt[si % 2]
            P.dma(hb_[:, :, :w], k.hmx[:, :, h0:h0 + w])
            bt = brt[0]
            for n in range(4):
                P.dma(bt[n][:, :, :w], k.br[n].rearrange("(c p) t -> p c t", p=128)[:, :, c0:c0 + w])
            for dc in range(KC):
                for n in range(4):
                    pg = ps[1 + cnt % 2]
                    pp = ps[3 + cnt % 2]
                    s_ = sg[cnt % 2]
                    cnt += 1
                    col = n * D + dc * 128
                    for c in range(KC):
                        P.mm(pg[:, :w], wg[:, c, col:col + 128], hb_[:, c, :w], start=(c == 0), stop=(c == KC - 1))
                    for kk in range(4):
                        P.mm(pp[:, :w], wb[:, n, kk, dc * 128:(dc + 1) * 128], bt[n][:, kk, :w], start=(kk == 0), stop=(kk == 3))
                    P.act(s_[:, :w], pg[:, :w], AF.Sigmoid)
                    if n == 0:
                        P.tt(acc[:, :w], s_[:, :w], pp[:, :w], ALU.mult)
                    else:
                        t = tm[n % 2]
                        P.tt(t[:, :w], s_[:, :w], pp[:, :w], ALU.mult)
                        if n < 3:
                            P.tt(acc[:, :w], acc[:, :w], t[:, :w], ALU.add, e="pool")
                        else:
                            P.tt(mt[:, dc, :w], acc[:, :w], t[:, :w], ALU.add, e="pool")
            for dc in range(KC):
                po = ps[5 + dc % 2]
                for kk in range(KC):
                    P.mm(po[:, :w], wo[:, kk, dc * 128:(dc + 1) * 128], mt[:, kk, :w], start=(kk == 0), stop=(kk == KC - 1))
                P.copy(ob[:, dc, :w], po[:, :w], e="act")
            post_norm_residual(P, k, ob, 1, [(0, c0, w, st)], rstd, ps[0], sq, tm, xbm)


def declare_hyena_inputs(P, I, nl, with_ctx=True):
    nc = P.nc

    def inp(name, shape):
        I[name] = nc.dram_tensor(name, list(shape), F32, kind="ExternalInput").ap()
    inp("hy_conv_w", [nl, 3, 1536]); inp("hy_conv_b", [nl, 1536]); inp("hy_bias", [nl, 512])
    inp("hy_w1", [nl, 33, 64]); inp("hy_b1", [nl, 64]); inp("hy_freq1", [nl, 64])
    inp("hy_w2", [nl, 64, 64]); inp("hy_b2", [nl, 64]); inp("hy_freq2", [nl, 64]); inp("hy_w3", [nl, 64, 1024])
    for nm, N in (("lat", NLAT), ("ctx", NCTX)):
        if nm == "ctx" and not with_ctx:
            continue
        for t in ("c", "s", "ct", "st"):
            inp("dft_%s_%s" % (t, nm), [N, N])
        inp("hy_zT_%s" % nm, [33, N]); inp("hy_decay_%s" % nm, [N, 512])


PI = math.pi


def sin_reduced(P, dst, src, shp, tmp):
    P.ts(tmp, src, PI, -2.0 * PI, op0=ALU.is_gt, op1=ALU.mult)
    P.tt(src, src, tmp, ALU.add)
    P.ts(tmp, src, -PI, 2.0 * PI, op0=ALU.is_lt, op1=ALU.mult)
    P.tt(src, src, tmp, ALU.add)
    P.act(dst, src, AF.Sin)


def hyena_branch(P, k, l, seq):
    nc = P.nc
    I = k.I
    nm, N, hbase, xbase = seq
    NT = N // 128
    NF = NT
    CW = min(512, N)
    w_in = I["w_in"][l].rearrange("(c p) n -> p c n", p=128)
    ps = k.ps
    dft_c = I["dft_c_%s" % nm].rearrange("(tc p) f -> p tc f", p=128)
    dft_s = I["dft_s_%s" % nm].rearrange("(tc p) f -> p tc f", p=128)
    dft_ct = I["dft_ct_%s" % nm].rearrange("(fc p) t -> p fc t", p=128)
    dft_st = I["dft_st_%s" % nm].rearrange("(fc p) t -> p fc t", p=128)
    with P.scope():
        Kre = P.sbuf("hy_Kre", [128, NF, 512], BF16)
        Kim = P.sbuf("hy_Kim", [128, NF, 512], BF16)
        Ct = [P.sbuf("hy_Ct%d" % i, [128, NT, 128], BF16) for i in range(2)]
        St = [P.sbuf("hy_St%d" % i, [128, NT, 128], BF16) for i in range(2)]
        tA = P.sbuf("hy_tA", [128, 512], F32)
        tB = P.sbuf("hy_tB", [128, 512], F32)
        tC = P.sbuf("hy_tC", [128, 512], F32)
        tD = P.sbuf("hy_tD", [128, 512], F32)
        with P.scope():
            w1 = P.sbuf("hy_w1", [33, 64], F32); P.dma(w1[:], I["hy_w1"][l])
            w2 = P.sbuf("hy_w2", [64, 64], F32); P.dma(w2[:], I["hy_w2"][l])
            w3 = P.sbuf("hy_w3", [64, 1024], F32); P.dma(w3[:], I["hy_w3"][l])
            cols = P.sbuf("hy_cols", [64, 6], F32)
            for j, nm_ in enumerate(["hy_b1", "hy_freq1", "hy_b2", "hy_freq2"]):
                P.dma(cols[:, j:j + 1], I[nm_][l].rearrange("(p o) -> p o", o=1))
            P.tt(cols[:, 4:5], cols[:, 0:1], cols[:, 1:2], ALU.mult)
            P.tt(cols[:, 5:6], cols[:, 2:3], cols[:, 3:4], ALU.mult)
            zT = P.sbuf("hy_zT", [33, N], F32); P.dma(zT[:], I["hy_zT_%s" % nm])
            h1 = P.sbuf("hy_h1", [64, N], F32)
            h2 = P.sbuf("hy_h2", [64, N], F32)
            filt = P.sbuf("hy_filt", [128, NT, 1024], BF16)
            dec = [P.sbuf("hy_dec%d" % i, [128, 512], F32) for i in range(2)]
            for cb in range(N // CW):
                cs = slice(cb * CW, (cb + 1) * CW)
                P.mm(ps[1][0:64, :CW], w1[:, :], zT[:, cs], start=True, stop=True)
                P.act(tA[0:64, :CW], ps[1][0:64, :CW], AF.Identity, bias=cols[:, 4:5], scale=cols[:, 1:2])
                sin_reduced(P, h1[:, cs], tA[0:64, :CW], None, tB[0:64, :CW])
            for cb in range(N // CW):
                cs = slice(cb * CW, (cb + 1) * CW)
                P.mm(ps[1][0:64, :CW], w2[:, :], h1[:, cs], start=True, stop=True)
                P.act(tA[0:64, :CW], ps[1][0:64, :CW], AF.Identity, bias=cols[:, 5:6], scale=cols[:, 3:4])
                sin_reduced(P, h2[:, cs], tA[0:64, :CW], None, tB[0:64, :CW])
            for tc in range(NT):
                d = dec[tc % 2]
                P.dma(d[:], I["hy_decay_%s" % nm][tc * 128:(tc + 1) * 128, :])
                for hf in range(2):
                    pp = ps[1 + hf]
                    P.mm(pp[:, :], h2[:, tc * 128:(tc + 1) * 128], w3[:, hf * 512:(hf + 1) * 512], start=True, stop=True)
                    P.tt(filt[:, tc, hf * 512:(hf + 1) * 512], pp[:, :], d[:], ALU.mult)
            for fc in range(NF):
                c_ = Ct[fc % 2]; s_ = St[fc % 2]
                P.dma(c_[:], dft_c[:, :, fc * 128:(fc + 1) * 128], q="pool")
                P.dma(s_[:], dft_s[:, :, fc * 128:(fc + 1) * 128], q="pool")
                for bi, (mat, half) in enumerate([(c_, 0), (s_, 0), (c_, 1), (s_, 1)]):
                    for tc in range(NT):
                        P.mm(ps[1 + bi][:, :], mat[:, tc, :], filt[:, tc, half * 512:(half + 1) * 512],
                             start=(tc == 0), stop=(tc == NT - 1))
                P.copy(tA[:], ps[1][:, :], e="act")
                P.tt(Kre[:, fc, :], tA[:], ps[3][:, :], ALU.add)
                P.copy(tB[:], ps[2][:, :], e="act")
                P.tt(Kim[:, fc, :], tB[:], ps[4][:, :], ALU.subtract)
        s_bf = P.sbuf("hy_s", [128, NT, 512], BF16)
        x0_bf = P.sbuf("hy_x0", [128, NT, 512], BF16)
        with P.scope():
            wraw = P.sbuf("hy_wraw", [128, KC, 512], BF16)
            Wk = [P.sbuf("hy_Wk%d" % i, [128, KC, 512], BF16) for i in range(3)]
            cw = P.sbuf("hy_cw", [128, 512], F32)
            cbf = P.sbuf("hy_cbf", [1, 1536], F32)
            P.dma(cbf[:], I["hy_conv_b"][l].rearrange("(o n) -> o n", o=1))
            cbb = P.sbuf("hy_cbb", [1, 1536], BF16)
            P.copy(cbb[:], cbf[:])
            for blk in range(3):
                P.dma(wraw[:], w_in[:, :, C_HY + blk * 512:C_HY + (blk + 1) * 512], q="pool")
                for kk in range(3):
                    P.dma(cw[:], I["hy_conv_w"][l, kk, blk * 512:(blk + 1) * 512].partition_broadcast(128))
                    P.tt(Wk[kk][:], wraw[:], cw[:].unsqueeze(1).broadcast_to([128, KC, 512]), ALU.mult)
                for tc in range(NT):
                    pp = ps[1 + tc % 2]
                    for kk in range(3):
                        c0 = hbase + tc * 128 + kk - 1
                        for c in range(KC):
                            P.mm(pp[:, :], k.hmix[:, c, c0:c0 + 128], Wk[kk][:, c, :], start=(kk == 0 and c == 0), stop=False)
                    P.mm(pp[:, :], k.ones_bf[0:1, 0:128], cbb[0:1, blk * 512:(blk + 1) * 512], start=False, stop=True)
                    if blk == 0:
                        P.copy(s_bf[:, tc, :], pp[:, :], e="act")
                    elif blk == 1:
                        P.tt(s_bf[:, tc, :], s_bf[:, tc, :], pp[:, :], ALU.mult)
                    else:
                        P.copy(x0_bf[:, tc, :], pp[:, :], e="act")
        Yre = P.sbuf("hy_Yre", [128, NF, 512], BF16)
        Yim = P.sbuf("hy_Yim", [128, NF, 512], BF16)
        for fc in range(NF):
            c_ = Ct[fc % 2]; s_ = St[fc % 2]
            P.dma(c_[:], dft_c[:, :, fc * 128:(fc + 1) * 128], q="pool")
            P.dma(s_[:], dft_s[:, :, fc * 128:(fc + 1) * 128], q="pool")
            pa = ps[1 + 2 * (fc % 2)]; pb = ps[2 + 2 * (fc % 2)]
            for tc in range(NT):
                P.mm(pa[:, :], c_[:, tc, :], s_bf[:, tc, :], start=(tc == 0), stop=(tc == NT - 1))
            for tc in range(NT):
                P.mm(pb[:, :], s_[:, tc, :], s_bf[:, tc, :], start=(tc == 0), stop=(tc == NT - 1))
            P.copy(tA[:], pa[:, :], e="act")
            P.copy(tB[:], pb[:, :], e="act")
            P.tt(tC[:], tA[:], Kre[:, fc, :], ALU.mult)
            P.tt(tD[:], tB[:], Kim[:, fc, :], ALU.mult, e="pool")
            P.tt(Yre[:, fc, :], tC[:], tD[:], ALU.subtract)
            P.tt(tC[:], tA[:], Kim[:, fc, :], ALU.mult, e="pool")
            P.tt(tD[:], tB[:], Kre[:, fc, :], ALU.mult)
            P.tt(Yim[:, fc, :], tC[:], tD[:], ALU.add, e="pool")
        bd = P.sbuf("hy_bd", [128, 512], F32)
        P.dma(bd[:], I["hy_bias"][l].partition_broadcast(128))
        bT = P.sbuf("hy_bT", [128, 4, N], BF16)
        otm = [P.sbuf("hy_otm%d" % i, [128, 512], BF16) for i in range(2)]
        for tc in range(NT):
            c_ = Ct[tc % 2]; s_ = St[tc % 2]
            P.dma(c_[:], dft_ct[:, :, tc * 128:(tc + 1) * 128], q="pool")
            P.dma(s_[:], dft_st[:, :, tc * 128:(tc + 1) * 128], q="pool")
            pp = ps[1 + tc % 2]
            for fc in range(NF):
                P.mm(pp[:, :], c_[:, fc, :], Yre[:, fc, :], start=(fc == 0), stop=False)
            for fc in range(NF):
                P.mm(pp[:, :], s_[:, fc, :], Yim[:, fc, :], start=False, stop=(fc == NF - 1))
            P.tt(tA[:], s_bf[:, tc, :], bd[:], ALU.mult)
            P.stt(tB[:], pp[:, :], 1.0 / N, tA[:], ALU.mult, ALU.add)
            o = otm[tc % 2]
            P.tt(o[:], tB[:], x0_bf[:, tc, :], ALU.mult, e="pool")
            pt = ps[5 + tc % 2][:].bitcast(BF16)
            for j in range(4):
                P.transpose(pt[:, j * 128:(j + 1) * 128], o[:, j * 128:(j + 1) * 128], k.ident_bf[:])
            P.copy(bT[:, :, tc * 128:(tc + 1) * 128], pt[:, 0:512].rearrange("p (j t) -> p j t", j=4), e="act")
        P.dma(k.br[1].rearrange("(c p) t -> p c t", p=128)[:, :, xbase:xbase + N], bT[:], q="sp")


HY_LAT = ("lat", NLAT, LT0, NCTX)
HY_CTX = ("ctx", NCTX, CT0, 0)


NCH = 144


def declare_s5_inputs(P, I, nl):
    nc = P.nc

    def inp(name, shape):
        I[name] = nc.dram_tensor(name, list(shape), F32, kind="ExternalInput").ap()
    inp("s5_lam_re", [nl, 2, 2048]); inp("s5_lam_im", [nl, 2, 2048]); inp("s5_log_dt", [nl, 2, 32])
    inp("s5_b_re", [nl, 2, 2048, 16]); inp("s5_b_im", [nl, 2, 2048, 16])
    inp("s5_c_re", [nl, 2, 32, 16, 64]); inp("s5_c_im", [nl, 2, 32, 16, 64])
    inp("s5_d", [nl, 512]); inp("s5_w_glu", [nl, 512, 1024]); inp("s5_b_glu", [nl, 1024])
    inp("s5_mask", [2, 2, 128, 256])


def s5_setup(P, k):
    nc = P.nc
    k.u_tm = nc.dram_tensor("s5_u_tm", [T, 512], F32, kind="Internal").ap()
    k.y_tm = nc.dram_tensor("s5_y_tm", [T, 512], F32, kind="Internal").ap()


def s5_part1(P, k, l):
    I = k.I
    w_in = I["w_in"][l].rearrange("(c p) n -> p c n", p=128)
    ps = k.ps
    with P.scope():
        wu = P.sbuf("s5_wu", [128, KC, 512], BF16)
        P.dma(wu[:], w_in[:, :, C_U:C_U + 512], q="pool")
        ut = [P.sbuf("s5_ut%d" % i, [128, 512], F32) for i in range(2)]
        for tc in range(18):
            h0 = hcol(tc * 128)
            pp = ps[1 + tc % 2]
            for c in range(KC):
                P.mm(pp[:, :], k.hmix[:, c, h0:h0 + 128], wu[:, c, :], start=(c == 0), stop=(c == KC - 1))
            P.copy(ut[tc % 2][:], pp[:, :], e=("act" if tc % 2 else "dve"))
            P.dma(k.u_tm[tc * 128:(tc + 1) * 128, :], ut[tc % 2][:], q="sp")


def cmul(P, o_re, o_im, a_re, a_im, b_re, b_im, t1, t2, neg_im=False):
    P.tt(t1, a_re, b_re, ALU.mult)
    P.tt(t2, a_im, b_im, ALU.mult, e="pool")
    P.tt(o_re, t1, t2, ALU.subtract)
    P.tt(t1, a_re, b_im, ALU.mult)
    P.tt(t2, a_im, b_re, ALU.mult, e="pool")
    if neg_im:
        P.stt(o_im, t1, -1.0, t2, ALU.mult, ALU.subtract)
    else:
        P.tt(o_im, t1, t2, ALU.add)


def s5_part2(P, k, l, ctx_out):
    nc = P.nc
    I = k.I
    ps = k.ps
    with P.scope():
        U = P.sbuf("s5_U", [128, 32, 2, NCH], BF16)
        M = P.sbuf("s5_M", [128, 32, 2, 256], BF16)
        Qre = [P.sbuf("s5_Qre%d" % d, [128, 16, 256], BF16) for d in range(2)]
        nQim = [P.sbuf("s5_nQim%d" % d, [128, 16, 256], BF16) for d in range(2)]
        Xre = [P.sbuf("s5_Xre%d" % d, [128, 16, NCH], BF16) for d in range(2)]
        Xim = [P.sbuf("s5_Xim%d" % d, [128, 16, NCH], BF16) for d in range(2)]
        with P.scope():
            uc = P.sbuf("s5_uc", [128, 16 * 512], F32)
            ucv = uc[:].rearrange("p (g j h) -> p g j h", g=32, j=16)
            for (part, np_, c_lo) in (("lat", 128, 16), ("ctx", 16, 0)):
                rows = k.u_tm[NCTX:T, :] if part == "lat" else k.u_tm[0:NCTX, :]
                src = rows.rearrange("(c j) (g h) -> c g j h", j=16, g=32)
                for g in range(32):
                    P.dma(ucv[0:np_, g], src[:, g])
                cnt = 0
                for g in range(32):
                    for a in range(2):
                        pp = ps[1 + cnt % 4]
                        cnt += 1
                        P.transpose(pp[:, 0:np_], uc[0:np_, g * 256 + a * 128:g * 256 + (a + 1) * 128], k.ident[0:np_, 0:np_])
                        P.copy(U[:, g, a, c_lo:c_lo + np_], pp[:, 0:np_], e=("act" if cnt % 2 else "dve"))
        for d in range(2):
            with P.scope():
                sm = P.sbuf("s5_sm", [128, 40, 16], F32)
                slot = [0]

                def S():
                    i = slot[0]
                    slot[0] += 1
                    return sm[:, i, :]
                lre = S(); lim = S(); dt = S()
                P.dma(lre, I["s5_lam_re"][l, d].rearrange("(pr q) -> q pr", q=128), allow_slow_non_contiguous=True)
                P.dma(lim, I["s5_lam_im"][l, d].rearrange("(pr q) -> q pr", q=128), allow_slow_non_contiguous=True)
                ldt = I["s5_log_dt"][l, d].rearrange("(pr g2) -> g2 pr", g2=2)
                for g2 in range(2):
                    P.dma(sm[g2 * 64:(g2 + 1) * 64, 2, :], ldt[g2].partition_broadcast(64), allow_slow_non_contiguous=True)
                P.act(dt, dt, AF.Exp)
                P.ts(lre, lre, -1e-4, None, op0=ALU.min)
                a_ = S(); th = S(); t1 = S(); t2 = S()
                P.tt(a_, lre, dt, ALU.mult)
                P.tt(th, lim, dt, ALU.mult)
                mag = S(); imag = S()
                P.act(mag, a_, AF.Exp)
                P.act(imag, a_, AF.Exp, scale=-1.0)
                for _ in range(4):
                    P.ts(t1, th, PI, -2.0 * PI, op0=ALU.is_gt, op1=ALU.mult)
                    P.tt(th, th, t1, ALU.add)
                thc = S()
                P.ts(thc, th, PI / 2, None, op0=ALU.add)
                P.ts(t1, thc, PI, -2.0 * PI, op0=ALU.is_gt, op1=ALU.mult)
                P.tt(thc, thc, t1, ALU.add)
                sn = S(); cs = S()
                P.act(sn, th, AF.Sin)
                P.act(cs, thc, AF.Sin)
                lbr = S(); lbi = S(); lir = S(); lii = S()
                P.tt(lbr, mag, cs, ALU.mult); P.tt(lbi, mag, sn, ALU.mult)
                P.tt(lir, imag, cs, ALU.mult); P.stt(lii, imag, -1.0, sn, ALU.mult, ALU.mult)
                den = S(); icr = S(); ici = S()
                P.tt(den, lre, lre, ALU.mult); P.tt(t1, lim, lim, ALU.mult); P.tt(den, den, t1, ALU.add)
                P.recip(den, den)
                P.tt(icr, lre, den, ALU.mult); P.stt(ici, lim, -1.0, den, ALU.mult, ALU.mult)
                lm1 = S(); cfr = S(); cfi = S()
                P.ts(lm1, lbr, -1.0, None, op0=ALU.add)
                cmul(P, cfr, cfi, lm1, lbi, icr, ici, t1, t2)
                Ppr = P.sbuf("s5_Ppr", [128, 16, 17], F32); Ppi = P.sbuf("s5_Ppi", [128, 16, 17], F32)
                Pnr = P.sbuf("s5_Pnr", [128, 16, 17], F32); Pni = P.sbuf("s5_Pni", [128, 16, 17], F32)
                tw1 = P.sbuf("s5_tw1", [128, 16, 8], F32); tw2 = P.sbuf("s5_tw2", [128, 16, 8], F32)
                for (tr, ti, br_, bi_) in ((Ppr, Ppi, lbr, lbi), (Pnr, Pni, lir, lii)):
                    P.memset(tr[:, :, 0:1], 1.0); P.memset(ti[:, :, 0:1], 0.0)
                    P.copy(tr[:, :, 1], br_); P.copy(ti[:, :, 1], bi_)
                    n = 2
                    while n <= 16:
                        sqr = S() if False else None
                        h = n // 2
                        cmul(P, tr[:, :, n], ti[:, :, n], tr[:, :, h], ti[:, :, h], tr[:, :, h], ti[:, :, h], tw1[:, :, 0], tw2[:, :, 0])
                        cnt_ = min(n, 17 - n) - 1
                        if cnt_ > 0:
                            bre = tr[:, :, n:n + 1].broadcast_to([128, 16, cnt_]) if False else None
                            cmul(P, tr[:, :, n + 1:n + 1 + cnt_], ti[:, :, n + 1:n + 1 + cnt_],
                                 tr[:, :, 1:1 + cnt_], ti[:, :, 1:1 + cnt_],
                                 tr[:, :, n:n + 1].to_broadcast([128, 16, cnt_]), ti[:, :, n:n + 1].to_broadcast([128, 16, cnt_]),
                                 tw1[:, :, 0:cnt_], tw2[:, :, 0:cnt_])
                        n *= 2
                Ar = P.sbuf("s5_Ar", [128, 8, 16], F32); Ai = P.sbuf("s5_Ai", [128, 8, 16], F32); nAi = P.sbuf("s5_nAi", [128, 8, 16], F32)
                P.copy(Ar[:, 0, :], Ppr[:, :, 16]); P.copy(Ai[:, 0, :], Ppi[:, :, 16])
                for kk in range(1, 8):
                    cmul(P, Ar[:, kk, :], Ai[:, kk, :], Ar[:, kk - 1, :], Ai[:, kk - 1, :], Ar[:, kk - 1, :], Ai[:, kk - 1, :], t1, t2)
                P.ts(nAi[:], Ai[:], -1.0, None, op0=ALU.mult)
                Bre = P.sbuf("s5_Bre", [128, 16, 16], F32); Bim = P.sbuf("s5_Bim", [128, 16, 16], F32)
                P.dma(Bre[:], I["s5_b_re"][l, d].rearrange("(pr q) h -> q pr h", q=128))
                P.dma(Bim[:], I["s5_b_im"][l, d].rearrange("(pr q) h -> q pr h", q=128))
                Cre = P.sbuf("s5_Cre", [128, 16, 16], F32); Cim = P.sbuf("s5_Cim", [128, 16, 16], F32)
                for g2 in range(2):
                    for (dst, nm_) in ((Cre, "s5_c_re"), (Cim, "s5_c_im")):
                        src = I[nm_][l, d].rearrange("(pr g2) h p -> g2 pr p h", g2=2)[g2]
                        for pr_ in range(16):
                            P.dma(dst[g2 * 64:(g2 + 1) * 64, pr_, :], src[pr_], allow_slow_non_contiguous=True)
                tb1 = P.sbuf("s5_tb1", [128, 16, 16], F32); tb2 = P.sbuf("s5_tb2", [128, 16, 16], F32)
                bbr = P.sbuf("s5_bbr", [128, 16, 16], F32); bbi = P.sbuf("s5_bbi", [128, 16, 16], F32)
                bc = lambda x: x.unsqueeze(2).to_broadcast([128, 16, 16])
                cmul(P, bbr[:], bbi[:], Bre[:], Bim[:], bc(cfr), bc(cfi), tb1[:], tb2[:])
                if d == 1:
                    cmul(P, Bre[:], Bim[:], bbr[:], bbi[:], bc(Pnr[:, :, 15]), bc(Pni[:, :, 15]), tb1[:], tb2[:])
                    vbr, vbi = Bre, Bim
                    cr2 = P.sbuf("s5_cr2", [128, 16, 16], F32); ci2 = P.sbuf("s5_ci2", [128, 16, 16], F32)
                    cmul(P, cr2[:], ci2[:], Cre[:], Cim[:], bc(Ppr[:, :, 15]), bc(Ppi[:, :, 15]), tb1[:], tb2[:])
                    vcr, vci = cr2, ci2
                    tabP, tabQ = (Ppr, Ppi), (Pnr, Pni)
                else:
                    vbr, vbi = bbr, bbi
                    vcr, vci = Cre, Cim
                    tabP, tabQ = (Pnr, Pni), (Ppr, Ppi)
                Pre = P.sbuf("s5_Pre", [128, 16, 256], BF16); Pim = P.sbuf("s5_Pim", [128, 16, 256], BF16)
                to1 = P.sbuf("s5_to1", [128, 4, 256], F32); to2 = P.sbuf("s5_to2", [128, 4, 256], F32)
                v4 = lambda x: x.rearrange("p a (j h) -> p a j h", j=16)
                for pg in range(4):
                    sl = slice(pg * 4, pg * 4 + 4)
                    tj = lambda tab: tab[:, sl, 0:16].unsqueeze(3).to_broadcast([128, 4, 16, 16])
                    vh = lambda v: v[:, sl, :].unsqueeze(2).to_broadcast([128, 4, 16, 16])
                    cmul(P, v4(Pre[:, sl, :]), v4(Pim[:, sl, :]), tj(tabP[0]), tj(tabP[1]), vh(vbr), vh(vbi), v4(to1[:]), v4(to2[:]))
                    cmul(P, v4(Qre[d][:, sl, :]), v4(nQim[d][:, sl, :]), tj(tabQ[0]), tj(tabQ[1]), vh(vcr), vh(vci), v4(to1[:]), v4(to2[:]),
                         neg_im=True)
                msk = [P.sbuf("s5_msk%d" % a, [128, 256], F32) for a in range(2)]
                for a in range(2):
                    P.dma(msk[a][:], I["s5_mask"][d, a])
                tm = [P.sbuf("s5_tm%d" % i, [128, 256], BF16) for i in range(2)]
                cnt = 0
                for g in range(32):
                    pr_, g2 = g // 2, g % 2
                    rows = slice(g2 * 64, (g2 + 1) * 64)
                    for a in range(2):
                        pp = ps[1 + cnt % 4]
                        cnt += 1
                        P.mm(pp[:, 0:256], Pre[rows, pr_, a * 128:(a + 1) * 128], Qre[d][rows, pr_, :], start=True, stop=False)
                        P.mm(pp[:, 0:256], Pim[rows, pr_, a * 128:(a + 1) * 128], nQim[d][rows, pr_, :], start=False, stop=True)
                        if d == 0:
                            P.tt(M[:, g, a, :], pp[:, 0:256], msk[a][:], ALU.mult)
                        else:
                            t = tm[cnt % 2]
                            P.tt(t[:], pp[:, 0:256], msk[a][:], ALU.mult)
                            P.tt(M[:, g, a, :], M[:, g, a, :], t[:], ALU.add, e="pool")
                PTre = P.sbuf("s5_PTre", [128, 16, 2, 128], BF16); PTim = P.sbuf("s5_PTim", [128, 16, 2, 128], BF16)
                cnt = 0
                for pr_ in range(16):
                    for a in range(2):
                        for (src, dst) in ((Pre, PTre), (Pim, PTim)):
                            pt = ps[5 + cnt % 2][:].bitcast(BF16)
                            cnt += 1
                            P.transpose(pt[:, 0:128], src[:, pr_, a * 128:(a + 1) * 128], k.ident_bf[:])
                            P.copy(dst[:, pr_, a, :], pt[:, 0:128], e=("act" if cnt % 2 else "dve"))
                SA = [P.sbuf("s5_SAre", [128, 16, NCH], F32), P.sbuf("s5_SAim", [128, 16, NCH], F32)]
                SB = [P.sbuf("s5_SBre", [128, 16, NCH], F32), P.sbuf("s5_SBim", [128, 16, NCH], F32)]
                ts_ = P.sbuf("s5_ts", [128, NCH], F32); ts2 = P.sbuf("s5_ts2", [128, NCH], F32)
                for pr_ in range(16):
                    pre_, pim_ = ps[1 + 2 * (pr_ % 2)], ps[2 + 2 * (pr_ % 2)]
                    for (pt_, PT) in ((pre_, PTre), (pim_, PTim)):
                        for g2 in range(2):
                            g = 2 * pr_ + g2
                            rows = slice(g2 * 64, (g2 + 1) * 64)
                            if d == 0:
                                for a in range(2):
                                    P.mm(pt_[rows, 0:NCH], PT[:, pr_, a, rows], U[:, g, a, :], start=(a == 0), stop=(a == 1))
                            else:
                                for a in range(2):
                                    P.mm(pt_[rows, 0:128], PT[:, pr_, a, rows], U[:, g, a, 16:NCH], start=(a == 0), stop=(a == 1))
                                for a in range(2):
                                    P.mm(pt_[rows, 128:NCH], PT[:, pr_, a, rows], U[:, g, a, 0:16], start=(a == 0), stop=(a == 1))
                    P.ts(ts_[:], pre_[:, 0:NCH], Ar[:, 0, pr_:pr_ + 1], None, op0=ALU.mult)
                    P.stt(SA[0][:, pr_, :], pim_[:, 0:NCH], nAi[:, 0, pr_:pr_ + 1], ts_[:], ALU.mult, ALU.add)
                    P.ts(ts2[:], pim_[:, 0:NCH], Ar[:, 0, pr_:pr_ + 1], None, op0=ALU.mult)
                    P.stt(SA[1][:, pr_, :], pre_[:, 0:NCH], Ai[:, 0, pr_:pr_ + 1], ts2[:], ALU.mult, ALU.add)
                cur, nxt = SA, SB
                for kk in range(8):
                    sh = 1 << kk
                    n_ = NCH - sh
                    for pr_ in range(16):
                        if d == 0:
                            dst_r, dst_i = nxt[0][:, pr_, sh:NCH], nxt[1][:, pr_, sh:NCH]
                            src_r, src_i = cur[0][:, pr_, 0:n_], cur[1][:, pr_, 0:n_]
                            own_r, own_i = cur[0][:, pr_, sh:NCH], cur[1][:, pr_, sh:NCH]
                        else:
                            dst_r, dst_i = nxt[0][:, pr_, 0:n_], nxt[1][:, pr_, 0:n_]
                            src_r, src_i = cur[0][:, pr_, sh:NCH], cur[1][:, pr_, sh:NCH]
                            own_r, own_i = cur[0][:, pr_, 0:n_], cur[1][:, pr_, 0:n_]
                        P.stt(ts_[:, 0:n_], src_r, Ar[:, kk, pr_:pr_ + 1], own_r, ALU.mult, ALU.add)
                        P.stt(dst_r, src_i, nAi[:, kk, pr_:pr_ + 1], ts_[:, 0:n_], ALU.mult, ALU.add)
                        P.stt(ts2[:, 0:n_], src_i, Ar[:, kk, pr_:pr_ + 1], own_i, ALU.mult, ALU.add)
                        P.stt(dst_i, src_r, Ai[:, kk, pr_:pr_ + 1], ts2[:, 0:n_], ALU.mult, ALU.add)
                    for ri in range(2):
                        if d == 0:
                            P.copy(nxt[ri][:, :, 0:sh], cur[ri][:, :, 0:sh], e="pool")
                        else:
                            P.copy(nxt[ri][:, :, n_:NCH], cur[ri][:, :, n_:NCH], e="pool")
                    cur, nxt = nxt, cur
                for ri, X in ((0, Xre[d]), (1, Xim[d])):
                    W = cur[ri]
                    if d == 0:
                        P.memset(X[:, :, 0:1], 0.0)
                        P.copy(X[:, :, 1:NCH], W[:, :, 0:NCH - 1], e=("act" if ri else "dve"))
                    else:
                        P.copy(X[:, :, 0:15], W[:, :, 129:144], e="act")
                        P.memset(X[:, :, 15:16], 0.0)
                        P.copy(X[:, :, 16:143], W[:, :, 1:128], e="dve")
                        P.copy(X[:, :, 143:144], W[:, :, 128:129], e="act")
        with P.scope():
            ycl = P.sbuf("s5_ycl", [128, 16 * 512], F32)
            ycc = P.sbuf("s5_ycc", [16, 16 * 512], F32)
            yclv = ycl[:].rearrange("p (j g h) -> p j g h", j=16, g=32)
            yccv = ycc[:].rearrange("p (j g h) -> p j g h", j=16, g=32)
            ysb = [P.sbuf("s5_ysb%d" % i, [128, NCH], F32) for i in range(2)]
            cnt = 0
            for g in range(32):
                pr_, g2 = g // 2, g % 2
                rows = slice(g2 * 64, (g2 + 1) * 64)
                for b in range(2):
                    pp = ps[1 + cnt % 2]
                    ys = ysb[cnt % 2]
                    pt1 = ps[3 + cnt % 2]
                    pt2 = ps[5 + cnt % 2]
                    cnt += 1
                    cols = slice(b * 128, (b + 1) * 128)
                    P.mm(pp[:, 0:NCH], M[:, g, 0, cols], U[:, g, 0, :], start=True, stop=False)
                    P.mm(pp[:, 0:NCH], M[:, g, 1, cols], U[:, g, 1, :], start=False, stop=False)
                    for d in range(2):
                        P.mm(pp[:, 0:NCH], Qre[d][rows, pr_, cols], Xre[d][rows, pr_, :], start=False, stop=False)
                        P.mm(pp[:, 0:NCH], nQim[d][rows, pr_, cols], Xim[d][rows, pr_, :], start=False, stop=(d == 1))
                    P.copy(ys[:], pp[:, 0:NCH], e="act")
                    P.transpose(pt1[:, 0:128], ys[:, 16:NCH], k.ident[:])
                    P.copy(yclv[:, b * 8:(b + 1) * 8, g, :], pt1[:, 0:128].rearrange("p (j h) -> p j h", j=8), e="dve")
                    P.transpose(pt2[0:16, 0:128], ys[:, 0:16], k.ident[:])
                    P.copy(yccv[:, b * 8:(b + 1) * 8, g, :], pt2[0:16, 0:128].rearrange("p (j h) -> p j h", j=8), e="pool" if False else "dve")
            P.dma(k.y_tm[NCTX:T, :].rearrange("(c j) n -> c (j n)", j=16), ycl[:], q="sp")
            P.dma(k.y_tm[0:NCTX, :].rearrange("(c j) n -> c (j n)", j=16), ycc[:], q="sp")
        with P.scope():
            dbc = P.sbuf("s5_dbc", [128, 512], F32)
            P.dma(dbc[:], I["s5_d"][l].partition_broadcast(128))
            wgl = P.sbuf("s5_wgl", [128, 4, 1024], BF16)
            P.dma(wgl[:], I["s5_w_glu"][l].rearrange("(kk p) n -> p kk n", p=128), q="pool")
            bgl = P.sbuf("s5_bgl", [128, 8], F32)
            load_col_vec(P, bgl[:], I["s5_b_glu"][l], 8)
            gT = P.sbuf("s5_gT", [128, 4, T], BF16)
            yt = [P.sbuf("s5_yt%d" % i, [128, 512], F32) for i in range(2)]
            ut = [P.sbuf("s5_ut%d" % i, [128, 512], F32) for i in range(2)]
            w1_ = P.sbuf("s5_w1", [128, 512], F32); w2_ = P.sbuf("s5_w2", [128, 512], F32)
            gtm = [P.sbuf("s5_gtm%d" % i, [128, 512], BF16) for i in range(2)]
            tcs = list(range(18)) if ctx_out else list(range(2, 18))
            for tc in tcs:
                y = yt[tc % 2]; u = ut[tc % 2]
                P.dma(y[:], k.y_tm[tc * 128:(tc + 1) * 128, :])
                P.dma(u[:], k.u_tm[tc * 128:(tc + 1) * 128, :])
                P.tt(u[:], u[:], dbc[:], ALU.mult, e="pool")
                P.tt(y[:], y[:], u[:], ALU.add)
                P.tt(w1_[:], y[:], y[:], ALU.mult, e="pool")
                P.ts(w1_[:], w1_[:], 0.044715, 1.0, op0=ALU.mult, op1=ALU.add)
                P.tt(w1_[:], w1_[:], y[:], ALU.mult)
                P.act(w2_[:], w1_[:], AF.Sigmoid, scale=1.5957691216057308)
                gt_ = gtm[tc % 2]
                P.tt(gt_[:], y[:], w2_[:], ALU.mult, e="pool")
                pt = ps[5 + tc % 2][:].bitcast(BF16)
                for j in range(4):
                    P.transpose(pt[:, j * 128:(j + 1) * 128], gt_[:, j * 128:(j + 1) * 128], k.ident_bf[:])
                P.copy(gT[:, :, tc * 128:(tc + 1) * 128], pt[:, 0:512].rearrange("p (j t) -> p j t", j=4), e="act")
            cT = P.sbuf("s5_cT", [128, T], BF16)
            sg = [P.sbuf("s5_sg%d" % i, [128, 512], F32) for i in range(2)]
            subs = ALLSUBS if ctx_out else LATSUBS
            cnt = 0
            for m in range(4):
                for (c0, w, st) in subs:
                    pa = ps[1 + cnt % 2]; pb = ps[3 + cnt % 2]; s_ = sg[cnt % 2]
                    cnt += 1
                    for kk in range(4):
                        P.mm(pa[:, :w], wgl[:, kk, m * 128:(m + 1) * 128], gT[:, kk, c0:c0 + w], start=(kk == 0), stop=(kk == 3))
                    for kk in range(4):
                        P.mm(pb[:, :w], wgl[:, kk, 512 + m * 128:512 + (m + 1) * 128], gT[:, kk, c0:c0 + w], start=(kk == 0), stop=(kk == 3))
                    P.act(s_[:, :w], pb[:, :w], AF.Sigmoid, bias=bgl[:, 4 + m:5 + m])
                    P.stt(cT[:, c0:c0 + w], pa[:, :w], bgl[:, m:m + 1], s_[:, :w], ALU.add, ALU.mult)
                cols0 = 0 if ctx_out else NCTX
                P.dma(k.br[2][m * 128:(m + 1) * 128, cols0:T], cT[:, cols0:T], q="sp")


def build_full(nl=2, upto=None):
    P = Prog()
    nc = P.nc
    k = K()
    k.I = declare_inputs(P, nl)
    declare_mixer_inputs(P, k.I, nl)
    declare_na_inputs(P, k.I, nl)
    declare_hyena_inputs(P, k.I, nl)
    declare_s5_inputs(P, k.I, nl)
    yT = nc.dram_tensor("yT", [D, NLAT], F32, kind="ExternalOutput").ap()
    setup_common(P, k)
    mixer_setup(P, k)
    s5_setup(P, k)
    geo = na_geometry()
    P.dma(k.xres, k.I["xT"].rearrange("(c p) t -> p c t", p=128))
    for l in range(nl):
        ctx_out = l < nl - 1
        with P.scope():
            layer_mods(P, k, l)
        with P.scope():
            ffn_sublayer(P, k, l, 0, FULL_BLOCKS)
        with P.scope():
            alloc_hmix(P, k)
            mixer_modnorm(P, k)
            mla_branch(P, k, l, ctx_out)
            na_branch(P, k, l, ctx_out, geo)
            hyena_branch(P, k, l, HY_LAT)
            if ctx_out:
                hyena_branch(P, k, l, HY_CTX)
            s5_part1(P, k, l)
        s5_part2(P, k, l, ctx_out)
        merge_phase(P, k, l, ctx_out)
        with P.scope():
            ffn_sublayer(P, k, l, 2, FULL_BLOCKS if ctx_out else LAT_BLOCKS)
    P.dma(yT.rearrange("(c p) t -> p c t", p=128), k.xres[:, :, NCTX:T], q="sp")
    P.finish("sp")
    P.close()
    return P, k


_CACHE = {}


def _host_constants():
    if "c" in _CACHE:
        return _CACHE["c"]
    C, S = rope_tables()
    mv, sel, cm = na_const_tables()
    m = {"ident": np.eye(128, dtype=np.float32), "rope_c": C, "rope_s": S,
         "na_mv": np.ascontiguousarray(mv.reshape(2, -1)), "na_sel": sel, "na_cm": cm, "s5_mask": s5_masks()}
    for nm_, N in (("lat", NLAT), ("ctx", NCTX)):
        hc = hyena_consts(N)
        for t in ("c", "s", "ct", "st"):
            m["dft_%s_%s" % (t, nm_)] = hc[t]
        m["hy_zT_%s" % nm_] = hc["zT"]
        m["hy_decay_%s" % nm_] = hc["decay"]
    _CACHE["c"] = m
    return m


def kernel(**inputs):
    nl = 2
    if "prog" not in _CACHE:
        _CACHE["prog"] = build_full(nl)
    P, k = _CACHE["prog"]
    shared = dict(_host_constants())
    shared["na_G"] = np.ascontiguousarray(
        np.stack([na_bias_gather(np.asarray(inputs["na_rpb"][l], np.float32)).reshape(128, -1) for l in range(nl)], 0))
    for nm, ap in k.I.items():
        if nm in shared or nm in ("xT", "cvec"):
            continue
        shared[nm] = np.ascontiguousarray(np.asarray(inputs[nm], np.float32).reshape(ap.shape))
    x = np.asarray(inputs["x"], np.float32)
    ctx = np.asarray(inputs["ctx"], np.float32)
    c = np.asarray(inputs["c"], np.float32)
    c_ctx = np.asarray(inputs["c_ctx"], np.float32)
    B = x.shape[0]
    in_maps = []
    for b in range(B):
        m = dict(shared)
        m["xT"] = np.ascontiguousarray(np.concatenate([ctx[b], x[b]], 0).T)
        m["cvec"] = np.ascontiguousarray(np.stack([c[b], c_ctx], 1))
        in_maps.append(m)
    res = run_bass_kernel_spmd(P.nc, in_maps, core_ids=list(range(B)))
    out = np.stack([np.asarray(res.results[b]["yT"], np.float32).T for b in range(B)], 0)
    return np.ascontiguousarray(out)
```

```python
import numpy as np
import concourse.bass as bass
import concourse.mybir as mybir
from concourse.bass_utils import run_bass_kernel_spmd

F32 = mybir.dt.float32
F32R = mybir.dt.float32r
BF16 = mybir.dt.bfloat16
AF = mybir.ActivationFunctionType
ALU = mybir.AluOpType
AX = mybir.AxisListType

SEM_ROLL = 30000


class Prog:
    def __init__(self, n_dma_sems=16):
        self.nc = bass.Bass("TRN2", target_bir_lowering=False)
        nc = self.nc
        self.eng = {"pe": nc.tensor, "act": nc.scalar, "dve": nc.vector,
                    "pool": nc.gpsimd, "sp": nc.sync}
        self._ctx = []
        self._scopes = []
        self._in_scope_alloc = False
        self._uid = 0
        self.sem = {}
        self.cnt = {}
        self.nsem = 0
        for e in self.eng:
            self._new_eng_sem(e)
        self.dma_sems = {}
        self.dma_rr = {}
        for q in ("sp", "pool", "act"):
            self.dma_sems[q] = []
            for i in range(n_dma_sems if q != "act" else 4):
                s = self._enter(nc.semaphore("dq_%s%d" % (q, i)))
                self.dma_sems[q].append([s, 0])
            self.dma_rr[q] = 0
        self.waited = {e: {} for e in self.eng}
        self.regions = {}
        self.n_inst = 0
        self.n_wait = 0

    def _enter(self, cm):
        v = cm.__enter__()
        if self._scopes and self._in_scope_alloc:
            self._scopes[-1].append((cm, v))
        else:
            self._ctx.append(cm)
        return v

    class _Scope:
        def __init__(self, P):
            self.P = P

        def __enter__(self):
            self.P._scopes.append([])
            return self

        def __exit__(self, *a):
            P = self.P
            items = P._scopes.pop()
            toks = []
            for cm, v in items:
                nm = v.name if hasattr(v, "name") else None
                for r in P.regions.pop(nm, []):
                    toks.append((r[5], r[6]))
            if toks:
                for e in P.eng:
                    P._emit_waits(e, toks)
            for cm, v in reversed(items):
                cm.__exit__(None, None, None)
            return False

    def scope(self):
        return Prog._Scope(self)

    def _new_eng_sem(self, e):
        s = self._enter(self.nc.semaphore("s_%s_%d" % (e, self.nsem)))
        self.nsem += 1
        self.sem[e] = s
        self.cnt[e] = 0

    def close(self):
        for cm in reversed(self._ctx):
            cm.__exit__(None, None, None)
        self._ctx = []

    def sbuf(self, name, shape, dtype=F32):
        self._uid += 1
        self._in_scope_alloc = True
        try:
            return self._enter(self.nc.sbuf_tensor("%s_%d" % (name, self._uid), list(shape), dtype))
        finally:
            self._in_scope_alloc = False

    def psum(self, name, shape=(128, 512), dtype=F32):
        return self._enter(self.nc.psum_tensor(name, list(shape), dtype))

    def dram(self, name, shape, dtype=F32, kind="Internal"):
        return self.nc.dram_tensor(name, list(shape), dtype, kind=kind)

    @staticmethod
    def _region(ap):
        space = str(ap.space)
        name = ap.name
        aps = ap.ap
        off = int(ap.offset)
        if "DRAM" in space:
            ext = sum((c - 1) * abs(s) for s, c in aps)
            neg = sum((c - 1) * s for s, c in aps if s < 0)
            lo = off + neg
            return name, 0, 1, lo, lo + ext + 1, False
        pstep, pcnt = aps[0]
        if pstep == 0:
            pstep = 1 << 40
        p0 = off // pstep if pstep < (1 << 39) else 0
        lo = off - p0 * pstep if pstep < (1 << 39) else off
        ext = sum((c - 1) * abs(s) for s, c in aps[1:])
        is_psum = "PSUM" in space
        if is_psum:
            return name, 0, 128, 0, 1 << 30, True
        return name, p0, p0 + pcnt, lo, lo + ext + 1, False

    def _deps(self, reads, writes):
        toks = []
        info = []
        for ap, is_w in [(a, False) for a in reads] + [(a, True) for a in writes]:
            name, p0, p1, lo, hi, excl = self._region(ap)
            w = is_w or excl
            lst = self.regions.setdefault(name, [])
            for r in lst:
                if r[1] <= p0 or p1 <= r[0] or r[3] <= lo or hi <= r[2]:
                    continue
                if w or r[4]:
                    toks.append((r[5], r[6]))
            info.append((name, p0, p1, lo, hi, w))
        return toks, info

    def _record(self, info, sem, val):
        for name, p0, p1, lo, hi, w in info:
            lst = self.regions[name]
            if w:
                lst[:] = [r for r in lst if not (p0 <= r[0] and r[1] <= p1 and lo <= r[2] and r[3] <= hi)]
                lst.append([p0, p1, lo, hi, True, sem, val])
            else:
                for r in lst:
                    if (not r[4]) and r[0] == p0 and r[1] == p1 and r[2] == lo and r[3] == hi and r[5] is sem:
                        r[6] = max(r[6], val)
                        break
                else:
                    lst.append([p0, p1, lo, hi, False, sem, val])

    def _emit_waits(self, e, toks):
        best = {}
        for s, v in toks:
            k = id(s)
            if k not in best or best[k][1] < v:
                best[k] = (s, v)
        wd = self.waited[e]
        for k, (s, v) in best.items():
            if wd.get(k, 0) >= v:
                continue
            self.eng[e].wait_ge(s, v)
            wd[k] = v
            self.n_wait += 1

    def op(self, e, fn, reads, writes):
        toks, info = self._deps(reads, writes)
        if e == "pe":
            toks = [t for t in toks if t[0] is not self.sem["pe"]]
        self._emit_waits(e, toks)
        ins = fn()
        if self.cnt[e] >= SEM_ROLL:
            self._new_eng_sem(e)
        self.cnt[e] += 1
        ins.then_inc(self.sem[e], 1)
        self._record(info, self.sem[e], self.cnt[e])
        self.n_inst += 1
        return ins

    def dma(self, out, in_, q="sp", **kw):
        toks, info = self._deps([in_], [out])
        ent = self.dma_sems[q][self.dma_rr[q]]
        self.dma_rr[q] = (self.dma_rr[q] + 1) % len(self.dma_sems[q])
        s = ent[0]
        if ent[1] > 0:
            toks.append((s, ent[1]))
        self._emit_waits(q, toks)
        ent[1] += 16
        ins = self.eng[q].dma_start(out=out, in_=in_, **kw)
        ins.then_inc(s, 16)
        self._record(info, s, ent[1])
        self.n_inst += 1
        return ins

    def finish(self, e="sp"):
        toks = []
        for lst in self.regions.values():
            for r in lst:
                toks.append((r[5], r[6]))
        self._emit_waits(e, toks)

    def mm(self, out, lhsT, rhs, start=True, stop=True, **kw):
        return self.op("pe", lambda: self.nc.tensor.matmul(out, lhsT, rhs, start=start, stop=stop, **kw),
                       [lhsT, rhs], [out])

    def transpose(self, out, in_, ident):
        return self.op("pe", lambda: self.nc.tensor.transpose(out, in_, ident), [in_, ident], [out])

    def act(self, out, in_, func, bias=None, scale=1.0, e="act", **kw):
        reads = [in_]
        if bias is not None and not isinstance(bias, (int, float)):
            reads.append(bias)
        if not isinstance(scale, (int, float)):
            reads.append(scale)
        kw2 = dict(kw)
        if bias is not None:
            kw2["bias"] = bias
        writes = [out]
        if "accum_out" in kw2:
            writes.append(kw2["accum_out"])
        return self.op(e, lambda: self.nc.scalar.activation(out=out, in_=in_, func=func, scale=scale, **kw2),
                       reads, writes)

    def _veng(self, e):
        return self.nc.vector if e == "dve" else self.nc.gpsimd

    def tt(self, out, in0, in1, op, e="dve"):
        return self.op(e, lambda: self._veng(e).tensor_tensor(out=out, in0=in0, in1=in1, op=op), [in0, in1], [out])

    def ts(self, out, in0, s1, s2=None, op0=ALU.mult, op1=None, e="dve", **kw):
        reads = [in0] + [s for s in (s1, s2) if s is not None and not isinstance(s, (int, float))]
        writes = [out] + ([kw["accum_out"]] if "accum_out" in kw else [])
        if op1 is None:
            return self.op(e, lambda: self._veng(e).tensor_scalar(out=out, in0=in0, scalar1=s1, scalar2=None, op0=op0, **kw),
                           reads, writes)
        return self.op(e, lambda: self._veng(e).tensor_scalar(out=out, in0=in0, scalar1=s1, scalar2=s2, op0=op0, op1=op1, **kw),
                       reads, writes)

    def stt(self, out, in0, scalar, in1, op0, op1, e="dve"):
        reads = [in0, in1] + ([] if isinstance(scalar, (int, float)) else [scalar])
        return self.op(e, lambda: self.nc.vector.scalar_tensor_tensor(out=out, in0=in0, scalar=scalar, in1=in1, op0=op0, op1=op1),
                       reads, [out])

    def copy(self, out, in_, e="dve"):
        if e == "act":
            return self.op("act", lambda: self.nc.scalar.copy(out=out, in_=in_), [in_], [out])
        return self.op(e, lambda: self._veng(e).tensor_copy(out=out, in_=in_), [in_], [out])

    def memset(self, ap, val, e="dve"):
        return self.op(e, lambda: self._veng(e).memset(ap, val), [], [ap])

    def recip(self, out, in_):
        return self.op("dve", lambda: self.nc.vector.reciprocal(out=out, in_=in_), [in_], [out])

    def scan(self, out, d0, d1, initial, op0=ALU.mult, op1=ALU.add):
        reads = [d0, d1] + ([] if isinstance(initial, (int, float)) else [initial])
        return self.op("dve", lambda: self.nc.vector.tensor_tensor_scan(out=out, data0=d0, data1=d1, initial=initial, op0=op0, op1=op1),
                       reads, [out])


import numpy as np
def rope_tables(n=2048, grid_w=64, base=10000.0):
    q = 16
    t = np.arange(n)
    pos = np.stack([t // grid_w, t % grid_w], -1).astype(np.float32)
    inv = (base ** (-np.arange(q, dtype=np.float32) / q)).astype(np.float32)
    ang = pos[:, :, None] * inv
    C = np.zeros((64, n), np.float32); S = np.zeros((64, n), np.float32)
    for a in range(2):
        for hf in range(2):
            for j in range(q):
                f = a * 32 + hf * 16 + j
                C[f] = np.cos(ang[:, a, j])
                S[f] = (-1.0 if hf == 0 else 1.0) * np.sin(ang[:, a, j])
    return C, S

NEG = -30000.0
def na_geometry():
    rows = 32
    start = lambda r: min(max(r - 4, 0), rows - 8)
    geo = []
    for i in range(16):
        rs = [2 * i, 2 * i + 1]
        lo = min(start(r) for r in rs); hi = max(start(r) + 7 for r in rs)
        lst = []
        for j in range(lo // 2, hi // 2 + 1):
            codes = []
            for r in rs:
                v0 = start(r) <= 2 * j <= start(r) + 7
                v1 = start(r) <= 2 * j + 1 <= start(r) + 7
                code = {(True, True): 0, (False, True): 1, (True, False): 2, (False, False): 3}[(v0, v1)]
                codes.append(code)
            dr0 = 2 * (j - i)
            idxp = 7 - dr0
            assert 0 <= idxp <= 14, (i, j, idxp)
            lst.append((j, idxp, codes[0] * 4 + codes[1]))
        geo.append(lst)
    return geo

def na_const_tables():
    mv = np.zeros((2, 16, 128), np.float32)
    vecs = [np.zeros(128), np.r_[np.full(64, NEG), np.zeros(64)], np.r_[np.zeros(64), np.full(64, NEG)], np.full(128, NEG)]
    for c0 in range(4):
        for c1 in range(4):
            mv[0, c0 * 4 + c1] = vecs[c0]; mv[1, c0 * 4 + c1] = vecs[c1]
    sel = np.zeros((2, 128), np.float32); sel[0, :64] = 1; sel[1, 64:] = 1
    col = np.arange(64)
    c0 = np.clip(col - 8, 0, 48)
    inwin = (col[None, :] >= c0[:, None]) & (col[None, :] < c0[:, None] + 16)
    cm = np.where(inwin.T, 0.0, NEG).astype(np.float32)
    cm = np.concatenate([cm, cm], 0)
    return mv, sel, cm

def na_bias_gather(rpb):
    kc = np.arange(64)[:, None]; qc = np.arange(64)[None, :]
    dc = np.clip(kc - qc + 15, 0, 30)
    G = np.zeros((128, 8, 16, 64), np.float32)
    for idxp in range(16):
        for krl in range(2):
            dr = 7 - idxp + krl
            row = dr + 7
            if not (0 <= row <= 14):
                row = 0
            G[krl * 64:(krl + 1) * 64, :, idxp, :] = np.transpose(rpb[:, row][:, dc], (1, 0, 2))
    return G

def hyena_consts(N):
    t = np.arange(N, dtype=np.float64)[:, None]; f = np.arange(N, dtype=np.float64)[None, :]
    ang = 2.0 * np.pi * (f + 0.5) * t / (2.0 * N)
    Cm = np.cos(ang).astype(np.float32); Sm = np.sin(ang).astype(np.float32)
    bands = 16
    tt = np.arange(N, dtype=np.float32)
    t01 = np.linspace(0.0, 1.0, N, dtype=np.float32)[:, None]
    a2 = (np.float32(2.0 * np.pi) * tt / np.float32(N))[:, None] * np.linspace(1e-4, bands - 1, bands, dtype=np.float32)
    z = np.concatenate([t01, np.cos(a2), -np.sin(a2)], -1).astype(np.float32)
    max_decay = np.log(1e-2) / 0.3; min_decay = np.log(1e-2) / 1.5
    deltas = np.abs(np.linspace(min_decay, max_decay, 512, dtype=np.float32))
    decay = np.exp(-t01 * deltas).astype(np.float32)
    import ml_dtypes
    nch = N // 128

    def tiles(M):
        a = M.reshape(nch, 128, nch, 128)
        return np.ascontiguousarray(a.transpose(2, 1, 0, 3)).astype(ml_dtypes.bfloat16)
    return dict(c=tiles(Cm), s=tiles(Sm), ct=tiles(np.ascontiguousarray(Cm.T)), st=tiles(np.ascontiguousarray(Sm.T)),
                zT=np.ascontiguousarray(z.T), decay=decay)

def s5_masks():
    m = np.zeros((2, 2, 128, 256), np.float32)
    for a in range(2):
        for il in range(8):
            i = a * 8 + il
            for j in range(16):
                if j >= i:
                    m[0, a, il * 16:(il + 1) * 16, j * 16:(j + 1) * 16] = 1.0
                if j <= i:
                    m[1, a, il * 16:(il + 1) * 16, j * 16:(j + 1) * 16] = 1.0
    return m

import math
import numpy as np

D = 1024
KC = 8
NCTX = 256
NLAT = 2048
T = NCTX + NLAT
FH = 2816
FHC = FH // 128
EPS = 1e-6
N_IN = 8256


class K:
    pass


def declare_inputs(P, nl):
    nc = P.nc
    I = {}

    def inp(name, shape):
        I[name] = nc.dram_tensor(name, list(shape), F32, kind="ExternalInput").ap()

    inp("xT", [D, T])
    inp("cvec", [D, 2])
    inp("ident", [128, 128])
    inp("w_mod", [nl, D, 9 * D])
    inp("b_mod", [nl, 9 * D])
    inp("norm_g", [nl, 6, D])
    inp("ffn_w_in", [nl, 2, D, 2 * FH])
    inp("ffn_w_out", [nl, 2, FH, D])
    return I


def setup_common(P, k):
    k.ident = P.sbuf("ident", [128, 128], F32)
    P.dma(k.ident[:], k.I["ident"])
    k.ident_bf = P.sbuf("ident_bf", [128, 128], BF16)
    P.copy(k.ident_bf[:], k.ident[:])
    k.ones_bf = P.sbuf("ones_bf", [128, 128], BF16)
    P.memset(k.ones_bf[:], 1.0)
    k.eps_col = P.sbuf("eps_col", [128, 1], F32)
    P.memset(k.eps_col[:], EPS)
    k.xres = P.nc.dram_tensor("xres", [128, KC, T], F32, kind="Internal").ap()
    k.ps = [P.psum("psb%d" % i) for i in range(8)]
    cv = P.sbuf("cv", [128, KC, 2], F32)
    P.dma(cv[:], k.I["cvec"].rearrange("(c p) n -> p c n", p=128))
    k.actv = P.sbuf("actv", [128, KC, 2], BF16)
    P.act(k.actv[:], cv[:], AF.Silu)
    k.modT = P.sbuf("modT", [128, 72, 2], F32)
    k.normg = P.sbuf("normg", [128, 48], F32)
    k.Asc = P.sbuf("Asc", [128, 3, KC, 2], F32)
    k.Bsh = P.sbuf("Bsh", [128, 3, KC, 2], F32)
    k.Gg = P.sbuf("Gg", [128, 3, KC, 2], F32)


def layer_mods(P, k, l):
    nc = P.nc
    wm = k.I["w_mod"][l].rearrange("(c p) n -> p c n", p=128)
    bm_t = P.sbuf("bm_t", [72, 128], F32)
    P.dma(bm_t[:], k.I["b_mod"][l].rearrange("(m f) -> m f", f=128))
    ng_t = P.sbuf("ng_t", [48, 128], F32)
    P.dma(ng_t[:], k.I["norm_g"][l].rearrange("g (c f) -> (g c) f", f=128))
    ps_m = k.ps[0]
    ps_t = k.ps[1]
    wt = [P.sbuf("wmod%d" % i, [128, KC, 512], BF16) for i in range(2)]
    for j in range(18):
        w = wt[j % 2]
        P.dma(w[:], wm[:, :, j * 512:(j + 1) * 512], q="pool")
        for mm in range(4):
            m = j * 4 + mm
            for c in range(KC):
                P.mm(ps_m[:, 2 * m:2 * m + 2], w[:, c, mm * 128:(mm + 1) * 128], k.actv[:, c, :],
                     start=(c == 0), stop=(c == KC - 1))
    P.transpose(ps_t[:, 0:72], bm_t[:], k.ident[0:72, 0:72])
    bmT = P.sbuf("bmT", [128, 72], F32)
    P.copy(bmT[:], ps_t[:, 0:72])
    P.tt(k.modT[:], ps_m[:, 0:144].rearrange("p (m s) -> p m s", s=2),
         bmT[:].unsqueeze(2).broadcast_to([128, 72, 2]), ALU.add)
    P.transpose(ps_t[:, 128:176], ng_t[:], k.ident[0:48, 0:48])
    P.copy(k.normg[:], ps_t[:, 128:176])
    for s in range(3):
        base = 3 * s
        gpre = k.normg[:, (2 * s) * 8:(2 * s + 1) * 8].unsqueeze(2).broadcast_to([128, KC, 2])
        gpost = k.normg[:, (2 * s + 1) * 8:(2 * s + 2) * 8].unsqueeze(2).broadcast_to([128, KC, 2])
        P.stt(k.Asc[:, s], k.modT[:, (base + 1) * 8:(base + 2) * 8, :], 1.0, gpre, ALU.add, ALU.mult)
        P.copy(k.Bsh[:, s], k.modT[:, base * 8:(base + 1) * 8, :])
        P.stt(k.Gg[:, s], k.modT[:, (base + 2) * 8:(base + 3) * 8, :], (1.0 if s == 1 else 0.5), gpost, ALU.mult, ALU.mult)


def sumsq_rstd(P, k, src_fn, nchunks, subs, rstd, ps_ss, sq_tiles, inv_n):
    for (o, w) in subs:
        for c in range(nchunks):
            sq = sq_tiles[c % len(sq_tiles)]
            P.act(sq[:, :w], src_fn(c, o, w), AF.Square)
            P.mm(ps_ss[:, :w], k.ones_bf[:], sq[:, :w], start=(c == 0), stop=(c == nchunks - 1))
        P.act(rstd[:, o:o + w], ps_ss[:, :w], AF.Sqrt, bias=k.eps_col[:], scale=inv_n)
        P.recip(rstd[:, o:o + w], rstd[:, o:o + w])


def ffn_sublayer(P, k, l, s, blocks):
    fi = s // 2
    w_in = k.I["ffn_w_in"][l, fi].rearrange("(c p) n -> p c n", p=128)
    w_out = k.I["ffn_w_out"][l, fi].rearrange("(j p) n -> p j n", p=128)
    maxw = max(sum(w for (_, w, _) in b) for b in blocks)
    hb = P.sbuf("ffn_h", [128, KC, maxw], BF16)
    xb = P.sbuf("ffn_x", [128, KC, maxw], F32)
    gb = P.sbuf("ffn_g", [128, FHC, maxw], BF16)
    ob = P.sbuf("ffn_o", [128, KC, maxw], BF16)
    rstd = P.sbuf("ffn_rstd", [128, maxw], F32)
    tmp = [P.sbuf("ffn_tmp%d" % i, [128, 512], F32) for i in range(2)]
    sq = [P.sbuf("ffn_sq%d" % i, [128, 512], BF16) for i in range(2)]
    sl = [P.sbuf("ffn_sl%d" % i, [128, 512], F32) for i in range(2)]
    wi = [P.sbuf("ffn_wi%d" % i, [128, KC, 1024], BF16) for i in range(2)]
    wo = [P.sbuf("ffn_wo%d" % i, [128, FHC, 128], BF16) for i in range(2)]
    ps_ss = k.ps[0]
    ps_a = [k.ps[1], k.ps[2]]
    ps_b = [k.ps[3], k.ps[4]]
    ps_o = [k.ps[5], k.ps[6]]
    cnt = 0
    for blk in blocks:
        subs = []
        o = 0
        for (c0, w, st) in blk:
            subs.append((o, c0, w, st))
            o += w
        for (o, c0, w, st) in subs:
            P.dma(xb[:, :, o:o + w], k.xres[:, :, c0:c0 + w])
        for (o, c0, w, st) in subs:
            sumsq_rstd(P, k, lambda c, oo, ww: xb[:, c, oo:oo + ww], KC,
                       [(o, w)], rstd, ps_ss, sq, 1.0 / D)
            for c in range(KC):
                t = tmp[c % 2]
                P.tt(t[:, :w], xb[:, c, o:o + w], rstd[:, o:o + w], ALU.mult, e=("pool" if c % 3 == 2 else "dve"))
                P.ts(hb[:, c, o:o + w], t[:, :w], k.Asc[:, s, c, st:st + 1], k.Bsh[:, s, c, st:st + 1],
                     op0=ALU.mult, op1=ALU.add)
        for j4 in range((FHC + 3) // 4):
            w = wi[j4 % 2]
            nj = min(4, FHC - 4 * j4)
            P.dma(w[:, :, 0:nj * 128], w_in[:, :, j4 * 512:j4 * 512 + nj * 128], q="pool")
            P.dma(w[:, :, 512:512 + nj * 128], w_in[:, :, FH + j4 * 512:FH + j4 * 512 + nj * 128], q="pool")
            for jj in range(nj):
                j = 4 * j4 + jj
                for (o, c0, ww, st) in subs:
                    pa = ps_a[cnt % 2]
                    pb = ps_b[cnt % 2]
                    slt = sl[cnt % 2]
                    cnt += 1
                    for c in range(KC):
                        P.mm(pa[:, :ww], w[:, c, jj * 128:(jj + 1) * 128], hb[:, c, o:o + ww],
                             start=(c == 0), stop=(c == KC - 1))
                    for c in range(KC):
                        P.mm(pb[:, :ww], w[:, c, 512 + jj * 128:512 + (jj + 1) * 128], hb[:, c, o:o + ww],
                             start=(c == 0), stop=(c == KC - 1))
                    P.act(slt[:, :ww], pa[:, :ww], AF.Silu)
                    P.tt(gb[:, j, o:o + ww], slt[:, :ww], pb[:, :ww], ALU.mult)
        for c in range(KC):
            w = wo[c % 2]
            P.dma(w[:], w_out[:, :, c * 128:(c + 1) * 128], q="pool")
            for (o, c0, ww, st) in subs:
                po = ps_o[cnt % 2]
                cnt += 1
                for j in range(FHC):
                    P.mm(po[:, :ww], w[:, j, :], gb[:, j, o:o + ww], start=(j == 0), stop=(j == FHC - 1))
                P.copy(ob[:, c, o:o + ww], po[:, :ww], e="act")
        for (o, c0, w, st) in subs:
            sumsq_rstd(P, k, lambda c, oo, ww: ob[:, c, oo:oo + ww], KC, [(o, w)], rstd, ps_ss, sq, 1.0 / D)
            for c in range(KC):
                t = tmp[c % 2]
                P.stt(t[:, :w], ob[:, c, o:o + w], k.Gg[:, s, c, st:st + 1], rstd[:, o:o + w], ALU.mult, ALU.mult)
                P.tt(xb[:, c, o:o + w], xb[:, c, o:o + w], t[:, :w], ALU.add, e=("pool" if c % 3 == 2 else "dve"))
            P.dma(k.xres[:, :, c0:c0 + w], xb[:, :, o:o + w], q="sp")


FULL_BLOCKS = [
    [(0, 256, 1), (256, 512, 0), (768, 384, 0)],
    [(1152, 384, 0), (1536, 384, 0), (1920, 384, 0)],
]
LAT_BLOCKS = [
    [(256, 512, 0), (768, 512, 0)],
    [(1280, 512, 0), (1792, 512, 0)],
]


CT0 = 2
LT0 = 260
HW = 2310
ALLSUBS = [(0, 256, 1), (256, 512, 0), (768, 512, 0), (1280, 512, 0), (1792, 512, 0)]
LATSUBS = ALLSUBS[1:]
C_CKV, C_KR, C_NK, C_NV, C_U, C_CQ, C_NQ, C_HY, C_GT = 0, 256, 320, 832, 1344, 1856, 2112, 2624, 4160


def hcol(xc):
    return xc + CT0 if xc < NCTX else xc - NCTX + LT0


def declare_mixer_inputs(P, I, nl):
    nc = P.nc

    def inp(name, shape):
        I[name] = nc.dram_tensor(name, list(shape), F32, kind="ExternalInput").ap()
    inp("w_in", [nl, D, N_IN])
    inp("mla_g_q", [nl, 256]); inp("mla_g_kv", [nl, 256])
    inp("mla_w_uq", [nl, 256, 4, 192]); inp("mla_w_ukv", [nl, 256, 4, 256])
    inp("w_branch", [nl, 4, 512, D]); inp("w_out", [nl, D, D])
    inp("rope_c", [64, NLAT]); inp("rope_s", [64, NLAT])


def mixer_setup(P, k):
    nc = P.nc
    k.br = [nc.dram_tensor("br%d" % n, [512, T], BF16, kind="Internal").ap() for n in range(4)]
    k.hmx = nc.dram_tensor("hmx", [128, KC, HW], BF16, kind="Internal").ap()


def alloc_hmix(P, k):
    k.hmix = P.sbuf("hmix", [128, KC, HW], BF16)
    for c0 in (0, 258, 2308):
        P.memset(k.hmix[:, :, c0:c0 + 2], 0.0)


def mixer_modnorm(P, k):
    with P.scope():
        rstd = P.sbuf("mn_rstd", [128, 512], F32)
        tmp = [P.sbuf("mn_tmp%d" % i, [128, 512], F32) for i in range(2)]
        sq = [P.sbuf("mn_sq%d" % i, [128, 512], BF16) for i in range(2)]
        xt = [P.sbuf("mn_x%d" % i, [128, KC, 512], F32) for i in range(2)]
        for si, (c0, w, st) in enumerate(ALLSUBS):
            xb = xt[si % 2]
            P.dma(xb[:, :, :w], k.xres[:, :, c0:c0 + w])
            sumsq_rstd(P, k, lambda c, oo, ww, xb=xb: xb[:, c, oo:oo + ww], KC, [(0, w)], rstd, k.ps[0], sq, 1.0 / D)
            h0 = hcol(c0)
            for c in range(KC):
                t = tmp[c % 2]
                P.tt(t[:, :w], xb[:, c, 0:w], rstd[:, 0:w], ALU.mult, e=("pool" if c % 3 == 2 else "dve"))
                P.ts(k.hmix[:, c, h0:h0 + w], t[:, :w], k.Asc[:, 1, c, st:st + 1], k.Bsh[:, 1, c, st:st + 1],
                     op0=ALU.mult, op1=ALU.add)
    P.dma(k.hmx, k.hmix[:], q="sp")


def load_col_vec(P, dst, src_1d, nchunk):
    P.dma(dst, src_1d.rearrange("(c p) -> p c", p=128), allow_slow_non_contiguous=True)


def mla_branch(P, k, l, ctx_out):
    nc = P.nc
    I = k.I
    w_in = I["w_in"][l].rearrange("(c p) n -> p c n", p=128)
    SC = 192.0 ** -0.5
    subs = ALLSUBS
    qsubs = ALLSUBS if ctx_out else LATSUBS
    with P.scope():
        wckv = P.sbuf("wckv", [128, KC, 256], BF16)
        P.dma(wckv[:], w_in[:, :, C_CKV:C_CKV + 256], q="pool")
        wkr = P.sbuf("wkr", [128, KC, 128], BF16)
        P.dma(wkr[:, :, 0:64], w_in[:, :, C_KR:C_KR + 64], q="pool")
        for a in range(2):
            for hf in range(2):
                P.dma(wkr[:, :, 64 + a * 32 + hf * 16:64 + a * 32 + hf * 16 + 16],
                      w_in[:, :, C_KR + a * 32 + (1 - hf) * 16:C_KR + a * 32 + (1 - hf) * 16 + 16], q="pool")
        wcq = P.sbuf("wcq", [128, KC, 256], BF16)
        P.dma(wcq[:], w_in[:, :, C_CQ:C_CQ + 256], q="pool")
        wukv = P.sbuf("wukv", [128, 2, 4, 256], BF16)
        P.dma(wukv[:], I["mla_w_ukv"][l].rearrange("(c p) h e -> p c h e", p=128), q="pool")
        wuq = P.sbuf("wuq", [128, 2, 4, 192], BF16)
        P.dma(wuq[:], I["mla_w_uq"][l].rearrange("(c p) h e -> p c h e", p=128), q="pool")
        wuqs = P.sbuf("wuqs", [128, 2, 4, 64], BF16)
        uq_r = I["mla_w_uq"][l].rearrange("(c p) h e -> p c h e", p=128)
        for a in range(2):
            for hf in range(2):
                for c in range(2):
                    P.dma(wuqs[:, c, :, a * 32 + hf * 16:a * 32 + hf * 16 + 16],
                          uq_r[:, c, :, 128 + a * 32 + (1 - hf) * 16:128 + a * 32 + (1 - hf) * 16 + 16], q="pool")
        gkv = P.sbuf("gkv", [128, 2], F32)
        load_col_vec(P, gkv[:], I["mla_g_kv"][l], 2)
        gq = P.sbuf("gq", [128, 2], F32)
        load_col_vec(P, gq[:], I["mla_g_q"][l], 2)
        ropc_t = [P.sbuf("ropc%d" % i, [64, 512], F32) for i in range(2)]
        rops_t = [P.sbuf("rops%d" % i, [64, 512], F32) for i in range(2)]
        rcnt = [0]

        def rope_tabs(l0, w):
            i = rcnt[0] % 2
            rcnt[0] += 1
            P.dma(ropc_t[i][:, :w], I["rope_c"][:, l0:l0 + w])
            P.dma(rops_t[i][:, :w], I["rope_s"][:, l0:l0 + w])
            return ropc_t[i], rops_t[i]
        nkv = P.sbuf("nkv", [128, 2, T], BF16)
        nq = P.sbuf("nq", [128, 2, T], BF16)
        krope = P.sbuf("krope", [64, T], BF16)
        vall = P.sbuf("vall", [128, 18, 128], BF16)
        aT = P.sbuf("aT", [128, T], BF16)
        raw = P.sbuf("raw", [128, 2, 512], F32)
        rstd = P.sbuf("rstd", [128, 512], F32)
        sq = [P.sbuf("sq%d" % i, [128, 512], BF16) for i in range(2)]
        t1 = P.sbuf("t1", [128, 512], F32)
        t2 = P.sbuf("t2", [128, 512], F32)
        ps = k.ps

        def lowrank_norm(wt, gvec, dst):
            for (c0, w, st) in (subs if dst is nkv else qsubs):
                h0 = hcol(c0)
                for m in range(2):
                    for c in range(KC):
                        P.mm(ps[1 + m][:, :w], wt[:, c, m * 128:(m + 1) * 128], k.hmix[:, c, h0:h0 + w],
                             start=(c == 0), stop=(c == KC - 1))
                    P.copy(raw[:, m, :w], ps[1 + m][:, :w], e="act")
                sumsq_rstd(P, k, lambda c, oo, ww: raw[:, c, oo:oo + ww], 2, [(0, w)], rstd, ps[0], sq, 1.0 / 256)
                for m in range(2):
                    P.stt(dst[:, m, c0:c0 + w], raw[:, m, :w], gvec[:, m:m + 1], rstd[:, :w], ALU.mult, ALU.mult)

        lowrank_norm(wckv, gkv, nkv)
        lowrank_norm(wcq, gq, nq)
        for (c0, w, st) in subs:
            h0 = hcol(c0)
            for hh in range(2):
                for c in range(KC):
                    P.mm(ps[1 + hh][0:64, :w], wkr[:, c, hh * 64:(hh + 1) * 64], k.hmix[:, c, h0:h0 + w],
                         start=(c == 0), stop=(c == KC - 1))
            if st == 1:
                P.copy(krope[:, c0:c0 + w], ps[1][0:64, :w], e="act")
            else:
                l0 = c0 - NCTX
                rc, rs = rope_tabs(l0, w)
                P.tt(t1[0:64, :w], ps[1][0:64, :w], rc[:, :w], ALU.mult)
                P.tt(t2[0:64, :w], ps[2][0:64, :w], rs[:, :w], ALU.mult)
                P.tt(krope[:, c0:c0 + w], t1[0:64, :w], t2[0:64, :w], ALU.add, e="pool")
        knT = P.sbuf("knT", [128, T], BF16)
        qnT = P.sbuf("qnT", [128, T], BF16)
        qrope = P.sbuf("qrope", [64, T], BF16)
        pT = [P.sbuf("pT%d" % i, [128, 512], BF16) for i in range(3)]
        rden = P.sbuf("rden", [128, 512], F32)
        for hd in range(4):
            for tc in range(18):
                pv = ps[1 + tc % 2]
                for c in range(2):
                    P.mm(pv[:, 0:128], nkv[:, c, tc * 128:(tc + 1) * 128], wukv[:, c, hd, 128:256], start=(c == 0), stop=(c == 1))
                P.copy(vall[:, tc, :], pv[:, 0:128], e=("act" if tc % 2 else "dve"))
            for (c0, w, st) in subs:
                for c in range(2):
                    P.mm(ps[1][:, :w], wukv[:, c, hd, 0:128], nkv[:, c, c0:c0 + w], start=(c == 0), stop=(c == 1))
                P.copy(knT[:, c0:c0 + w], ps[1][:, :w], e="act")
            for (c0, w, st) in qsubs:
                for c in range(2):
                    P.mm(ps[1][:, :w], wuq[:, c, hd, 0:128], nq[:, c, c0:c0 + w], start=(c == 0), stop=(c == 1))
                P.copy(qnT[:, c0:c0 + w], ps[1][:, :w], e="act")
                for c in range(2):
                    P.mm(ps[2][0:64, :w], wuq[:, c, hd, 128:192], nq[:, c, c0:c0 + w], start=(c == 0), stop=(c == 1))
                if st == 1:
                    P.copy(qrope[:, c0:c0 + w], ps[2][0:64, :w], e="dve")
                else:
                    for c in range(2):
                        P.mm(ps[3][0:64, :w], wuqs[:, c, hd, :], nq[:, c, c0:c0 + w], start=(c == 0), stop=(c == 1))
                    l0 = c0 - NCTX
                    rc, rs = rope_tabs(l0, w)
                    P.tt(t1[0:64, :w], ps[2][0:64, :w], rc[:, :w], ALU.mult)
                    P.tt(t2[0:64, :w], ps[3][0:64, :w], rs[:, :w], ALU.mult)
                    P.tt(qrope[:, c0:c0 + w], t1[0:64, :w], t2[0:64, :w], ALU.add, e="pool")
            cnt = 0
            for qi, (c0, w, st) in enumerate(qsubs):
                kcs = list(range(2)) if st == 1 else list(range(18))
                pso, psd = (ps[6], ps[7]) if qi % 2 == 0 else (ps[2], ps[3])
                base = cnt
                cnt += len(kcs)

                def emit_s(i):
                    kc = kcs[i]
                    pss = ps[4 + (base + i) % 2]
                    P.mm(pss[:, :w], knT[:, kc * 128:(kc + 1) * 128], qnT[:, c0:c0 + w], start=True, stop=False)
                    P.mm(pss[:, :w], krope[:, kc * 128:(kc + 1) * 128], qrope[:, c0:c0 + w], start=False, stop=True)
                emit_s(0)
                for i, kc in enumerate(kcs):
                    if i + 1 < len(kcs):
                        emit_s(i + 1)
                    pss = ps[4 + (base + i) % 2]
                    p = pT[(base + i) % 3]
                    P.act(p[:, :w], pss[:, :w], AF.Exp, scale=SC)
                    P.mm(pso[:, :w], vall[:, kc, :], p[:, :w], start=(i == 0), stop=(i == len(kcs) - 1))
                    P.mm(psd[:, :w], k.ones_bf[:], p[:, :w], start=(i == 0), stop=(i == len(kcs) - 1))
                P.recip(rden[:, :w], psd[:, :w])
                P.tt(aT[:, c0:c0 + w], pso[:, :w], rden[:, :w], ALU.mult)
            cols0 = 0 if ctx_out else NCTX
            P.dma(k.br[0][hd * 128:(hd + 1) * 128, cols0:T], aT[:, cols0:T], q="sp")


def declare_na_inputs(P, I, nl):
    nc = P.nc

    def inp(name, shape):
        I[name] = nc.dram_tensor(name, list(shape), F32, kind="ExternalInput").ap()
    inp("na_G", [nl, 128, 8 * 16 * 64])
    inp("na_mv", [2, 16 * 128]); inp("na_sel", [2, 128]); inp("na_cm", [128, 64])


def na_branch(P, k, l, ctx_out, geo):
    nc = P.nc
    I = k.I
    w_in = I["w_in"][l].rearrange("(c p) n -> p c n", p=128)
    ps = k.ps
    with P.scope():
        BP = P.sbuf("na_BP", [128, 8, 16, 64], BF16)
        cm = P.sbuf("na_cm", [128, 64], F32)
        P.dma(cm[:], I["na_cm"])
        gt = [P.sbuf("na_gt%d" % i, [128, 16, 64], F32) for i in range(2)]
        Gr = I["na_G"][l].rearrange("p (h i q) -> p h i q", h=8, i=16)
        for h in range(8):
            P.dma(gt[h % 2][:], Gr[:, h])
            P.tt(BP[:, h], gt[h % 2][:], cm[:].unsqueeze(1).broadcast_to([128, 16, 64]), ALU.add)
        mvf = P.sbuf("na_mvf", [2, 16 * 128], F32)
        P.dma(mvf[:], I["na_mv"])
        mv = P.sbuf("na_mv", [2, 16 * 128], BF16)
        P.copy(mv[:], mvf[:])
        self_f = P.sbuf("na_self", [2, 128], F32)
        P.dma(self_f[:], I["na_sel"])
        sel = P.sbuf("na_sel", [2, 128], BF16)
        P.copy(sel[:], self_f[:])
        wk = P.sbuf("na_wk", [128, KC, 128], BF16)
        wq = P.sbuf("na_wq", [128, KC, 128], BF16)
        wv = P.sbuf("na_wv", [128, KC, 128], BF16)
        KT = P.sbuf("na_KT", [128, T], BF16)
        QT = P.sbuf("na_QT", [128, T], BF16)
        V = P.sbuf("na_V", [128, 18, 128], BF16)
        dT = P.sbuf("na_dT", [128, T], BF16)
        pT = [P.sbuf("na_pT%d" % i, [128, 128], BF16) for i in range(3)]
        rden = P.sbuf("na_rden", [128, 128], F32)
        qsubs = ALLSUBS if ctx_out else LATSUBS
        cnt = 0
        for hp in range(4):
            P.dma(wk[:], w_in[:, :, C_NK + hp * 128:C_NK + (hp + 1) * 128], q="pool")
            P.dma(wq[:], w_in[:, :, C_NQ + hp * 128:C_NQ + (hp + 1) * 128], q="pool")
            P.dma(wv[:], w_in[:, :, C_NV + hp * 128:C_NV + (hp + 1) * 128], q="pool")
            for (c0, w, st) in ALLSUBS:
                h0 = hcol(c0)
                for c in range(KC):
                    P.mm(ps[1][:, :w], wk[:, c, :], k.hmix[:, c, h0:h0 + w], start=(c == 0), stop=(c == KC - 1))
                P.copy(KT[:, c0:c0 + w], ps[1][:, :w], e="act")
            for (c0, w, st) in qsubs:
                h0 = hcol(c0)
                for c in range(KC):
                    P.mm(ps[2][:, :w], wq[:, c, :], k.hmix[:, c, h0:h0 + w], start=(c == 0), stop=(c == KC - 1))
                P.ts(QT[:, c0:c0 + w], ps[2][:, :w], 0.125, None, op0=ALU.mult)
            for tc in range(18):
                h0 = hcol(tc * 128)
                pv = ps[1 + tc % 2]
                for c in range(KC):
                    P.mm(pv[:, 0:128], k.hmix[:, c, h0:h0 + 128], wv[:, c, :], start=(c == 0), stop=(c == KC - 1))
                P.copy(V[:, tc, :], pv[:, 0:128], e=("act" if tc % 2 else "dve"))
            qblocks = []
            if ctx_out:
                qblocks += [(0, []), (128, [])]
            for i in range(16):
                qblocks.append((NCTX + i * 128, geo[i]))
            for qi, (q0, loc) in enumerate(qblocks):
                chunks = [(0, None, None), (1, None, None)] + [(2 + j, idxp, combo) for (j, idxp, combo) in loc]
                pso, psd = (ps[6], ps[7]) if qi % 2 == 0 else (ps[2], ps[3])
                work = [(hh, ci) for hh in range(2) for ci in range(len(chunks))]
                base = cnt
                cnt += len(work)

                def emit_s(wi):
                    hh, ci = work[wi]
                    kc, idxp, combo = chunks[ci]
                    h = 2 * hp + hh
                    pr = slice(hh * 64, (hh + 1) * 64)
                    pss = ps[4 + (base + wi) % 2]
                    last_s = (idxp is None)
                    P.mm(pss[:, 0:128], KT[pr, kc * 128:(kc + 1) * 128], QT[pr, q0:q0 + 128], start=True, stop=last_s)
                    if idxp is not None:
                        need_mask = combo != 0
                        P.mm(pss[:, 0:128], k.ident_bf[:], BP[:, h, idxp:idxp + 2, :], start=False, stop=not need_mask)
                        if need_mask:
                            P.mm(pss[:, 0:128], mv[0:2, combo * 128:(combo + 1) * 128], sel[0:2, :], start=False, stop=True)
                emit_s(0)
                for wi, (hh, ci) in enumerate(work):
                    if wi + 1 < len(work):
                        emit_s(wi + 1)
                    kc = chunks[ci][0]
                    pr = slice(hh * 64, (hh + 1) * 64)
                    pss = ps[4 + (base + wi) % 2]
                    p = pT[(base + wi) % 3]
                    P.act(p[:], pss[:, 0:128], AF.Exp)
                    P.mm(pso[pr, 0:128], V[:, kc, pr], p[:], start=(ci == 0), stop=(ci == len(chunks) - 1))
                    P.mm(psd[pr, 0:128], k.ones_bf[:, 0:64], p[:], start=(ci == 0), stop=(ci == len(chunks) - 1))
                P.recip(rden[:], psd[:, 0:128])
                P.tt(dT[:, q0:q0 + 128], pso[:, 0:128], rden[:], ALU.mult)
            cols0 = 0 if ctx_out else NCTX
            P.dma(k.br[3][hp * 128:(hp + 1) * 128, cols0:T], dT[:, cols0:T], q="sp")


def post_norm_residual(P, k, ob, s, subs_local, rstd, ps_ss, sq, tmp, xb):
    for (o, c0, w, st) in subs_local:
        P.dma(xb[:, :, o:o + w], k.xres[:, :, c0:c0 + w])
        sumsq_rstd(P, k, lambda c, oo, ww: ob[:, c, oo:oo + ww], KC, [(o, w)], rstd, ps_ss, sq, 1.0 / D)
        for c in range(KC):
            t = tmp[c % 2]
            P.stt(t[:, :w], ob[:, c, o:o + w], k.Gg[:, s, c, st:st + 1], rstd[:, o:o + w], ALU.mult, ALU.mult)
            P.tt(xb[:, c, o:o + w], xb[:, c, o:o + w], t[:, :w], ALU.add, e=("pool" if c % 3 == 2 else "dve"))
        P.dma(k.xres[:, :, c0:c0 + w], xb[:, :, o:o + w], q="sp")


def merge_phase(P, k, l, ctx_out):
    I = k.I
    w_in = I["w_in"][l].rearrange("(c p) n -> p c n", p=128)
    ps = k.ps
    subs = ALLSUBS if ctx_out else LATSUBS
    with P.scope():
        wg = P.sbuf("mg_wg", [128, KC, 4 * D], BF16)
        for j in range(8):
            P.dma(wg[:, :, j * 512:(j + 1) * 512], w_in[:, :, C_GT + j * 512:C_GT + (j + 1) * 512], q="pool")
        wb = P.sbuf("mg_wb", [128, 4, 4, D], BF16)
        for n in range(4):
            P.dma(wb[:, n], I["w_branch"][l, n].rearrange("(kk p) d -> p kk d", p=128), q="pool")
        wo = P.sbuf("mg_wo", [128, KC, D], BF16)
        for j in range(2):
            P.dma(wo[:, :, j * 512:(j + 1) * 512], I["w_out"][l].rearrange("(kk p) d -> p kk d", p=128)[:, :, j * 512:(j + 1) * 512], q="pool")
        brt = [[P.sbuf("mg_br%d_%d" % (n, i), [128, 4, 512], BF16) for n in range(4)] for i in range(1)]
        mt = P.sbuf("mg_mt", [128, KC, 512], BF16)
        ob = P.sbuf("mg_ob", [128, KC, 512], BF16)
        sg = [P.sbuf("mg_sg%d" % i, [128, 512], F32) for i in range(2)]
        acc = P.sbuf("mg_acc", [128, 512], F32)
        tm = [P.sbuf("mg_tm%d" % i, [128, 512], F32) for i in range(2)]
        rstd = P.sbuf("mg_rstd", [128, 512], F32)
        sq = [P.sbuf("mg_sq%d" % i, [128, 512], BF16) for i in range(2)]
        xbm = P.sbuf("mg_x", [128, KC, 512], F32)
        hbt = [P.sbuf("mg_h%d" % i, [128, KC, 512], BF16) for i in range(2)]
        cnt = 0
        for si, (c0, w, st) in enumerate(subs):
            h0 = hcol(c0)
            hb_ = hbt[si % 2]
            P.dma(hb_[:, :, :w], k.hmx[:, :, h0:h0 + w])
            bt = brt[0]
            for n in range(4):
                P.dma(bt[n][:, :, :w], k.br[n].rearrange("(c p) t -> p c t", p=128)[:, :, c0:c0 + w])
            for dc in range(KC):
                for n in range(4):
                    pg = ps[1 + cnt % 2]
                    pp = ps[3 + cnt % 2]
                    s_ = sg[cnt % 2]
                    cnt += 1
                    col = n * D + dc * 128
                    for c in range(KC):
                        P.mm(pg[:, :w], wg[:, c, col:col + 128], hb_[:, c, :w], start=(c == 0), stop=(c == KC - 1))
                    for kk in range(4):
                        P.mm(pp[:, :w], wb[:, n, kk, dc * 128:(dc + 1) * 128], bt[n][:, kk, :w], start=(kk == 0), stop=(kk == 3))
                    P.act(s_[:, :w], pg[:, :w], AF.Sigmoid)
                    if n == 0:
                        P.tt(acc[:, :w], s_[:, :w], pp[:, :w], ALU.mult)
                    else:
                        t = tm[n % 2]
                        P.tt(t[:, :w], s_[:, :w], pp[:, :w], ALU.mult)
                        if n < 3:
                            P.tt(acc[:, :w], acc[:, :w], t[:, :w], ALU.add, e="pool")
                        else:
                            P.tt(mt[:, dc, :w], acc[:, :w], t[:, :w], ALU.add, e="pool")
            for dc in range(KC):
                po = ps[5 + dc % 2]
                for kk in range(KC):
                    P.mm(po[:, :w], wo[:, kk, dc * 128:(dc + 1) * 128], mt[:, kk, :w], start=(kk == 0), stop=(kk == KC - 1))
                P.copy(ob[:, dc, :w], po[:, :w], e="act")
            post_norm_residual(P, k, ob, 1, [(0, c0, w, st)], rstd, ps[0], sq, tm, xbm)


def declare_hyena_inputs(P, I, nl, with_ctx=True):
    nc = P.nc

    def inp(name, shape):
        I[name] = nc.dram_tensor(name, list(shape), F32, kind="ExternalInput").ap()
    inp("hy_conv_w", [nl, 3, 1536]); inp("hy_conv_b", [nl, 1536]); inp("hy_bias", [nl, 512])
    inp("hy_w1", [nl, 33, 64]); inp("hy_b1", [nl, 64]); inp("hy_freq1", [nl, 64])
    inp("hy_w2", [nl, 64, 64]); inp("hy_b2", [nl, 64]); inp("hy_freq2", [nl, 64]); inp("hy_w3", [nl, 64, 1024])
    for nm, N in (("lat", NLAT), ("ctx", NCTX)):
        if nm == "ctx" and not with_ctx:
            continue
        for t in ("c", "s", "ct", "st"):
            I["dft_%s_%s" % (t, nm)] = nc.dram_tensor("dft_%s_%s" % (t, nm), [N // 128, 128, N // 128, 128], BF16, kind="ExternalInput").ap()
        inp("hy_zT_%s" % nm, [33, N]); inp("hy_decay_%s" % nm, [N, 512])


PI = math.pi


def sin_reduced(P, dst, src, shp, tmp):
    P.ts(tmp, src, PI, -2.0 * PI, op0=ALU.is_gt, op1=ALU.mult)
    P.tt(src, src, tmp, ALU.add)
    P.ts(tmp, src, -PI, 2.0 * PI, op0=ALU.is_lt, op1=ALU.mult)
    P.tt(src, src, tmp, ALU.add)
    P.act(dst, src, AF.Sin)


def hyena_branch(P, k, l, seq):
    nc = P.nc
    I = k.I
    nm, N, hbase, xbase = seq
    NT = N // 128
    NF = NT
    CW = min(512, N)
    w_in = I["w_in"][l].rearrange("(c p) n -> p c n", p=128)
    ps = k.ps
    dft_c = I["dft_c_%s" % nm]
    dft_s = I["dft_s_%s" % nm]
    dft_ct = I["dft_ct_%s" % nm]
    dft_st = I["dft_st_%s" % nm]
    with P.scope():
        Kre = P.sbuf("hy_Kre", [128, NF, 512], BF16)
        Kim = P.sbuf("hy_Kim", [128, NF, 512], BF16)
        Ct = [P.sbuf("hy_Ct%d" % i, [128, NT, 128], BF16) for i in range(2)]
        St = [P.sbuf("hy_St%d" % i, [128, NT, 128], BF16) for i in range(2)]
        tA = P.sbuf("hy_tA", [128, 512], F32)
        tB = P.sbuf("hy_tB", [128, 512], F32)
        tC = P.sbuf("hy_tC", [128, 512], F32)
        tD = P.sbuf("hy_tD", [128, 512], F32)
        with P.scope():
            w1 = P.sbuf("hy_w1", [33, 64], F32); P.dma(w1[:], I["hy_w1"][l])
            w2 = P.sbuf("hy_w2", [64, 64], F32); P.dma(w2[:], I["hy_w2"][l])
            w3 = P.sbuf("hy_w3", [64, 1024], F32); P.dma(w3[:], I["hy_w3"][l])
            cols = P.sbuf("hy_cols", [64, 6], F32)
            for j, nm_ in enumerate(["hy_b1", "hy_freq1", "hy_b2", "hy_freq2"]):
                P.dma(cols[:, j:j + 1], I[nm_][l].rearrange("(p o) -> p o", o=1))
            P.tt(cols[:, 4:5], cols[:, 0:1], cols[:, 1:2], ALU.mult)
            P.tt(cols[:, 5:6], cols[:, 2:3], cols[:, 3:4], ALU.mult)
            zT = P.sbuf("hy_zT", [33, N], F32); P.dma(zT[:], I["hy_zT_%s" % nm])
            h1 = P.sbuf("hy_h1", [64, N], F32)
            h2 = P.sbuf("hy_h2", [64, N], F32)
            filt = P.sbuf("hy_filt", [128, NT, 1024], BF16)
            dec = [P.sbuf("hy_dec%d" % i, [128, 512], F32) for i in range(2)]
            for cb in range(N // CW):
                cs = slice(cb * CW, (cb + 1) * CW)
                P.mm(ps[1][0:64, :CW], w1[:, :], zT[:, cs], start=True, stop=True)
                P.act(tA[0:64, :CW], ps[1][0:64, :CW], AF.Identity, bias=cols[:, 4:5], scale=cols[:, 1:2])
                sin_reduced(P, h1[:, cs], tA[0:64, :CW], None, tB[0:64, :CW])
            for cb in range(N // CW):
                cs = slice(cb * CW, (cb + 1) * CW)
                P.mm(ps[1][0:64, :CW], w2[:, :], h1[:, cs], start=True, stop=True)
                P.act(tA[0:64, :CW], ps[1][0:64, :CW], AF.Identity, bias=cols[:, 5:6], scale=cols[:, 3:4])
                sin_reduced(P, h2[:, cs], tA[0:64, :CW], None, tB[0:64, :CW])
            for tc in range(NT):
                d = dec[tc % 2]
                P.dma(d[:], I["hy_decay_%s" % nm][tc * 128:(tc + 1) * 128, :])
                for hf in range(2):
                    pp = ps[1 + hf]
                    P.mm(pp[:, :], h2[:, tc * 128:(tc + 1) * 128], w3[:, hf * 512:(hf + 1) * 512], start=True, stop=True)
                    P.tt(filt[:, tc, hf * 512:(hf + 1) * 512], pp[:, :], d[:], ALU.mult)
            for fc in range(NF):
                c_ = Ct[fc % 2]; s_ = St[fc % 2]
                P.dma(c_[:], dft_c[fc])
                P.dma(s_[:], dft_s[fc])
                for bi, (mat, half) in enumerate([(c_, 0), (s_, 0), (c_, 1), (s_, 1)]):
                    for tc in range(NT):
                        P.mm(ps[1 + bi][:, :], mat[:, tc, :], filt[:, tc, half * 512:(half + 1) * 512],
                             start=(tc == 0), stop=(tc == NT - 1))
                P.copy(tA[:], ps[1][:, :], e="act")
                P.tt(Kre[:, fc, :], tA[:], ps[3][:, :], ALU.add)
                P.copy(tB[:], ps[2][:, :], e="act")
                P.tt(Kim[:, fc, :], tB[:], ps[4][:, :], ALU.subtract)
        s_bf = P.sbuf("hy_s", [128, NT, 512], BF16)
        x0_bf = P.sbuf("hy_x0", [128, NT, 512], BF16)
        with P.scope():
            wraw = P.sbuf("hy_wraw", [128, KC, 512], BF16)
            Wk = [P.sbuf("hy_Wk%d" % i, [128, KC, 512], BF16) for i in range(3)]
            cw = P.sbuf("hy_cw", [128, 512], F32)
            cbf = P.sbuf("hy_cbf", [1, 1536], F32)
            P.dma(cbf[:], I["hy_conv_b"][l].rearrange("(o n) -> o n", o=1))
            cbb = P.sbuf("hy_cbb", [1, 1536], BF16)
            P.copy(cbb[:], cbf[:])
            for blk in range(3):
                P.dma(wraw[:], w_in[:, :, C_HY + blk * 512:C_HY + (blk + 1) * 512], q="pool")
                for kk in range(3):
                    P.dma(cw[:], I["hy_conv_w"][l, kk, blk * 512:(blk + 1) * 512].partition_broadcast(128))
                    P.tt(Wk[kk][:], wraw[:], cw[:].unsqueeze(1).broadcast_to([128, KC, 512]), ALU.mult)
                for tc in range(NT):
                    pp = ps[1 + tc % 2]
                    for kk in range(3):
                        c0 = hbase + tc * 128 + kk - 1
                        for c in range(KC):
                            P.mm(pp[:, :], k.hmix[:, c, c0:c0 + 128], Wk[kk][:, c, :], start=(kk == 0 and c == 0), stop=False)
                    P.mm(pp[:, :], k.ones_bf[0:1, 0:128], cbb[0:1, blk * 512:(blk + 1) * 512], start=False, stop=True)
                    if blk == 0:
                        P.copy(s_bf[:, tc, :], pp[:, :], e="act")
                    elif blk == 1:
                        P.tt(s_bf[:, tc, :], s_bf[:, tc, :], pp[:, :], ALU.mult)
                    else:
                        P.copy(x0_bf[:, tc, :], pp[:, :], e="act")
        Yre = P.sbuf("hy_Yre", [128, NF, 512], BF16)
        Yim = P.sbuf("hy_Yim", [128, NF, 512], BF16)
        for fc in range(NF):
            c_ = Ct[fc % 2]; s_ = St[fc % 2]
            P.dma(c_[:], dft_c[fc])
            P.dma(s_[:], dft_s[fc])
            pa = ps[1 + 2 * (fc % 2)]; pb = ps[2 + 2 * (fc % 2)]
            for tc in range(NT):
                P.mm(pa[:, :], c_[:, tc, :], s_bf[:, tc, :], start=(tc == 0), stop=(tc == NT - 1))
            for tc in range(NT):
                P.mm(pb[:, :], s_[:, tc, :], s_bf[:, tc, :], start=(tc == 0), stop=(tc == NT - 1))
            P.copy(tA[:], pa[:, :], e="act")
            P.copy(tB[:], pb[:, :], e="act")
            P.tt(tC[:], tA[:], Kre[:, fc, :], ALU.mult)
            P.tt(tD[:], tB[:], Kim[:, fc, :], ALU.mult, e="pool")
            P.tt(Yre[:, fc, :], tC[:], tD[:], ALU.subtract)
            P.tt(tC[:], tA[:], Kim[:, fc, :], ALU.mult, e="pool")
            P.tt(tD[:], tB[:], Kre[:, fc, :], ALU.mult)
            P.tt(Yim[:, fc, :], tC[:], tD[:], ALU.add, e="pool")
        bd = P.sbuf("hy_bd", [128, 512], F32)
        P.dma(bd[:], I["hy_bias"][l].partition_broadcast(128))
        bT = P.sbuf("hy_bT", [128, 4, N], BF16)
        otm = [P.sbuf("hy_otm%d" % i, [128, 512], BF16) for i in range(2)]
        for tc in range(NT):
            c_ = Ct[tc % 2]; s_ = St[tc % 2]
            P.dma(c_[:], dft_ct[tc])
            P.dma(s_[:], dft_st[tc])
            pp = ps[1 + tc % 2]
            for fc in range(NF):
                P.mm(pp[:, :], c_[:, fc, :], Yre[:, fc, :], start=(fc == 0), stop=False)
            for fc in range(NF):
                P.mm(pp[:, :], s_[:, fc, :], Yim[:, fc, :], start=False, stop=(fc == NF - 1))
            P.tt(tA[:], s_bf[:, tc, :], bd[:], ALU.mult)
            P.stt(tB[:], pp[:, :], 1.0 / N, tA[:], ALU.mult, ALU.add)
            o = otm[tc % 2]
            P.tt(o[:], tB[:], x0_bf[:, tc, :], ALU.mult, e="pool")
            pt = ps[5 + tc % 2][:].bitcast(BF16)
            for j in range(4):
                P.transpose(pt[:, j * 128:(j + 1) * 128], o[:, j * 128:(j + 1) * 128], k.ident_bf[:])
            P.copy(bT[:, :, tc * 128:(tc + 1) * 128], pt[:, 0:512].rearrange("p (j t) -> p j t", j=4), e="act")
        P.dma(k.br[1].rearrange("(c p) t -> p c t", p=128)[:, :, xbase:xbase + N], bT[:], q="sp")


HY_LAT = ("lat", NLAT, LT0, NCTX)
HY_CTX = ("ctx", NCTX, CT0, 0)


NCH = 144


def declare_s5_inputs(P, I, nl):
    nc = P.nc

    def inp(name, shape):
        I[name] = nc.dram_tensor(name, list(shape), F32, kind="ExternalInput").ap()
    inp("s5_lam_re", [nl, 2, 2048]); inp("s5_lam_im", [nl, 2, 2048]); inp("s5_log_dt", [nl, 2, 32])
    inp("s5_b_re", [nl, 2, 2048, 16]); inp("s5_b_im", [nl, 2, 2048, 16])
    inp("s5_c_re", [nl, 2, 32, 16, 64]); inp("s5_c_im", [nl, 2, 32, 16, 64])
    inp("s5_d", [nl, 512]); inp("s5_w_glu", [nl, 512, 1024]); inp("s5_b_glu", [nl, 1024])
    inp("s5_mask", [2, 2, 128, 256])


def s5_setup(P, k):
    nc = P.nc
    k.u_tm = nc.dram_tensor("s5_u_tm", [T, 512], F32, kind="Internal").ap()
    k.y_tm = nc.dram_tensor("s5_y_tm", [T, 512], F32, kind="Internal").ap()


def s5_part1(P, k, l):
    I = k.I
    w_in = I["w_in"][l].rearrange("(c p) n -> p c n", p=128)
    ps = k.ps
    with P.scope():
        wu = P.sbuf("s5_wu", [128, KC, 512], BF16)
        P.dma(wu[:], w_in[:, :, C_U:C_U + 512], q="pool")
        ut = [P.sbuf("s5_ut%d" % i, [128, 512], F32) for i in range(2)]
        for tc in range(18):
            h0 = hcol(tc * 128)
            pp = ps[1 + tc % 2]
            for c in range(KC):
                P.mm(pp[:, :], k.hmix[:, c, h0:h0 + 128], wu[:, c, :], start=(c == 0), stop=(c == KC - 1))
            P.copy(ut[tc % 2][:], pp[:, :], e=("act" if tc % 2 else "dve"))
            P.dma(k.u_tm[tc * 128:(tc + 1) * 128, :], ut[tc % 2][:], q="sp")


def cmul(P, o_re, o_im, a_re, a_im, b_re, b_im, t1, t2, neg_im=False):
    P.tt(t1, a_re, b_re, ALU.mult)
    P.tt(t2, a_im, b_im, ALU.mult, e="pool")
    P.tt(o_re, t1, t2, ALU.subtract)
    P.tt(t1, a_re, b_im, ALU.mult)
    P.tt(t2, a_im, b_re, ALU.mult, e="pool")
    if neg_im:
        P.stt(o_im, t1, -1.0, t2, ALU.mult, ALU.subtract)
    else:
        P.tt(o_im, t1, t2, ALU.add)


def s5_part2(P, k, l, ctx_out):
    nc = P.nc
    I = k.I
    ps = k.ps
    with P.scope():
        U = P.sbuf("s5_U", [128, 32, 2, NCH], BF16)
        M = P.sbuf("s5_M", [128, 32, 2, 256], BF16)
        Qre = [P.sbuf("s5_Qre%d" % d, [128, 16, 256], BF16) for d in range(2)]
        nQim = [P.sbuf("s5_nQim%d" % d, [128, 16, 256], BF16) for d in range(2)]
        Xre = [P.sbuf("s5_Xre%d" % d, [128, 16, NCH], BF16) for d in range(2)]
        Xim = [P.sbuf("s5_Xim%d" % d, [128, 16, NCH], BF16) for d in range(2)]
        with P.scope():
            uc = P.sbuf("s5_uc", [128, 16 * 512], F32)
            ucv = uc[:].rearrange("p (g j h) -> p g j h", g=32, j=16)
            for (part, np_, c_lo) in (("lat", 128, 16), ("ctx", 16, 0)):
                rows = k.u_tm[NCTX:T, :] if part == "lat" else k.u_tm[0:NCTX, :]
                src = rows.rearrange("(c j) (g h) -> c g j h", j=16, g=32)
                for g in range(32):
                    P.dma(ucv[0:np_, g], src[:, g])
                cnt = 0
                for g in range(32):
                    for a in range(2):
                        pp = ps[1 + cnt % 4]
                        cnt += 1
                        P.transpose(pp[:, 0:np_], uc[0:np_, g * 256 + a * 128:g * 256 + (a + 1) * 128], k.ident[0:np_, 0:np_])
                        P.copy(U[:, g, a, c_lo:c_lo + np_], pp[:, 0:np_], e=("act" if cnt % 2 else "dve"))
        for d in range(2):
            with P.scope():
                sm = P.sbuf("s5_sm", [128, 40, 16], F32)
                slot = [0]

                def S():
                    i = slot[0]
                    slot[0] += 1
                    return sm[:, i, :]
                lre = S(); lim = S(); dt = S()
                P.dma(lre, I["s5_lam_re"][l, d].rearrange("(pr q) -> q pr", q=128), allow_slow_non_contiguous=True)
                P.dma(lim, I["s5_lam_im"][l, d].rearrange("(pr q) -> q pr", q=128), allow_slow_non_contiguous=True)
                ldt = I["s5_log_dt"][l, d].rearrange("(pr g2) -> g2 pr", g2=2)
                for g2 in range(2):
                    P.dma(sm[g2 * 64:(g2 + 1) * 64, 2, :], ldt[g2].partition_broadcast(64), allow_slow_non_contiguous=True)
                P.act(dt, dt, AF.Exp)
                P.ts(lre, lre, -1e-4, None, op0=ALU.min)
                a_ = S(); th = S(); t1 = S(); t2 = S()
                P.tt(a_, lre, dt, ALU.mult)
                P.tt(th, lim, dt, ALU.mult)
                mag = S(); imag = S()
                P.act(mag, a_, AF.Exp)
                P.act(imag, a_, AF.Exp, scale=-1.0)
                for _ in range(4):
                    P.ts(t1, th, PI, -2.0 * PI, op0=ALU.is_gt, op1=ALU.mult)
                    P.tt(th, th, t1, ALU.add)
                thc = S()
                P.ts(thc, th, PI / 2, None, op0=ALU.add)
                P.ts(t1, thc, PI, -2.0 * PI, op0=ALU.is_gt, op1=ALU.mult)
                P.tt(thc, thc, t1, ALU.add)
                sn = S(); cs = S()
                P.act(sn, th, AF.Sin)
                P.act(cs, thc, AF.Sin)
                lbr = S(); lbi = S(); lir = S(); lii = S()
                P.tt(lbr, mag, cs, ALU.mult); P.tt(lbi, mag, sn, ALU.mult)
                P.tt(lir, imag, cs, ALU.mult); P.stt(lii, imag, -1.0, sn, ALU.mult, ALU.mult)
                den = S(); icr = S(); ici = S()
                P.tt(den, lre, lre, ALU.mult); P.tt(t1, lim, lim, ALU.mult); P.tt(den, den, t1, ALU.add)
                P.recip(den, den)
                P.tt(icr, lre, den, ALU.mult); P.stt(ici, lim, -1.0, den, ALU.mult, ALU.mult)
                lm1 = S(); cfr = S(); cfi = S()
                P.ts(lm1, lbr, -1.0, None, op0=ALU.add)
                cmul(P, cfr, cfi, lm1, lbi, icr, ici, t1, t2)
                Ppr = P.sbuf("s5_Ppr", [128, 16, 17], F32); Ppi = P.sbuf("s5_Ppi", [128, 16, 17], F32)
                Pnr = P.sbuf("s5_Pnr", [128, 16, 17], F32); Pni = P.sbuf("s5_Pni", [128, 16, 17], F32)
                tw1 = P.sbuf("s5_tw1", [128, 16, 8], F32); tw2 = P.sbuf("s5_tw2", [128, 16, 8], F32)
                for (tr, ti, br_, bi_) in ((Ppr, Ppi, lbr, lbi), (Pnr, Pni, lir, lii)):
                    P.memset(tr[:, :, 0:1], 1.0); P.memset(ti[:, :, 0:1], 0.0)
                    P.copy(tr[:, :, 1], br_); P.copy(ti[:, :, 1], bi_)
                    n = 2
                    while n <= 16:
                        sqr = S() if False else None
                        h = n // 2
                        cmul(P, tr[:, :, n], ti[:, :, n], tr[:, :, h], ti[:, :, h], tr[:, :, h], ti[:, :, h], tw1[:, :, 0], tw2[:, :, 0])
                        cnt_ = min(n, 17 - n) - 1
                        if cnt_ > 0:
                            bre = tr[:, :, n:n + 1].broadcast_to([128, 16, cnt_]) if False else None
                            cmul(P, tr[:, :, n + 1:n + 1 + cnt_], ti[:, :, n + 1:n + 1 + cnt_],
                                 tr[:, :, 1:1 + cnt_], ti[:, :, 1:1 + cnt_],
                                 tr[:, :, n:n + 1].to_broadcast([128, 16, cnt_]), ti[:, :, n:n + 1].to_broadcast([128, 16, cnt_]),
                                 tw1[:, :, 0:cnt_], tw2[:, :, 0:cnt_])
                        n *= 2
                Ar = P.sbuf("s5_Ar", [128, 8, 16], F32); Ai = P.sbuf("s5_Ai", [128, 8, 16], F32); nAi = P.sbuf("s5_nAi", [128, 8, 16], F32)
                P.copy(Ar[:, 0, :], Ppr[:, :, 16]); P.copy(Ai[:, 0, :], Ppi[:, :, 16])
                for kk in range(1, 8):
                    cmul(P, Ar[:, kk, :], Ai[:, kk, :], Ar[:, kk - 1, :], Ai[:, kk - 1, :], Ar[:, kk - 1, :], Ai[:, kk - 1, :], t1, t2)
                P.ts(nAi[:], Ai[:], -1.0, None, op0=ALU.mult)
                Bre = P.sbuf("s5_Bre", [128, 16, 16], F32); Bim = P.sbuf("s5_Bim", [128, 16, 16], F32)
                P.dma(Bre[:], I["s5_b_re"][l, d].rearrange("(pr q) h -> q pr h", q=128))
                P.dma(Bim[:], I["s5_b_im"][l, d].rearrange("(pr q) h -> q pr h", q=128))
                Cre = P.sbuf("s5_Cre", [128, 16, 16], F32); Cim = P.sbuf("s5_Cim", [128, 16, 16], F32)
                for g2 in range(2):
                    for (dst, nm_) in ((Cre, "s5_c_re"), (Cim, "s5_c_im")):
                        src = I[nm_][l, d].rearrange("(pr g2) h p -> g2 pr p h", g2=2)[g2]
                        for pr_ in range(16):
                            P.dma(dst[g2 * 64:(g2 + 1) * 64, pr_, :], src[pr_], allow_slow_non_contiguous=True)
                tb1 = P.sbuf("s5_tb1", [128, 16, 16], F32); tb2 = P.sbuf("s5_tb2", [128, 16, 16], F32)
                bbr = P.sbuf("s5_bbr", [128, 16, 16], F32); bbi = P.sbuf("s5_bbi", [128, 16, 16], F32)
                bc = lambda x: x.unsqueeze(2).to_broadcast([128, 16, 16])
                cmul(P, bbr[:], bbi[:], Bre[:], Bim[:], bc(cfr), bc(cfi), tb1[:], tb2[:])
                if d == 1:
                    cmul(P, Bre[:], Bim[:], bbr[:], bbi[:], bc(Pnr[:, :, 15]), bc(Pni[:, :, 15]), tb1[:], tb2[:])
                    vbr, vbi = Bre, Bim
                    cr2 = P.sbuf("s5_cr2", [128, 16, 16], F32); ci2 = P.sbuf("s5_ci2", [128, 16, 16], F32)
                    cmul(P, cr2[:], ci2[:], Cre[:], Cim[:], bc(Ppr[:, :, 15]), bc(Ppi[:, :, 15]), tb1[:], tb2[:])
                    vcr, vci = cr2, ci2
                    tabP, tabQ = (Ppr, Ppi), (Pnr, Pni)
                else:
                    vbr, vbi = bbr, bbi
                    vcr, vci = Cre, Cim
                    tabP, tabQ = (Pnr, Pni), (Ppr, Ppi)
                Pre = P.sbuf("s5_Pre", [128, 16, 256], BF16); Pim = P.sbuf("s5_Pim", [128, 16, 256], BF16)
                to1 = P.sbuf("s5_to1", [128, 4, 256], F32); to2 = P.sbuf("s5_to2", [128, 4, 256], F32)
                v4 = lambda x: x.rearrange("p a (j h) -> p a j h", j=16)
                for pg in range(4):
                    sl = slice(pg * 4, pg * 4 + 4)
                    tj = lambda tab: tab[:, sl, 0:16].unsqueeze(3).to_broadcast([128, 4, 16, 16])
                    vh = lambda v: v[:, sl, :].unsqueeze(2).to_broadcast([128, 4, 16, 16])
                    cmul(P, v4(Pre[:, sl, :]), v4(Pim[:, sl, :]), tj(tabP[0]), tj(tabP[1]), vh(vbr), vh(vbi), v4(to1[:]), v4(to2[:]))
                    cmul(P, v4(Qre[d][:, sl, :]), v4(nQim[d][:, sl, :]), tj(tabQ[0]), tj(tabQ[1]), vh(vcr), vh(vci), v4(to1[:]), v4(to2[:]),
                         neg_im=True)
                msk = [P.sbuf("s5_msk%d" % a, [128, 256], F32) for a in range(2)]
                for a in range(2):
                    P.dma(msk[a][:], I["s5_mask"][d, a])
                tm = [P.sbuf("s5_tm%d" % i, [128, 256], BF16) for i in range(2)]
                cnt = 0
                for g in range(32):
                    pr_, g2 = g // 2, g % 2
                    rows = slice(g2 * 64, (g2 + 1) * 64)
                    for a in range(2):
                        pp = ps[1 + cnt % 4]
                        cnt += 1
                        P.mm(pp[:, 0:256], Pre[rows, pr_, a * 128:(a + 1) * 128], Qre[d][rows, pr_, :], start=True, stop=False)
                        P.mm(pp[:, 0:256], Pim[rows, pr_, a * 128:(a + 1) * 128], nQim[d][rows, pr_, :], start=False, stop=True)
                        if d == 0:
                            P.tt(M[:, g, a, :], pp[:, 0:256], msk[a][:], ALU.mult)
                        else:
                            t = tm[cnt % 2]
                            P.tt(t[:], pp[:, 0:256], msk[a][:], ALU.mult)
                            P.tt(M[:, g, a, :], M[:, g, a, :], t[:], ALU.add, e="pool")
                PTre = P.sbuf("s5_PTre", [128, 16, 2, 128], BF16); PTim = P.sbuf("s5_PTim", [128, 16, 2, 128], BF16)
                cnt = 0
                for pr_ in range(16):
                    for a in range(2):
                        for (src, dst) in ((Pre, PTre), (Pim, PTim)):
                            pt = ps[5 + cnt % 2][:].bitcast(BF16)
                            cnt += 1
                            P.transpose(pt[:, 0:128], src[:, pr_, a * 128:(a + 1) * 128], k.ident_bf[:])
                            P.copy(dst[:, pr_, a, :], pt[:, 0:128], e=("act" if cnt % 2 else "dve"))
                SA = [P.sbuf("s5_SAre", [128, 16, NCH], F32), P.sbuf("s5_SAim", [128, 16, NCH], F32)]
                SB = [P.sbuf("s5_SBre", [128, 16, NCH], F32), P.sbuf("s5_SBim", [128, 16, NCH], F32)]
                ts_ = P.sbuf("s5_ts", [128, NCH], F32); ts2 = P.sbuf("s5_ts2", [128, NCH], F32)
                for pr_ in range(16):
                    pre_, pim_ = ps[1 + 2 * (pr_ % 2)], ps[2 + 2 * (pr_ % 2)]
                    for (pt_, PT) in ((pre_, PTre), (pim_, PTim)):
                        for g2 in range(2):
                            g = 2 * pr_ + g2
                            rows = slice(g2 * 64, (g2 + 1) * 64)
                            if d == 0:
                                for a in range(2):
                                    P.mm(pt_[rows, 0:NCH], PT[:, pr_, a, rows], U[:, g, a, :], start=(a == 0), stop=(a == 1))
                            else:
                                for a in range(2):
                                    P.mm(pt_[rows, 0:128], PT[:, pr_, a, rows], U[:, g, a, 16:NCH], start=(a == 0), stop=(a == 1))
                                for a in range(2):
                                    P.mm(pt_[rows, 128:NCH], PT[:, pr_, a, rows], U[:, g, a, 0:16], start=(a == 0), stop=(a == 1))
                    P.ts(ts_[:], pre_[:, 0:NCH], Ar[:, 0, pr_:pr_ + 1], None, op0=ALU.mult)
                    P.stt(SA[0][:, pr_, :], pim_[:, 0:NCH], nAi[:, 0, pr_:pr_ + 1], ts_[:], ALU.mult, ALU.add)
                    P.ts(ts2[:], pim_[:, 0:NCH], Ar[:, 0, pr_:pr_ + 1], None, op0=ALU.mult)
                    P.stt(SA[1][:, pr_, :], pre_[:, 0:NCH], Ai[:, 0, pr_:pr_ + 1], ts2[:], ALU.mult, ALU.add)
                cur, nxt = SA, SB
                for kk in range(8):
                    sh = 1 << kk
                    n_ = NCH - sh
                    for pr_ in range(16):
                        if d == 0:
                            dst_r, dst_i = nxt[0][:, pr_, sh:NCH], nxt[1][:, pr_, sh:NCH]
                            src_r, src_i = cur[0][:, pr_, 0:n_], cur[1][:, pr_, 0:n_]
                            own_r, own_i = cur[0][:, pr_, sh:NCH], cur[1][:, pr_, sh:NCH]
                        else:
                            dst_r, dst_i = nxt[0][:, pr_, 0:n_], nxt[1][:, pr_, 0:n_]
                            src_r, src_i = cur[0][:, pr_, sh:NCH], cur[1][:, pr_, sh:NCH]
                            own_r, own_i = cur[0][:, pr_, 0:n_], cur[1][:, pr_, 0:n_]
                        P.stt(ts_[:, 0:n_], src_r, Ar[:, kk, pr_:pr_ + 1], own_r, ALU.mult, ALU.add)
                        P.stt(dst_r, src_i, nAi[:, kk, pr_:pr_ + 1], ts_[:, 0:n_], ALU.mult, ALU.add)
                        P.stt(ts2[:, 0:n_], src_i, Ar[:, kk, pr_:pr_ + 1], own_i, ALU.mult, ALU.add)
                        P.stt(dst_i, src_r, Ai[:, kk, pr_:pr_ + 1], ts2[:, 0:n_], ALU.mult, ALU.add)
                    for ri in range(2):
                        if d == 0:
                            P.copy(nxt[ri][:, :, 0:sh], cur[ri][:, :, 0:sh], e="pool")
                        else:
                            P.copy(nxt[ri][:, :, n_:NCH], cur[ri][:, :, n_:NCH], e="pool")
                    cur, nxt = nxt, cur
                for ri, X in ((0, Xre[d]), (1, Xim[d])):
                    W = cur[ri]
                    if d == 0:
                        P.memset(X[:, :, 0:1], 0.0)
                        P.copy(X[:, :, 1:NCH], W[:, :, 0:NCH - 1], e=("act" if ri else "dve"))
                    else:
                        P.copy(X[:, :, 0:15], W[:, :, 129:144], e="act")
                        P.memset(X[:, :, 15:16], 0.0)
                        P.copy(X[:, :, 16:143], W[:, :, 1:128], e="dve")
                        P.copy(X[:, :, 143:144], W[:, :, 128:129], e="act")
        with P.scope():
            ycl = P.sbuf("s5_ycl", [128, 16 * 512], F32)
            ycc = P.sbuf("s5_ycc", [16, 16 * 512], F32)
            yclv = ycl[:].rearrange("p (j g h) -> p j g h", j=16, g=32)
            yccv = ycc[:].rearrange("p (j g h) -> p j g h", j=16, g=32)
            ysb = [P.sbuf("s5_ysb%d" % i, [128, NCH], F32) for i in range(2)]
            cnt = 0
            for g in range(32):
                pr_, g2 = g // 2, g % 2
                rows = slice(g2 * 64, (g2 + 1) * 64)
                for b in range(2):
                    pp = ps[1 + cnt % 2]
                    ys = ysb[cnt % 2]
                    pt1 = ps[3 + cnt % 2]
                    pt2 = ps[5 + cnt % 2]
                    cnt += 1
                    cols = slice(b * 128, (b + 1) * 128)
                    P.mm(pp[:, 0:NCH], M[:, g, 0, cols], U[:, g, 0, :], start=True, stop=False)
                    P.mm(pp[:, 0:NCH], M[:, g, 1, cols], U[:, g, 1, :], start=False, stop=False)
                    for d in range(2):
                        P.mm(pp[:, 0:NCH], Qre[d][rows, pr_, cols], Xre[d][rows, pr_, :], start=False, stop=False)
                        P.mm(pp[:, 0:NCH], nQim[d][rows, pr_, cols], Xim[d][rows, pr_, :], start=False, stop=(d == 1))
                    P.copy(ys[:], pp[:, 0:NCH], e="act")
                    P.transpose(pt1[:, 0:128], ys[:, 16:NCH], k.ident[:])
                    P.copy(yclv[:, b * 8:(b + 1) * 8, g, :], pt1[:, 0:128].rearrange("p (j h) -> p j h", j=8), e="dve")
                    P.transpose(pt2[0:16, 0:128], ys[:, 0:16], k.ident[:])
                    P.copy(yccv[:, b * 8:(b + 1) * 8, g, :], pt2[0:16, 0:128].rearrange("p (j h) -> p j h", j=8), e="pool" if False else "dve")
            P.dma(k.y_tm[NCTX:T, :].rearrange("(c j) n -> c (j n)", j=16), ycl[:], q="sp")
            P.dma(k.y_tm[0:NCTX, :].rearrange("(c j) n -> c (j n)", j=16), ycc[:], q="sp")
        with P.scope():
            dbc = P.sbuf("s5_dbc", [128, 512], F32)
            P.dma(dbc[:], I["s5_d"][l].partition_broadcast(128))
            wgl = P.sbuf("s5_wgl", [128, 4, 1024], BF16)
            P.dma(wgl[:], I["s5_w_glu"][l].rearrange("(kk p) n -> p kk n", p=128), q="pool")
            bgl = P.sbuf("s5_bgl", [128, 8], F32)
            load_col_vec(P, bgl[:], I["s5_b_glu"][l], 8)
            gT = P.sbuf("s5_gT", [128, 4, T], BF16)
            yt = [P.sbuf("s5_yt%d" % i, [128, 512], F32) for i in range(2)]
            ut = [P.sbuf("s5_ut%d" % i, [128, 512], F32) for i in range(2)]
            w1_ = P.sbuf("s5_w1", [128, 512], F32); w2_ = P.sbuf("s5_w2", [128, 512], F32)
            gtm = [P.sbuf("s5_gtm%d" % i, [128, 512], BF16) for i in range(2)]
            tcs = list(range(18)) if ctx_out else list(range(2, 18))
            for tc in tcs:
                y = yt[tc % 2]; u = ut[tc % 2]
                P.dma(y[:], k.y_tm[tc * 128:(tc + 1) * 128, :])
                P.dma(u[:], k.u_tm[tc * 128:(tc + 1) * 128, :])
                P.tt(u[:], u[:], dbc[:], ALU.mult, e="pool")
                P.tt(y[:], y[:], u[:], ALU.add)
                P.tt(w1_[:], y[:], y[:], ALU.mult, e="pool")
                P.ts(w1_[:], w1_[:], 0.044715, 1.0, op0=ALU.mult, op1=ALU.add)
                P.tt(w1_[:], w1_[:], y[:], ALU.mult)
                P.act(w2_[:], w1_[:], AF.Sigmoid, scale=1.5957691216057308)
                gt_ = gtm[tc % 2]
                P.tt(gt_[:], y[:], w2_[:], ALU.mult, e="pool")
                pt = ps[5 + tc % 2][:].bitcast(BF16)
                for j in range(4):
                    P.transpose(pt[:, j * 128:(j + 1) * 128], gt_[:, j * 128:(j + 1) * 128], k.ident_bf[:])
                P.copy(gT[:, :, tc * 128:(tc + 1) * 128], pt[:, 0:512].rearrange("p (j t) -> p j t", j=4), e="act")
            cT = P.sbuf("s5_cT", [128, T], BF16)
            sg = [P.sbuf("s5_sg%d" % i, [128, 512], F32) for i in range(2)]
            subs = ALLSUBS if ctx_out else LATSUBS
            cnt = 0
            for m in range(4):
                for (c0, w, st) in subs:
                    pa = ps[1 + cnt % 2]; pb = ps[3 + cnt % 2]; s_ = sg[cnt % 2]
                    cnt += 1
                    for kk in range(4):
                        P.mm(pa[:, :w], wgl[:, kk, m * 128:(m + 1) * 128], gT[:, kk, c0:c0 + w], start=(kk == 0), stop=(kk == 3))
                    for kk in range(4):
                        P.mm(pb[:, :w], wgl[:, kk, 512 + m * 128:512 + (m + 1) * 128], gT[:, kk, c0:c0 + w], start=(kk == 0), stop=(kk == 3))
                    P.act(s_[:, :w], pb[:, :w], AF.Sigmoid, bias=bgl[:, 4 + m:5 + m])
                    P.stt(cT[:, c0:c0 + w], pa[:, :w], bgl[:, m:m + 1], s_[:, :w], ALU.add, ALU.mult)
                cols0 = 0 if ctx_out else NCTX
                P.dma(k.br[2][m * 128:(m + 1) * 128, cols0:T], cT[:, cols0:T], q="sp")


def build_full(nl=2, upto=None):
    P = Prog()
    nc = P.nc
    k = K()
    k.I = declare_inputs(P, nl)
    declare_mixer_inputs(P, k.I, nl)
    declare_na_inputs(P, k.I, nl)
    declare_hyena_inputs(P, k.I, nl)
    declare_s5_inputs(P, k.I, nl)
    yT = nc.dram_tensor("yT", [D, NLAT], F32, kind="ExternalOutput").ap()
    setup_common(P, k)
    mixer_setup(P, k)
    s5_setup(P, k)
    geo = na_geometry()
    P.dma(k.xres, k.I["xT"].rearrange("(c p) t -> p c t", p=128))
    for l in range(nl):
        ctx_out = l < nl - 1
        with P.scope():
            layer_mods(P, k, l)
        with P.scope():
            ffn_sublayer(P, k, l, 0, FULL_BLOCKS)
        with P.scope():
            alloc_hmix(P, k)
            mixer_modnorm(P, k)
            mla_branch(P, k, l, ctx_out)
            na_branch(P, k, l, ctx_out, geo)
            hyena_branch(P, k, l, HY_LAT)
            if ctx_out:
                hyena_branch(P, k, l, HY_CTX)
            s5_part1(P, k, l)
        s5_part2(P, k, l, ctx_out)
        merge_phase(P, k, l, ctx_out)
        with P.scope():
            ffn_sublayer(P, k, l, 2, FULL_BLOCKS if ctx_out else LAT_BLOCKS)
    P.dma(yT.rearrange("(c p) t -> p c t", p=128), k.xres[:, :, NCTX:T], q="sp")
    P.finish("sp")
    P.close()
    return P, k


_CACHE = {}


def _host_constants():
    if "c" in _CACHE:
        return _CACHE["c"]
    C, S = rope_tables()
    mv, sel, cm = na_const_tables()
    m = {"ident": np.eye(128, dtype=np.float32), "rope_c": C, "rope_s": S,
         "na_mv": np.ascontiguousarray(mv.reshape(2, -1)), "na_sel": sel, "na_cm": cm, "s5_mask": s5_masks()}
    for nm_, N in (("lat", NLAT), ("ctx", NCTX)):
        hc = hyena_consts(N)
        for t in ("c", "s", "ct", "st"):
            m["dft_%s_%s" % (t, nm_)] = hc[t]
        m["hy_zT_%s" % nm_] = hc["zT"]
        m["hy_decay_%s" % nm_] = hc["decay"]
    _CACHE["c"] = m
    return m


def kernel(**inputs):
    nl = 2
    if "prog" not in _CACHE:
        _CACHE["prog"] = build_full(nl)
    P, k = _CACHE["prog"]
    shared = dict(_host_constants())
    shared["na_G"] = np.ascontiguousarray(
        np.stack([na_bias_gather(np.asarray(inputs["na_rpb"][l], np.float32)).reshape(128, -1) for l in range(nl)], 0))
    for nm, ap in k.I.items():
        if nm in shared or nm in ("xT", "cvec"):
            continue
        shared[nm] = np.ascontiguousarray(np.asarray(inputs[nm], np.float32).reshape(ap.shape))
    x = np.asarray(inputs["x"], np.float32)
    ctx = np.asarray(inputs["ctx"], np.float32)
    c = np.asarray(inputs["c"], np.float32)
    c_ctx = np.asarray(inputs["c_ctx"], np.float32)
    B = x.shape[0]
    in_maps = []
    for b in range(B):
        m = dict(shared)
        m["xT"] = np.ascontiguousarray(np.concatenate([ctx[b], x[b]], 0).T)
        m["cvec"] = np.ascontiguousarray(np.stack([c[b], c_ctx], 1))
        in_maps.append(m)
    res = run_bass_kernel_spmd(P.nc, in_maps, core_ids=list(range(B)))
    out = np.stack([np.asarray(res.results[b]["yT"], np.float32).T for b in range(B)], 0)
    return np.ascontiguousarray(out)
```

```python
import numpy as np
import concourse.bass as bass
import concourse.mybir as mybir
from concourse.bass_utils import run_bass_kernel_spmd

F32 = mybir.dt.float32
F32R = mybir.dt.float32r
BF16 = mybir.dt.bfloat16
AF = mybir.ActivationFunctionType
ALU = mybir.AluOpType
AX = mybir.AxisListType

SEM_ROLL = 30000


class Prog:
    def __init__(self, n_dma_sems=16):
        self.nc = bass.Bass("TRN2", target_bir_lowering=False)
        nc = self.nc
        self.eng = {"pe": nc.tensor, "act": nc.scalar, "dve": nc.vector,
                    "pool": nc.gpsimd, "sp": nc.sync}
        self._ctx = []
        self._scopes = []
        self._in_scope_alloc = False
        self._uid = 0
        self.sem = {}
        self.cnt = {}
        self.nsem = 0
        for e in self.eng:
            self._new_eng_sem(e)
        self.dma_sems = {}
        self.dma_rr = {}
        for q in ("sp", "pool", "act"):
            self.dma_sems[q] = []
            for i in range(n_dma_sems if q != "act" else 4):
                s = self._enter(nc.semaphore("dq_%s%d" % (q, i)))
                self.dma_sems[q].append([s, 0])
            self.dma_rr[q] = 0
        self.waited = {e: {} for e in self.eng}
        self.regions = {}
        self.n_inst = 0
        self.n_wait = 0

    def _enter(self, cm):
        v = cm.__enter__()
        if self._scopes and self._in_scope_alloc:
            self._scopes[-1].append((cm, v))
        else:
            self._ctx.append(cm)
        return v

    class _Scope:
        def __init__(self, P):
            self.P = P

        def __enter__(self):
            self.P._scopes.append([])
            return self

        def __exit__(self, *a):
            P = self.P
            items = P._scopes.pop()
            toks = []
            for cm, v in items:
                nm = v.name if hasattr(v, "name") else None
                for r in P.regions.pop(nm, []):
                    toks.append((r[5], r[6]))
            if toks:
                for e in P.eng:
                    P._emit_waits(e, toks)
            for cm, v in reversed(items):
                cm.__exit__(None, None, None)
            return False

    def scope(self):
        return Prog._Scope(self)

    def _new_eng_sem(self, e):
        s = self._enter(self.nc.semaphore("s_%s_%d" % (e, self.nsem)))
        self.nsem += 1
        self.sem[e] = s
        self.cnt[e] = 0

    def close(self):
        for cm in reversed(self._ctx):
            cm.__exit__(None, None, None)
        self._ctx = []

    def sbuf(self, name, shape, dtype=F32):
        self._uid += 1
        self._in_scope_alloc = True
        try:
            return self._enter(self.nc.sbuf_tensor("%s_%d" % (name, self._uid), list(shape), dtype))
        finally:
            self._in_scope_alloc = False

    def psum(self, name, shape=(128, 512), dtype=F32):
        return self._enter(self.nc.psum_tensor(name, list(shape), dtype))

    def dram(self, name, shape, dtype=F32, kind="Internal"):
        return self.nc.dram_tensor(name, list(shape), dtype, kind=kind)

    @staticmethod
    def _region(ap):
        space = str(ap.space)
        name = ap.name
        aps = ap.ap
        off = int(ap.offset)
        if "DRAM" in space:
            ext = sum((c - 1) * abs(s) for s, c in aps)
            neg = sum((c - 1) * s for s, c in aps if s < 0)
            lo = off + neg
            return name, 0, 1, lo, lo + ext + 1, False
        pstep, pcnt = aps[0]
        if pstep == 0:
            pstep = 1 << 40
        p0 = off // pstep if pstep < (1 << 39) else 0
        lo = off - p0 * pstep if pstep < (1 << 39) else off
        ext = sum((c - 1) * abs(s) for s, c in aps[1:])
        is_psum = "PSUM" in space
        if is_psum:
            return name, 0, 128, 0, 1 << 30, True
        return name, p0, p0 + pcnt, lo, lo + ext + 1, False

    def _deps(self, reads, writes):
        toks = []
        info = []
        for ap, is_w in [(a, False) for a in reads] + [(a, True) for a in writes]:
            name, p0, p1, lo, hi, excl = self._region(ap)
            w = is_w or excl
            lst = self.regions.setdefault(name, [])
            for r in lst:
                if r[1] <= p0 or p1 <= r[0] or r[3] <= lo or hi <= r[2]:
                    continue
                if w or r[4]:
                    toks.append((r[5], r[6]))
            info.append((name, p0, p1, lo, hi, w))
        return toks, info

    def _record(self, info, sem, val):
        for name, p0, p1, lo, hi, w in info:
            lst = self.regions[name]
            if w:
                lst[:] = [r for r in lst if not (p0 <= r[0] and r[1] <= p1 and lo <= r[2] and r[3] <= hi)]
                lst.append([p0, p1, lo, hi, True, sem, val])
            else:
                for r in lst:
                    if (not r[4]) and r[0] == p0 and r[1] == p1 and r[2] == lo and r[3] == hi and r[5] is sem:
                        r[6] = max(r[6], val)
                        break
                else:
                    lst.append([p0, p1, lo, hi, False, sem, val])

    def _emit_waits(self, e, toks):
        best = {}
        for s, v in toks:
            k = id(s)
            if k not in best or best[k][1] < v:
                best[k] = (s, v)
        wd = self.waited[e]
        for k, (s, v) in best.items():
            if wd.get(k, 0) >= v:
                continue
            self.eng[e].wait_ge(s, v)
            wd[k] = v
            self.n_wait += 1

    def op(self, e, fn, reads, writes):
        toks, info = self._deps(reads, writes)
        if e == "pe":
            toks = [t for t in toks if t[0] is not self.sem["pe"]]
        self._emit_waits(e, toks)
        ins = fn()
        if self.cnt[e] >= SEM_ROLL:
            self._new_eng_sem(e)
        self.cnt[e] += 1
        ins.then_inc(self.sem[e], 1)
        self._record(info, self.sem[e], self.cnt[e])
        self.n_inst += 1
        return ins

    def dma(self, out, in_, q="sp", **kw):
        toks, info = self._deps([in_], [out])
        ent = self.dma_sems[q][self.dma_rr[q]]
        self.dma_rr[q] = (self.dma_rr[q] + 1) % len(self.dma_sems[q])
        s = ent[0]
        if ent[1] > 0:
            toks.append((s, ent[1]))
        self._emit_waits(q, toks)
        ent[1] += 16
        ins = self.eng[q].dma_start(out=out, in_=in_, **kw)
        ins.then_inc(s, 16)
        self._record(info, s, ent[1])
        self.n_inst += 1
        return ins

    def finish(self, e="sp"):
        toks = []
        for lst in self.regions.values():
            for r in lst:
                toks.append((r[5], r[6]))
        self._emit_waits(e, toks)

    def mm(self, out, lhsT, rhs, start=True, stop=True, **kw):
        return self.op("pe", lambda: self.nc.tensor.matmul(out, lhsT, rhs, start=start, stop=stop, **kw),
                       [lhsT, rhs], [out])

    def transpose(self, out, in_, ident):
        return self.op("pe", lambda: self.nc.tensor.transpose(out, in_, ident), [in_, ident], [out])

    def act(self, out, in_, func, bias=None, scale=1.0, e="act", **kw):
        reads = [in_]
        if bias is not None and not isinstance(bias, (int, float)):
            reads.append(bias)
        if not isinstance(scale, (int, float)):
            reads.append(scale)
        kw2 = dict(kw)
        if bias is not None:
            kw2["bias"] = bias
        writes = [out]
        if "accum_out" in kw2:
            writes.append(kw2["accum_out"])
        return self.op(e, lambda: self.nc.scalar.activation(out=out, in_=in_, func=func, scale=scale, **kw2),
                       reads, writes)

    def _veng(self, e):
        return self.nc.vector if e == "dve" else self.nc.gpsimd

    def tt(self, out, in0, in1, op, e="dve"):
        return self.op(e, lambda: self._veng(e).tensor_tensor(out=out, in0=in0, in1=in1, op=op), [in0, in1], [out])

    def ts(self, out, in0, s1, s2=None, op0=ALU.mult, op1=None, e="dve", **kw):
        reads = [in0] + [s for s in (s1, s2) if s is not None and not isinstance(s, (int, float))]
        writes = [out] + ([kw["accum_out"]] if "accum_out" in kw else [])
        if op1 is None:
            return self.op(e, lambda: self._veng(e).tensor_scalar(out=out, in0=in0, scalar1=s1, scalar2=None, op0=op0, **kw),
                           reads, writes)
        return self.op(e, lambda: self._veng(e).tensor_scalar(out=out, in0=in0, scalar1=s1, scalar2=s2, op0=op0, op1=op1, **kw),
                       reads, writes)

    def stt(self, out, in0, scalar, in1, op0, op1, e="dve"):
        reads = [in0, in1] + ([] if isinstance(scalar, (int, float)) else [scalar])
        return self.op(e, lambda: self.nc.vector.scalar_tensor_tensor(out=out, in0=in0, scalar=scalar, in1=in1, op0=op0, op1=op1),
                       reads, [out])

    def copy(self, out, in_, e="dve"):
        if e == "act":
            return self.op("act", lambda: self.nc.scalar.copy(out=out, in_=in_), [in_], [out])
        return self.op(e, lambda: self._veng(e).tensor_copy(out=out, in_=in_), [in_], [out])

    def memset(self, ap, val, e="dve"):
        return self.op(e, lambda: self._veng(e).memset(ap, val), [], [ap])

    def recip(self, out, in_):
        return self.op("dve", lambda: self.nc.vector.reciprocal(out=out, in_=in_), [in_], [out])

    def scan(self, out, d0, d1, initial, op0=ALU.mult, op1=ALU.add):
        reads = [d0, d1] + ([] if isinstance(initial, (int, float)) else [initial])
        return self.op("dve", lambda: self.nc.vector.tensor_tensor_scan(out=out, data0=d0, data1=d1, initial=initial, op0=op0, op1=op1),
                       reads, [out])


import numpy as np
def rope_tables(n=2048, grid_w=64, base=10000.0):
    q = 16
    t = np.arange(n)
    pos = np.stack([t // grid_w, t % grid_w], -1).astype(np.float32)
    inv = (base ** (-np.arange(q, dtype=np.float32) / q)).astype(np.float32)
    ang = pos[:, :, None] * inv
    C = np.zeros((64, n), np.float32); S = np.zeros((64, n), np.float32)
    for a in range(2):
        for hf in range(2):
            for j in range(q):
                f = a * 32 + hf * 16 + j
                C[f] = np.cos(ang[:, a, j])
                S[f] = (-1.0 if hf == 0 else 1.0) * np.sin(ang[:, a, j])
    return C, S

NEG = -30000.0
def na_geometry():
    rows = 32
    start = lambda r: min(max(r - 4, 0), rows - 8)
    geo = []
    for i in range(16):
        rs = [2 * i, 2 * i + 1]
        lo = min(start(r) for r in rs); hi = max(start(r) + 7 for r in rs)
        lst = []
        for j in range(lo // 2, hi // 2 + 1):
            codes = []
            for r in rs:
                v0 = start(r) <= 2 * j <= start(r) + 7
                v1 = start(r) <= 2 * j + 1 <= start(r) + 7
                code = {(True, True): 0, (False, True): 1, (True, False): 2, (False, False): 3}[(v0, v1)]
                codes.append(code)
            dr0 = 2 * (j - i)
            idxp = 7 - dr0
            assert 0 <= idxp <= 14, (i, j, idxp)
            lst.append((j, idxp, codes[0] * 4 + codes[1]))
        geo.append(lst)
    return geo

def na_const_tables():
    mv = np.zeros((2, 16, 128), np.float32)
    vecs = [np.zeros(128), np.r_[np.full(64, NEG), np.zeros(64)], np.r_[np.zeros(64), np.full(64, NEG)], np.full(128, NEG)]
    for c0 in range(4):
        for c1 in range(4):
            mv[0, c0 * 4 + c1] = vecs[c0]; mv[1, c0 * 4 + c1] = vecs[c1]
    sel = np.zeros((2, 128), np.float32); sel[0, :64] = 1; sel[1, 64:] = 1
    col = np.arange(64)
    c0 = np.clip(col - 8, 0, 48)
    inwin = (col[None, :] >= c0[:, None]) & (col[None, :] < c0[:, None] + 16)
    cm = np.where(inwin.T, 0.0, NEG).astype(np.float32)
    cm = np.concatenate([cm, cm], 0)
    return mv, sel, cm

def na_bias_gather(rpb):
    kc = np.arange(64)[:, None]; qc = np.arange(64)[None, :]
    dc = np.clip(kc - qc + 15, 0, 30)
    G = np.zeros((128, 8, 16, 64), np.float32)
    for idxp in range(16):
        for krl in range(2):
            dr = 7 - idxp + krl
            row = dr + 7
            if not (0 <= row <= 14):
                row = 0
            G[krl * 64:(krl + 1) * 64, :, idxp, :] = np.transpose(rpb[:, row][:, dc], (1, 0, 2))
    return G

def hyena_consts(N):
    t = np.arange(N, dtype=np.float64)[:, None]; f = np.arange(N, dtype=np.float64)[None, :]
    ang = 2.0 * np.pi * (f + 0.5) * t / (2.0 * N)
    Cm = np.cos(ang).astype(np.float32); Sm = np.sin(ang).astype(np.float32)
    bands = 16
    tt = np.arange(N, dtype=np.float32)
    t01 = np.linspace(0.0, 1.0, N, dtype=np.float32)[:, None]
    a2 = (np.float32(2.0 * np.pi) * tt / np.float32(N))[:, None] * np.linspace(1e-4, bands - 1, bands, dtype=np.float32)
    z = np.concatenate([t01, np.cos(a2), -np.sin(a2)], -1).astype(np.float32)
    max_decay = np.log(1e-2) / 0.3; min_decay = np.log(1e-2) / 1.5
    deltas = np.abs(np.linspace(min_decay, max_decay, 512, dtype=np.float32))
    decay = np.exp(-t01 * deltas).astype(np.float32)
    import ml_dtypes
    nch = N // 128

    def tiles(M):
        a = M.reshape(nch, 128, nch, 128)
        return np.ascontiguousarray(a.transpose(2, 1, 0, 3)).astype(ml_dtypes.bfloat16)
    return dict(c=tiles(Cm), s=tiles(Sm), ct=tiles(np.ascontiguousarray(Cm.T)), st=tiles(np.ascontiguousarray(Sm.T)),
                zT=np.ascontiguousarray(z.T), decay=decay)

def s5_masks():
    m = np.zeros((2, 2, 128, 256), np.float32)
    for a in range(2):
        for il in range(8):
            i = a * 8 + il
            for j in range(16):
                if j >= i:
                    m[0, a, il * 16:(il + 1) * 16, j * 16:(j + 1) * 16] = 1.0
                if j <= i:
                    m[1, a, il * 16:(il + 1) * 16, j * 16:(j + 1) * 16] = 1.0
    return m

import math
import numpy as np

D = 1024
KC = 8
NCTX = 256
NLAT = 2048
T = NCTX + NLAT
FH = 2816
FHC = FH // 128
EPS = 1e-6
N_IN = 8256


class K:
    pass


def declare_inputs(P, nl):
    nc = P.nc
    I = {}

    def inp(name, shape):
        I[name] = nc.dram_tensor(name, list(shape), F32, kind="ExternalInput").ap()

    inp("xT", [D, T])
    inp("cvec", [D, 2])
    inp("ident", [128, 128])
    inp("w_mod", [nl, D, 9 * D])
    inp("b_mod", [nl, 9 * D])
    inp("norm_g", [nl, 6, D])
    inp("ffn_w_in", [nl, 2, D, 2 * FH])
    inp("ffn_w_out", [nl, 2, FH, D])
    return I


def setup_common(P, k):
    k.ident = P.sbuf("ident", [128, 128], F32)
    P.dma(k.ident[:], k.I["ident"])
    k.ident_bf = P.sbuf("ident_bf", [128, 128], BF16)
    P.copy(k.ident_bf[:], k.ident[:])
    k.ones_bf = P.sbuf("ones_bf", [128, 128], BF16)
    P.memset(k.ones_bf[:], 1.0)
    k.eps_col = P.sbuf("eps_col", [128, 1], F32)
    P.memset(k.eps_col[:], EPS)
    k.xres = P.nc.dram_tensor("xres", [128, KC, T], F32, kind="Internal").ap()
    k.ps = [P.psum("psb%d" % i) for i in range(8)]
    cv = P.sbuf("cv", [128, KC, 2], F32)
    P.dma(cv[:], k.I["cvec"].rearrange("(c p) n -> p c n", p=128))
    k.actv = P.sbuf("actv", [128, KC, 2], BF16)
    P.act(k.actv[:], cv[:], AF.Silu)
    k.modT = P.sbuf("modT", [128, 72, 2], F32)
    k.normg = P.sbuf("normg", [128, 48], F32)
    k.Asc = P.sbuf("Asc", [128, 3, KC, 2], F32)
    k.Bsh = P.sbuf("Bsh", [128, 3, KC, 2], F32)
    k.Gg = P.sbuf("Gg", [128, 3, KC, 2], F32)


def layer_mods(P, k, l):
    nc = P.nc
    wm = k.I["w_mod"][l].rearrange("(c p) n -> p c n", p=128)
    bm_t = P.sbuf("bm_t", [72, 128], F32)
    P.dma(bm_t[:], k.I["b_mod"][l].rearrange("(m f) -> m f", f=128))
    ng_t = P.sbuf("ng_t", [48, 128], F32)
    P.dma(ng_t[:], k.I["norm_g"][l].rearrange("g (c f) -> (g c) f", f=128))
    ps_m = k.ps[0]
    ps_t = k.ps[1]
    wt = [P.sbuf("wmod%d" % i, [128, KC, 512], BF16) for i in range(2)]
    for j in range(18):
        w = wt[j % 2]
        P.dma(w[:], wm[:, :, j * 512:(j + 1) * 512], q="pool")
        for mm in range(4):
            m = j * 4 + mm
            for c in range(KC):
                P.mm(ps_m[:, 2 * m:2 * m + 2], w[:, c, mm * 128:(mm + 1) * 128], k.actv[:, c, :],
                     start=(c == 0), stop=(c == KC - 1))
    P.transpose(ps_t[:, 0:72], bm_t[:], k.ident[0:72, 0:72])
    bmT = P.sbuf("bmT", [128, 72], F32)
    P.copy(bmT[:], ps_t[:, 0:72])
    P.tt(k.modT[:], ps_m[:, 0:144].rearrange("p (m s) -> p m s", s=2),
         bmT[:].unsqueeze(2).broadcast_to([128, 72, 2]), ALU.add)
    P.transpose(ps_t[:, 128:176], ng_t[:], k.ident[0:48, 0:48])
    P.copy(k.normg[:], ps_t[:, 128:176])
    for s in range(3):
        base = 3 * s
        gpre = k.normg[:, (2 * s) * 8:(2 * s + 1) * 8].unsqueeze(2).broadcast_to([128, KC, 2])
        gpost = k.normg[:, (2 * s + 1) * 8:(2 * s + 2) * 8].unsqueeze(2).broadcast_to([128, KC, 2])
        P.stt(k.Asc[:, s], k.modT[:, (base + 1) * 8:(base + 2) * 8, :], 1.0, gpre, ALU.add, ALU.mult)
        P.copy(k.Bsh[:, s], k.modT[:, base * 8:(base + 1) * 8, :])
        P.stt(k.Gg[:, s], k.modT[:, (base + 2) * 8:(base + 3) * 8, :], (1.0 if s == 1 else 0.5), gpost, ALU.mult, ALU.mult)


def sumsq_rstd(P, k, src_fn, nchunks, subs, rstd, ps_ss, sq_tiles, inv_n):
    for (o, w) in subs:
        for c in range(nchunks):
            sq = sq_tiles[c % len(sq_tiles)]
            P.act(sq[:, :w], src_fn(c, o, w), AF.Square)
            P.mm(ps_ss[:, :w], k.ones_bf[:], sq[:, :w], start=(c == 0), stop=(c == nchunks - 1))
        P.act(rstd[:, o:o + w], ps_ss[:, :w], AF.Sqrt, bias=k.eps_col[:], scale=inv_n)
        P.recip(rstd[:, o:o + w], rstd[:, o:o + w])


def ffn_sublayer(P, k, l, s, blocks):
    fi = s // 2
    w_in = k.I["ffn_w_in"][l, fi].rearrange("(c p) n -> p c n", p=128)
    w_out = k.I["ffn_w_out"][l, fi].rearrange("(j p) n -> p j n", p=128)
    maxw = max(sum(w for (_, w, _) in b) for b in blocks)
    hbs = [P.sbuf("ffn_h%d" % i, [128, KC, maxw], BF16) for i in range(2)]
    gb = P.sbuf("ffn_g", [128, FHC, maxw], BF16)
    ob = P.sbuf("ffn_o", [128, KC, maxw], BF16)
    xs = [P.sbuf("ffn_x%d" % i, [128, KC, 512], F32) for i in range(2)]
    rstd = [P.sbuf("ffn_rstd%d" % i, [128, 512], F32) for i in range(2)]
    tmp = [P.sbuf("ffn_tmp%d" % i, [128, 512], F32) for i in range(2)]
    sq = [P.sbuf("ffn_sq%d" % i, [128, 512], BF16) for i in range(2)]
    sl = [P.sbuf("ffn_sl%d" % i, [128, 512], F32) for i in range(2)]
    wi = [P.sbuf("ffn_wi%d" % i, [128, KC, 1024], BF16) for i in range(2)]
    wo = [P.sbuf("ffn_wo%d" % i, [128, FHC, 128], BF16) for i in range(2)]
    ps_ss = k.ps[0]
    ps_a = [k.ps[1], k.ps[2]]
    ps_b = [k.ps[3], k.ps[4]]
    ps_o = [k.ps[5], k.ps[6]]
    st_ = {"cnt": 0, "x": 0, "w": 0}

    def subs_of(blk):
        subs = []
        o = 0
        for (c0, w, st) in blk:
            subs.append((o, c0, w, st))
            o += w
        return subs

    def eng3(c):
        return "dve"

    def prenorm(blk, hb):
        for (o, c0, w, st) in subs_of(blk):
            xb = xs[st_["x"] % 2]
            rs = rstd[st_["x"] % 2]
            st_["x"] += 1
            P.dma(xb[:, :, :w], k.xres[:, :, c0:c0 + w])
            sumsq_rstd(P, k, lambda c, oo, ww, xb=xb: xb[:, c, oo:oo + ww], KC, [(0, w)], rs, ps_ss, sq, 1.0 / D)
            for c in range(KC):
                t = tmp[c % 2]
                P.tt(t[:, :w], xb[:, c, :w], rs[:, :w], ALU.mult, e=eng3(c))
                P.ts(hb[:, c, o:o + w], t[:, :w], k.Asc[:, s, c, st:st + 1], k.Bsh[:, s, c, st:st + 1],
                     op0=ALU.mult, op1=ALU.add)

    def hidden(blk, hb):
        subs = subs_of(blk)
        for j4 in range((FHC + 3) // 4):
            w = wi[st_["w"] % 2]
            st_["w"] += 1
            nj = min(4, FHC - 4 * j4)
            P.dma(w[:, :, 0:nj * 128], w_in[:, :, j4 * 512:j4 * 512 + nj * 128], q="pool")
            P.dma(w[:, :, 512:512 + nj * 128], w_in[:, :, FH + j4 * 512:FH + j4 * 512 + nj * 128], q="pool")
            for jj in range(nj):
                j = 4 * j4 + jj
                for (o, c0, ww, st) in subs:
                    cnt = st_["cnt"]
                    pa = ps_a[cnt % 2]
                    pb = ps_b[cnt % 2]
                    slt = sl[cnt % 2]
                    st_["cnt"] += 1
                    for c in range(KC):
                        P.mm(pa[:, :ww], w[:, c, jj * 128:(jj + 1) * 128], hb[:, c, o:o + ww],
                             start=(c == 0), stop=(c == KC - 1))
                    for c in range(KC):
                        P.mm(pb[:, :ww], w[:, c, 512 + jj * 128:512 + (jj + 1) * 128], hb[:, c, o:o + ww],
                             start=(c == 0), stop=(c == KC - 1))
                    P.act(slt[:, :ww], pa[:, :ww], AF.Silu)
                    P.tt(gb[:, j, o:o + ww], slt[:, :ww], pb[:, :ww], ALU.mult)

    def outproj(blk):
        subs = subs_of(blk)
        for c in range(KC):
            w = wo[c % 2]
            P.dma(w[:], w_out[:, :, c * 128:(c + 1) * 128], q="pool")
            for (o, c0, ww, st) in subs:
                po = ps_o[st_["cnt"] % 2]
                st_["cnt"] += 1
                for j in range(FHC):
                    P.mm(po[:, :ww], w[:, j, :], gb[:, j, o:o + ww], start=(j == 0), stop=(j == FHC - 1))
                P.copy(ob[:, c, o:o + ww], po[:, :ww], e="act")

    def postnorm(blk):
        subs = subs_of(blk)
        base = st_["x"]
        st_["x"] += len(subs)
        P.dma(xs[base % 2][:, :, :subs[0][2]], k.xres[:, :, subs[0][1]:subs[0][1] + subs[0][2]])
        for si, (o, c0, w, st) in enumerate(subs):
            xb = xs[(base + si) % 2]
            rs = rstd[(base + si) % 2]
            if si + 1 < len(subs):
                (o2, c2, w2, st2) = subs[si + 1]
                P.dma(xs[(base + si + 1) % 2][:, :, :w2], k.xres[:, :, c2:c2 + w2])
            sumsq_rstd(P, k, lambda c, oo, ww, o=o: ob[:, c, o + oo:o + oo + ww], KC, [(0, w)], rs, ps_ss, sq, 1.0 / D)
            for c in range(KC):
                t = tmp[c % 2]
                P.stt(t[:, :w], ob[:, c, o:o + w], k.Gg[:, s, c, st:st + 1], rs[:, :w], ALU.mult, ALU.mult)
                P.tt(xb[:, c, :w], xb[:, c, :w], t[:, :w], ALU.add, e=eng3(c))
            P.dma(k.xres[:, :, c0:c0 + w], xb[:, :, :w], q="sp")

    prenorm(blocks[0], hbs[0])
    for bi, blk in enumerate(blocks):
        hidden(blk, hbs[bi % 2])
        if bi + 1 < len(blocks):
            prenorm(blocks[bi + 1], hbs[(bi + 1) % 2])
        outproj(blk)
        postnorm(blk)


FULL_BLOCKS = [
    [(0, 256, 1), (256, 512, 0), (768, 384, 0)],
    [(1152, 384, 0), (1536, 384, 0), (1920, 384, 0)],
]
LAT_BLOCKS = [
    [(256, 512, 0), (768, 512, 0)],
    [(1280, 512, 0), (1792, 512, 0)],
]


CT0 = 2
LT0 = 260
HW = 2310
ALLSUBS = [(0, 256, 1), (256, 512, 0), (768, 512, 0), (1280, 512, 0), (1792, 512, 0)]
LATSUBS = ALLSUBS[1:]
C_CKV, C_KR, C_NK, C_NV, C_U, C_CQ, C_NQ, C_HY, C_GT = 0, 256, 320, 832, 1344, 1856, 2112, 2624, 4160


def hcol(xc):
    return xc + CT0 if xc < NCTX else xc - NCTX + LT0


def declare_mixer_inputs(P, I, nl):
    nc = P.nc

    def inp(name, shape):
        I[name] = nc.dram_tensor(name, list(shape), F32, kind="ExternalInput").ap()
    inp("w_in", [nl, D, N_IN])
    inp("mla_g_q", [nl, 256]); inp("mla_g_kv", [nl, 256])
    inp("mla_w_uq", [nl, 256, 4, 192]); inp("mla_w_ukv", [nl, 256, 4, 256])
    inp("w_branch", [nl, 4, 512, D]); inp("w_out", [nl, D, D])
    inp("rope_c", [64, NLAT]); inp("rope_s", [64, NLAT])


def mixer_setup(P, k):
    nc = P.nc
    k.br = [nc.dram_tensor("br%d" % n, [512, T], BF16, kind="Internal").ap() for n in range(4)]
    k.hmx = nc.dram_tensor("hmx", [128, KC, HW], BF16, kind="Internal").ap()


def alloc_hmix(P, k):
    k.hmix = P.sbuf("hmix", [128, KC, HW], BF16)
    for c0 in (0, 258, 2308):
        P.memset(k.hmix[:, :, c0:c0 + 2], 0.0)


def mixer_modnorm(P, k):
    with P.scope():
        rstd = P.sbuf("mn_rstd", [128, 512], F32)
        tmp = [P.sbuf("mn_tmp%d" % i, [128, 512], F32) for i in range(2)]
        sq = [P.sbuf("mn_sq%d" % i, [128, 512], BF16) for i in range(2)]
        xt = [P.sbuf("mn_x%d" % i, [128, KC, 512], F32) for i in range(2)]
        for si, (c0, w, st) in enumerate(ALLSUBS):
            xb = xt[si % 2]
            P.dma(xb[:, :, :w], k.xres[:, :, c0:c0 + w])
            sumsq_rstd(P, k, lambda c, oo, ww, xb=xb: xb[:, c, oo:oo + ww], KC, [(0, w)], rstd, k.ps[0], sq, 1.0 / D)
            h0 = hcol(c0)
            for c in range(KC):
                t = tmp[c % 2]
                P.tt(t[:, :w], xb[:, c, 0:w], rstd[:, 0:w], ALU.mult, e=("pool" if c % 3 == 2 else "dve"))
                P.ts(k.hmix[:, c, h0:h0 + w], t[:, :w], k.Asc[:, 1, c, st:st + 1], k.Bsh[:, 1, c, st:st + 1],
                     op0=ALU.mult, op1=ALU.add)
    P.dma(k.hmx, k.hmix[:], q="sp")


def load_col_vec(P, dst, src_1d, nchunk):
    P.dma(dst, src_1d.rearrange("(c p) -> p c", p=128), allow_slow_non_contiguous=True)


def mla_branch(P, k, l, ctx_out):
    nc = P.nc
    I = k.I
    w_in = I["w_in"][l].rearrange("(c p) n -> p c n", p=128)
    SC = 192.0 ** -0.5
    subs = ALLSUBS
    qsubs = ALLSUBS if ctx_out else LATSUBS
    with P.scope():
        wckv = P.sbuf("wckv", [128, KC, 256], BF16)
        P.dma(wckv[:], w_in[:, :, C_CKV:C_CKV + 256], q="pool")
        wkr = P.sbuf("wkr", [128, KC, 128], BF16)
        P.dma(wkr[:, :, 0:64], w_in[:, :, C_KR:C_KR + 64], q="pool")
        for a in range(2):
            for hf in range(2):
                P.dma(wkr[:, :, 64 + a * 32 + hf * 16:64 + a * 32 + hf * 16 + 16],
                      w_in[:, :, C_KR + a * 32 + (1 - hf) * 16:C_KR + a * 32 + (1 - hf) * 16 + 16], q="pool")
        wcq = P.sbuf("wcq", [128, KC, 256], BF16)
        P.dma(wcq[:], w_in[:, :, C_CQ:C_CQ + 256], q="pool")
        wukv = P.sbuf("wukv", [128, 2, 4, 256], BF16)
        P.dma(wukv[:], I["mla_w_ukv"][l].rearrange("(c p) h e -> p c h e", p=128), q="pool")
        wuq = P.sbuf("wuq", [128, 2, 4, 192], BF16)
        P.dma(wuq[:], I["mla_w_uq"][l].rearrange("(c p) h e -> p c h e", p=128), q="pool")
        wuqs = P.sbuf("wuqs", [128, 2, 4, 64], BF16)
        uq_r = I["mla_w_uq"][l].rearrange("(c p) h e -> p c h e", p=128)
        for a in range(2):
            for hf in range(2):
                for c in range(2):
                    P.dma(wuqs[:, c, :, a * 32 + hf * 16:a * 32 + hf * 16 + 16],
                          uq_r[:, c, :, 128 + a * 32 + (1 - hf) * 16:128 + a * 32 + (1 - hf) * 16 + 16], q="pool")
        gkv = P.sbuf("gkv", [128, 2], F32)
        load_col_vec(P, gkv[:], I["mla_g_kv"][l], 2)
        gq = P.sbuf("gq", [128, 2], F32)
        load_col_vec(P, gq[:], I["mla_g_q"][l], 2)
        ropc_t = [P.sbuf("ropc%d" % i, [64, 512], F32) for i in range(2)]
        rops_t = [P.sbuf("rops%d" % i, [64, 512], F32) for i in range(2)]
        rcnt = [0]

        def rope_tabs(l0, w):
            i = rcnt[0] % 2
            rcnt[0] += 1
            P.dma(ropc_t[i][:, :w], I["rope_c"][:, l0:l0 + w])
            P.dma(rops_t[i][:, :w], I["rope_s"][:, l0:l0 + w])
            return ropc_t[i], rops_t[i]
        nkv = P.sbuf("nkv", [128, 2, T], BF16)
        nq = P.sbuf("nq", [128, 2, T], BF16)
        krope = P.sbuf("krope", [64, T], BF16)
        vall = P.sbuf("vall", [128, 18, 128], BF16)
        aT = P.sbuf("aT", [128, T], BF16)
        raw = P.sbuf("raw", [128, 2, 512], F32)
        rstd = P.sbuf("rstd", [128, 512], F32)
        sq = [P.sbuf("sq%d" % i, [128, 512], BF16) for i in range(2)]
        t1 = P.sbuf("t1", [128, 512], F32)
        t2 = P.sbuf("t2", [128, 512], F32)
        ps = k.ps

        def lowrank_norm(wt, gvec, dst):
            for (c0, w, st) in (subs if dst is nkv else qsubs):
                h0 = hcol(c0)
                for m in range(2):
                    for c in range(KC):
                        P.mm(ps[1 + m][:, :w], wt[:, c, m * 128:(m + 1) * 128], k.hmix[:, c, h0:h0 + w],
                             start=(c == 0), stop=(c == KC - 1))
                    P.copy(raw[:, m, :w], ps[1 + m][:, :w], e="act")
                sumsq_rstd(P, k, lambda c, oo, ww: raw[:, c, oo:oo + ww], 2, [(0, w)], rstd, ps[0], sq, 1.0 / 256)
                for m in range(2):
                    P.stt(dst[:, m, c0:c0 + w], raw[:, m, :w], gvec[:, m:m + 1], rstd[:, :w], ALU.mult, ALU.mult)

        lowrank_norm(wckv, gkv, nkv)
        lowrank_norm(wcq, gq, nq)
        for (c0, w, st) in subs:
            h0 = hcol(c0)
            for hh in range(2):
                for c in range(KC):
                    P.mm(ps[1 + hh][0:64, :w], wkr[:, c, hh * 64:(hh + 1) * 64], k.hmix[:, c, h0:h0 + w],
                         start=(c == 0), stop=(c == KC - 1))
            if st == 1:
                P.copy(krope[:, c0:c0 + w], ps[1][0:64, :w], e="act")
            else:
                l0 = c0 - NCTX
                rc, rs = rope_tabs(l0, w)
                P.tt(t1[0:64, :w], ps[1][0:64, :w], rc[:, :w], ALU.mult)
                P.tt(t2[0:64, :w], ps[2][0:64, :w], rs[:, :w], ALU.mult)
                P.tt(krope[:, c0:c0 + w], t1[0:64, :w], t2[0:64, :w], ALU.add, e="pool")
        knT = P.sbuf("knT", [128, T], BF16)
        qnT = P.sbuf("qnT", [128, T], BF16)
        qrope = P.sbuf("qrope", [64, T], BF16)
        pT = [P.sbuf("pT%d" % i, [128, 512], BF16) for i in range(3)]
        rden = P.sbuf("rden", [128, 512], F32)
        for hd in range(4):
            for tc in range(18):
                pv = ps[1 + tc % 2]
                for c in range(2):
                    P.mm(pv[:, 0:128], nkv[:, c, tc * 128:(tc + 1) * 128], wukv[:, c, hd, 128:256], start=(c == 0), stop=(c == 1))
                P.copy(vall[:, tc, :], pv[:, 0:128], e=("act" if tc % 2 else "dve"))
            for (c0, w, st) in subs:
                for c in range(2):
                    P.mm(ps[1][:, :w], wukv[:, c, hd, 0:128], nkv[:, c, c0:c0 + w], start=(c == 0), stop=(c == 1))
                P.copy(knT[:, c0:c0 + w], ps[1][:, :w], e="act")
            for (c0, w, st) in qsubs:
                for c in range(2):
                    P.mm(ps[1][:, :w], wuq[:, c, hd, 0:128], nq[:, c, c0:c0 + w], start=(c == 0), stop=(c == 1))
                P.copy(qnT[:, c0:c0 + w], ps[1][:, :w], e="act")
                for c in range(2):
                    P.mm(ps[2][0:64, :w], wuq[:, c, hd, 128:192], nq[:, c, c0:c0 + w], start=(c == 0), stop=(c == 1))
                if st == 1:
                    P.copy(qrope[:, c0:c0 + w], ps[2][0:64, :w], e="dve")
                else:
                    for c in range(2):
                        P.mm(ps[3][0:64, :w], wuqs[:, c, hd, :], nq[:, c, c0:c0 + w], start=(c == 0), stop=(c == 1))
                    l0 = c0 - NCTX
                    rc, rs = rope_tabs(l0, w)
                    P.tt(t1[0:64, :w], ps[2][0:64, :w], rc[:, :w], ALU.mult)
                    P.tt(t2[0:64, :w], ps[3][0:64, :w], rs[:, :w], ALU.mult)
                    P.tt(qrope[:, c0:c0 + w], t1[0:64, :w], t2[0:64, :w], ALU.add, e="pool")
            cnt = 0
            for qi, (c0, w, st) in enumerate(qsubs):
                kcs = list(range(2)) if st == 1 else list(range(18))
                pso, psd = (ps[6], ps[7]) if qi % 2 == 0 else (ps[2], ps[3])
                base = cnt
                cnt += len(kcs)

                def emit_s(i):
                    kc = kcs[i]
                    pss = ps[4 + (base + i) % 2]
                    P.mm(pss[:, :w], knT[:, kc * 128:(kc + 1) * 128], qnT[:, c0:c0 + w], start=True, stop=False)
                    P.mm(pss[:, :w], krope[:, kc * 128:(kc + 1) * 128], qrope[:, c0:c0 + w], start=False, stop=True)
                emit_s(0)
                for i, kc in enumerate(kcs):
                    if i + 1 < len(kcs):
                        emit_s(i + 1)
                    pss = ps[4 + (base + i) % 2]
                    p = pT[(base + i) % 3]
                    P.act(p[:, :w], pss[:, :w], AF.Exp, scale=SC)
                    P.mm(pso[:, :w], vall[:, kc, :], p[:, :w], start=(i == 0), stop=(i == len(kcs) - 1))
                    P.mm(psd[:, :w], k.ones_bf[:], p[:, :w], start=(i == 0), stop=(i == len(kcs) - 1))
                P.recip(rden[:, :w], psd[:, :w])
                P.tt(aT[:, c0:c0 + w], pso[:, :w], rden[:, :w], ALU.mult)
            cols0 = 0 if ctx_out else NCTX
            P.dma(k.br[0][hd * 128:(hd + 1) * 128, cols0:T], aT[:, cols0:T], q="sp")


def declare_na_inputs(P, I, nl):
    nc = P.nc

    def inp(name, shape):
        I[name] = nc.dram_tensor(name, list(shape), F32, kind="ExternalInput").ap()
    inp("na_G", [nl, 128, 8 * 16 * 64])
    inp("na_mv", [2, 16 * 128]); inp("na_sel", [2, 128]); inp("na_cm", [128, 64])


def na_branch(P, k, l, ctx_out, geo):
    nc = P.nc
    I = k.I
    w_in = I["w_in"][l].rearrange("(c p) n -> p c n", p=128)
    ps = k.ps
    with P.scope():
        BP = P.sbuf("na_BP", [128, 8, 16, 64], BF16)
        cm = P.sbuf("na_cm", [128, 64], F32)
        P.dma(cm[:], I["na_cm"])
        gt = [P.sbuf("na_gt%d" % i, [128, 16, 64], F32) for i in range(2)]
        Gr = I["na_G"][l].rearrange("p (h i q) -> p h i q", h=8, i=16)
        for h in range(8):
            P.dma(gt[h % 2][:], Gr[:, h])
            P.tt(BP[:, h], gt[h % 2][:], cm[:].unsqueeze(1).broadcast_to([128, 16, 64]), ALU.add)
        mvf = P.sbuf("na_mvf", [2, 16 * 128], F32)
        P.dma(mvf[:], I["na_mv"])
        mv = P.sbuf("na_mv", [2, 16 * 128], BF16)
        P.copy(mv[:], mvf[:])
        self_f = P.sbuf("na_self", [2, 128], F32)
        P.dma(self_f[:], I["na_sel"])
        sel = P.sbuf("na_sel", [2, 128], BF16)
        P.copy(sel[:], self_f[:])
        wk = P.sbuf("na_wk", [128, KC, 128], BF16)
        wq = P.sbuf("na_wq", [128, KC, 128], BF16)
        wv = P.sbuf("na_wv", [128, KC, 128], BF16)
        KT = P.sbuf("na_KT", [128, T], BF16)
        QT = P.sbuf("na_QT", [128, T], BF16)
        V = P.sbuf("na_V", [128, 18, 128], BF16)
        dT = P.sbuf("na_dT", [128, T], BF16)
        pT = [P.sbuf("na_pT%d" % i, [128, 128], BF16) for i in range(3)]
        rden = P.sbuf("na_rden", [128, 128], F32)
        qsubs = ALLSUBS if ctx_out else LATSUBS
        cnt = 0
        for hp in range(4):
            P.dma(wk[:], w_in[:, :, C_NK + hp * 128:C_NK + (hp + 1) * 128], q="pool")
            P.dma(wq[:], w_in[:, :, C_NQ + hp * 128:C_NQ + (hp + 1) * 128], q="pool")
            P.dma(wv[:], w_in[:, :, C_NV + hp * 128:C_NV + (hp + 1) * 128], q="pool")
            for (c0, w, st) in ALLSUBS:
                h0 = hcol(c0)
                for c in range(KC):
                    P.mm(ps[1][:, :w], wk[:, c, :], k.hmix[:, c, h0:h0 + w], start=(c == 0), stop=(c == KC - 1))
                P.copy(KT[:, c0:c0 + w], ps[1][:, :w], e="act")
            for (c0, w, st) in qsubs:
                h0 = hcol(c0)
                for c in range(KC):
                    P.mm(ps[2][:, :w], wq[:, c, :], k.hmix[:, c, h0:h0 + w], start=(c == 0), stop=(c == KC - 1))
                P.ts(QT[:, c0:c0 + w], ps[2][:, :w], 0.125, None, op0=ALU.mult)
            for tc in range(18):
                h0 = hcol(tc * 128)
                pv = ps[1 + tc % 2]
                for c in range(KC):
                    P.mm(pv[:, 0:128], k.hmix[:, c, h0:h0 + 128], wv[:, c, :], start=(c == 0), stop=(c == KC - 1))
                P.copy(V[:, tc, :], pv[:, 0:128], e=("act" if tc % 2 else "dve"))
            qblocks = []
            if ctx_out:
                qblocks += [(0, []), (128, [])]
            for i in range(16):
                qblocks.append((NCTX + i * 128, geo[i]))
            for qi, (q0, loc) in enumerate(qblocks):
                chunks = [(0, None, None), (1, None, None)] + [(2 + j, idxp, combo) for (j, idxp, combo) in loc]
                pso, psd = (ps[6], ps[7]) if qi % 2 == 0 else (ps[2], ps[3])
                work = [(hh, ci) for hh in range(2) for ci in range(len(chunks))]
                base = cnt
                cnt += len(work)

                def emit_s(wi):
                    hh, ci = work[wi]
                    kc, idxp, combo = chunks[ci]
                    h = 2 * hp + hh
                    pr = slice(hh * 64, (hh + 1) * 64)
                    pss = ps[4 + (base + wi) % 2]
                    last_s = (idxp is None)
                    P.mm(pss[:, 0:128], KT[pr, kc * 128:(kc + 1) * 128], QT[pr, q0:q0 + 128], start=True, stop=last_s)
                    if idxp is not None:
                        need_mask = combo != 0
                        P.mm(pss[:, 0:128], k.ident_bf[:], BP[:, h, idxp:idxp + 2, :], start=False, stop=not need_mask)
                        if need_mask:
                            P.mm(pss[:, 0:128], mv[0:2, combo * 128:(combo + 1) * 128], sel[0:2, :], start=False, stop=True)
                emit_s(0)
                for wi, (hh, ci) in enumerate(work):
                    if wi + 1 < len(work):
                        emit_s(wi + 1)
                    kc = chunks[ci][0]
                    pr = slice(hh * 64, (hh + 1) * 64)
                    pss = ps[4 + (base + wi) % 2]
                    p = pT[(base + wi) % 3]
                    P.act(p[:], pss[:, 0:128], AF.Exp)
                    P.mm(pso[pr, 0:128], V[:, kc, pr], p[:], start=(ci == 0), stop=(ci == len(chunks) - 1))
                    P.mm(psd[pr, 0:128], k.ones_bf[:, 0:64], p[:], start=(ci == 0), stop=(ci == len(chunks) - 1))
                P.recip(rden[:], psd[:, 0:128])
                P.tt(dT[:, q0:q0 + 128], pso[:, 0:128], rden[:], ALU.mult)
            cols0 = 0 if ctx_out else NCTX
            P.dma(k.br[3][hp * 128:(hp + 1) * 128, cols0:T], dT[:, cols0:T], q="sp")


def post_norm_residual(P, k, ob, s, subs_local, rstd, ps_ss, sq, tmp, xb):
    for (o, c0, w, st) in subs_local:
        P.dma(xb[:, :, o:o + w], k.xres[:, :, c0:c0 + w])
        sumsq_rstd(P, k, lambda c, oo, ww: ob[:, c, oo:oo + ww], KC, [(o, w)], rstd, ps_ss, sq, 1.0 / D)
        for c in range(KC):
            t = tmp[c % 2]
            P.stt(t[:, :w], ob[:, c, o:o + w], k.Gg[:, s, c, st:st + 1], rstd[:, o:o + w], ALU.mult, ALU.mult)
            P.tt(xb[:, c, o:o + w], xb[:, c, o:o + w], t[:, :w], ALU.add, e=("pool" if c % 3 == 2 else "dve"))
        P.dma(k.xres[:, :, c0:c0 + w], xb[:, :, o:o + w], q="sp")


def merge_phase(P, k, l, ctx_out):
    I = k.I
    w_in = I["w_in"][l].rearrange("(c p) n -> p c n", p=128)
    ps = k.ps
    subs = ALLSUBS if ctx_out else LATSUBS
    with P.scope():
        wg = P.sbuf("mg_wg", [128, KC, 4 * D], BF16)
        wb = P.sbuf("mg_wb", [128, 4, 4, D], BF16)
        for half in range(2):
            for j in range(half, 8, 2):
                P.dma(wg[:, :, j * 512:(j + 1) * 512], w_in[:, :, C_GT + j * 512:C_GT + (j + 1) * 512], q="pool")
            for n in range(4):
                P.dma(wb[:, n, :, half * 512:(half + 1) * 512],
                      I["w_branch"][l, n].rearrange("(kk p) d -> p kk d", p=128)[:, :, half * 512:(half + 1) * 512], q="pool")
        wo = P.sbuf("mg_wo", [128, KC, D], BF16)
        for j in range(2):
            P.dma(wo[:, :, j * 512:(j + 1) * 512], I["w_out"][l].rearrange("(kk p) d -> p kk d", p=128)[:, :, j * 512:(j + 1) * 512], q="pool")
        brt = [[P.sbuf("mg_br%d_%d" % (n, i), [128, 4, 512], BF16) for n in range(4)] for i in range(1)]
        mt = P.sbuf("mg_mt", [128, KC, 512], BF16)
        ob = P.sbuf("mg_ob", [128, KC, 512], BF16)
        sg = [P.sbuf("mg_sg%d" % i, [128, 512], F32) for i in range(2)]
        acc = P.sbuf("mg_acc", [128, 512], F32)
        tm = [P.sbuf("mg_tm%d" % i, [128, 512], F32) for i in range(2)]
        rstd = P.sbuf("mg_rstd", [128, 512], F32)
        sq = [P.sbuf("mg_sq%d" % i, [128, 512], BF16) for i in range(2)]
        xbm = P.sbuf("mg_x", [128, KC, 512], F32)
        hbt = [P.sbuf("mg_h%d" % i, [128, KC, 512], BF16) for i in range(2)]
        cnt = 0
        for si, (c0, w, st) in enumerate(subs):
            h0 = hcol(c0)
            hb_ = hbt[si % 2]
            P.dma(hb_[:, :, :w], k.hmx[:, :, h0:h0 + w])
            bt = brt[0]
            for n in range(4):
                P.dma(bt[n][:, :, :w], k.br[n].rearrange("(c p) t -> p c t", p=128)[:, :, c0:c0 + w])
            for dc in range(KC):
                for n in range(4):
                    pg = ps[1 + cnt % 2]
                    pp = ps[3 + cnt % 2]
                    s_ = sg[cnt % 2]
                    cnt += 1
                    col = n * D + dc * 128
                    for c in range(KC):
                        P.mm(pg[:, :w], wg[:, c, col:col + 128], hb_[:, c, :w], start=(c == 0), stop=(c == KC - 1))
                    for kk in range(4):
                        P.mm(pp[:, :w], wb[:, n, kk, dc * 128:(dc + 1) * 128], bt[n][:, kk, :w], start=(kk == 0), stop=(kk == 3))
                    P.act(s_[:, :w], pg[:, :w], AF.Sigmoid)
                    if n == 0:
                        P.tt(acc[:, :w], s_[:, :w], pp[:, :w], ALU.mult)
                    else:
                        t = tm[n % 2]
                        P.tt(t[:, :w], s_[:, :w], pp[:, :w], ALU.mult)
                        if n < 3:
                            P.tt(acc[:, :w], acc[:, :w], t[:, :w], ALU.add, e="pool")
                        else:
                            P.tt(mt[:, dc, :w], acc[:, :w], t[:, :w], ALU.add, e="pool")
            for dc in range(KC):
                po = ps[5 + dc % 2]
                for kk in range(KC):
                    P.mm(po[:, :w], wo[:, kk, dc * 128:(dc + 1) * 128], mt[:, kk, :w], start=(kk == 0), stop=(kk == KC - 1))
                P.copy(ob[:, dc, :w], po[:, :w], e="act")
            post_norm_residual(P, k, ob, 1, [(0, c0, w, st)], rstd, ps[0], sq, tm, xbm)


def declare_hyena_inputs(P, I, nl, with_ctx=True):
    nc = P.nc

    def inp(name, shape):
        I[name] = nc.dram_tensor(name, list(shape), F32, kind="ExternalInput").ap()
    inp("hy_conv_w", [nl, 3, 1536]); inp("hy_conv_b", [nl, 1536]); inp("hy_bias", [nl, 512])
    inp("hy_w1", [nl, 33, 64]); inp("hy_b1", [nl, 64]); inp("hy_freq1", [nl, 64])
    inp("hy_w2", [nl, 64, 64]); inp("hy_b2", [nl, 64]); inp("hy_freq2", [nl, 64]); inp("hy_w3", [nl, 64, 1024])
    for nm, N in (("lat", NLAT), ("ctx", NCTX)):
        if nm == "ctx" and not with_ctx:
            continue
        for t in ("c", "s", "ct", "st"):
            I["dft_%s_%s" % (t, nm)] = nc.dram_tensor("dft_%s_%s" % (t, nm), [N // 128, 128, N // 128, 128], BF16, kind="ExternalInput").ap()
        inp("hy_zT_%s" % nm, [33, N]); inp("hy_decay_%s" % nm, [N, 512])


PI = math.pi


def sin_reduced(P, dst, src, shp, tmp):
    P.ts(tmp, src, PI, -2.0 * PI, op0=ALU.is_gt, op1=ALU.mult)
    P.tt(src, src, tmp, ALU.add)
    P.ts(tmp, src, -PI, 2.0 * PI, op0=ALU.is_lt, op1=ALU.mult)
    P.tt(src, src, tmp, ALU.add)
    P.act(dst, src, AF.Sin)


def hyena_branch(P, k, l, seq):
    nc = P.nc
    I = k.I
    nm, N, hbase, xbase = seq
    NT = N // 128
    NF = NT
    CW = min(512, N)
    w_in = I["w_in"][l].rearrange("(c p) n -> p c n", p=128)
    ps = k.ps
    dft_c = I["dft_c_%s" % nm]
    dft_s = I["dft_s_%s" % nm]
    dft_ct = I["dft_ct_%s" % nm]
    dft_st = I["dft_st_%s" % nm]
    with P.scope():
        Kre = P.sbuf("hy_Kre", [128, NF, 512], BF16)
        Kim = P.sbuf("hy_Kim", [128, NF, 512], BF16)
        Ct = [P.sbuf("hy_Ct%d" % i, [128, NT, 128], BF16) for i in range(2)]
        St = [P.sbuf("hy_St%d" % i, [128, NT, 128], BF16) for i in range(2)]
        tA = P.sbuf("hy_tA", [128, 512], F32)
        tB = P.sbuf("hy_tB", [128, 512], F32)
        tC = P.sbuf("hy_tC", [128, 512], F32)
        tD = P.sbuf("hy_tD", [128, 512], F32)
        with P.scope():
            w1 = P.sbuf("hy_w1", [33, 64], F32); P.dma(w1[:], I["hy_w1"][l])
            w2 = P.sbuf("hy_w2", [64, 64], F32); P.dma(w2[:], I["hy_w2"][l])
            w3 = P.sbuf("hy_w3", [64, 1024], F32); P.dma(w3[:], I["hy_w3"][l])
            cols = P.sbuf("hy_cols", [64, 6], F32)
            for j, nm_ in enumerate(["hy_b1", "hy_freq1", "hy_b2", "hy_freq2"]):
                P.dma(cols[:, j:j + 1], I[nm_][l].rearrange("(p o) -> p o", o=1))
            P.tt(cols[:, 4:5], cols[:, 0:1], cols[:, 1:2], ALU.mult)
            P.tt(cols[:, 5:6], cols[:, 2:3], cols[:, 3:4], ALU.mult)
            zT = P.sbuf("hy_zT", [33, N], F32); P.dma(zT[:], I["hy_zT_%s" % nm])
            h1 = P.sbuf("hy_h1", [64, N], F32)
            h2 = P.sbuf("hy_h2", [64, N], F32)
            filt = P.sbuf("hy_filt", [128, NT, 1024], BF16)
            dec = [P.sbuf("hy_dec%d" % i, [128, 512], F32) for i in range(2)]
            for cb in range(N // CW):
                cs = slice(cb * CW, (cb + 1) * CW)
                P.mm(ps[1][0:64, :CW], w1[:, :], zT[:, cs], start=True, stop=True)
                P.act(tA[0:64, :CW], ps[1][0:64, :CW], AF.Identity, bias=cols[:, 4:5], scale=cols[:, 1:2])
                sin_reduced(P, h1[:, cs], tA[0:64, :CW], None, tB[0:64, :CW])
            for cb in range(N // CW):
                cs = slice(cb * CW, (cb + 1) * CW)
                P.mm(ps[1][0:64, :CW], w2[:, :], h1[:, cs], start=True, stop=True)
                P.act(tA[0:64, :CW], ps[1][0:64, :CW], AF.Identity, bias=cols[:, 5:6], scale=cols[:, 3:4])
                sin_reduced(P, h2[:, cs], tA[0:64, :CW], None, tB[0:64, :CW])
            for tc in range(NT):
                d = dec[tc % 2]
                P.dma(d[:], I["hy_decay_%s" % nm][tc * 128:(tc + 1) * 128, :])
                for hf in range(2):
                    pp = ps[1 + hf]
                    P.mm(pp[:, :], h2[:, tc * 128:(tc + 1) * 128], w3[:, hf * 512:(hf + 1) * 512], start=True, stop=True)
                    P.tt(filt[:, tc, hf * 512:(hf + 1) * 512], pp[:, :], d[:], ALU.mult)
            for fc in range(NF):
                c_ = Ct[fc % 2]; s_ = St[fc % 2]
                P.dma(c_[:], dft_c[fc])
                P.dma(s_[:], dft_s[fc])
                for bi, (mat, half) in enumerate([(c_, 0), (s_, 0), (c_, 1), (s_, 1)]):
                    for tc in range(NT):
                        P.mm(ps[1 + bi][:, :], mat[:, tc, :], filt[:, tc, half * 512:(half + 1) * 512],
                             start=(tc == 0), stop=(tc == NT - 1))
                P.copy(tA[:], ps[1][:, :], e="act")
                P.tt(Kre[:, fc, :], tA[:], ps[3][:, :], ALU.add)
                P.copy(tB[:], ps[2][:, :], e="act")
                P.tt(Kim[:, fc, :], tB[:], ps[4][:, :], ALU.subtract)
        s_bf = P.sbuf("hy_s", [128, NT, 512], BF16)
        x0_bf = P.sbuf("hy_x0", [128, NT, 512], BF16)
        with P.scope():
            wraw = P.sbuf("hy_wraw", [128, KC, 512], BF16)
            Wk = [P.sbuf("hy_Wk%d" % i, [128, KC, 512], BF16) for i in range(3)]
            cw = P.sbuf("hy_cw", [128, 512], F32)
            cbf = P.sbuf("hy_cbf", [1, 1536], F32)
            P.dma(cbf[:], I["hy_conv_b"][l].rearrange("(o n) -> o n", o=1))
            cbb = P.sbuf("hy_cbb", [1, 1536], BF16)
            P.copy(cbb[:], cbf[:])
            for blk in range(3):
                P.dma(wraw[:], w_in[:, :, C_HY + blk * 512:C_HY + (blk + 1) * 512], q="pool")
                for kk in range(3):
                    P.dma(cw[:], I["hy_conv_w"][l, kk, blk * 512:(blk + 1) * 512].partition_broadcast(128))
                    P.tt(Wk[kk][:], wraw[:], cw[:].unsqueeze(1).broadcast_to([128, KC, 512]), ALU.mult)
                for tc in range(NT):
                    pp = ps[1 + tc % 2]
                    for kk in range(3):
                        c0 = hbase + tc * 128 + kk - 1
                        for c in range(KC):
                            P.mm(pp[:, :], k.hmix[:, c, c0:c0 + 128], Wk[kk][:, c, :], start=(kk == 0 and c == 0), stop=False)
                    P.mm(pp[:, :], k.ones_bf[0:1, 0:128], cbb[0:1, blk * 512:(blk + 1) * 512], start=False, stop=True)
                    if blk == 0:
                        P.copy(s_bf[:, tc, :], pp[:, :], e="act")
                    elif blk == 1:
                        P.tt(s_bf[:, tc, :], s_bf[:, tc, :], pp[:, :], ALU.mult)
                    else:
                        P.copy(x0_bf[:, tc, :], pp[:, :], e="act")
        Yre = P.sbuf("hy_Yre", [128, NF, 512], BF16)
        Yim = P.sbuf("hy_Yim", [128, NF, 512], BF16)
        for fc in range(NF):
            c_ = Ct[fc % 2]; s_ = St[fc % 2]
            P.dma(c_[:], dft_c[fc])
            P.dma(s_[:], dft_s[fc])
            pa = ps[1 + 2 * (fc % 2)]; pb = ps[2 + 2 * (fc % 2)]
            for tc in range(NT):
                P.mm(pa[:, :], c_[:, tc, :], s_bf[:, tc, :], start=(tc == 0), stop=(tc == NT - 1))
            for tc in range(NT):
                P.mm(pb[:, :], s_[:, tc, :], s_bf[:, tc, :], start=(tc == 0), stop=(tc == NT - 1))
            P.copy(tA[:], pa[:, :], e="act")
            P.copy(tB[:], pb[:, :], e="act")
            P.tt(tC[:], tA[:], Kre[:, fc, :], ALU.mult)
            P.tt(tD[:], tB[:], Kim[:, fc, :], ALU.mult, e="pool")
            P.tt(Yre[:, fc, :], tC[:], tD[:], ALU.subtract)
            P.tt(tC[:], tA[:], Kim[:, fc, :], ALU.mult, e="pool")
            P.tt(tD[:], tB[:], Kre[:, fc, :], ALU.mult)
            P.tt(Yim[:, fc, :], tC[:], tD[:], ALU.add, e="pool")
        bd = P.sbuf("hy_bd", [128, 512], F32)
        P.dma(bd[:], I["hy_bias"][l].partition_broadcast(128))
        bT = P.sbuf("hy_bT", [128, 4, N], BF16)
        otm = [P.sbuf("hy_otm%d" % i, [128, 512], BF16) for i in range(2)]
        for tc in range(NT):
            c_ = Ct[tc % 2]; s_ = St[tc % 2]
            P.dma(c_[:], dft_ct[tc])
            P.dma(s_[:], dft_st[tc])
            pp = ps[1 + tc % 2]
            for fc in range(NF):
                P.mm(pp[:, :], c_[:, fc, :], Yre[:, fc, :], start=(fc == 0), stop=False)
            for fc in range(NF):
                P.mm(pp[:, :], s_[:, fc, :], Yim[:, fc, :], start=False, stop=(fc == NF - 1))
            P.tt(tA[:], s_bf[:, tc, :], bd[:], ALU.mult)
            P.stt(tB[:], pp[:, :], 1.0 / N, tA[:], ALU.mult, ALU.add)
            o = otm[tc % 2]
            P.tt(o[:], tB[:], x0_bf[:, tc, :], ALU.mult, e="pool")
            pt = ps[5 + tc % 2][:].bitcast(BF16)
            for j in range(4):
                P.transpose(pt[:, j * 128:(j + 1) * 128], o[:, j * 128:(j + 1) * 128], k.ident_bf[:])
            P.copy(bT[:, :, tc * 128:(tc + 1) * 128], pt[:, 0:512].rearrange("p (j t) -> p j t", j=4), e="act")
        P.dma(k.br[1].rearrange("(c p) t -> p c t", p=128)[:, :, xbase:xbase + N], bT[:], q="sp")


HY_LAT = ("lat", NLAT, LT0, NCTX)
HY_CTX = ("ctx", NCTX, CT0, 0)


NCH = 144


def declare_s5_inputs(P, I, nl):
    nc = P.nc

    def inp(name, shape):
        I[name] = nc.dram_tensor(name, list(shape), F32, kind="ExternalInput").ap()
    inp("s5_lam_re", [nl, 2, 2048]); inp("s5_lam_im", [nl, 2, 2048]); inp("s5_log_dt", [nl, 2, 32])
    inp("s5_b_re", [nl, 2, 2048, 16]); inp("s5_b_im", [nl, 2, 2048, 16])
    inp("s5_c_re", [nl, 2, 32, 16, 64]); inp("s5_c_im", [nl, 2, 32, 16, 64])
    inp("s5_d", [nl, 512]); inp("s5_w_glu", [nl, 512, 1024]); inp("s5_b_glu", [nl, 1024])
    inp("s5_mask", [2, 2, 128, 256])


def s5_setup(P, k):
    nc = P.nc
    k.u_tm = nc.dram_tensor("s5_u_tm", [T, 512], F32, kind="Internal").ap()
    k.y_tm = nc.dram_tensor("s5_y_tm", [T, 512], F32, kind="Internal").ap()


def s5_part1(P, k, l):
    I = k.I
    w_in = I["w_in"][l].rearrange("(c p) n -> p c n", p=128)
    ps = k.ps
    with P.scope():
        wu = P.sbuf("s5_wu", [128, KC, 512], BF16)
        P.dma(wu[:], w_in[:, :, C_U:C_U + 512], q="pool")
        ut = [P.sbuf("s5_ut%d" % i, [128, 512], F32) for i in range(2)]
        for tc in range(18):
            h0 = hcol(tc * 128)
            pp = ps[1 + tc % 2]
            for c in range(KC):
                P.mm(pp[:, :], k.hmix[:, c, h0:h0 + 128], wu[:, c, :], start=(c == 0), stop=(c == KC - 1))
            P.copy(ut[tc % 2][:], pp[:, :], e=("act" if tc % 2 else "dve"))
            P.dma(k.u_tm[tc * 128:(tc + 1) * 128, :], ut[tc % 2][:], q="sp")


def cmul(P, o_re, o_im, a_re, a_im, b_re, b_im, t1, t2, neg_im=False):
    P.tt(t1, a_re, b_re, ALU.mult)
    P.tt(t2, a_im, b_im, ALU.mult, e="pool")
    P.tt(o_re, t1, t2, ALU.subtract)
    P.tt(t1, a_re, b_im, ALU.mult)
    P.tt(t2, a_im, b_re, ALU.mult, e="pool")
    if neg_im:
        P.stt(o_im, t1, -1.0, t2, ALU.mult, ALU.subtract)
    else:
        P.tt(o_im, t1, t2, ALU.add)


def s5_part2(P, k, l, ctx_out):
    nc = P.nc
    I = k.I
    ps = k.ps
    with P.scope():
        U = P.sbuf("s5_U", [128, 32, 2, NCH], BF16)
        M = P.sbuf("s5_M", [128, 32, 2, 256], BF16)
        Qre = [P.sbuf("s5_Qre%d" % d, [128, 16, 256], BF16) for d in range(2)]
        nQim = [P.sbuf("s5_nQim%d" % d, [128, 16, 256], BF16) for d in range(2)]
        Xre = [P.sbuf("s5_Xre%d" % d, [128, 16, NCH], BF16) for d in range(2)]
        Xim = [P.sbuf("s5_Xim%d" % d, [128, 16, NCH], BF16) for d in range(2)]
        with P.scope():
            uc = P.sbuf("s5_uc", [128, 16 * 512], F32)
            uc2 = P.sbuf("s5_uc2", [128, 16 * 512], F32)
            ucv = uc[:].rearrange("p (j g h) -> p g j h", j=16, g=32)
            uc2v = uc2[:].rearrange("p (g j h) -> p g j h", g=32, j=16)
            for (part, np_, c_lo) in (("lat", 128, 16), ("ctx", 16, 0)):
                rows = k.u_tm[NCTX:T, :] if part == "lat" else k.u_tm[0:NCTX, :]
                P.dma(uc[0:np_, :], rows.rearrange("(c j) n -> c (j n)", j=16))
                for gi, eng in enumerate(("act", "dve", "act", "pool")):
                    gs = slice(gi * 8, (gi + 1) * 8)
                    P.copy(uc2v[0:np_, gs], ucv[0:np_, gs], e=eng)
                cnt = 0
                for g in range(32):
                    for a in range(2):
                        pp = ps[1 + cnt % 4]
                        cnt += 1
                        P.transpose(pp[:, 0:np_], uc2[0:np_, g * 256 + a * 128:g * 256 + (a + 1) * 128], k.ident[0:np_, 0:np_])
                        P.copy(U[:, g, a, c_lo:c_lo + np_], pp[:, 0:np_], e=("act" if cnt % 2 else "dve"))
        for d in range(2):
            with P.scope():
                Ar = P.sbuf("s5_Ar", [128, 8, 16], F32); Ai = P.sbuf("s5_Ai", [128, 8, 16], F32); nAi = P.sbuf("s5_nAi", [128, 8, 16], F32)
                PTre = P.sbuf("s5_PTre", [128, 16, 2, 128], BF16); PTim = P.sbuf("s5_PTim", [128, 16, 2, 128], BF16)
                with P.scope():
                    sm = P.sbuf("s5_sm", [128, 40, 16], F32)
                    slot = [0]

                    def S():
                        i = slot[0]
                        slot[0] += 1
                        return sm[:, i, :]
                    lre = S(); lim = S(); dt = S()
                    praw = P.sbuf("s5_praw", [16, 2, 128], F32)
                    P.dma(praw[:, 0, :], I["s5_lam_re"][l, d].rearrange("(pr q) -> pr q", q=128))
                    P.dma(praw[:, 1, :], I["s5_lam_im"][l, d].rearrange("(pr q) -> pr q", q=128))
                    P.transpose(ps[1][:, 0:16], praw[:, 0, :], k.ident[0:16, 0:16])
                    P.transpose(ps[1][:, 16:32], praw[:, 1, :], k.ident[0:16, 0:16])
                    P.copy(lre, ps[1][:, 0:16])
                    P.copy(lim, ps[1][:, 16:32])
                    ldt2 = P.sbuf("s5_ldt2", [2, 16], F32)
                    P.dma(ldt2[:], I["s5_log_dt"][l, d].rearrange("(pr g2) -> g2 pr", g2=2), allow_slow_non_contiguous=True)
                    self_ = P.sbuf("s5_self", [2, 128], F32)
                    P.dma(self_[:], I["na_sel"])
                    P.mm(ps[2][:, 0:16], self_[:, :], ldt2[:, :], start=True, stop=True)
                    P.copy(dt, ps[2][:, 0:16])
                    P.act(dt, dt, AF.Exp)
                    P.ts(lre, lre, -1e-4, None, op0=ALU.min)
                    a_ = S(); th = S(); t1 = S(); t2 = S()
                    P.tt(a_, lre, dt, ALU.mult)
                    P.tt(th, lim, dt, ALU.mult)
                    mag = S(); imag = S()
                    P.act(mag, a_, AF.Exp)
                    P.act(imag, a_, AF.Exp, scale=-1.0)
                    for _ in range(4):
                        P.ts(t1, th, PI, -2.0 * PI, op0=ALU.is_gt, op1=ALU.mult)
                        P.tt(th, th, t1, ALU.add)
                    thc = S()
                    P.ts(thc, th, PI / 2, None, op0=ALU.add)
                    P.ts(t1, thc, PI, -2.0 * PI, op0=ALU.is_gt, op1=ALU.mult)
                    P.tt(thc, thc, t1, ALU.add)
                    sn = S(); cs = S()
                    P.act(sn, th, AF.Sin)
                    P.act(cs, thc, AF.Sin)
                    lbr = S(); lbi = S(); lir = S(); lii = S()
                    P.tt(lbr, mag, cs, ALU.mult); P.tt(lbi, mag, sn, ALU.mult)
                    P.tt(lir, imag, cs, ALU.mult); P.stt(lii, imag, -1.0, sn, ALU.mult, ALU.mult)
                    den = S(); icr = S(); ici = S()
                    P.tt(den, lre, lre, ALU.mult); P.tt(t1, lim, lim, ALU.mult); P.tt(den, den, t1, ALU.add)
                    P.recip(den, den)
                    P.tt(icr, lre, den, ALU.mult); P.stt(ici, lim, -1.0, den, ALU.mult, ALU.mult)
                    lm1 = S(); cfr = S(); cfi = S()
                    P.ts(lm1, lbr, -1.0, None, op0=ALU.add)
                    cmul(P, cfr, cfi, lm1, lbi, icr, ici, t1, t2)
                    Ppr = P.sbuf("s5_Ppr", [128, 16, 17], F32); Ppi = P.sbuf("s5_Ppi", [128, 16, 17], F32)
                    Pnr = P.sbuf("s5_Pnr", [128, 16, 17], F32); Pni = P.sbuf("s5_Pni", [128, 16, 17], F32)
                    tw1 = P.sbuf("s5_tw1", [128, 16, 8], F32); tw2 = P.sbuf("s5_tw2", [128, 16, 8], F32)
                    for (tr, ti, br_, bi_) in ((Ppr, Ppi, lbr, lbi), (Pnr, Pni, lir, lii)):
                        P.memset(tr[:, :, 0:1], 1.0); P.memset(ti[:, :, 0:1], 0.0)
                        P.copy(tr[:, :, 1], br_); P.copy(ti[:, :, 1], bi_)
                        n = 2
                        while n <= 16:
                            sqr = S() if False else None
                            h = n // 2
                            cmul(P, tr[:, :, n], ti[:, :, n], tr[:, :, h], ti[:, :, h], tr[:, :, h], ti[:, :, h], tw1[:, :, 0], tw2[:, :, 0])
                            cnt_ = min(n, 17 - n) - 1
                            if cnt_ > 0:
                                bre = tr[:, :, n:n + 1].broadcast_to([128, 16, cnt_]) if False else None
                                cmul(P, tr[:, :, n + 1:n + 1 + cnt_], ti[:, :, n + 1:n + 1 + cnt_],
                                     tr[:, :, 1:1 + cnt_], ti[:, :, 1:1 + cnt_],
                                     tr[:, :, n:n + 1].to_broadcast([128, 16, cnt_]), ti[:, :, n:n + 1].to_broadcast([128, 16, cnt_]),
                                     tw1[:, :, 0:cnt_], tw2[:, :, 0:cnt_])
                            n *= 2
                    P.copy(Ar[:, 0, :], Ppr[:, :, 16]); P.copy(Ai[:, 0, :], Ppi[:, :, 16])
                    for kk in range(1, 8):
                        cmul(P, Ar[:, kk, :], Ai[:, kk, :], Ar[:, kk - 1, :], Ai[:, kk - 1, :], Ar[:, kk - 1, :], Ai[:, kk - 1, :], t1, t2)
                    P.ts(nAi[:], Ai[:], -1.0, None, op0=ALU.mult)
                    Bre = P.sbuf("s5_Bre", [128, 16, 16], F32); Bim = P.sbuf("s5_Bim", [128, 16, 16], F32)
                    P.dma(Bre[:], I["s5_b_re"][l, d].rearrange("(pr q) h -> q pr h", q=128))
                    P.dma(Bim[:], I["s5_b_im"][l, d].rearrange("(pr q) h -> q pr h", q=128))
                    Cre = P.sbuf("s5_Cre", [128, 16, 16], F32); Cim = P.sbuf("s5_Cim", [128, 16, 16], F32)
                    craw = P.sbuf("s5_craw", [128, 2, 4, 64], F32)
                    for ri, nm_ in enumerate(("s5_c_re", "s5_c_im")):
                        P.dma(craw[:, ri], I[nm_][l, d].rearrange("(a gl) h p -> (gl h) a p", a=4))
                    for ri, dst in enumerate((Cre, Cim)):
                        for a in range(4):
                            pa_ = ps[3 + a % 2]
                            P.mm(pa_[0:64, 0:128], craw[:, ri, a, :], k.ident[:], start=True, stop=True)
                            P.mm(pa_[64:128, 0:128], craw[:, ri, a, :], k.ident[:], start=True, stop=True)
                            v_ = pa_[:, 0:128].rearrange("p (pl g2 h) -> p pl g2 h", g2=2, h=16)
                            P.copy(dst[0:64, a * 4:(a + 1) * 4, :], v_[0:64, :, 0, :], e="act")
                            P.copy(dst[64:128, a * 4:(a + 1) * 4, :], v_[64:128, :, 1, :], e="dve")
                    tb1 = P.sbuf("s5_tb1", [128, 16, 16], F32); tb2 = P.sbuf("s5_tb2", [128, 16, 16], F32)
                    bbr = P.sbuf("s5_bbr", [128, 16, 16], F32); bbi = P.sbuf("s5_bbi", [128, 16, 16], F32)
                    bc = lambda x: x.unsqueeze(2).to_broadcast([128, 16, 16])
                    cmul(P, bbr[:], bbi[:], Bre[:], Bim[:], bc(cfr), bc(cfi), tb1[:], tb2[:])
                    if d == 1:
                        cmul(P, Bre[:], Bim[:], bbr[:], bbi[:], bc(Pnr[:, :, 15]), bc(Pni[:, :, 15]), tb1[:], tb2[:])
                        vbr, vbi = Bre, Bim
                        cr2 = P.sbuf("s5_cr2", [128, 16, 16], F32); ci2 = P.sbuf("s5_ci2", [128, 16, 16], F32)
                        cmul(P, cr2[:], ci2[:], Cre[:], Cim[:], bc(Ppr[:, :, 15]), bc(Ppi[:, :, 15]), tb1[:], tb2[:])
                        vcr, vci = cr2, ci2
                        tabP, tabQ = (Ppr, Ppi), (Pnr, Pni)
                    else:
                        vbr, vbi = bbr, bbi
                        vcr, vci = Cre, Cim
                        tabP, tabQ = (Pnr, Pni), (Ppr, Ppi)
                    Pre = P.sbuf("s5_Pre", [128, 16, 256], BF16); Pim = P.sbuf("s5_Pim", [128, 16, 256], BF16)
                    to1 = P.sbuf("s5_to1", [128, 4, 256], F32); to2 = P.sbuf("s5_to2", [128, 4, 256], F32)
                    v4 = lambda x: x.rearrange("p a (j h) -> p a j h", j=16)
                    for pg in range(4):
                        sl = slice(pg * 4, pg * 4 + 4)
                        tj = lambda tab: tab[:, sl, 0:16].unsqueeze(3).to_broadcast([128, 4, 16, 16])
                        vh = lambda v: v[:, sl, :].unsqueeze(2).to_broadcast([128, 4, 16, 16])
                        cmul(P, v4(Pre[:, sl, :]), v4(Pim[:, sl, :]), tj(tabP[0]), tj(tabP[1]), vh(vbr), vh(vbi), v4(to1[:]), v4(to2[:]))
                        cmul(P, v4(Qre[d][:, sl, :]), v4(nQim[d][:, sl, :]), tj(tabQ[0]), tj(tabQ[1]), vh(vcr), vh(vci), v4(to1[:]), v4(to2[:]),
                             neg_im=True)
                    msk = [P.sbuf("s5_msk%d" % a, [128, 256], F32) for a in range(2)]
                    for a in range(2):
                        P.dma(msk[a][:], I["s5_mask"][d, a])
                    tm = [P.sbuf("s5_tm%d" % i, [128, 256], BF16) for i in range(2)]
                    cnt = 0
                    for g in range(32):
                        pr_, g2 = g // 2, g % 2
                        rows = slice(g2 * 64, (g2 + 1) * 64)
                        for a in range(2):
                            pp = ps[1 + cnt % 4]
                            cnt += 1
                            P.mm(pp[:, 0:256], Pre[rows, pr_, a * 128:(a + 1) * 128], Qre[d][rows, pr_, :], start=True, stop=False)
                            P.mm(pp[:, 0:256], Pim[rows, pr_, a * 128:(a + 1) * 128], nQim[d][rows, pr_, :], start=False, stop=True)
                            if d == 0:
                                P.tt(M[:, g, a, :], pp[:, 0:256], msk[a][:], ALU.mult)
                            else:
                                t = tm[cnt % 2]
                                P.tt(t[:], pp[:, 0:256], msk[a][:], ALU.mult)
                                P.tt(M[:, g, a, :], M[:, g, a, :], t[:], ALU.add, e="pool")
                    cnt = 0
                    for pr_ in range(16):
                        for a in range(2):
                            for (src, dst) in ((Pre, PTre), (Pim, PTim)):
                                pt = ps[5 + cnt % 2][:].bitcast(BF16)
                                cnt += 1
                                P.transpose(pt[:, 0:128], src[:, pr_, a * 128:(a + 1) * 128], k.ident_bf[:])
                                P.copy(dst[:, pr_, a, :], pt[:, 0:128], e=("act" if cnt % 2 else "dve"))
                SA = [P.sbuf("s5_SAre", [128, 16, NCH], F32), P.sbuf("s5_SAim", [128, 16, NCH], F32)]
                SB = [P.sbuf("s5_SBre", [128, 16, NCH], F32), P.sbuf("s5_SBim", [128, 16, NCH], F32)]
                ts_ = P.sbuf("s5_ts", [128, NCH], F32); ts2 = P.sbuf("s5_ts2", [128, NCH], F32)
                for pr_ in range(16):
                    pre_, pim_ = ps[1 + 2 * (pr_ % 2)], ps[2 + 2 * (pr_ % 2)]
                    for (pt_, PT) in ((pre_, PTre), (pim_, PTim)):
                        for g2 in range(2):
                            g = 2 * pr_ + g2
                            rows = slice(g2 * 64, (g2 + 1) * 64)
                            if d == 0:
                                for a in range(2):
                                    P.mm(pt_[rows, 0:NCH], PT[:, pr_, a, rows], U[:, g, a, :], start=(a == 0), stop=(a == 1))
                            else:
                                for a in range(2):
                                    P.mm(pt_[rows, 0:128], PT[:, pr_, a, rows], U[:, g, a, 16:NCH], start=(a == 0), stop=(a == 1))
                                for a in range(2):
                                    P.mm(pt_[rows, 128:NCH], PT[:, pr_, a, rows], U[:, g, a, 0:16], start=(a == 0), stop=(a == 1))
                    P.ts(ts_[:], pre_[:, 0:NCH], Ar[:, 0, pr_:pr_ + 1], None, op0=ALU.mult)
                    P.stt(SA[0][:, pr_, :], pim_[:, 0:NCH], nAi[:, 0, pr_:pr_ + 1], ts_[:], ALU.mult, ALU.add)
                    P.ts(ts2[:], pim_[:, 0:NCH], Ar[:, 0, pr_:pr_ + 1], None, op0=ALU.mult)
                    P.stt(SA[1][:, pr_, :], pre_[:, 0:NCH], Ai[:, 0, pr_:pr_ + 1], ts2[:], ALU.mult, ALU.add)
                tsa = [P.sbuf("s5_tsa%d" % i, [128, NCH], F32) for i in range(4)]
                tsb = [P.sbuf("s5_tsb%d" % i, [128, NCH], F32) for i in range(4)]
                cur, nxt = SA, SB
                for kk in range(8):
                    sh = 1 << kk
                    n_ = NCH - sh
                    for pg in range(4):
                        prs = list(range(pg * 4, pg * 4 + 4))

                        def views(pr_):
                            if d == 0:
                                return (nxt[0][:, pr_, sh:NCH], nxt[1][:, pr_, sh:NCH],
                                        cur[0][:, pr_, 0:n_], cur[1][:, pr_, 0:n_],
                                        cur[0][:, pr_, sh:NCH], cur[1][:, pr_, sh:NCH])
                            return (nxt[0][:, pr_, 0:n_], nxt[1][:, pr_, 0:n_],
                                    cur[0][:, pr_, sh:NCH], cur[1][:, pr_, sh:NCH],
                                    cur[0][:, pr_, 0:n_], cur[1][:, pr_, 0:n_])
                        V = [views(pr_) for pr_ in prs]
                        for i, pr_ in enumerate(prs):
                            P.stt(tsa[i][:, 0:n_], V[i][2], Ar[:, kk, pr_:pr_ + 1], V[i][4], ALU.mult, ALU.add)
                        for i, pr_ in enumerate(prs):
                            P.stt(tsb[i][:, 0:n_], V[i][3], Ar[:, kk, pr_:pr_ + 1], V[i][5], ALU.mult, ALU.add)
                        for i, pr_ in enumerate(prs):
                            P.stt(V[i][0], V[i][3], nAi[:, kk, pr_:pr_ + 1], tsa[i][:, 0:n_], ALU.mult, ALU.add)
                        for i, pr_ in enumerate(prs):
                            P.stt(V[i][1], V[i][2], Ai[:, kk, pr_:pr_ + 1], tsb[i][:, 0:n_], ALU.mult, ALU.add)
                    for ri in range(2):
                        if d == 0:
                            P.copy(nxt[ri][:, :, 0:sh], cur[ri][:, :, 0:sh], e="pool")
                        else:
                            P.copy(nxt[ri][:, :, n_:NCH], cur[ri][:, :, n_:NCH], e="pool")
                    cur, nxt = nxt, cur
                for ri, X in ((0, Xre[d]), (1, Xim[d])):
                    W = cur[ri]
                    if d == 0:
                        P.memset(X[:, :, 0:1], 0.0)
                        P.copy(X[:, :, 1:NCH], W[:, :, 0:NCH - 1], e=("act" if ri else "dve"))
                    else:
                        P.copy(X[:, :, 0:15], W[:, :, 129:144], e="act")
                        P.memset(X[:, :, 15:16], 0.0)
                        P.copy(X[:, :, 16:143], W[:, :, 1:128], e="dve")
                        P.copy(X[:, :, 143:144], W[:, :, 128:129], e="act")
        with P.scope():
            ycl = P.sbuf("s5_ycl", [128, 16 * 512], F32)
            ycc = P.sbuf("s5_ycc", [16, 16 * 512], F32)
            yclv = ycl[:].rearrange("p (j g h) -> p j g h", j=16, g=32)
            yccv = ycc[:].rearrange("p (j g h) -> p j g h", j=16, g=32)
            ysb = [P.sbuf("s5_ysb%d" % i, [128, NCH], F32) for i in range(2)]
            cnt = 0
            for g in range(32):
                pr_, g2 = g // 2, g % 2
                rows = slice(g2 * 64, (g2 + 1) * 64)
                for b in range(2):
                    pp = ps[1 + cnt % 2]
                    ys = ysb[cnt % 2]
                    pt1 = ps[3 + cnt % 2]
                    pt2 = ps[5 + cnt % 2]
                    cnt += 1
                    cols = slice(b * 128, (b + 1) * 128)
                    P.mm(pp[:, 0:NCH], M[:, g, 0, cols], U[:, g, 0, :], start=True, stop=False)
                    P.mm(pp[:, 0:NCH], M[:, g, 1, cols], U[:, g, 1, :], start=False, stop=False)
                    for d in range(2):
                        P.mm(pp[:, 0:NCH], Qre[d][rows, pr_, cols], Xre[d][rows, pr_, :], start=False, stop=False)
                        P.mm(pp[:, 0:NCH], nQim[d][rows, pr_, cols], Xim[d][rows, pr_, :], start=False, stop=(d == 1))
                    P.copy(ys[:], pp[:, 0:NCH], e="act")
                    P.transpose(pt1[:, 0:128], ys[:, 16:NCH], k.ident[:])
                    P.copy(yclv[:, b * 8:(b + 1) * 8, g, :], pt1[:, 0:128].rearrange("p (j h) -> p j h", j=8), e="dve")
                    P.transpose(pt2[0:16, 0:128], ys[:, 0:16], k.ident[:])
                    P.copy(yccv[:, b * 8:(b + 1) * 8, g, :], pt2[0:16, 0:128].rearrange("p (j h) -> p j h", j=8), e="pool" if False else "dve")
            P.dma(k.y_tm[NCTX:T, :].rearrange("(c j) n -> c (j n)", j=16), ycl[:], q="sp")
            P.dma(k.y_tm[0:NCTX, :].rearrange("(c j) n -> c (j n)", j=16), ycc[:], q="sp")
        with P.scope():
            dbc = P.sbuf("s5_dbc", [128, 512], F32)
            P.dma(dbc[:], I["s5_d"][l].partition_broadcast(128))
            wgl = P.sbuf("s5_wgl", [128, 4, 1024], BF16)
            P.dma(wgl[:], I["s5_w_glu"][l].rearrange("(kk p) n -> p kk n", p=128), q="pool")
            bgl = P.sbuf("s5_bgl", [128, 8], F32)
            load_col_vec(P, bgl[:], I["s5_b_glu"][l], 8)
            gT = P.sbuf("s5_gT", [128, 4, T], BF16)
            yt = [P.sbuf("s5_yt%d" % i, [128, 512], F32) for i in range(2)]
            ut = [P.sbuf("s5_ut%d" % i, [128, 512], F32) for i in range(2)]
            w1_ = P.sbuf("s5_w1", [128, 512], F32); w2_ = P.sbuf("s5_w2", [128, 512], F32)
            gtm = [P.sbuf("s5_gtm%d" % i, [128, 512], BF16) for i in range(2)]
            tcs = list(range(18)) if ctx_out else list(range(2, 18))
            for tc in tcs:
                y = yt[tc % 2]; u = ut[tc % 2]
                P.dma(y[:], k.y_tm[tc * 128:(tc + 1) * 128, :])
                P.dma(u[:], k.u_tm[tc * 128:(tc + 1) * 128, :])
                P.tt(u[:], u[:], dbc[:], ALU.mult)
                P.tt(y[:], y[:], u[:], ALU.add, e="pool")
                P.act(w1_[:], y[:], AF.Square)
                P.ts(w1_[:], w1_[:], 0.044715, 1.0, op0=ALU.mult, op1=ALU.add)
                P.tt(w1_[:], w1_[:], y[:], ALU.mult)
                P.act(w2_[:], w1_[:], AF.Sigmoid, scale=1.5957691216057308)
                gt_ = gtm[tc % 2]
                P.tt(gt_[:], y[:], w2_[:], ALU.mult, e="pool")
                pt = ps[5 + tc % 2][:].bitcast(BF16)
                for j in range(4):
                    P.transpose(pt[:, j * 128:(j + 1) * 128], gt_[:, j * 128:(j + 1) * 128], k.ident_bf[:])
                P.copy(gT[:, :, tc * 128:(tc + 1) * 128], pt[:, 0:512].rearrange("p (j t) -> p j t", j=4), e="act")
            cT = P.sbuf("s5_cT", [128, T], BF16)
            sg = [P.sbuf("s5_sg%d" % i, [128, 512], F32) for i in range(2)]
            subs = ALLSUBS if ctx_out else LATSUBS
            cnt = 0
            for m in range(4):
                for (c0, w, st) in subs:
                    pa = ps[1 + cnt % 2]; pb = ps[3 + cnt % 2]; s_ = sg[cnt % 2]
                    cnt += 1
                    for kk in range(4):
                        P.mm(pa[:, :w], wgl[:, kk, m * 128:(m + 1) * 128], gT[:, kk, c0:c0 + w], start=(kk == 0), stop=(kk == 3))
                    for kk in range(4):
                        P.mm(pb[:, :w], wgl[:, kk, 512 + m * 128:512 + (m + 1) * 128], gT[:, kk, c0:c0 + w], start=(kk == 0), stop=(kk == 3))
                    P.act(s_[:, :w], pb[:, :w], AF.Sigmoid, bias=bgl[:, 4 + m:5 + m])
                    P.stt(cT[:, c0:c0 + w], pa[:, :w], bgl[:, m:m + 1], s_[:, :w], ALU.add, ALU.mult)
                cols0 = 0 if ctx_out else NCTX
                P.dma(k.br[2][m * 128:(m + 1) * 128, cols0:T], cT[:, cols0:T], q="sp")


def build_full(nl=2, upto=None):
    P = Prog()
    nc = P.nc
    k = K()
    k.I = declare_inputs(P, nl)
    declare_mixer_inputs(P, k.I, nl)
    declare_na_inputs(P, k.I, nl)
    declare_hyena_inputs(P, k.I, nl)
    declare_s5_inputs(P, k.I, nl)
    yT = nc.dram_tensor("yT", [D, NLAT], F32, kind="ExternalOutput").ap()
    setup_common(P, k)
    mixer_setup(P, k)
    s5_setup(P, k)
    geo = na_geometry()
    P.dma(k.xres, k.I["xT"].rearrange("(c p) t -> p c t", p=128))
    for l in range(nl):
        ctx_out = l < nl - 1
        with P.scope():
            layer_mods(P, k, l)
        with P.scope():
            ffn_sublayer(P, k, l, 0, FULL_BLOCKS)
        with P.scope():
            alloc_hmix(P, k)
            mixer_modnorm(P, k)
            mla_branch(P, k, l, ctx_out)
            na_branch(P, k, l, ctx_out, geo)
            hyena_branch(P, k, l, HY_LAT)
            if ctx_out:
                hyena_branch(P, k, l, HY_CTX)
            s5_part1(P, k, l)
        s5_part2(P, k, l, ctx_out)
        merge_phase(P, k, l, ctx_out)
        with P.scope():
            ffn_sublayer(P, k, l, 2, FULL_BLOCKS if ctx_out else LAT_BLOCKS)
    P.dma(yT.rearrange("(c p) t -> p c t", p=128), k.xres[:, :, NCTX:T], q="sp")
    P.finish("sp")
    P.close()
    return P, k


_CACHE = {}


def _host_constants():
    if "c" in _CACHE:
        return _CACHE["c"]
    C, S = rope_tables()
    mv, sel, cm = na_const_tables()
    m = {"ident": np.eye(128, dtype=np.float32), "rope_c": C, "rope_s": S,
         "na_mv": np.ascontiguousarray(mv.reshape(2, -1)), "na_sel": sel, "na_cm": cm, "s5_mask": s5_masks()}
    for nm_, N in (("lat", NLAT), ("ctx", NCTX)):
        hc = hyena_consts(N)
        for t in ("c", "s", "ct", "st"):
            m["dft_%s_%s" % (t, nm_)] = hc[t]
        m["hy_zT_%s" % nm_] = hc["zT"]
        m["hy_decay_%s" % nm_] = hc["decay"]
    _CACHE["c"] = m
    return m


def kernel(**inputs):
    nl = 2
    if "prog" not in _CACHE:
        _CACHE["prog"] = build_full(nl)
    P, k = _CACHE["prog"]
    shared = dict(_host_constants())
    shared["na_G"] = np.ascontiguousarray(
        np.stack([na_bias_gather(np.asarray(inputs["na_rpb"][l], np.float32)).reshape(128, -1) for l in range(nl)], 0))
    for nm, ap in k.I.items():
        if nm in shared or nm in ("xT", "cvec"):
            continue
        shared[nm] = np.ascontiguousarray(np.asarray(inputs[nm], np.float32).reshape(ap.shape))
    x = np.asarray(inputs["x"], np.float32)
    ctx = np.asarray(inputs["ctx"], np.float32)
    c = np.asarray(inputs["c"], np.float32)
    c_ctx = np.asarray(inputs["c_ctx"], np.float32)
    B = x.shape[0]
    in_maps = []
    for b in range(B):
        m = dict(shared)
        m["xT"] = np.ascontiguousarray(np.concatenate([ctx[b], x[b]], 0).T)
        m["cvec"] = np.ascontiguousarray(np.stack([c[b], c_ctx], 1))
        in_maps.append(m)
    res = run_bass_kernel_spmd(P.nc, in_maps, core_ids=list(range(B)))
    out = np.stack([np.asarray(res.results[b]["yT"], np.float32).T for b in range(B)], 0)
    return np.ascontiguousarray(out)
```

```python
import numpy as np
import concourse.bass as bass
import concourse.mybir as mybir
from concourse.bass_utils import run_bass_kernel_spmd

F32 = mybir.dt.float32
F32R = mybir.dt.float32r
BF16 = mybir.dt.bfloat16
AF = mybir.ActivationFunctionType
ALU = mybir.AluOpType
AX = mybir.AxisListType

SEM_ROLL = 30000


class Prog:
    def __init__(self, n_dma_sems=16):
        self.nc = bass.Bass("TRN2", target_bir_lowering=False)
        nc = self.nc
        self.eng = {"pe": nc.tensor, "act": nc.scalar, "dve": nc.vector,
                    "pool": nc.gpsimd, "sp": nc.sync}
        self._ctx = []
        self._scopes = []
        self._in_scope_alloc = False
        self._uid = 0
        self.sem = {}
        self.cnt = {}
        self.nsem = 0
        for e in self.eng:
            self._new_eng_sem(e)
        self.dma_sems = {}
        self.dma_rr = {}
        for q in ("sp", "pool", "act"):
            self.dma_sems[q] = []
            for i in range(n_dma_sems if q != "act" else 4):
                s = self._enter(nc.semaphore("dq_%s%d" % (q, i)))
                self.dma_sems[q].append([s, 0])
            self.dma_rr[q] = 0
        self.waited = {e: {} for e in self.eng}
        self.regions = {}
        self.n_inst = 0
        self.n_wait = 0

    def _enter(self, cm):
        v = cm.__enter__()
        if self._scopes and self._in_scope_alloc:
            self._scopes[-1].append((cm, v))
        else:
            self._ctx.append(cm)
        return v

    class _Scope:
        def __init__(self, P):
            self.P = P

        def __enter__(self):
            self.P._scopes.append([])
            return self

        def __exit__(self, *a):
            P = self.P
            items = P._scopes.pop()
            toks = []
            for cm, v in items:
                nm = v.name if hasattr(v, "name") else None
                for r in P.regions.pop(nm, []):
                    toks.append((r[5], r[6]))
            if toks:
                for e in P.eng:
                    P._emit_waits(e, toks)
            for cm, v in reversed(items):
                cm.__exit__(None, None, None)
            return False

    def scope(self):
        return Prog._Scope(self)

    def _new_eng_sem(self, e):
        s = self._enter(self.nc.semaphore("s_%s_%d" % (e, self.nsem)))
        self.nsem += 1
        self.sem[e] = s
        self.cnt[e] = 0

    def close(self):
        for cm in reversed(self._ctx):
            cm.__exit__(None, None, None)
        self._ctx = []

    def sbuf(self, name, shape, dtype=F32):
        self._uid += 1
        self._in_scope_alloc = True
        try:
            return self._enter(self.nc.sbuf_tensor("%s_%d" % (name, self._uid), list(shape), dtype))
        finally:
            self._in_scope_alloc = False

    def psum(self, name, shape=(128, 512), dtype=F32):
        return self._enter(self.nc.psum_tensor(name, list(shape), dtype))

    def dram(self, name, shape, dtype=F32, kind="Internal"):
        return self.nc.dram_tensor(name, list(shape), dtype, kind=kind)

    @staticmethod
    def _region(ap):
        space = str(ap.space)
        name = ap.name
        aps = ap.ap
        off = int(ap.offset)
        if "DRAM" in space:
            ext = sum((c - 1) * abs(s) for s, c in aps)
            neg = sum((c - 1) * s for s, c in aps if s < 0)
            lo = off + neg
            return name, 0, 1, lo, lo + ext + 1, False
        pstep, pcnt = aps[0]
        if pstep == 0:
            pstep = 1 << 40
        p0 = off // pstep if pstep < (1 << 39) else 0
        lo = off - p0 * pstep if pstep < (1 << 39) else off
        ext = sum((c - 1) * abs(s) for s, c in aps[1:])
        is_psum = "PSUM" in space
        if is_psum:
            return name, 0, 128, 0, 1 << 30, True
        return name, p0, p0 + pcnt, lo, lo + ext + 1, False

    def _deps(self, reads, writes):
        toks = []
        info = []
        for ap, is_w in [(a, False) for a in reads] + [(a, True) for a in writes]:
            name, p0, p1, lo, hi, excl = self._region(ap)
            w = is_w or excl
            lst = self.regions.setdefault(name, [])
            for r in lst:
                if r[1] <= p0 or p1 <= r[0] or r[3] <= lo or hi <= r[2]:
                    continue
                if w or r[4]:
                    toks.append((r[5], r[6]))
            info.append((name, p0, p1, lo, hi, w))
        return toks, info

    def _record(self, info, sem, val):
        for name, p0, p1, lo, hi, w in info:
            lst = self.regions[name]
            if w:
                lst[:] = [r for r in lst if not (p0 <= r[0] and r[1] <= p1 and lo <= r[2] and r[3] <= hi)]
                lst.append([p0, p1, lo, hi, True, sem, val])
            else:
                for r in lst:
                    if (not r[4]) and r[0] == p0 and r[1] == p1 and r[2] == lo and r[3] == hi and r[5] is sem:
                        r[6] = max(r[6], val)
                        break
                else:
                    lst.append([p0, p1, lo, hi, False, sem, val])

    def _emit_waits(self, e, toks):
        best = {}
        for s, v in toks:
            k = id(s)
            if k not in best or best[k][1] < v:
                best[k] = (s, v)
        wd = self.waited[e]
        for k, (s, v) in best.items():
            if wd.get(k, 0) >= v:
                continue
            self.eng[e].wait_ge(s, v)
            wd[k] = v
            self.n_wait += 1

    def op(self, e, fn, reads, writes):
        toks, info = self._deps(reads, writes)
        if e == "pe":
            toks = [t for t in toks if t[0] is not self.sem["pe"]]
        self._emit_waits(e, toks)
        ins = fn()
        if self.cnt[e] >= SEM_ROLL:
            self._new_eng_sem(e)
        self.cnt[e] += 1
        ins.then_inc(self.sem[e], 1)
        self._record(info, self.sem[e], self.cnt[e])
        self.n_inst += 1
        return ins

    def dma(self, out, in_, q="sp", **kw):
        toks, info = self._deps([in_], [out])
        ent = self.dma_sems[q][self.dma_rr[q]]
        self.dma_rr[q] = (self.dma_rr[q] + 1) % len(self.dma_sems[q])
        s = ent[0]
        if ent[1] > 0:
            toks.append((s, ent[1]))
        self._emit_waits(q, toks)
        ent[1] += 16
        ins = self.eng[q].dma_start(out=out, in_=in_, **kw)
        ins.then_inc(s, 16)
        self._record(info, s, ent[1])
        self.n_inst += 1
        return ins

    def finish(self, e="sp"):
        toks = []
        for lst in self.regions.values():
            for r in lst:
                toks.append((r[5], r[6]))
        self._emit_waits(e, toks)

    def mm(self, out, lhsT, rhs, start=True, stop=True, **kw):
        return self.op("pe", lambda: self.nc.tensor.matmul(out, lhsT, rhs, start=start, stop=stop, **kw),
                       [lhsT, rhs], [out])

    def transpose(self, out, in_, ident):
        return self.op("pe", lambda: self.nc.tensor.transpose(out, in_, ident), [in_, ident], [out])

    def act(self, out, in_, func, bias=None, scale=1.0, e="act", **kw):
        reads = [in_]
        if bias is not None and not isinstance(bias, (int, float)):
            reads.append(bias)
        if not isinstance(scale, (int, float)):
            reads.append(scale)
        kw2 = dict(kw)
        if bias is not None:
            kw2["bias"] = bias
        writes = [out]
        if "accum_out" in kw2:
            writes.append(kw2["accum_out"])
        return self.op(e, lambda: self.nc.scalar.activation(out=out, in_=in_, func=func, scale=scale, **kw2),
                       reads, writes)

    def _veng(self, e):
        return self.nc.vector if e == "dve" else self.nc.gpsimd

    def tt(self, out, in0, in1, op, e="dve"):
        return self.op(e, lambda: self._veng(e).tensor_tensor(out=out, in0=in0, in1=in1, op=op), [in0, in1], [out])

    def ts(self, out, in0, s1, s2=None, op0=ALU.mult, op1=None, e="dve", **kw):
        reads = [in0] + [s for s in (s1, s2) if s is not None and not isinstance(s, (int, float))]
        writes = [out] + ([kw["accum_out"]] if "accum_out" in kw else [])
        if op1 is None:
            return self.op(e, lambda: self._veng(e).tensor_scalar(out=out, in0=in0, scalar1=s1, scalar2=None, op0=op0, **kw),
                           reads, writes)
        return self.op(e, lambda: self._veng(e).tensor_scalar(out=out, in0=in0, scalar1=s1, scalar2=s2, op0=op0, op1=op1, **kw),
                       reads, writes)

    def stt(self, out, in0, scalar, in1, op0, op1, e="dve"):
        reads = [in0, in1] + ([] if isinstance(scalar, (int, float)) else [scalar])
        return self.op(e, lambda: self.nc.vector.scalar_tensor_tensor(out=out, in0=in0, scalar=scalar, in1=in1, op0=op0, op1=op1),
                       reads, [out])

    def copy(self, out, in_, e="dve"):
        if e == "act":
            return self.op("act", lambda: self.nc.scalar.copy(out=out, in_=in_), [in_], [out])
        return self.op(e, lambda: self._veng(e).tensor_copy(out=out, in_=in_), [in_], [out])

    def memset(self, ap, val, e="dve"):
        return self.op(e, lambda: self._veng(e).memset(ap, val), [], [ap])

    def recip(self, out, in_):
        return self.op("dve", lambda: self.nc.vector.reciprocal(out=out, in_=in_), [in_], [out])

    def scan(self, out, d0, d1, initial, op0=ALU.mult, op1=ALU.add):
        reads = [d0, d1] + ([] if isinstance(initial, (int, float)) else [initial])
        return self.op("dve", lambda: self.nc.vector.tensor_tensor_scan(out=out, data0=d0, data1=d1, initial=initial, op0=op0, op1=op1),
                       reads, [out])


import numpy as np
def rope_tables(n=2048, grid_w=64, base=10000.0):
    q = 16
    t = np.arange(n)
    pos = np.stack([t // grid_w, t % grid_w], -1).astype(np.float32)
    inv = (base ** (-np.arange(q, dtype=np.float32) / q)).astype(np.float32)
    ang = pos[:, :, None] * inv
    C = np.zeros((64, n), np.float32); S = np.zeros((64, n), np.float32)
    for a in range(2):
        for hf in range(2):
            for j in range(q):
                f = a * 32 + hf * 16 + j
                C[f] = np.cos(ang[:, a, j])
                S[f] = (-1.0 if hf == 0 else 1.0) * np.sin(ang[:, a, j])
    return C, S

NEG = -30000.0
def na_geometry():
    rows = 32
    start = lambda r: min(max(r - 4, 0), rows - 8)
    geo = []
    for i in range(16):
        rs = [2 * i, 2 * i + 1]
        lo = min(start(r) for r in rs); hi = max(start(r) + 7 for r in rs)
        lst = []
        for j in range(lo // 2, hi // 2 + 1):
            codes = []
            for r in rs:
                v0 = start(r) <= 2 * j <= start(r) + 7
                v1 = start(r) <= 2 * j + 1 <= start(r) + 7
                code = {(True, True): 0, (False, True): 1, (True, False): 2, (False, False): 3}[(v0, v1)]
                codes.append(code)
            dr0 = 2 * (j - i)
            idxp = 7 - dr0
            assert 0 <= idxp <= 14, (i, j, idxp)
            lst.append((j, idxp, codes[0] * 4 + codes[1]))
        geo.append(lst)
    return geo

def na_const_tables():
    mv = np.zeros((2, 16, 128), np.float32)
    vecs = [np.zeros(128), np.r_[np.full(64, NEG), np.zeros(64)], np.r_[np.zeros(64), np.full(64, NEG)], np.full(128, NEG)]
    for c0 in range(4):
        for c1 in range(4):
            mv[0, c0 * 4 + c1] = vecs[c0]; mv[1, c0 * 4 + c1] = vecs[c1]
    sel = np.zeros((2, 128), np.float32); sel[0, :64] = 1; sel[1, 64:] = 1
    col = np.arange(64)
    c0 = np.clip(col - 8, 0, 48)
    inwin = (col[None, :] >= c0[:, None]) & (col[None, :] < c0[:, None] + 16)
    cm = np.where(inwin.T, 0.0, NEG).astype(np.float32)
    cm = np.concatenate([cm, cm], 0)
    return mv, sel, cm

def na_bias_gather(rpb):
    kc = np.arange(64)[:, None]; qc = np.arange(64)[None, :]
    dc = np.clip(kc - qc + 15, 0, 30)
    G = np.zeros((128, 8, 16, 64), np.float32)
    for idxp in range(16):
        for krl in range(2):
            dr = 7 - idxp + krl
            row = dr + 7
            if not (0 <= row <= 14):
                row = 0
            G[krl * 64:(krl + 1) * 64, :, idxp, :] = np.transpose(rpb[:, row][:, dc], (1, 0, 2))
    return G

def hyena_consts(N):
    t = np.arange(N, dtype=np.float64)[:, None]; f = np.arange(N, dtype=np.float64)[None, :]
    ang = 2.0 * np.pi * (f + 0.5) * t / (2.0 * N)
    Cm = np.cos(ang).astype(np.float32); Sm = np.sin(ang).astype(np.float32)
    bands = 16
    tt = np.arange(N, dtype=np.float32)
    t01 = np.linspace(0.0, 1.0, N, dtype=np.float32)[:, None]
    a2 = (np.float32(2.0 * np.pi) * tt / np.float32(N))[:, None] * np.linspace(1e-4, bands - 1, bands, dtype=np.float32)
    z = np.concatenate([t01, np.cos(a2), -np.sin(a2)], -1).astype(np.float32)
    max_decay = np.log(1e-2) / 0.3; min_decay = np.log(1e-2) / 1.5
    deltas = np.abs(np.linspace(min_decay, max_decay, 512, dtype=np.float32))
    decay = np.exp(-t01 * deltas).astype(np.float32)
    import ml_dtypes
    nch = N // 128

    def tiles(M):
        a = M.reshape(nch, 128, nch, 128)
        return np.ascontiguousarray(a.transpose(2, 1, 0, 3)).astype(ml_dtypes.bfloat16)
    return dict(c=tiles(Cm), s=tiles(Sm), ct=tiles(np.ascontiguousarray(Cm.T)), st=tiles(np.ascontiguousarray(Sm.T)),
                zT=np.ascontiguousarray(z.T), decay=decay)

def s5_masks():
    m = np.zeros((2, 2, 128, 256), np.float32)
    for a in range(2):
        for il in range(8):
            i = a * 8 + il
            for j in range(16):
                if j >= i:
                    m[0, a, il * 16:(il + 1) * 16, j * 16:(j + 1) * 16] = 1.0
                if j <= i:
                    m[1, a, il * 16:(il + 1) * 16, j * 16:(j + 1) * 16] = 1.0
    return m

import math
import numpy as np

D = 1024
KC = 8
NCTX = 256
NLAT = 2048
T = NCTX + NLAT
FH = 2816
FHC = FH // 128
EPS = 1e-6
N_IN = 8256


class K:
    pass


def declare_inputs(P, nl):
    nc = P.nc
    I = {}

    def inp(name, shape):
        I[name] = nc.dram_tensor(name, list(shape), F32, kind="ExternalInput").ap()

    inp("xT", [D, T])
    inp("cvec", [D, 2])
    inp("ident", [128, 128])
    inp("w_mod", [nl, D, 9 * D])
    inp("b_mod", [nl, 9 * D])
    inp("norm_g", [nl, 6, D])
    inp("ffn_w_in", [nl, 2, D, 2 * FH])
    inp("ffn_w_out", [nl, 2, FH, D])
    return I


def setup_common(P, k):
    k.ident = P.sbuf("ident", [128, 128], F32)
    P.dma(k.ident[:], k.I["ident"])
    k.ident_bf = P.sbuf("ident_bf", [128, 128], BF16)
    P.copy(k.ident_bf[:], k.ident[:])
    k.ones_bf = P.sbuf("ones_bf", [128, 128], BF16)
    P.memset(k.ones_bf[:], 1.0)
    k.eps_col = P.sbuf("eps_col", [128, 1], F32)
    P.memset(k.eps_col[:], EPS)
    k.xres = P.nc.dram_tensor("xres", [128, KC, T], F32, kind="Internal").ap()
    k.ps = [P.psum("psb%d" % i) for i in range(8)]
    cv = P.sbuf("cv", [128, KC, 2], F32)
    P.dma(cv[:], k.I["cvec"].rearrange("(c p) n -> p c n", p=128))
    k.actv = P.sbuf("actv", [128, KC, 2], BF16)
    P.act(k.actv[:], cv[:], AF.Silu)
    k.modT = P.sbuf("modT", [128, 72, 2], F32)
    k.normg = P.sbuf("normg", [128, 48], F32)
    k.Asc = P.sbuf("Asc", [128, 3, KC, 2], F32)
    k.Bsh = P.sbuf("Bsh", [128, 3, KC, 2], F32)
    k.Gg = P.sbuf("Gg", [128, 3, KC, 2], F32)


def layer_mods(P, k, l):
    nc = P.nc
    wm = k.I["w_mod"][l].rearrange("(c p) n -> p c n", p=128)
    bm_t = P.sbuf("bm_t", [72, 128], F32)
    P.dma(bm_t[:], k.I["b_mod"][l].rearrange("(m f) -> m f", f=128))
    ng_t = P.sbuf("ng_t", [48, 128], F32)
    P.dma(ng_t[:], k.I["norm_g"][l].rearrange("g (c f) -> (g c) f", f=128))
    ps_m = k.ps[0]
    ps_t = k.ps[1]
    wt = [P.sbuf("wmod%d" % i, [128, KC, 512], BF16) for i in range(4)]
    for j in range(18):
        w = wt[j % 4]
        P.dma(w[:], wm[:, :, j * 512:(j + 1) * 512], q="pool")
        for mm in range(4):
            m = j * 4 + mm
            for c in range(KC):
                P.mm(ps_m[:, 2 * m:2 * m + 2], w[:, c, mm * 128:(mm + 1) * 128], k.actv[:, c, :],
                     start=(c == 0), stop=(c == KC - 1))
    P.transpose(ps_t[:, 0:72], bm_t[:], k.ident[0:72, 0:72])
    bmT = P.sbuf("bmT", [128, 72], F32)
    P.copy(bmT[:], ps_t[:, 0:72])
    P.tt(k.modT[:], ps_m[:, 0:144].rearrange("p (m s) -> p m s", s=2),
         bmT[:].unsqueeze(2).broadcast_to([128, 72, 2]), ALU.add)
    P.transpose(ps_t[:, 128:176], ng_t[:], k.ident[0:48, 0:48])
    P.copy(k.normg[:], ps_t[:, 128:176])
    for s in range(3):
        base = 3 * s
        gpre = k.normg[:, (2 * s) * 8:(2 * s + 1) * 8].unsqueeze(2).broadcast_to([128, KC, 2])
        gpost = k.normg[:, (2 * s + 1) * 8:(2 * s + 2) * 8].unsqueeze(2).broadcast_to([128, KC, 2])
        P.stt(k.Asc[:, s], k.modT[:, (base + 1) * 8:(base + 2) * 8, :], 1.0, gpre, ALU.add, ALU.mult)
        P.copy(k.Bsh[:, s], k.modT[:, base * 8:(base + 1) * 8, :])
        P.stt(k.Gg[:, s], k.modT[:, (base + 2) * 8:(base + 3) * 8, :], (1.0 if s == 1 else 0.5), gpost, ALU.mult, ALU.mult)


def sumsq_rstd(P, k, src_fn, nchunks, subs, rstd, ps_ss, sq_tiles, inv_n):
    for (o, w) in subs:
        for c in range(nchunks):
            sq = sq_tiles[c % len(sq_tiles)]
            P.act(sq[:, :w], src_fn(c, o, w), AF.Square)
            P.mm(ps_ss[:, :w], k.ones_bf[:], sq[:, :w], start=(c == 0), stop=(c == nchunks - 1))
        P.act(rstd[:, o:o + w], ps_ss[:, :w], AF.Sqrt, bias=k.eps_col[:], scale=inv_n)
        P.recip(rstd[:, o:o + w], rstd[:, o:o + w])


def ffn_sublayer(P, k, l, s, blocks):
    fi = s // 2
    w_in = k.I["ffn_w_in"][l, fi].rearrange("(c p) n -> p c n", p=128)
    w_out = k.I["ffn_w_out"][l, fi].rearrange("(j p) n -> p j n", p=128)
    maxw = max(sum(w for (_, w, _) in b) for b in blocks)
    hbs = [P.sbuf("ffn_h%d" % i, [128, KC, maxw], BF16) for i in range(2)]
    gb = P.sbuf("ffn_g", [128, FHC, maxw], BF16)
    ob = P.sbuf("ffn_o", [128, KC, maxw], BF16)
    xs = [P.sbuf("ffn_x%d" % i, [128, KC, 512], F32) for i in range(2)]
    rstd = [P.sbuf("ffn_rstd%d" % i, [128, 512], F32) for i in range(2)]
    tmp = [P.sbuf("ffn_tmp%d" % i, [128, 512], F32) for i in range(2)]
    sq = [P.sbuf("ffn_sq%d" % i, [128, 512], BF16) for i in range(2)]
    sl = [P.sbuf("ffn_sl%d" % i, [128, 512], F32) for i in range(2)]
    wi = [P.sbuf("ffn_wi%d" % i, [128, KC, 1024], BF16) for i in range(2)]
    wo = [P.sbuf("ffn_wo%d" % i, [128, FHC, 128], BF16) for i in range(2)]
    ps_ss = k.ps[0]
    ps_a = [k.ps[1], k.ps[2]]
    ps_b = [k.ps[3], k.ps[4]]
    ps_o = [k.ps[5], k.ps[6]]
    st_ = {"cnt": 0, "x": 0, "w": 0}

    def subs_of(blk):
        subs = []
        o = 0
        for (c0, w, st) in blk:
            subs.append((o, c0, w, st))
            o += w
        return subs

    def eng3(c):
        return "dve"

    def prenorm(blk, hb):
        for (o, c0, w, st) in subs_of(blk):
            xb = xs[st_["x"] % 2]
            rs = rstd[st_["x"] % 2]
            st_["x"] += 1
            P.dma(xb[:, :, :w], k.xres[:, :, c0:c0 + w])
            sumsq_rstd(P, k, lambda c, oo, ww, xb=xb: xb[:, c, oo:oo + ww], KC, [(0, w)], rs, ps_ss, sq, 1.0 / D)
            for c in range(KC):
                t = tmp[c % 2]
                P.tt(t[:, :w], xb[:, c, :w], rs[:, :w], ALU.mult, e=eng3(c))
                P.ts(hb[:, c, o:o + w], t[:, :w], k.Asc[:, s, c, st:st + 1], k.Bsh[:, s, c, st:st + 1],
                     op0=ALU.mult, op1=ALU.add)

    def hidden(blk, hb):
        subs = subs_of(blk)
        for j4 in range((FHC + 3) // 4):
            w = wi[st_["w"] % 2]
            st_["w"] += 1
            nj = min(4, FHC - 4 * j4)
            P.dma(w[:, :, 0:nj * 128], w_in[:, :, j4 * 512:j4 * 512 + nj * 128], q="pool")
            P.dma(w[:, :, 512:512 + nj * 128], w_in[:, :, FH + j4 * 512:FH + j4 * 512 + nj * 128], q="pool")
            for jj in range(nj):
                j = 4 * j4 + jj
                for (o, c0, ww, st) in subs:
                    cnt = st_["cnt"]
                    pa = ps_a[cnt % 2]
                    pb = ps_b[cnt % 2]
                    slt = sl[cnt % 2]
                    st_["cnt"] += 1
                    for c in range(KC):
                        P.mm(pa[:, :ww], w[:, c, jj * 128:(jj + 1) * 128], hb[:, c, o:o + ww],
                             start=(c == 0), stop=(c == KC - 1))
                    for c in range(KC):
                        P.mm(pb[:, :ww], w[:, c, 512 + jj * 128:512 + (jj + 1) * 128], hb[:, c, o:o + ww],
                             start=(c == 0), stop=(c == KC - 1))
                    P.act(slt[:, :ww], pa[:, :ww], AF.Silu)
                    P.tt(gb[:, j, o:o + ww], slt[:, :ww], pb[:, :ww], ALU.mult)

    def outproj(blk):
        subs = subs_of(blk)
        for c in range(KC):
            w = wo[c % 2]
            P.dma(w[:], w_out[:, :, c * 128:(c + 1) * 128], q="pool")
            for (o, c0, ww, st) in subs:
                po = ps_o[st_["cnt"] % 2]
                st_["cnt"] += 1
                for j in range(FHC):
                    P.mm(po[:, :ww], w[:, j, :], gb[:, j, o:o + ww], start=(j == 0), stop=(j == FHC - 1))
                P.copy(ob[:, c, o:o + ww], po[:, :ww], e="act")

    def postnorm(blk):
        subs = subs_of(blk)
        base = st_["x"]
        st_["x"] += len(subs)
        P.dma(xs[base % 2][:, :, :subs[0][2]], k.xres[:, :, subs[0][1]:subs[0][1] + subs[0][2]])
        for si, (o, c0, w, st) in enumerate(subs):
            xb = xs[(base + si) % 2]
            rs = rstd[(base + si) % 2]
            if si + 1 < len(subs):
                (o2, c2, w2, st2) = subs[si + 1]
                P.dma(xs[(base + si + 1) % 2][:, :, :w2], k.xres[:, :, c2:c2 + w2])
            sumsq_rstd(P, k, lambda c, oo, ww, o=o: ob[:, c, o + oo:o + oo + ww], KC, [(0, w)], rs, ps_ss, sq, 1.0 / D)
            for c in range(KC):
                t = tmp[c % 2]
                P.stt(t[:, :w], ob[:, c, o:o + w], k.Gg[:, s, c, st:st + 1], rs[:, :w], ALU.mult, ALU.mult)
                P.tt(xb[:, c, :w], xb[:, c, :w], t[:, :w], ALU.add, e=eng3(c))
            P.dma(k.xres[:, :, c0:c0 + w], xb[:, :, :w], q="sp")

    prenorm(blocks[0], hbs[0])
    for bi, blk in enumerate(blocks):
        hidden(blk, hbs[bi % 2])
        if bi + 1 < len(blocks):
            prenorm(blocks[bi + 1], hbs[(bi + 1) % 2])
        outproj(blk)
        postnorm(blk)


FULL_BLOCKS = [
    [(0, 256, 1), (256, 512, 0), (768, 384, 0)],
    [(1152, 384, 0), (1536, 384, 0), (1920, 384, 0)],
]
LAT_BLOCKS = [
    [(256, 512, 0), (768, 512, 0)],
    [(1280, 512, 0), (1792, 512, 0)],
]


CT0 = 2
LT0 = 260
HW = 2310
ALLSUBS = [(0, 256, 1), (256, 512, 0), (768, 512, 0), (1280, 512, 0), (1792, 512, 0)]
LATSUBS = ALLSUBS[1:]
C_CKV, C_KR, C_NK, C_NV, C_U, C_CQ, C_NQ, C_HY, C_GT = 0, 256, 320, 832, 1344, 1856, 2112, 2624, 4160


def hcol(xc):
    return xc + CT0 if xc < NCTX else xc - NCTX + LT0


def declare_mixer_inputs(P, I, nl):
    nc = P.nc

    def inp(name, shape):
        I[name] = nc.dram_tensor(name, list(shape), F32, kind="ExternalInput").ap()
    inp("w_in", [nl, D, N_IN])
    inp("mla_g_q", [nl, 256]); inp("mla_g_kv", [nl, 256])
    inp("mla_w_uq", [nl, 256, 4, 192]); inp("mla_w_ukv", [nl, 256, 4, 256])
    inp("w_branch", [nl, 4, 512, D]); inp("w_out", [nl, D, D])
    inp("rope_c", [64, NLAT]); inp("rope_s", [64, NLAT])


def mixer_setup(P, k):
    nc = P.nc
    k.br = [nc.dram_tensor("br%d" % n, [512, T], BF16, kind="Internal").ap() for n in range(4)]
    k.hmx = nc.dram_tensor("hmx", [128, KC, HW], BF16, kind="Internal").ap()


def alloc_hmix(P, k):
    k.hmix = P.sbuf("hmix", [128, KC, HW], BF16)
    for c0 in (0, 258, 2308):
        P.memset(k.hmix[:, :, c0:c0 + 2], 0.0)


def mixer_modnorm(P, k):
    with P.scope():
        rstd = P.sbuf("mn_rstd", [128, 512], F32)
        tmp = [P.sbuf("mn_tmp%d" % i, [128, 512], F32) for i in range(2)]
        sq = [P.sbuf("mn_sq%d" % i, [128, 512], BF16) for i in range(2)]
        xt = [P.sbuf("mn_x%d" % i, [128, KC, 512], F32) for i in range(2)]
        for si, (c0, w, st) in enumerate(ALLSUBS):
            xb = xt[si % 2]
            P.dma(xb[:, :, :w], k.xres[:, :, c0:c0 + w])
            sumsq_rstd(P, k, lambda c, oo, ww, xb=xb: xb[:, c, oo:oo + ww], KC, [(0, w)], rstd, k.ps[0], sq, 1.0 / D)
            h0 = hcol(c0)
            for c in range(KC):
                t = tmp[c % 2]
                P.stt(t[:, :w], xb[:, c, 0:w], k.Asc[:, 1, c, st:st + 1], rstd[:, 0:w], ALU.mult, ALU.mult)
                P.act(k.hmix[:, c, h0:h0 + w], t[:, :w], AF.Identity, bias=k.Bsh[:, 1, c, st:st + 1])
    P.dma(k.hmx, k.hmix[:], q="sp")


def load_col_vec(P, dst, src_1d, nchunk):
    P.dma(dst, src_1d.rearrange("(c p) -> p c", p=128), allow_slow_non_contiguous=True)


def mla_branch(P, k, l, ctx_out):
    nc = P.nc
    I = k.I
    w_in = I["w_in"][l].rearrange("(c p) n -> p c n", p=128)
    SC = 192.0 ** -0.5
    subs = ALLSUBS
    qsubs = ALLSUBS if ctx_out else LATSUBS
    with P.scope():
        wckv = P.sbuf("wckv", [128, KC, 256], BF16)
        P.dma(wckv[:], w_in[:, :, C_CKV:C_CKV + 256], q="pool")
        wkr = P.sbuf("wkr", [128, KC, 128], BF16)
        P.dma(wkr[:, :, 0:64], w_in[:, :, C_KR:C_KR + 64], q="pool")
        for a in range(2):
            for hf in range(2):
                P.dma(wkr[:, :, 64 + a * 32 + hf * 16:64 + a * 32 + hf * 16 + 16],
                      w_in[:, :, C_KR + a * 32 + (1 - hf) * 16:C_KR + a * 32 + (1 - hf) * 16 + 16], q="pool")
        wcq = P.sbuf("wcq", [128, KC, 256], BF16)
        P.dma(wcq[:], w_in[:, :, C_CQ:C_CQ + 256], q="pool")
        wukv = P.sbuf("wukv", [128, 2, 4, 256], BF16)
        P.dma(wukv[:], I["mla_w_ukv"][l].rearrange("(c p) h e -> p c h e", p=128), q="pool")
        wuq = P.sbuf("wuq", [128, 2, 4, 192], BF16)
        P.dma(wuq[:], I["mla_w_uq"][l].rearrange("(c p) h e -> p c h e", p=128), q="pool")
        wuqs = P.sbuf("wuqs", [128, 2, 4, 64], BF16)
        uq_r = I["mla_w_uq"][l].rearrange("(c p) h e -> p c h e", p=128)
        for a in range(2):
            for hf in range(2):
                for c in range(2):
                    P.dma(wuqs[:, c, :, a * 32 + hf * 16:a * 32 + hf * 16 + 16],
                          uq_r[:, c, :, 128 + a * 32 + (1 - hf) * 16:128 + a * 32 + (1 - hf) * 16 + 16], q="pool")
        gkv = P.sbuf("gkv", [128, 2], F32)
        load_col_vec(P, gkv[:], I["mla_g_kv"][l], 2)
        gq = P.sbuf("gq", [128, 2], F32)
        load_col_vec(P, gq[:], I["mla_g_q"][l], 2)
        ropc_t = [P.sbuf("ropc%d" % i, [64, 512], F32) for i in range(2)]
        rops_t = [P.sbuf("rops%d" % i, [64, 512], F32) for i in range(2)]
        rcnt = [0]

        def rope_tabs(l0, w):
            i = rcnt[0] % 2
            rcnt[0] += 1
            P.dma(ropc_t[i][:, :w], I["rope_c"][:, l0:l0 + w])
            P.dma(rops_t[i][:, :w], I["rope_s"][:, l0:l0 + w])
            return ropc_t[i], rops_t[i]
        nkv = P.sbuf("nkv", [128, 2, T], BF16)
        nq = P.sbuf("nq", [128, 2, T], BF16)
        krope = P.sbuf("krope", [64, T], BF16)
        vall = P.sbuf("vall", [128, 18, 128], BF16)
        aT = P.sbuf("aT", [128, T], BF16)
        raw = P.sbuf("raw", [128, 2, 512], F32)
        rstd = P.sbuf("rstd", [128, 512], F32)
        sq = [P.sbuf("sq%d" % i, [128, 512], BF16) for i in range(2)]
        t1 = P.sbuf("t1", [128, 512], F32)
        t2 = P.sbuf("t2", [128, 512], F32)
        ps = k.ps

        def lowrank_norm(wt, gvec, dst):
            for (c0, w, st) in (subs if dst is nkv else qsubs):
                h0 = hcol(c0)
                for m in range(2):
                    for c in range(KC):
                        P.mm(ps[1 + m][:, :w], wt[:, c, m * 128:(m + 1) * 128], k.hmix[:, c, h0:h0 + w],
                             start=(c == 0), stop=(c == KC - 1))
                    P.copy(raw[:, m, :w], ps[1 + m][:, :w], e="act")
                sumsq_rstd(P, k, lambda c, oo, ww: raw[:, c, oo:oo + ww], 2, [(0, w)], rstd, ps[0], sq, 1.0 / 256)
                for m in range(2):
                    P.stt(dst[:, m, c0:c0 + w], raw[:, m, :w], gvec[:, m:m + 1], rstd[:, :w], ALU.mult, ALU.mult)

        lowrank_norm(wckv, gkv, nkv)
        lowrank_norm(wcq, gq, nq)
        for (c0, w, st) in subs:
            h0 = hcol(c0)
            for hh in range(2):
                for c in range(KC):
                    P.mm(ps[1 + hh][0:64, :w], wkr[:, c, hh * 64:(hh + 1) * 64], k.hmix[:, c, h0:h0 + w],
                         start=(c == 0), stop=(c == KC - 1))
            if st == 1:
                P.copy(krope[:, c0:c0 + w], ps[1][0:64, :w], e="act")
            else:
                l0 = c0 - NCTX
                rc, rs = rope_tabs(l0, w)
                P.tt(t1[0:64, :w], ps[1][0:64, :w], rc[:, :w], ALU.mult)
                P.tt(t2[0:64, :w], ps[2][0:64, :w], rs[:, :w], ALU.mult)
                P.tt(krope[:, c0:c0 + w], t1[0:64, :w], t2[0:64, :w], ALU.add, e="pool")
        knT = P.sbuf("knT", [128, T], BF16)
        qnT = P.sbuf("qnT", [128, T], BF16)
        qrope = P.sbuf("qrope", [64, T], BF16)
        pT = [P.sbuf("pT%d" % i, [128, 512], BF16) for i in range(3)]
        rden = P.sbuf("rden", [128, 512], F32)
        for hd in range(4):
            for tc in range(18):
                pv = ps[1 + tc % 2]
                for c in range(2):
                    P.mm(pv[:, 0:128], nkv[:, c, tc * 128:(tc + 1) * 128], wukv[:, c, hd, 128:256], start=(c == 0), stop=(c == 1))
                P.copy(vall[:, tc, :], pv[:, 0:128], e=("act" if tc % 2 else "dve"))
            for (c0, w, st) in subs:
                for c in range(2):
                    P.mm(ps[1][:, :w], wukv[:, c, hd, 0:128], nkv[:, c, c0:c0 + w], start=(c == 0), stop=(c == 1))
                P.copy(knT[:, c0:c0 + w], ps[1][:, :w], e="act")
            for (c0, w, st) in qsubs:
                for c in range(2):
                    P.mm(ps[1][:, :w], wuq[:, c, hd, 0:128], nq[:, c, c0:c0 + w], start=(c == 0), stop=(c == 1))
                P.copy(qnT[:, c0:c0 + w], ps[1][:, :w], e="act")
                for c in range(2):
                    P.mm(ps[2][0:64, :w], wuq[:, c, hd, 128:192], nq[:, c, c0:c0 + w], start=(c == 0), stop=(c == 1))
                if st == 1:
                    P.copy(qrope[:, c0:c0 + w], ps[2][0:64, :w], e="dve")
                else:
                    for c in range(2):
                        P.mm(ps[3][0:64, :w], wuqs[:, c, hd, :], nq[:, c, c0:c0 + w], start=(c == 0), stop=(c == 1))
                    l0 = c0 - NCTX
                    rc, rs = rope_tabs(l0, w)
                    P.tt(t1[0:64, :w], ps[2][0:64, :w], rc[:, :w], ALU.mult)
                    P.tt(t2[0:64, :w], ps[3][0:64, :w], rs[:, :w], ALU.mult)
                    P.tt(qrope[:, c0:c0 + w], t1[0:64, :w], t2[0:64, :w], ALU.add, e="pool")
            cnt = 0
            for qi, (c0, w, st) in enumerate(qsubs):
                kcs = list(range(2)) if st == 1 else list(range(18))
                pso, psd = (ps[6], ps[7]) if qi % 2 == 0 else (ps[2], ps[3])
                base = cnt
                cnt += len(kcs)

                def emit_s(i):
                    kc = kcs[i]
                    pss = ps[4 + (base + i) % 2]
                    P.mm(pss[:, :w], knT[:, kc * 128:(kc + 1) * 128], qnT[:, c0:c0 + w], start=True, stop=False)
                    P.mm(pss[:, :w], krope[:, kc * 128:(kc + 1) * 128], qrope[:, c0:c0 + w], start=False, stop=True)
                emit_s(0)
                for i, kc in enumerate(kcs):
                    if i + 1 < len(kcs):
                        emit_s(i + 1)
                    pss = ps[4 + (base + i) % 2]
                    p = pT[(base + i) % 3]
                    P.act(p[:, :w], pss[:, :w], AF.Exp, scale=SC)
                    P.mm(pso[:, :w], vall[:, kc, :], p[:, :w], start=(i == 0), stop=(i == len(kcs) - 1))
                    P.mm(psd[:, :w], k.ones_bf[:], p[:, :w], start=(i == 0), stop=(i == len(kcs) - 1))
                P.recip(rden[:, :w], psd[:, :w])
                P.tt(aT[:, c0:c0 + w], pso[:, :w], rden[:, :w], ALU.mult)
            cols0 = 0 if ctx_out else NCTX
            P.dma(k.br[0][hd * 128:(hd + 1) * 128, cols0:T], aT[:, cols0:T], q="sp")


def declare_na_inputs(P, I, nl):
    nc = P.nc

    def inp(name, shape):
        I[name] = nc.dram_tensor(name, list(shape), F32, kind="ExternalInput").ap()
    inp("na_G", [nl, 128, 8 * 16 * 64])
    inp("na_mv", [2, 16 * 128]); inp("na_sel", [2, 128]); inp("na_cm", [128, 64])


def na_branch(P, k, l, ctx_out, geo):
    nc = P.nc
    I = k.I
    w_in = I["w_in"][l].rearrange("(c p) n -> p c n", p=128)
    ps = k.ps
    with P.scope():
        BP = P.sbuf("na_BP", [128, 8, 16, 64], BF16)
        cm = P.sbuf("na_cm", [128, 64], F32)
        P.dma(cm[:], I["na_cm"])
        gt = [P.sbuf("na_gt%d" % i, [128, 16, 64], F32) for i in range(2)]
        Gr = I["na_G"][l].rearrange("p (h i q) -> p h i q", h=8, i=16)
        for h in range(8):
            P.dma(gt[h % 2][:], Gr[:, h])
            P.tt(BP[:, h], gt[h % 2][:], cm[:].unsqueeze(1).broadcast_to([128, 16, 64]), ALU.add)
        mvf = P.sbuf("na_mvf", [2, 16 * 128], F32)
        P.dma(mvf[:], I["na_mv"])
        mv = P.sbuf("na_mv", [2, 16 * 128], BF16)
        P.copy(mv[:], mvf[:])
        self_f = P.sbuf("na_self", [2, 128], F32)
        P.dma(self_f[:], I["na_sel"])
        sel = P.sbuf("na_sel", [2, 128], BF16)
        P.copy(sel[:], self_f[:])
        wk = P.sbuf("na_wk", [128, KC, 128], BF16)
        wq = P.sbuf("na_wq", [128, KC, 128], BF16)
        wv = P.sbuf("na_wv", [128, KC, 128], BF16)
        KT = P.sbuf("na_KT", [128, T], BF16)
        QT = P.sbuf("na_QT", [128, T], BF16)
        V = P.sbuf("na_V", [128, 18, 128], BF16)
        dT = P.sbuf("na_dT", [128, T], BF16)
        pT = [P.sbuf("na_pT%d" % i, [128, 128], BF16) for i in range(3)]
        rden = P.sbuf("na_rden", [128, 128], F32)
        qsubs = ALLSUBS if ctx_out else LATSUBS
        cnt = 0
        for hp in range(4):
            P.dma(wk[:], w_in[:, :, C_NK + hp * 128:C_NK + (hp + 1) * 128], q="pool")
            P.dma(wq[:], w_in[:, :, C_NQ + hp * 128:C_NQ + (hp + 1) * 128], q="pool")
            P.dma(wv[:], w_in[:, :, C_NV + hp * 128:C_NV + (hp + 1) * 128], q="pool")
            for (c0, w, st) in ALLSUBS:
                h0 = hcol(c0)
                for c in range(KC):
                    P.mm(ps[1][:, :w], wk[:, c, :], k.hmix[:, c, h0:h0 + w], start=(c == 0), stop=(c == KC - 1))
                P.copy(KT[:, c0:c0 + w], ps[1][:, :w], e="act")
            for (c0, w, st) in qsubs:
                h0 = hcol(c0)
                for c in range(KC):
                    P.mm(ps[2][:, :w], wq[:, c, :], k.hmix[:, c, h0:h0 + w], start=(c == 0), stop=(c == KC - 1))
                P.ts(QT[:, c0:c0 + w], ps[2][:, :w], 0.125, None, op0=ALU.mult)
            for tc in range(18):
                h0 = hcol(tc * 128)
                pv = ps[1 + tc % 2]
                for c in range(KC):
                    P.mm(pv[:, 0:128], k.hmix[:, c, h0:h0 + 128], wv[:, c, :], start=(c == 0), stop=(c == KC - 1))
                P.copy(V[:, tc, :], pv[:, 0:128], e=("act" if tc % 2 else "dve"))
            qblocks = []
            if ctx_out:
                qblocks += [(0, []), (128, [])]
            for i in range(16):
                qblocks.append((NCTX + i * 128, geo[i]))
            for qi, (q0, loc) in enumerate(qblocks):
                chunks = [(0, None, None), (1, None, None)] + [(2 + j, idxp, combo) for (j, idxp, combo) in loc]
                pso, psd = (ps[6], ps[7]) if qi % 2 == 0 else (ps[2], ps[3])
                work = [(hh, ci) for ci in range(len(chunks)) for hh in range(2)]
                base = cnt
                cnt += len(work)

                def emit_s(wi):
                    hh, ci = work[wi]
                    kc, idxp, combo = chunks[ci]
                    h = 2 * hp + hh
                    pr = slice(hh * 64, (hh + 1) * 64)
                    pss = ps[4 + (base + wi) % 2]
                    last_s = (idxp is None)
                    P.mm(pss[:, 0:128], KT[pr, kc * 128:(kc + 1) * 128], QT[pr, q0:q0 + 128], start=True, stop=last_s)
                    if idxp is not None:
                        need_mask = combo != 0
                        P.mm(pss[:, 0:128], k.ident_bf[:], BP[:, h, idxp:idxp + 2, :], start=False, stop=not need_mask)
                        if need_mask:
                            P.mm(pss[:, 0:128], mv[0:2, combo * 128:(combo + 1) * 128], sel[0:2, :], start=False, stop=True)
                emit_s(0)
                for wi, (hh, ci) in enumerate(work):
                    if wi + 1 < len(work):
                        emit_s(wi + 1)
                    kc = chunks[ci][0]
                    pr = slice(hh * 64, (hh + 1) * 64)
                    pss = ps[4 + (base + wi) % 2]
                    p = pT[(base + wi) % 3]
                    P.act(p[:], pss[:, 0:128], AF.Exp)
                    P.mm(pso[pr, 0:128], V[:, kc, pr], p[:], start=(ci == 0), stop=(ci == len(chunks) - 1))
                    P.mm(psd[pr, 0:128], k.ones_bf[:, 0:64], p[:], start=(ci == 0), stop=(ci == len(chunks) - 1))
                P.recip(rden[:], psd[:, 0:128])
                P.tt(dT[:, q0:q0 + 128], pso[:, 0:128], rden[:], ALU.mult)
            cols0 = 0 if ctx_out else NCTX
            P.dma(k.br[3][hp * 128:(hp + 1) * 128, cols0:T], dT[:, cols0:T], q="sp")


def post_norm_residual(P, k, ob, s, subs_local, rstd, ps_ss, sq, tmp, xb):
    for (o, c0, w, st) in subs_local:
        P.dma(xb[:, :, o:o + w], k.xres[:, :, c0:c0 + w])
        sumsq_rstd(P, k, lambda c, oo, ww: ob[:, c, oo:oo + ww], KC, [(o, w)], rstd, ps_ss, sq, 1.0 / D)
        for c in range(KC):
            t = tmp[c % 2]
            P.stt(t[:, :w], ob[:, c, o:o + w], k.Gg[:, s, c, st:st + 1], rstd[:, o:o + w], ALU.mult, ALU.mult)
            P.tt(xb[:, c, o:o + w], xb[:, c, o:o + w], t[:, :w], ALU.add, e=("pool" if c % 3 == 2 else "dve"))
        P.dma(k.xres[:, :, c0:c0 + w], xb[:, :, o:o + w], q="sp")


def merge_phase(P, k, l, ctx_out):
    I = k.I
    w_in = I["w_in"][l].rearrange("(c p) n -> p c n", p=128)
    ps = k.ps
    subs = ALLSUBS if ctx_out else LATSUBS
    with P.scope():
        wg = P.sbuf("mg_wg", [128, KC, 4 * D], BF16)
        wb = P.sbuf("mg_wb", [128, 4, 4, D], BF16)
        for half in range(2):
            for j in range(half, 8, 2):
                P.dma(wg[:, :, j * 512:(j + 1) * 512], w_in[:, :, C_GT + j * 512:C_GT + (j + 1) * 512], q="pool")
            for n in range(4):
                P.dma(wb[:, n, :, half * 512:(half + 1) * 512],
                      I["w_branch"][l, n].rearrange("(kk p) d -> p kk d", p=128)[:, :, half * 512:(half + 1) * 512], q="pool")
        wo = P.sbuf("mg_wo", [128, KC, D], BF16)
        for j in range(2):
            P.dma(wo[:, :, j * 512:(j + 1) * 512], I["w_out"][l].rearrange("(kk p) d -> p kk d", p=128)[:, :, j * 512:(j + 1) * 512], q="pool")
        brt = [[P.sbuf("mg_br%d_%d" % (n, i), [128, 4, 512], BF16) for n in range(4)] for i in range(1)]
        mt = P.sbuf("mg_mt", [128, KC, 512], BF16)
        ob = P.sbuf("mg_ob", [128, KC, 512], BF16)
        sg = [P.sbuf("mg_sg%d" % i, [128, 512], F32) for i in range(2)]
        acc = P.sbuf("mg_acc", [128, 512], F32)
        tm = [P.sbuf("mg_tm%d" % i, [128, 512], F32) for i in range(2)]
        rstd = P.sbuf("mg_rstd", [128, 512], F32)
        sq = [P.sbuf("mg_sq%d" % i, [128, 512], BF16) for i in range(2)]
        xbm = P.sbuf("mg_x", [128, KC, 512], F32)
        hbt = [P.sbuf("mg_h%d" % i, [128, KC, 512], BF16) for i in range(2)]
        cnt = 0
        for si, (c0, w, st) in enumerate(subs):
            h0 = hcol(c0)
            hb_ = hbt[si % 2]
            P.dma(hb_[:, :, :w], k.hmx[:, :, h0:h0 + w])
            bt = brt[0]
            for n in range(4):
                P.dma(bt[n][:, :, :w], k.br[n].rearrange("(c p) t -> p c t", p=128)[:, :, c0:c0 + w])
            for dc in range(KC):
                for n in range(4):
                    pg = ps[1 + cnt % 2]
                    pp = ps[3 + cnt % 2]
                    s_ = sg[cnt % 2]
                    cnt += 1
                    col = n * D + dc * 128
                    for c in range(KC):
                        P.mm(pg[:, :w], wg[:, c, col:col + 128], hb_[:, c, :w], start=(c == 0), stop=(c == KC - 1))
                    for kk in range(4):
                        P.mm(pp[:, :w], wb[:, n, kk, dc * 128:(dc + 1) * 128], bt[n][:, kk, :w], start=(kk == 0), stop=(kk == 3))
                    P.act(s_[:, :w], pg[:, :w], AF.Sigmoid)
                    if n == 0:
                        P.tt(acc[:, :w], s_[:, :w], pp[:, :w], ALU.mult)
                    else:
                        t = tm[n % 2]
                        P.tt(t[:, :w], s_[:, :w], pp[:, :w], ALU.mult)
                        if n < 3:
                            P.tt(acc[:, :w], acc[:, :w], t[:, :w], ALU.add, e="pool")
                        else:
                            P.tt(mt[:, dc, :w], acc[:, :w], t[:, :w], ALU.add, e="pool")
            for dc in range(KC):
                po = ps[5 + dc % 2]
                for kk in range(KC):
                    P.mm(po[:, :w], wo[:, kk, dc * 128:(dc + 1) * 128], mt[:, kk, :w], start=(kk == 0), stop=(kk == KC - 1))
                P.copy(ob[:, dc, :w], po[:, :w], e="act")
            post_norm_residual(P, k, ob, 1, [(0, c0, w, st)], rstd, ps[0], sq, tm, xbm)


def declare_hyena_inputs(P, I, nl, with_ctx=True):
    nc = P.nc

    def inp(name, shape):
        I[name] = nc.dram_tensor(name, list(shape), F32, kind="ExternalInput").ap()
    inp("hy_conv_w", [nl, 3, 1536]); inp("hy_conv_b", [nl, 1536]); inp("hy_bias", [nl, 512])
    inp("hy_w1", [nl, 33, 64]); inp("hy_b1", [nl, 64]); inp("hy_freq1", [nl, 64])
    inp("hy_w2", [nl, 64, 64]); inp("hy_b2", [nl, 64]); inp("hy_freq2", [nl, 64]); inp("hy_w3", [nl, 64, 1024])
    for nm, N in (("lat", NLAT), ("ctx", NCTX)):
        if nm == "ctx" and not with_ctx:
            continue
        for t in ("c", "s", "ct", "st"):
            I["dft_%s_%s" % (t, nm)] = nc.dram_tensor("dft_%s_%s" % (t, nm), [N // 128, 128, N // 128, 128], BF16, kind="ExternalInput").ap()
        inp("hy_zT_%s" % nm, [33, N]); inp("hy_decay_%s" % nm, [N, 512])


PI = math.pi


def sin_reduced(P, dst, src, shp, tmp):
    P.ts(tmp, src, PI, -2.0 * PI, op0=ALU.is_gt, op1=ALU.mult)
    P.tt(src, src, tmp, ALU.add)
    P.ts(tmp, src, -PI, 2.0 * PI, op0=ALU.is_lt, op1=ALU.mult)
    P.tt(src, src, tmp, ALU.add)
    P.act(dst, src, AF.Sin)


def hyena_branch(P, k, l, seq):
    nc = P.nc
    I = k.I
    nm, N, hbase, xbase = seq
    NT = N // 128
    NF = NT
    CW = min(512, N)
    w_in = I["w_in"][l].rearrange("(c p) n -> p c n", p=128)
    ps = k.ps
    dft_c = I["dft_c_%s" % nm]
    dft_s = I["dft_s_%s" % nm]
    dft_ct = I["dft_ct_%s" % nm]
    dft_st = I["dft_st_%s" % nm]
    with P.scope():
        Kre = P.sbuf("hy_Kre", [128, NF, 512], BF16)
        Kim = P.sbuf("hy_Kim", [128, NF, 512], BF16)
        Ct = [P.sbuf("hy_Ct%d" % i, [128, NT, 128], BF16) for i in range(2)]
        St = [P.sbuf("hy_St%d" % i, [128, NT, 128], BF16) for i in range(2)]
        tA = P.sbuf("hy_tA", [128, 512], F32)
        tB = P.sbuf("hy_tB", [128, 512], F32)
        tC = P.sbuf("hy_tC", [128, 512], F32)
        tD = P.sbuf("hy_tD", [128, 512], F32)
        with P.scope():
            w1 = P.sbuf("hy_w1", [33, 64], F32); P.dma(w1[:], I["hy_w1"][l])
            w2 = P.sbuf("hy_w2", [64, 64], F32); P.dma(w2[:], I["hy_w2"][l])
            w3 = P.sbuf("hy_w3", [64, 1024], F32); P.dma(w3[:], I["hy_w3"][l])
            cols = P.sbuf("hy_cols", [64, 6], F32)
            for j, nm_ in enumerate(["hy_b1", "hy_freq1", "hy_b2", "hy_freq2"]):
                P.dma(cols[:, j:j + 1], I[nm_][l].rearrange("(p o) -> p o", o=1))
            P.tt(cols[:, 4:5], cols[:, 0:1], cols[:, 1:2], ALU.mult)
            P.tt(cols[:, 5:6], cols[:, 2:3], cols[:, 3:4], ALU.mult)
            zT = P.sbuf("hy_zT", [33, N], F32); P.dma(zT[:], I["hy_zT_%s" % nm])
            h1 = P.sbuf("hy_h1", [64, N], F32)
            h2 = P.sbuf("hy_h2", [64, N], F32)
            filt = P.sbuf("hy_filt", [128, NT, 1024], BF16)
            dec = [P.sbuf("hy_dec%d" % i, [128, 512], F32) for i in range(2)]
            for cb in range(N // CW):
                cs = slice(cb * CW, (cb + 1) * CW)
                P.mm(ps[1][0:64, :CW], w1[:, :], zT[:, cs], start=True, stop=True)
                P.act(tA[0:64, :CW], ps[1][0:64, :CW], AF.Identity, bias=cols[:, 4:5], scale=cols[:, 1:2])
                sin_reduced(P, h1[:, cs], tA[0:64, :CW], None, tB[0:64, :CW])
            for cb in range(N // CW):
                cs = slice(cb * CW, (cb + 1) * CW)
                P.mm(ps[1][0:64, :CW], w2[:, :], h1[:, cs], start=True, stop=True)
                P.act(tA[0:64, :CW], ps[1][0:64, :CW], AF.Identity, bias=cols[:, 5:6], scale=cols[:, 3:4])
                sin_reduced(P, h2[:, cs], tA[0:64, :CW], None, tB[0:64, :CW])
            for tc in range(NT):
                d = dec[tc % 2]
                P.dma(d[:], I["hy_decay_%s" % nm][tc * 128:(tc + 1) * 128, :])
                for hf in range(2):
                    pp = ps[1 + hf]
                    P.mm(pp[:, :], h2[:, tc * 128:(tc + 1) * 128], w3[:, hf * 512:(hf + 1) * 512], start=True, stop=True)
                    P.tt(filt[:, tc, hf * 512:(hf + 1) * 512], pp[:, :], d[:], ALU.mult)
            fsd = P.sbuf("hy_fsd", [128, NT, 1024], BF16)
            for tc in range(NT):
                P.tt(fsd[:, tc, 0:512], filt[:, tc, 0:512], filt[:, tc, 512:1024], ALU.add, e=("dve" if tc % 2 else "pool"))
                P.tt(fsd[:, tc, 512:1024], filt[:, tc, 0:512], filt[:, tc, 512:1024], ALU.subtract, e=("pool" if tc % 2 else "dve"))
            for fc in range(NF):
                c_ = Ct[fc % 2]; s_ = St[fc % 2]
                P.dma(c_[:], dft_c[fc])
                P.dma(s_[:], dft_s[fc])
                pa = ps[1 + 2 * (fc % 2)]; pb = ps[2 + 2 * (fc % 2)]
                for tc in range(NT):
                    P.mm(pa[:, :], c_[:, tc, :], fsd[:, tc, 0:512], start=(tc == 0), stop=(tc == NT - 1))
                for tc in range(NT):
                    P.mm(pb[:, :], s_[:, tc, :], fsd[:, tc, 512:1024], start=(tc == 0), stop=(tc == NT - 1))
                P.copy(Kre[:, fc, :], pa[:, :], e="act")
                P.copy(Kim[:, fc, :], pb[:, :], e="dve")
        s_bf = P.sbuf("hy_s", [128, NT, 512], BF16)
        x0_bf = P.sbuf("hy_x0", [128, NT, 512], BF16)
        with P.scope():
            wz = [P.sbuf("hy_wz%d" % i, [128, KC, 128], BF16) for i in range(3)]
            cwc = P.sbuf("hy_cwc", [128, 3, 12], F32)
            for kk in range(3):
                load_col_vec(P, cwc[:, kk, :], I["hy_conv_w"][l, kk], 12)
            cbc = P.sbuf("hy_cbc", [128, 12], F32)
            load_col_vec(P, cbc[:], I["hy_conv_b"][l], 12)
            zs = P.sbuf("hy_zs", [128, N + 2], F32)
            P.memset(zs[:, 0:1], 0.0)
            P.memset(zs[:, N + 1:N + 2], 0.0)
            uu = P.sbuf("hy_uu", [128, N], F32)
            vT = P.sbuf("hy_vT", [128, N], F32)
            sT = P.sbuf("hy_sT", [128, 4, N], BF16)
            x0T = P.sbuf("hy_x0T", [128, 4, N], BF16)
            wcnt = 0
            for m in range(4):
                for blk in range(3):
                    ch = blk * 4 + m
                    w_ = wz[wcnt % 3]
                    wcnt += 1
                    P.dma(w_[:], w_in[:, :, C_HY + ch * 128:C_HY + (ch + 1) * 128], q="pool")
                    for sb in range(N // CW):
                        pp = ps[1 + sb % 2]
                        for c in range(KC):
                            P.mm(pp[:, :CW], w_[:, c, :], k.hmix[:, c, hbase + sb * CW:hbase + (sb + 1) * CW],
                                 start=(c == 0), stop=(c == KC - 1))
                        P.copy(zs[:, 1 + sb * CW:1 + (sb + 1) * CW], pp[:, :CW], e="act")
                    P.act(uu[:], zs[:, 1:N + 1], AF.Identity, bias=cbc[:, ch:ch + 1], scale=cwc[:, 1, ch:ch + 1])
                    P.stt(uu[:], zs[:, 0:N], cwc[:, 0, ch:ch + 1], uu[:], ALU.mult, ALU.add)
                    if blk == 0:
                        P.stt(vT[:], zs[:, 2:N + 2], cwc[:, 2, ch:ch + 1], uu[:], ALU.mult, ALU.add)
                    elif blk == 1:
                        P.stt(uu[:], zs[:, 2:N + 2], cwc[:, 2, ch:ch + 1], uu[:], ALU.mult, ALU.add)
                        P.tt(sT[:, m, :], vT[:], uu[:], ALU.mult, e="pool")
                    else:
                        P.stt(x0T[:, m, :], zs[:, 2:N + 2], cwc[:, 2, ch:ch + 1], uu[:], ALU.mult, ALU.add)
            for tc in range(NT):
                for si_, (src, dst) in enumerate(((sT, s_bf), (x0T, x0_bf))):
                    pt = ps[3 + (2 * tc + si_) % 4][:].bitcast(BF16)
                    for m in range(4):
                        P.transpose(pt[:, m * 128:(m + 1) * 128], src[:, m, tc * 128:(tc + 1) * 128], k.ident_bf[:])
                    P.copy(dst[:, tc, :], pt[:, 0:512], e=("act" if si_ else "dve"))
        Yre = P.sbuf("hy_Yre", [128, NF, 512], BF16)
        Yim = P.sbuf("hy_Yim", [128, NF, 512], BF16)
        for fc in range(NF):
            c_ = Ct[fc % 2]; s_ = St[fc % 2]
            P.dma(c_[:], dft_c[fc])
            P.dma(s_[:], dft_s[fc])
            pa = ps[1 + 2 * (fc % 2)]; pb = ps[2 + 2 * (fc % 2)]
            for tc in range(NT):
                P.mm(pa[:, :], c_[:, tc, :], s_bf[:, tc, :], start=(tc == 0), stop=(tc == NT - 1))
            for tc in range(NT):
                P.mm(pb[:, :], s_[:, tc, :], s_bf[:, tc, :], start=(tc == 0), stop=(tc == NT - 1))
            P.copy(tA[:], pa[:, :], e="act")
            P.copy(tB[:], pb[:, :], e="act")
            P.tt(tC[:], tA[:], Kre[:, fc, :], ALU.mult)
            P.tt(tD[:], tB[:], Kim[:, fc, :], ALU.mult, e="pool")
            P.tt(Yre[:, fc, :], tC[:], tD[:], ALU.subtract)
            P.tt(tC[:], tA[:], Kim[:, fc, :], ALU.mult, e="pool")
            P.tt(tD[:], tB[:], Kre[:, fc, :], ALU.mult)
            P.tt(Yim[:, fc, :], tC[:], tD[:], ALU.add, e="pool")
        bd = P.sbuf("hy_bd", [128, 512], F32)
        P.dma(bd[:], I["hy_bias"][l].partition_broadcast(128))
        bT = P.sbuf("hy_bT", [128, 4, N], BF16)
        otm = [P.sbuf("hy_otm%d" % i, [128, 512], BF16) for i in range(2)]
        for tc in range(NT):
            c_ = Ct[tc % 2]; s_ = St[tc % 2]
            P.dma(c_[:], dft_ct[tc])
            P.dma(s_[:], dft_st[tc])
            pp = ps[1 + tc % 2]
            for fc in range(NF):
                P.mm(pp[:, :], c_[:, fc, :], Yre[:, fc, :], start=(fc == 0), stop=False)
            for fc in range(NF):
                P.mm(pp[:, :], s_[:, fc, :], Yim[:, fc, :], start=False, stop=(fc == NF - 1))
            P.tt(tA[:], s_bf[:, tc, :], bd[:], ALU.mult)
            P.stt(tB[:], pp[:, :], 1.0 / N, tA[:], ALU.mult, ALU.add)
            o = otm[tc % 2]
            P.tt(o[:], tB[:], x0_bf[:, tc, :], ALU.mult, e="pool")
            pt = ps[5 + tc % 2][:].bitcast(BF16)
            for j in range(4):
                P.transpose(pt[:, j * 128:(j + 1) * 128], o[:, j * 128:(j + 1) * 128], k.ident_bf[:])
            P.copy(bT[:, :, tc * 128:(tc + 1) * 128], pt[:, 0:512].rearrange("p (j t) -> p j t", j=4), e="act")
        P.dma(k.br[1].rearrange("(c p) t -> p c t", p=128)[:, :, xbase:xbase + N], bT[:], q="sp")


HY_LAT = ("lat", NLAT, LT0, NCTX)
HY_CTX = ("ctx", NCTX, CT0, 0)


NCH = 144


def declare_s5_inputs(P, I, nl):
    nc = P.nc

    def inp(name, shape):
        I[name] = nc.dram_tensor(name, list(shape), F32, kind="ExternalInput").ap()
    inp("s5_lam_re", [nl, 2, 2048]); inp("s5_lam_im", [nl, 2, 2048]); inp("s5_log_dt", [nl, 2, 32])
    inp("s5_b_re", [nl, 2, 2048, 16]); inp("s5_b_im", [nl, 2, 2048, 16])
    inp("s5_c_re", [nl, 2, 32, 16, 64]); inp("s5_c_im", [nl, 2, 32, 16, 64])
    inp("s5_d", [nl, 512]); inp("s5_w_glu", [nl, 512, 1024]); inp("s5_b_glu", [nl, 1024])
    inp("s5_mask", [2, 2, 128, 256])


def s5_setup(P, k):
    nc = P.nc
    k.u_tm = nc.dram_tensor("s5_u_tm", [T, 512], F32, kind="Internal").ap()
    k.y_tm = nc.dram_tensor("s5_y_tm", [T, 512], F32, kind="Internal").ap()


def s5_part1(P, k, l):
    I = k.I
    w_in = I["w_in"][l].rearrange("(c p) n -> p c n", p=128)
    ps = k.ps
    with P.scope():
        wu = P.sbuf("s5_wu", [128, KC, 512], BF16)
        P.dma(wu[:], w_in[:, :, C_U:C_U + 512], q="pool")
        ut = [P.sbuf("s5_ut%d" % i, [128, 512], F32) for i in range(2)]
        for tc in range(18):
            h0 = hcol(tc * 128)
            pp = ps[1 + tc % 2]
            for c in range(KC):
                P.mm(pp[:, :], k.hmix[:, c, h0:h0 + 128], wu[:, c, :], start=(c == 0), stop=(c == KC - 1))
            P.copy(ut[tc % 2][:], pp[:, :], e=("act" if tc % 2 else "dve"))
            P.dma(k.u_tm[tc * 128:(tc + 1) * 128, :], ut[tc % 2][:], q="sp")


def cmul(P, o_re, o_im, a_re, a_im, b_re, b_im, t1, t2, neg_im=False):
    P.tt(t1, a_re, b_re, ALU.mult)
    P.tt(t2, a_im, b_im, ALU.mult, e="pool")
    P.tt(o_re, t1, t2, ALU.subtract)
    P.tt(t1, a_re, b_im, ALU.mult)
    P.tt(t2, a_im, b_re, ALU.mult, e="pool")
    if neg_im:
        P.stt(o_im, t1, -1.0, t2, ALU.mult, ALU.subtract)
    else:
        P.tt(o_im, t1, t2, ALU.add)


def s5_part2(P, k, l, ctx_out):
    nc = P.nc
    I = k.I
    ps = k.ps
    with P.scope():
        U = P.sbuf("s5_U", [128, 32, 2, NCH], BF16)
        M = P.sbuf("s5_M", [128, 32, 2, 256], BF16)
        Qre = [P.sbuf("s5_Qre%d" % d, [128, 16, 256], BF16) for d in range(2)]
        nQim = [P.sbuf("s5_nQim%d" % d, [128, 16, 256], BF16) for d in range(2)]
        Xre = [P.sbuf("s5_Xre%d" % d, [128, 16, NCH], BF16) for d in range(2)]
        Xim = [P.sbuf("s5_Xim%d" % d, [128, 16, NCH], BF16) for d in range(2)]
        with P.scope():
            uc = P.sbuf("s5_uc", [128, 16 * 512], F32)
            uc2 = P.sbuf("s5_uc2", [128, 16 * 512], F32)
            ucv = uc[:].rearrange("p (j g h) -> p g j h", j=16, g=32)
            uc2v = uc2[:].rearrange("p (g j h) -> p g j h", g=32, j=16)
            for (part, np_, c_lo) in (("lat", 128, 16), ("ctx", 16, 0)):
                rows = k.u_tm[NCTX:T, :] if part == "lat" else k.u_tm[0:NCTX, :]
                P.dma(uc[0:np_, :], rows.rearrange("(c j) n -> c (j n)", j=16))
                for gi, eng in enumerate(("act", "dve", "act", "pool")):
                    gs = slice(gi * 8, (gi + 1) * 8)
                    P.copy(uc2v[0:np_, gs], ucv[0:np_, gs], e=eng)
                cnt = 0
                for g in range(32):
                    for a in range(2):
                        pp = ps[1 + cnt % 4]
                        cnt += 1
                        P.transpose(pp[:, 0:np_], uc2[0:np_, g * 256 + a * 128:g * 256 + (a + 1) * 128], k.ident[0:np_, 0:np_])
                        P.copy(U[:, g, a, c_lo:c_lo + np_], pp[:, 0:np_], e=("act" if cnt % 2 else "dve"))
        for d in range(2):
            with P.scope():
                Ar = P.sbuf("s5_Ar", [128, 8, 16], F32); Ai = P.sbuf("s5_Ai", [128, 8, 16], F32); nAi = P.sbuf("s5_nAi", [128, 8, 16], F32)
                PTre = P.sbuf("s5_PTre", [128, 16, 2, 128], BF16); PTim = P.sbuf("s5_PTim", [128, 16, 2, 128], BF16)
                with P.scope():
                    sm = P.sbuf("s5_sm", [128, 40, 16], F32)
                    slot = [0]

                    def S():
                        i = slot[0]
                        slot[0] += 1
                        return sm[:, i, :]
                    lre = S(); lim = S(); dt = S()
                    praw = P.sbuf("s5_praw", [16, 2, 128], F32)
                    P.dma(praw[:, 0, :], I["s5_lam_re"][l, d].rearrange("(pr q) -> pr q", q=128))
                    P.dma(praw[:, 1, :], I["s5_lam_im"][l, d].rearrange("(pr q) -> pr q", q=128))
                    P.transpose(ps[1][:, 0:16], praw[:, 0, :], k.ident[0:16, 0:16])
                    P.transpose(ps[1][:, 16:32], praw[:, 1, :], k.ident[0:16, 0:16])
                    P.copy(lre, ps[1][:, 0:16])
                    P.copy(lim, ps[1][:, 16:32])
                    ldt2 = P.sbuf("s5_ldt2", [2, 16], F32)
                    P.dma(ldt2[:], I["s5_log_dt"][l, d].rearrange("(pr g2) -> g2 pr", g2=2), allow_slow_non_contiguous=True)
                    self_ = P.sbuf("s5_self", [2, 128], F32)
                    P.dma(self_[:], I["na_sel"])
                    P.mm(ps[2][:, 0:16], self_[:, :], ldt2[:, :], start=True, stop=True)
                    P.copy(dt, ps[2][:, 0:16])
                    P.act(dt, dt, AF.Exp)
                    P.ts(lre, lre, -1e-4, None, op0=ALU.min)
                    a_ = S(); th = S(); t1 = S(); t2 = S()
                    P.tt(a_, lre, dt, ALU.mult)
                    P.tt(th, lim, dt, ALU.mult)
                    mag = S(); imag = S()
                    P.act(mag, a_, AF.Exp)
                    P.act(imag, a_, AF.Exp, scale=-1.0)
                    for _ in range(4):
                        P.ts(t1, th, PI, -2.0 * PI, op0=ALU.is_gt, op1=ALU.mult)
                        P.tt(th, th, t1, ALU.add)
                    thc = S()
                    P.ts(thc, th, PI / 2, None, op0=ALU.add)
                    P.ts(t1, thc, PI, -2.0 * PI, op0=ALU.is_gt, op1=ALU.mult)
                    P.tt(thc, thc, t1, ALU.add)
                    sn = S(); cs = S()
                    P.act(sn, th, AF.Sin)
                    P.act(cs, thc, AF.Sin)
                    lbr = S(); lbi = S(); lir = S(); lii = S()
                    P.tt(lbr, mag, cs, ALU.mult); P.tt(lbi, mag, sn, ALU.mult)
                    P.tt(lir, imag, cs, ALU.mult); P.stt(lii, imag, -1.0, sn, ALU.mult, ALU.mult)
                    den = S(); icr = S(); ici = S()
                    P.tt(den, lre, lre, ALU.mult); P.tt(t1, lim, lim, ALU.mult); P.tt(den, den, t1, ALU.add)
                    P.recip(den, den)
                    P.tt(icr, lre, den, ALU.mult); P.stt(ici, lim, -1.0, den, ALU.mult, ALU.mult)
                    lm1 = S(); cfr = S(); cfi = S()
                    P.ts(lm1, lbr, -1.0, None, op0=ALU.add)
                    cmul(P, cfr, cfi, lm1, lbi, icr, ici, t1, t2)
                    Ppr = P.sbuf("s5_Ppr", [128, 16, 17], F32); Ppi = P.sbuf("s5_Ppi", [128, 16, 17], F32)
                    Pnr = P.sbuf("s5_Pnr", [128, 16, 17], F32); Pni = P.sbuf("s5_Pni", [128, 16, 17], F32)
                    tw1 = P.sbuf("s5_tw1", [128, 16, 8], F32); tw2 = P.sbuf("s5_tw2", [128, 16, 8], F32)
                    for (tr, ti, br_, bi_) in ((Ppr, Ppi, lbr, lbi), (Pnr, Pni, lir, lii)):
                        P.memset(tr[:, :, 0:1], 1.0); P.memset(ti[:, :, 0:1], 0.0)
                        P.copy(tr[:, :, 1], br_); P.copy(ti[:, :, 1], bi_)
                        n = 2
                        while n <= 16:
                            sqr = S() if False else None
                            h = n // 2
                            cmul(P, tr[:, :, n], ti[:, :, n], tr[:, :, h], ti[:, :, h], tr[:, :, h], ti[:, :, h], tw1[:, :, 0], tw2[:, :, 0])
                            cnt_ = min(n, 17 - n) - 1
                            if cnt_ > 0:
                                bre = tr[:, :, n:n + 1].broadcast_to([128, 16, cnt_]) if False else None
                                cmul(P, tr[:, :, n + 1:n + 1 + cnt_], ti[:, :, n + 1:n + 1 + cnt_],
                                     tr[:, :, 1:1 + cnt_], ti[:, :, 1:1 + cnt_],
                                     tr[:, :, n:n + 1].to_broadcast([128, 16, cnt_]), ti[:, :, n:n + 1].to_broadcast([128, 16, cnt_]),
                                     tw1[:, :, 0:cnt_], tw2[:, :, 0:cnt_])
                            n *= 2
                    P.copy(Ar[:, 0, :], Ppr[:, :, 16]); P.copy(Ai[:, 0, :], Ppi[:, :, 16])
                    for kk in range(1, 8):
                        cmul(P, Ar[:, kk, :], Ai[:, kk, :], Ar[:, kk - 1, :], Ai[:, kk - 1, :], Ar[:, kk - 1, :], Ai[:, kk - 1, :], t1, t2)
                    P.ts(nAi[:], Ai[:], -1.0, None, op0=ALU.mult)
                    Bre = P.sbuf("s5_Bre", [128, 16, 16], F32); Bim = P.sbuf("s5_Bim", [128, 16, 16], F32)
                    P.dma(Bre[:], I["s5_b_re"][l, d].rearrange("(pr q) h -> q pr h", q=128))
                    P.dma(Bim[:], I["s5_b_im"][l, d].rearrange("(pr q) h -> q pr h", q=128))
                    Cre = P.sbuf("s5_Cre", [128, 16, 16], F32); Cim = P.sbuf("s5_Cim", [128, 16, 16], F32)
                    craw = P.sbuf("s5_craw", [128, 2, 4, 64], F32)
                    for ri, nm_ in enumerate(("s5_c_re", "s5_c_im")):
                        P.dma(craw[:, ri], I[nm_][l, d].rearrange("(a gl) h p -> (gl h) a p", a=4))
                    for ri, dst in enumerate((Cre, Cim)):
                        for a in range(4):
                            pa_ = ps[3 + a % 2]
                            P.mm(pa_[0:64, 0:128], craw[:, ri, a, :], k.ident[:], start=True, stop=True)
                            P.mm(pa_[64:128, 0:128], craw[:, ri, a, :], k.ident[:], start=True, stop=True)
                            v_ = pa_[:, 0:128].rearrange("p (pl g2 h) -> p pl g2 h", g2=2, h=16)
                            P.copy(dst[0:64, a * 4:(a + 1) * 4, :], v_[0:64, :, 0, :], e="act")
                            P.copy(dst[64:128, a * 4:(a + 1) * 4, :], v_[64:128, :, 1, :], e="dve")
                    tb1 = P.sbuf("s5_tb1", [128, 16, 16], F32); tb2 = P.sbuf("s5_tb2", [128, 16, 16], F32)
                    bbr = P.sbuf("s5_bbr", [128, 16, 16], F32); bbi = P.sbuf("s5_bbi", [128, 16, 16], F32)
                    bc = lambda x: x.unsqueeze(2).to_broadcast([128, 16, 16])
                    cmul(P, bbr[:], bbi[:], Bre[:], Bim[:], bc(cfr), bc(cfi), tb1[:], tb2[:])
                    if d == 1:
                        cmul(P, Bre[:], Bim[:], bbr[:], bbi[:], bc(Pnr[:, :, 15]), bc(Pni[:, :, 15]), tb1[:], tb2[:])
                        vbr, vbi = Bre, Bim
                        cr2 = P.sbuf("s5_cr2", [128, 16, 16], F32); ci2 = P.sbuf("s5_ci2", [128, 16, 16], F32)
                        cmul(P, cr2[:], ci2[:], Cre[:], Cim[:], bc(Ppr[:, :, 15]), bc(Ppi[:, :, 15]), tb1[:], tb2[:])
                        vcr, vci = cr2, ci2
                        tabP, tabQ = (Ppr, Ppi), (Pnr, Pni)
                    else:
                        vbr, vbi = bbr, bbi
                        vcr, vci = Cre, Cim
                        tabP, tabQ = (Pnr, Pni), (Ppr, Ppi)
                    Pre = P.sbuf("s5_Pre", [128, 16, 256], BF16); Pim = P.sbuf("s5_Pim", [128, 16, 256], BF16)
                    to1 = P.sbuf("s5_to1", [128, 4, 256], F32); to2 = P.sbuf("s5_to2", [128, 4, 256], F32)
                    v4 = lambda x: x.rearrange("p a (j h) -> p a j h", j=16)
                    for pg in range(4):
                        sl = slice(pg * 4, pg * 4 + 4)
                        tj = lambda tab: tab[:, sl, 0:16].unsqueeze(3).to_broadcast([128, 4, 16, 16])
                        vh = lambda v: v[:, sl, :].unsqueeze(2).to_broadcast([128, 4, 16, 16])
                        cmul(P, v4(Pre[:, sl, :]), v4(Pim[:, sl, :]), tj(tabP[0]), tj(tabP[1]), vh(vbr), vh(vbi), v4(to1[:]), v4(to2[:]))
                        cmul(P, v4(Qre[d][:, sl, :]), v4(nQim[d][:, sl, :]), tj(tabQ[0]), tj(tabQ[1]), vh(vcr), vh(vci), v4(to1[:]), v4(to2[:]),
                             neg_im=True)
                    msk = [P.sbuf("s5_msk%d" % a, [128, 256], F32) for a in range(2)]
                    for a in range(2):
                        P.dma(msk[a][:], I["s5_mask"][d, a])
                    tm = [P.sbuf("s5_tm%d" % i, [128, 256], BF16) for i in range(2)]
                    cnt = 0
                    for g in range(32):
                        pr_, g2 = g // 2, g % 2
                        rows = slice(g2 * 64, (g2 + 1) * 64)
                        for a in range(2):
                            pp = ps[1 + cnt % 4]
                            cnt += 1
                            P.mm(pp[:, 0:256], Pre[rows, pr_, a * 128:(a + 1) * 128], Qre[d][rows, pr_, :], start=True, stop=False)
                            P.mm(pp[:, 0:256], Pim[rows, pr_, a * 128:(a + 1) * 128], nQim[d][rows, pr_, :], start=False, stop=True)
                            if d == 0:
                                P.tt(M[:, g, a, :], pp[:, 0:256], msk[a][:], ALU.mult)
                            else:
                                t = tm[cnt % 2]
                                P.tt(t[:], pp[:, 0:256], msk[a][:], ALU.mult)
                                P.tt(M[:, g, a, :], M[:, g, a, :], t[:], ALU.add, e="pool")
                    cnt = 0
                    for pr_ in range(16):
                        for a in range(2):
                            for (src, dst) in ((Pre, PTre), (Pim, PTim)):
                                pt = ps[5 + cnt % 2][:].bitcast(BF16)
                                cnt += 1
                                P.transpose(pt[:, 0:128], src[:, pr_, a * 128:(a + 1) * 128], k.ident_bf[:])
                                P.copy(dst[:, pr_, a, :], pt[:, 0:128], e=("act" if cnt % 2 else "dve"))
                SA = [P.sbuf("s5_SAre", [128, 16, NCH], F32), P.sbuf("s5_SAim", [128, 16, NCH], F32)]
                SB = [P.sbuf("s5_SBre", [128, 16, NCH], F32), P.sbuf("s5_SBim", [128, 16, NCH], F32)]
                ts_ = P.sbuf("s5_ts", [128, NCH], F32); ts2 = P.sbuf("s5_ts2", [128, NCH], F32)
                for pr_ in range(16):
                    pre_, pim_ = ps[1 + 2 * (pr_ % 2)], ps[2 + 2 * (pr_ % 2)]
                    for (pt_, PT) in ((pre_, PTre), (pim_, PTim)):
                        for g2 in range(2):
                            g = 2 * pr_ + g2
                            rows = slice(g2 * 64, (g2 + 1) * 64)
                            if d == 0:
                                for a in range(2):
                                    P.mm(pt_[rows, 0:NCH], PT[:, pr_, a, rows], U[:, g, a, :], start=(a == 0), stop=(a == 1))
                            else:
                                for a in range(2):
                                    P.mm(pt_[rows, 0:128], PT[:, pr_, a, rows], U[:, g, a, 16:NCH], start=(a == 0), stop=(a == 1))
                                for a in range(2):
                                    P.mm(pt_[rows, 128:NCH], PT[:, pr_, a, rows], U[:, g, a, 0:16], start=(a == 0), stop=(a == 1))
                    P.ts(ts_[:], pre_[:, 0:NCH], Ar[:, 0, pr_:pr_ + 1], None, op0=ALU.mult)
                    P.stt(SA[0][:, pr_, :], pim_[:, 0:NCH], nAi[:, 0, pr_:pr_ + 1], ts_[:], ALU.mult, ALU.add)
                    P.ts(ts2[:], pim_[:, 0:NCH], Ar[:, 0, pr_:pr_ + 1], None, op0=ALU.mult)
                    P.stt(SA[1][:, pr_, :], pre_[:, 0:NCH], Ai[:, 0, pr_:pr_ + 1], ts2[:], ALU.mult, ALU.add)
                tsa = [P.sbuf("s5_tsa%d" % i, [128, NCH], F32) for i in range(4)]
                tsb = [P.sbuf("s5_tsb%d" % i, [128, NCH], F32) for i in range(4)]
                cur, nxt = SA, SB
                for kk in range(8):
                    sh = 1 << kk
                    n_ = NCH - sh
                    for pg in range(4):
                        prs = list(range(pg * 4, pg * 4 + 4))

                        def views(pr_):
                            if d == 0:
                                return (nxt[0][:, pr_, sh:NCH], nxt[1][:, pr_, sh:NCH],
                                        cur[0][:, pr_, 0:n_], cur[1][:, pr_, 0:n_],
                                        cur[0][:, pr_, sh:NCH], cur[1][:, pr_, sh:NCH])
                            return (nxt[0][:, pr_, 0:n_], nxt[1][:, pr_, 0:n_],
                                    cur[0][:, pr_, sh:NCH], cur[1][:, pr_, sh:NCH],
                                    cur[0][:, pr_, 0:n_], cur[1][:, pr_, 0:n_])
                        V = [views(pr_) for pr_ in prs]
                        for i, pr_ in enumerate(prs):
                            P.stt(tsa[i][:, 0:n_], V[i][2], Ar[:, kk, pr_:pr_ + 1], V[i][4], ALU.mult, ALU.add)
                        for i, pr_ in enumerate(prs):
                            P.stt(tsb[i][:, 0:n_], V[i][3], Ar[:, kk, pr_:pr_ + 1], V[i][5], ALU.mult, ALU.add)
                        for i, pr_ in enumerate(prs):
                            P.stt(V[i][0], V[i][3], nAi[:, kk, pr_:pr_ + 1], tsa[i][:, 0:n_], ALU.mult, ALU.add)
                        for i, pr_ in enumerate(prs):
                            P.stt(V[i][1], V[i][2], Ai[:, kk, pr_:pr_ + 1], tsb[i][:, 0:n_], ALU.mult, ALU.add)
                    for ri in range(2):
                        if d == 0:
                            P.copy(nxt[ri][:, :, 0:sh], cur[ri][:, :, 0:sh], e="pool")
                        else:
                            P.copy(nxt[ri][:, :, n_:NCH], cur[ri][:, :, n_:NCH], e="pool")
                    cur, nxt = nxt, cur
                for ri, X in ((0, Xre[d]), (1, Xim[d])):
                    W = cur[ri]
                    if d == 0:
                        P.memset(X[:, :, 0:1], 0.0)
                        P.copy(X[:, :, 1:NCH], W[:, :, 0:NCH - 1], e=("act" if ri else "dve"))
                    else:
                        P.copy(X[:, :, 0:15], W[:, :, 129:144], e="act")
                        P.memset(X[:, :, 15:16], 0.0)
                        P.copy(X[:, :, 16:143], W[:, :, 1:128], e="dve")
                        P.copy(X[:, :, 143:144], W[:, :, 128:129], e="act")
        with P.scope():
            ycl = P.sbuf("s5_ycl", [128, 16 * 512], F32)
            ycc = P.sbuf("s5_ycc", [16, 16 * 512], F32)
            yclv = ycl[:].rearrange("p (j g h) -> p j g h", j=16, g=32)
            yccv = ycc[:].rearrange("p (j g h) -> p j g h", j=16, g=32)
            ysb = [P.sbuf("s5_ysb%d" % i, [128, NCH], F32) for i in range(2)]
            items = [(g, b) for g in range(32) for b in range(2)]

            def y_mm(i):
                g, b = items[i]
                pr_, g2 = g // 2, g % 2
                rows = slice(g2 * 64, (g2 + 1) * 64)
                pp = ps[1 + i % 2]
                cols = slice(b * 128, (b + 1) * 128)
                P.mm(pp[:, 0:NCH], M[:, g, 0, cols], U[:, g, 0, :], start=True, stop=False)
                P.mm(pp[:, 0:NCH], M[:, g, 1, cols], U[:, g, 1, :], start=False, stop=False)
                for d in range(2):
                    P.mm(pp[:, 0:NCH], Qre[d][rows, pr_, cols], Xre[d][rows, pr_, :], start=False, stop=False)
                    P.mm(pp[:, 0:NCH], nQim[d][rows, pr_, cols], Xim[d][rows, pr_, :], start=False, stop=(d == 1))
                P.copy(ysb[i % 2][:], pp[:, 0:NCH], e="act")

            def y_tr(i):
                g, b = items[i]
                ys = ysb[i % 2]
                pt1 = ps[3 + i % 2]
                pt2 = ps[5 + i % 2]
                P.transpose(pt1[:, 0:128], ys[:, 16:NCH], k.ident[:])
                P.copy(yclv[:, b * 8:(b + 1) * 8, g, :], pt1[:, 0:128].rearrange("p (j h) -> p j h", j=8), e="dve")
                P.transpose(pt2[0:16, 0:128], ys[:, 0:16], k.ident[:])
                P.copy(yccv[:, b * 8:(b + 1) * 8, g, :], pt2[0:16, 0:128].rearrange("p (j h) -> p j h", j=8), e="dve")
            y_mm(0)
            for i in range(len(items)):
                if i + 1 < len(items):
                    y_mm(i + 1)
                y_tr(i)
            P.dma(k.y_tm[NCTX:T, :].rearrange("(c j) n -> c (j n)", j=16), ycl[:], q="sp")
            P.dma(k.y_tm[0:NCTX, :].rearrange("(c j) n -> c (j n)", j=16), ycc[:], q="sp")
        with P.scope():
            dbc = P.sbuf("s5_dbc", [128, 512], F32)
            P.dma(dbc[:], I["s5_d"][l].partition_broadcast(128))
            wgl = P.sbuf("s5_wgl", [128, 4, 1024], BF16)
            P.dma(wgl[:], I["s5_w_glu"][l].rearrange("(kk p) n -> p kk n", p=128), q="pool")
            bgl = P.sbuf("s5_bgl", [128, 8], F32)
            load_col_vec(P, bgl[:], I["s5_b_glu"][l], 8)
            gT = P.sbuf("s5_gT", [128, 4, T], BF16)
            yt = [P.sbuf("s5_yt%d" % i, [128, 512], F32) for i in range(2)]
            ut = [P.sbuf("s5_ut%d" % i, [128, 512], F32) for i in range(2)]
            w1_ = P.sbuf("s5_w1", [128, 512], F32); w2_ = P.sbuf("s5_w2", [128, 512], F32)
            gtm = [P.sbuf("s5_gtm%d" % i, [128, 512], BF16) for i in range(2)]
            tcs = list(range(18)) if ctx_out else list(range(2, 18))
            for tc in tcs:
                y = yt[tc % 2]; u = ut[tc % 2]
                P.dma(y[:], k.y_tm[tc * 128:(tc + 1) * 128, :])
                P.dma(u[:], k.u_tm[tc * 128:(tc + 1) * 128, :])
                P.tt(u[:], u[:], dbc[:], ALU.mult)
                P.tt(y[:], y[:], u[:], ALU.add, e="pool")
                P.act(w1_[:], y[:], AF.Square)
                P.ts(w1_[:], w1_[:], 0.044715, 1.0, op0=ALU.mult, op1=ALU.add)
                P.tt(w1_[:], w1_[:], y[:], ALU.mult)
                P.act(w2_[:], w1_[:], AF.Sigmoid, scale=1.5957691216057308)
                gt_ = gtm[tc % 2]
                P.tt(gt_[:], y[:], w2_[:], ALU.mult, e="pool")
                pt = ps[5 + tc % 2][:].bitcast(BF16)
                for j in range(4):
                    P.transpose(pt[:, j * 128:(j + 1) * 128], gt_[:, j * 128:(j + 1) * 128], k.ident_bf[:])
                P.copy(gT[:, :, tc * 128:(tc + 1) * 128], pt[:, 0:512].rearrange("p (j t) -> p j t", j=4), e="act")
            cT = P.sbuf("s5_cT", [128, T], BF16)
            sg = [P.sbuf("s5_sg%d" % i, [128, 512], F32) for i in range(2)]
            subs = ALLSUBS if ctx_out else LATSUBS
            cnt = 0
            for m in range(4):
                for (c0, w, st) in subs:
                    pa = ps[1 + cnt % 2]; pb = ps[3 + cnt % 2]; s_ = sg[cnt % 2]
                    cnt += 1
                    for kk in range(4):
                        P.mm(pa[:, :w], wgl[:, kk, m * 128:(m + 1) * 128], gT[:, kk, c0:c0 + w], start=(kk == 0), stop=(kk == 3))
                    for kk in range(4):
                        P.mm(pb[:, :w], wgl[:, kk, 512 + m * 128:512 + (m + 1) * 128], gT[:, kk, c0:c0 + w], start=(kk == 0), stop=(kk == 3))
                    P.act(s_[:, :w], pb[:, :w], AF.Sigmoid, bias=bgl[:, 4 + m:5 + m])
                    P.stt(cT[:, c0:c0 + w], pa[:, :w], bgl[:, m:m + 1], s_[:, :w], ALU.add, ALU.mult)
                cols0 = 0 if ctx_out else NCTX
                P.dma(k.br[2][m * 128:(m + 1) * 128, cols0:T], cT[:, cols0:T], q="sp")


def build_full(nl=2, upto=None):
    P = Prog()
    nc = P.nc
    k = K()
    k.I = declare_inputs(P, nl)
    declare_mixer_inputs(P, k.I, nl)
    declare_na_inputs(P, k.I, nl)
    declare_hyena_inputs(P, k.I, nl)
    declare_s5_inputs(P, k.I, nl)
    yT = nc.dram_tensor("yT", [D, NLAT], F32, kind="ExternalOutput").ap()
    setup_common(P, k)
    mixer_setup(P, k)
    s5_setup(P, k)
    geo = na_geometry()
    P.dma(k.xres, k.I["xT"].rearrange("(c p) t -> p c t", p=128))
    for l in range(nl):
        ctx_out = l < nl - 1
        with P.scope():
            layer_mods(P, k, l)
        with P.scope():
            ffn_sublayer(P, k, l, 0, FULL_BLOCKS)
        with P.scope():
            alloc_hmix(P, k)
            mixer_modnorm(P, k)
            mla_branch(P, k, l, ctx_out)
            na_branch(P, k, l, ctx_out, geo)
            hyena_branch(P, k, l, HY_LAT)
            if ctx_out:
                hyena_branch(P, k, l, HY_CTX)
            s5_part1(P, k, l)
        s5_part2(P, k, l, ctx_out)
        merge_phase(P, k, l, ctx_out)
        with P.scope():
            ffn_sublayer(P, k, l, 2, FULL_BLOCKS if ctx_out else LAT_BLOCKS)
    P.dma(yT.rearrange("(c p) t -> p c t", p=128), k.xres[:, :, NCTX:T], q="sp")
    P.finish("sp")
    P.close()
    return P, k


_CACHE = {}


def _host_constants():
    if "c" in _CACHE:
        return _CACHE["c"]
    C, S = rope_tables()
    mv, sel, cm = na_const_tables()
    m = {"ident": np.eye(128, dtype=np.float32), "rope_c": C, "rope_s": S,
         "na_mv": np.ascontiguousarray(mv.reshape(2, -1)), "na_sel": sel, "na_cm": cm, "s5_mask": s5_masks()}
    for nm_, N in (("lat", NLAT), ("ctx", NCTX)):
        hc = hyena_consts(N)
        for t in ("c", "s", "ct", "st"):
            m["dft_%s_%s" % (t, nm_)] = hc[t]
        m["hy_zT_%s" % nm_] = hc["zT"]
        m["hy_decay_%s" % nm_] = hc["decay"]
    _CACHE["c"] = m
    return m


def kernel(**inputs):
    nl = 2
    if "prog" not in _CACHE:
        _CACHE["prog"] = build_full(nl)
    P, k = _CACHE["prog"]
    shared = dict(_host_constants())
    shared["na_G"] = np.ascontiguousarray(
        np.stack([na_bias_gather(np.asarray(inputs["na_rpb"][l], np.float32)).reshape(128, -1) for l in range(nl)], 0))
    for nm, ap in k.I.items():
        if nm in shared or nm in ("xT", "cvec"):
            continue
        shared[nm] = np.ascontiguousarray(np.asarray(inputs[nm], np.float32).reshape(ap.shape))
    x = np.asarray(inputs["x"], np.float32)
    ctx = np.asarray(inputs["ctx"], np.float32)
    c = np.asarray(inputs["c"], np.float32)
    c_ctx = np.asarray(inputs["c_ctx"], np.float32)
    B = x.shape[0]
    in_maps = []
    for b in range(B):
        m = dict(shared)
        m["xT"] = np.ascontiguousarray(np.concatenate([ctx[b], x[b]], 0).T)
        m["cvec"] = np.ascontiguousarray(np.stack([c[b], c_ctx], 1))
        in_maps.append(m)
    res = run_bass_kernel_spmd(P.nc, in_maps, core_ids=list(range(B)))
    out = np.stack([np.asarray(res.results[b]["yT"], np.float32).T for b in range(B)], 0)
    return np.ascontiguousarray(out)
```

```python
import numpy as np
import concourse.bass as bass
import concourse.mybir as mybir
from concourse.bass_utils import run_bass_kernel_spmd

F32 = mybir.dt.float32
F32R = mybir.dt.float32r
BF16 = mybir.dt.bfloat16
AF = mybir.ActivationFunctionType
ALU = mybir.AluOpType
AX = mybir.AxisListType

SEM_ROLL = 30000


class Prog:
    def __init__(self, n_dma_sems=16):
        self.nc = bass.Bass("TRN2", target_bir_lowering=False)
        nc = self.nc
        self.eng = {"pe": nc.tensor, "act": nc.scalar, "dve": nc.vector,
                    "pool": nc.gpsimd, "sp": nc.sync}
        self._ctx = []
        self._scopes = []
        self._in_scope_alloc = False
        self._uid = 0
        self.sem = {}
        self.cnt = {}
        self.nsem = 0
        for e in self.eng:
            self._new_eng_sem(e)
        self.dma_sems = {}
        self.dma_rr = {}
        for q in ("sp", "pool", "act"):
            self.dma_sems[q] = []
            for i in range(n_dma_sems if q != "act" else 4):
                s = self._enter(nc.semaphore("dq_%s%d" % (q, i)))
                self.dma_sems[q].append([s, 0])
            self.dma_rr[q] = 0
        self.waited = {e: {} for e in self.eng}
        self.regions = {}
        self.n_inst = 0
        self.n_wait = 0

    def _enter(self, cm):
        v = cm.__enter__()
        if self._scopes and self._in_scope_alloc:
            self._scopes[-1].append((cm, v))
        else:
            self._ctx.append(cm)
        return v

    class _Scope:
        def __init__(self, P):
            self.P = P

        def __enter__(self):
            self.P._scopes.append([])
            return self

        def __exit__(self, *a):
            P = self.P
            items = P._scopes.pop()
            toks = []
            for cm, v in items:
                nm = v.name if hasattr(v, "name") else None
                for r in P.regions.pop(nm, []):
                    toks.append((r[5], r[6]))
            if toks:
                for e in P.eng:
                    P._emit_waits(e, toks)
            for cm, v in reversed(items):
                cm.__exit__(None, None, None)
            return False

    def scope(self):
        return Prog._Scope(self)

    def _new_eng_sem(self, e):
        s = self._enter(self.nc.semaphore("s_%s_%d" % (e, self.nsem)))
        self.nsem += 1
        self.sem[e] = s
        self.cnt[e] = 0

    def close(self):
        for cm in reversed(self._ctx):
            cm.__exit__(None, None, None)
        self._ctx = []

    def sbuf(self, name, shape, dtype=F32):
        self._uid += 1
        self._in_scope_alloc = True
        try:
            return self._enter(self.nc.sbuf_tensor("%s_%d" % (name, self._uid), list(shape), dtype))
        finally:
            self._in_scope_alloc = False

    def psum(self, name, shape=(128, 512), dtype=F32):
        return self._enter(self.nc.psum_tensor(name, list(shape), dtype))

    def dram(self, name, shape, dtype=F32, kind="Internal"):
        return self.nc.dram_tensor(name, list(shape), dtype, kind=kind)

    @staticmethod
    def _region(ap):
        space = str(ap.space)
        name = ap.name
        aps = ap.ap
        off = int(ap.offset)
        if "DRAM" in space:
            ext = sum((c - 1) * abs(s) for s, c in aps)
            neg = sum((c - 1) * s for s, c in aps if s < 0)
            lo = off + neg
            return name, 0, 1, lo, lo + ext + 1, False
        pstep, pcnt = aps[0]
        if pstep == 0:
            pstep = 1 << 40
        p0 = off // pstep if pstep < (1 << 39) else 0
        lo = off - p0 * pstep if pstep < (1 << 39) else off
        ext = sum((c - 1) * abs(s) for s, c in aps[1:])
        is_psum = "PSUM" in space
        if is_psum:
            return name, 0, 128, 0, 1 << 30, True
        return name, p0, p0 + pcnt, lo, lo + ext + 1, False

    def _deps(self, reads, writes):
        toks = []
        info = []
        for ap, is_w in [(a, False) for a in reads] + [(a, True) for a in writes]:
            name, p0, p1, lo, hi, excl = self._region(ap)
            w = is_w or excl
            lst = self.regions.setdefault(name, [])
            for r in lst:
                if r[1] <= p0 or p1 <= r[0] or r[3] <= lo or hi <= r[2]:
                    continue
                if w or r[4]:
                    toks.append((r[5], r[6]))
            info.append((name, p0, p1, lo, hi, w))
        return toks, info

    def _record(self, info, sem, val):
        for name, p0, p1, lo, hi, w in info:
            lst = self.regions[name]
            if w:
                lst[:] = [r for r in lst if not (p0 <= r[0] and r[1] <= p1 and lo <= r[2] and r[3] <= hi)]
                lst.append([p0, p1, lo, hi, True, sem, val])
            else:
                for r in lst:
                    if (not r[4]) and r[0] == p0 and r[1] == p1 and r[2] == lo and r[3] == hi and r[5] is sem:
                        r[6] = max(r[6], val)
                        break
                else:
                    lst.append([p0, p1, lo, hi, False, sem, val])

    def _emit_waits(self, e, toks):
        best = {}
        for s, v in toks:
            k = id(s)
            if k not in best or best[k][1] < v:
                best[k] = (s, v)
        wd = self.waited[e]
        for k, (s, v) in best.items():
            if wd.get(k, 0) >= v:
                continue
            self.eng[e].wait_ge(s, v)
            wd[k] = v
            self.n_wait += 1

    def op(self, e, fn, reads, writes):
        toks, info = self._deps(reads, writes)
        if e == "pe":
            toks = [t for t in toks if t[0] is not self.sem["pe"]]
        self._emit_waits(e, toks)
        ins = fn()
        if self.cnt[e] >= SEM_ROLL:
            self._new_eng_sem(e)
        self.cnt[e] += 1
        ins.then_inc(self.sem[e], 1)
        self._record(info, self.sem[e], self.cnt[e])
        self.n_inst += 1
        return ins

    def dma(self, out, in_, q="sp", **kw):
        toks, info = self._deps([in_], [out])
        ent = self.dma_sems[q][self.dma_rr[q]]
        self.dma_rr[q] = (self.dma_rr[q] + 1) % len(self.dma_sems[q])
        s = ent[0]
        if ent[1] > 0:
            toks.append((s, ent[1]))
        self._emit_waits(q, toks)
        ent[1] += 16
        ins = self.eng[q].dma_start(out=out, in_=in_, **kw)
        ins.then_inc(s, 16)
        self._record(info, s, ent[1])
        self.n_inst += 1
        return ins

    def finish(self, e="sp"):
        toks = []
        for lst in self.regions.values():
            for r in lst:
                toks.append((r[5], r[6]))
        self._emit_waits(e, toks)

    def mm(self, out, lhsT, rhs, start=True, stop=True, **kw):
        return self.op("pe", lambda: self.nc.tensor.matmul(out, lhsT, rhs, start=start, stop=stop, **kw),
                       [lhsT, rhs], [out])

    def transpose(self, out, in_, ident):
        return self.op("pe", lambda: self.nc.tensor.transpose(out, in_, ident), [in_, ident], [out])

    def act(self, out, in_, func, bias=None, scale=1.0, e="act", **kw):
        reads = [in_]
        if bias is not None and not isinstance(bias, (int, float)):
            reads.append(bias)
        if not isinstance(scale, (int, float)):
            reads.append(scale)
        kw2 = dict(kw)
        if bias is not None:
            kw2["bias"] = bias
        writes = [out]
        if "accum_out" in kw2:
            writes.append(kw2["accum_out"])
        return self.op(e, lambda: self.nc.scalar.activation(out=out, in_=in_, func=func, scale=scale, **kw2),
                       reads, writes)

    def _veng(self, e):
        return self.nc.vector if e == "dve" else self.nc.gpsimd

    def tt(self, out, in0, in1, op, e="dve"):
        return self.op(e, lambda: self._veng(e).tensor_tensor(out=out, in0=in0, in1=in1, op=op), [in0, in1], [out])

    def ts(self, out, in0, s1, s2=None, op0=ALU.mult, op1=None, e="dve", **kw):
        reads = [in0] + [s for s in (s1, s2) if s is not None and not isinstance(s, (int, float))]
        writes = [out] + ([kw["accum_out"]] if "accum_out" in kw else [])
        if op1 is None:
            return self.op(e, lambda: self._veng(e).tensor_scalar(out=out, in0=in0, scalar1=s1, scalar2=None, op0=op0, **kw),
                           reads, writes)
        return self.op(e, lambda: self._veng(e).tensor_scalar(out=out, in0=in0, scalar1=s1, scalar2=s2, op0=op0, op1=op1, **kw),
                       reads, writes)

    def stt(self, out, in0, scalar, in1, op0, op1, e="dve"):
        reads = [in0, in1] + ([] if isinstance(scalar, (int, float)) else [scalar])
        return self.op(e, lambda: self.nc.vector.scalar_tensor_tensor(out=out, in0=in0, scalar=scalar, in1=in1, op0=op0, op1=op1),
                       reads, [out])

    def copy(self, out, in_, e="dve"):
        if e == "act":
            return self.op("act", lambda: self.nc.scalar.copy(out=out, in_=in_), [in_], [out])
        return self.op(e, lambda: self._veng(e).tensor_copy(out=out, in_=in_), [in_], [out])

    def memset(self, ap, val, e="dve"):
        return self.op(e, lambda: self._veng(e).memset(ap, val), [], [ap])

    def recip(self, out, in_):
        return self.op("dve", lambda: self.nc.vector.reciprocal(out=out, in_=in_), [in_], [out])

    def scan(self, out, d0, d1, initial, op0=ALU.mult, op1=ALU.add):
        reads = [d0, d1] + ([] if isinstance(initial, (int, float)) else [initial])
        return self.op("dve", lambda: self.nc.vector.tensor_tensor_scan(out=out, data0=d0, data1=d1, initial=initial, op0=op0, op1=op1),
                       reads, [out])


import numpy as np
def rope_tables(n=2048, grid_w=64, base=10000.0):
    q = 16
    t = np.arange(n)
    pos = np.stack([t // grid_w, t % grid_w], -1).astype(np.float32)
    inv = (base ** (-np.arange(q, dtype=np.float32) / q)).astype(np.float32)
    ang = pos[:, :, None] * inv
    C = np.zeros((64, n), np.float32); S = np.zeros((64, n), np.float32)
    for a in range(2):
        for hf in range(2):
            for j in range(q):
                f = a * 32 + hf * 16 + j
                C[f] = np.cos(ang[:, a, j])
                S[f] = (-1.0 if hf == 0 else 1.0) * np.sin(ang[:, a, j])
    return C, S

NEG = -30000.0
def na_geometry():
    rows = 32
    start = lambda r: min(max(r - 4, 0), rows - 8)
    geo = []
    for i in range(16):
        rs = [2 * i, 2 * i + 1]
        lo = min(start(r) for r in rs); hi = max(start(r) + 7 for r in rs)
        lst = []
        for j in range(lo // 2, hi // 2 + 1):
            codes = []
            for r in rs:
                v0 = start(r) <= 2 * j <= start(r) + 7
                v1 = start(r) <= 2 * j + 1 <= start(r) + 7
                code = {(True, True): 0, (False, True): 1, (True, False): 2, (False, False): 3}[(v0, v1)]
                codes.append(code)
            dr0 = 2 * (j - i)
            idxp = 7 - dr0
            assert 0 <= idxp <= 14, (i, j, idxp)
            lst.append((j, idxp, codes[0] * 4 + codes[1]))
        geo.append(lst)
    return geo

def na_const_tables():
    mv = np.zeros((2, 16, 128), np.float32)
    vecs = [np.zeros(128), np.r_[np.full(64, NEG), np.zeros(64)], np.r_[np.zeros(64), np.full(64, NEG)], np.full(128, NEG)]
    for c0 in range(4):
        for c1 in range(4):
            mv[0, c0 * 4 + c1] = vecs[c0]; mv[1, c0 * 4 + c1] = vecs[c1]
    sel = np.zeros((2, 128), np.float32); sel[0, :64] = 1; sel[1, 64:] = 1
    col = np.arange(64)
    c0 = np.clip(col - 8, 0, 48)
    inwin = (col[None, :] >= c0[:, None]) & (col[None, :] < c0[:, None] + 16)
    cm = np.where(inwin.T, 0.0, NEG).astype(np.float32)
    cm = np.concatenate([cm, cm], 0)
    return mv, sel, cm

def na_bias_gather(rpb):
    kc = np.arange(64)[:, None]; qc = np.arange(64)[None, :]
    dc = np.clip(kc - qc + 15, 0, 30)
    G = np.zeros((128, 8, 16, 64), np.float32)
    for idxp in range(16):
        for krl in range(2):
            dr = 7 - idxp + krl
            row = dr + 7
            if not (0 <= row <= 14):
                row = 0
            G[krl * 64:(krl + 1) * 64, :, idxp, :] = np.transpose(rpb[:, row][:, dc], (1, 0, 2))
    return G

def hyena_consts(N):
    t = np.arange(N, dtype=np.float64)[:, None]; f = np.arange(N, dtype=np.float64)[None, :]
    ang = 2.0 * np.pi * (f + 0.5) * t / (2.0 * N)
    Cm = np.cos(ang).astype(np.float32); Sm = np.sin(ang).astype(np.float32)
    bands = 16
    tt = np.arange(N, dtype=np.float32)
    t01 = np.linspace(0.0, 1.0, N, dtype=np.float32)[:, None]
    a2 = (np.float32(2.0 * np.pi) * tt / np.float32(N))[:, None] * np.linspace(1e-4, bands - 1, bands, dtype=np.float32)
    z = np.concatenate([t01, np.cos(a2), -np.sin(a2)], -1).astype(np.float32)
    max_decay = np.log(1e-2) / 0.3; min_decay = np.log(1e-2) / 1.5
    deltas = np.abs(np.linspace(min_decay, max_decay, 512, dtype=np.float32))
    decay = np.exp(-t01 * deltas).astype(np.float32)
    import ml_dtypes
    nch = N // 128

    def tiles(M):
        a = M.reshape(nch, 128, nch, 128)
        return np.ascontiguousarray(a.transpose(2, 1, 0, 3)).astype(ml_dtypes.bfloat16)
    return dict(c=tiles(Cm), s=tiles(Sm), ct=tiles(np.ascontiguousarray(Cm.T)), st=tiles(np.ascontiguousarray(Sm.T)),
                zT=np.ascontiguousarray(z.T), decay=decay)

def s5_masks():
    m = np.zeros((2, 2, 128, 256), np.float32)
    for a in range(2):
        for il in range(8):
            i = a * 8 + il
            for j in range(16):
                if j >= i:
                    m[0, a, il * 16:(il + 1) * 16, j * 16:(j + 1) * 16] = 1.0
                if j <= i:
                    m[1, a, il * 16:(il + 1) * 16, j * 16:(j + 1) * 16] = 1.0
    return m


def na_variants(geo):
    var = sorted({(idxp, combo) for lst in geo for (j, idxp, combo) in lst if combo != 0})
    mv, sel, cm = na_const_tables()
    m01 = np.zeros((len(var), 128, 128), np.float32)
    for vi, (idxp, combo) in enumerate(var):
        for rl in range(2):
            m01[vi, :, rl * 64:(rl + 1) * 64] = (mv[rl, combo] == 0.0).astype(np.float32)[:, None]
    return var, m01

import math
import numpy as np

D = 1024
KC = 8
NCTX = 256
NLAT = 2048
T = NCTX + NLAT
FH = 2816
FHC = FH // 128
EPS = 1e-6
N_IN = 8256


class K:
    pass


def declare_inputs(P, nl):
    nc = P.nc
    I = {}

    def inp(name, shape):
        I[name] = nc.dram_tensor(name, list(shape), F32, kind="ExternalInput").ap()

    inp("xT", [D, T])
    inp("cvec", [D, 2])
    inp("ident", [128, 128])
    inp("w_mod", [nl, D, 9 * D])
    inp("b_mod", [nl, 9 * D])
    inp("norm_g", [nl, 6, D])
    inp("ffn_w_in", [nl, 2, D, 2 * FH])
    inp("ffn_w_out", [nl, 2, FH, D])
    return I


def setup_common(P, k):
    k.ident = P.sbuf("ident", [128, 128], F32)
    P.dma(k.ident[:], k.I["ident"])
    k.ident_bf = P.sbuf("ident_bf", [128, 128], BF16)
    P.copy(k.ident_bf[:], k.ident[:])
    k.ones_bf = P.sbuf("ones_bf", [128, 128], BF16)
    P.memset(k.ones_bf[:], 1.0)
    k.eps_col = P.sbuf("eps_col", [128, 1], F32)
    P.memset(k.eps_col[:], EPS)
    k.xres = P.nc.dram_tensor("xres", [128, KC, T], F32, kind="Internal").ap()
    k.ps = [P.psum("psb%d" % i) for i in range(8)]
    cv = P.sbuf("cv", [128, KC, 2], F32)
    P.dma(cv[:], k.I["cvec"].rearrange("(c p) n -> p c n", p=128))
    k.actv = P.sbuf("actv", [128, KC, 2], BF16)
    P.act(k.actv[:], cv[:], AF.Silu)
    k.modT = P.sbuf("modT", [128, 72, 2], F32)
    k.normg = P.sbuf("normg", [128, 48], F32)
    k.Asc = P.sbuf("Asc", [128, 3, KC, 2], F32)
    k.Bsh = P.sbuf("Bsh", [128, 3, KC, 2], F32)
    k.Gg = P.sbuf("Gg", [128, 3, KC, 2], F32)


def layer_mods(P, k, l):
    nc = P.nc
    wm = k.I["w_mod"][l].rearrange("(c p) n -> p c n", p=128)
    bm_t = P.sbuf("bm_t", [72, 128], F32)
    P.dma(bm_t[:], k.I["b_mod"][l].rearrange("(m f) -> m f", f=128))
    ng_t = P.sbuf("ng_t", [48, 128], F32)
    P.dma(ng_t[:], k.I["norm_g"][l].rearrange("g (c f) -> (g c) f", f=128))
    ps_m = k.ps[0]
    ps_t = k.ps[1]
    wt = [P.sbuf("wmod%d" % i, [128, KC, 512], BF16) for i in range(4)]
    for j in range(18):
        w = wt[j % 4]
        P.dma(w[:], wm[:, :, j * 512:(j + 1) * 512], q="pool")
        for mm in range(4):
            m = j * 4 + mm
            for c in range(KC):
                P.mm(ps_m[:, 2 * m:2 * m + 2], w[:, c, mm * 128:(mm + 1) * 128], k.actv[:, c, :],
                     start=(c == 0), stop=(c == KC - 1))
    P.transpose(ps_t[:, 0:72], bm_t[:], k.ident[0:72, 0:72])
    bmT = P.sbuf("bmT", [128, 72], F32)
    P.copy(bmT[:], ps_t[:, 0:72])
    P.tt(k.modT[:], ps_m[:, 0:144].rearrange("p (m s) -> p m s", s=2),
         bmT[:].unsqueeze(2).broadcast_to([128, 72, 2]), ALU.add)
    P.transpose(ps_t[:, 128:176], ng_t[:], k.ident[0:48, 0:48])
    P.copy(k.normg[:], ps_t[:, 128:176])
    for s in range(3):
        base = 3 * s
        gpre = k.normg[:, (2 * s) * 8:(2 * s + 1) * 8].unsqueeze(2).broadcast_to([128, KC, 2])
        gpost = k.normg[:, (2 * s + 1) * 8:(2 * s + 2) * 8].unsqueeze(2).broadcast_to([128, KC, 2])
        P.stt(k.Asc[:, s], k.modT[:, (base + 1) * 8:(base + 2) * 8, :], 1.0, gpre, ALU.add, ALU.mult)
        P.copy(k.Bsh[:, s], k.modT[:, base * 8:(base + 1) * 8, :])
        P.stt(k.Gg[:, s], k.modT[:, (base + 2) * 8:(base + 3) * 8, :], (1.0 if s == 1 else 0.5), gpost, ALU.mult, ALU.mult)


def sumsq_rstd(P, k, src_fn, nchunks, subs, rstd, ps_ss, sq_tiles, inv_n):
    for (o, w) in subs:
        for c in range(nchunks):
            sq = sq_tiles[c % len(sq_tiles)]
            P.act(sq[:, :w], src_fn(c, o, w), AF.Square)
            P.mm(ps_ss[:, :w], k.ones_bf[:], sq[:, :w], start=(c == 0), stop=(c == nchunks - 1))
        P.act(rstd[:, o:o + w], ps_ss[:, :w], AF.Sqrt, bias=k.eps_col[:], scale=inv_n)
        P.recip(rstd[:, o:o + w], rstd[:, o:o + w])


def ffn_sublayer(P, k, l, s, blocks):
    fi = s // 2
    w_in = k.I["ffn_w_in"][l, fi].rearrange("(c p) n -> p c n", p=128)
    w_out = k.I["ffn_w_out"][l, fi].rearrange("(j p) n -> p j n", p=128)
    maxw = max(sum(w for (_, w, _) in b) for b in blocks)
    hbs = [P.sbuf("ffn_h%d" % i, [128, KC, maxw], BF16) for i in range(2)]
    gb = P.sbuf("ffn_g", [128, FHC, maxw], BF16)
    ob = P.sbuf("ffn_o", [128, KC, maxw], BF16)
    xs = [P.sbuf("ffn_x%d" % i, [128, KC, 512], F32) for i in range(2)]
    rstd = [P.sbuf("ffn_rstd%d" % i, [128, 512], F32) for i in range(2)]
    tmp = [P.sbuf("ffn_tmp%d" % i, [128, 512], F32) for i in range(2)]
    sq = [P.sbuf("ffn_sq%d" % i, [128, 512], BF16) for i in range(2)]
    sl = [P.sbuf("ffn_sl%d" % i, [128, 512], F32) for i in range(2)]
    wi = [P.sbuf("ffn_wi%d" % i, [128, KC, 1024], BF16) for i in range(2)]
    wo = [P.sbuf("ffn_wo%d" % i, [128, FHC, 128], BF16) for i in range(2)]
    ps_ss = k.ps[0]
    ps_a = [k.ps[1], k.ps[2]]
    ps_b = [k.ps[3], k.ps[4]]
    ps_o = [k.ps[5], k.ps[6]]
    st_ = {"cnt": 0, "x": 0, "w": 0}

    def subs_of(blk):
        subs = []
        o = 0
        for (c0, w, st) in blk:
            subs.append((o, c0, w, st))
            o += w
        return subs

    def eng3(c):
        return "dve"

    def prenorm(blk, hb):
        for (o, c0, w, st) in subs_of(blk):
            xb = xs[st_["x"] % 2]
            rs = rstd[st_["x"] % 2]
            st_["x"] += 1
            P.dma(xb[:, :, :w], k.xres[:, :, c0:c0 + w])
            sumsq_rstd(P, k, lambda c, oo, ww, xb=xb: xb[:, c, oo:oo + ww], KC, [(0, w)], rs, ps_ss, sq, 1.0 / D)
            for c in range(KC):
                t = tmp[c % 2]
                P.tt(t[:, :w], xb[:, c, :w], rs[:, :w], ALU.mult, e=eng3(c))
                P.ts(hb[:, c, o:o + w], t[:, :w], k.Asc[:, s, c, st:st + 1], k.Bsh[:, s, c, st:st + 1],
                     op0=ALU.mult, op1=ALU.add)

    def hidden(blk, hb):
        subs = subs_of(blk)
        for j4 in range((FHC + 3) // 4):
            w = wi[st_["w"] % 2]
            st_["w"] += 1
            nj = min(4, FHC - 4 * j4)
            P.dma(w[:, :, 0:nj * 128], w_in[:, :, j4 * 512:j4 * 512 + nj * 128], q="pool")
            P.dma(w[:, :, 512:512 + nj * 128], w_in[:, :, FH + j4 * 512:FH + j4 * 512 + nj * 128], q="pool")
            for jj in range(nj):
                j = 4 * j4 + jj
                for (o, c0, ww, st) in subs:
                    cnt = st_["cnt"]
                    pa = ps_a[cnt % 2]
                    pb = ps_b[cnt % 2]
                    slt = sl[cnt % 2]
                    st_["cnt"] += 1
                    for c in range(KC):
                        P.mm(pa[:, :ww], w[:, c, jj * 128:(jj + 1) * 128], hb[:, c, o:o + ww],
                             start=(c == 0), stop=(c == KC - 1))
                    for c in range(KC):
                        P.mm(pb[:, :ww], w[:, c, 512 + jj * 128:512 + (jj + 1) * 128], hb[:, c, o:o + ww],
                             start=(c == 0), stop=(c == KC - 1))
                    P.act(slt[:, :ww], pa[:, :ww], AF.Silu)
                    P.tt(gb[:, j, o:o + ww], slt[:, :ww], pb[:, :ww], ALU.mult)

    def outproj(blk):
        subs = subs_of(blk)
        for c in range(KC):
            w = wo[c % 2]
            P.dma(w[:], w_out[:, :, c * 128:(c + 1) * 128], q="pool")
            for (o, c0, ww, st) in subs:
                po = ps_o[st_["cnt"] % 2]
                st_["cnt"] += 1
                for j in range(FHC):
                    P.mm(po[:, :ww], w[:, j, :], gb[:, j, o:o + ww], start=(j == 0), stop=(j == FHC - 1))
                P.copy(ob[:, c, o:o + ww], po[:, :ww], e="act")

    def postnorm(blk):
        subs = subs_of(blk)
        base = st_["x"]
        st_["x"] += len(subs)
        P.dma(xs[base % 2][:, :, :subs[0][2]], k.xres[:, :, subs[0][1]:subs[0][1] + subs[0][2]])
        for si, (o, c0, w, st) in enumerate(subs):
            xb = xs[(base + si) % 2]
            rs = rstd[(base + si) % 2]
            if si + 1 < len(subs):
                (o2, c2, w2, st2) = subs[si + 1]
                P.dma(xs[(base + si + 1) % 2][:, :, :w2], k.xres[:, :, c2:c2 + w2])
            sumsq_rstd(P, k, lambda c, oo, ww, o=o: ob[:, c, o + oo:o + oo + ww], KC, [(0, w)], rs, ps_ss, sq, 1.0 / D)
            for c in range(KC):
                t = tmp[c % 2]
                P.stt(t[:, :w], ob[:, c, o:o + w], k.Gg[:, s, c, st:st + 1], rs[:, :w], ALU.mult, ALU.mult)
                P.tt(xb[:, c, :w], xb[:, c, :w], t[:, :w], ALU.add, e=eng3(c))
            P.dma(k.xres[:, :, c0:c0 + w], xb[:, :, :w], q="sp")

    prenorm(blocks[0], hbs[0])
    for bi, blk in enumerate(blocks):
        hidden(blk, hbs[bi % 2])
        if bi + 1 < len(blocks):
            prenorm(blocks[bi + 1], hbs[(bi + 1) % 2])
        outproj(blk)
        postnorm(blk)


FULL_BLOCKS = [
    [(0, 256, 1), (256, 512, 0), (768, 384, 0)],
    [(1152, 384, 0), (1536, 384, 0), (1920, 384, 0)],
]
LAT_BLOCKS = [
    [(256, 512, 0), (768, 512, 0)],
    [(1280, 512, 0), (1792, 512, 0)],
]


CT0 = 2
LT0 = 260
HW = 2310
ALLSUBS = [(0, 256, 1), (256, 512, 0), (768, 512, 0), (1280, 512, 0), (1792, 512, 0)]
LATSUBS = ALLSUBS[1:]
C_CKV, C_KR, C_NK, C_NV, C_U, C_CQ, C_NQ, C_HY, C_GT = 0, 256, 320, 832, 1344, 1856, 2112, 2624, 4160


def hcol(xc):
    return xc + CT0 if xc < NCTX else xc - NCTX + LT0


def declare_mixer_inputs(P, I, nl):
    nc = P.nc

    def inp(name, shape):
        I[name] = nc.dram_tensor(name, list(shape), F32, kind="ExternalInput").ap()
    inp("w_in", [nl, D, N_IN])
    inp("mla_g_q", [nl, 256]); inp("mla_g_kv", [nl, 256])
    inp("mla_w_uq", [nl, 256, 4, 192]); inp("mla_w_ukv", [nl, 256, 4, 256])
    inp("w_branch", [nl, 4, 512, D]); inp("w_out", [nl, D, D])
    inp("rope_c", [64, NLAT]); inp("rope_s", [64, NLAT])


def mixer_setup(P, k):
    nc = P.nc
    k.br = [nc.dram_tensor("br%d" % n, [512, T], BF16, kind="Internal").ap() for n in range(4)]
    k.hmx = nc.dram_tensor("hmx", [128, KC, HW], BF16, kind="Internal").ap()


def alloc_hmix(P, k):
    k.hmix = P.sbuf("hmix", [128, KC, HW], BF16)
    for c0 in (0, 258, 2308):
        P.memset(k.hmix[:, :, c0:c0 + 2], 0.0)


def mixer_modnorm(P, k):
    with P.scope():
        rstd = P.sbuf("mn_rstd", [128, 512], F32)
        tmp = [P.sbuf("mn_tmp%d" % i, [128, 512], F32) for i in range(2)]
        sq = [P.sbuf("mn_sq%d" % i, [128, 512], BF16) for i in range(2)]
        xt = [P.sbuf("mn_x%d" % i, [128, KC, 512], F32) for i in range(2)]
        for si, (c0, w, st) in enumerate(ALLSUBS):
            xb = xt[si % 2]
            P.dma(xb[:, :, :w], k.xres[:, :, c0:c0 + w])
            sumsq_rstd(P, k, lambda c, oo, ww, xb=xb: xb[:, c, oo:oo + ww], KC, [(0, w)], rstd, k.ps[0], sq, 1.0 / D)
            h0 = hcol(c0)
            for c in range(KC):
                t = tmp[c % 2]
                P.stt(t[:, :w], xb[:, c, 0:w], k.Asc[:, 1, c, st:st + 1], rstd[:, 0:w], ALU.mult, ALU.mult)
                P.act(k.hmix[:, c, h0:h0 + w], t[:, :w], AF.Identity, bias=k.Bsh[:, 1, c, st:st + 1])
    P.dma(k.hmx, k.hmix[:], q="sp")


def load_col_vec(P, dst, src_1d, nchunk):
    P.dma(dst, src_1d.rearrange("(c p) -> p c", p=128), allow_slow_non_contiguous=True)


def mla_branch(P, k, l, ctx_out):
    nc = P.nc
    I = k.I
    w_in = I["w_in"][l].rearrange("(c p) n -> p c n", p=128)
    SC = 192.0 ** -0.5
    subs = ALLSUBS
    qsubs = ALLSUBS if ctx_out else LATSUBS
    with P.scope():
        wckv = P.sbuf("wckv", [128, KC, 256], BF16)
        P.dma(wckv[:], w_in[:, :, C_CKV:C_CKV + 256], q="pool")
        wkr = P.sbuf("wkr", [128, KC, 128], BF16)
        P.dma(wkr[:, :, 0:64], w_in[:, :, C_KR:C_KR + 64], q="pool")
        for a in range(2):
            for hf in range(2):
                P.dma(wkr[:, :, 64 + a * 32 + hf * 16:64 + a * 32 + hf * 16 + 16],
                      w_in[:, :, C_KR + a * 32 + (1 - hf) * 16:C_KR + a * 32 + (1 - hf) * 16 + 16], q="pool")
        wcq = P.sbuf("wcq", [128, KC, 256], BF16)
        P.dma(wcq[:], w_in[:, :, C_CQ:C_CQ + 256], q="pool")
        wukv = P.sbuf("wukv", [128, 2, 4, 256], BF16)
        P.dma(wukv[:], I["mla_w_ukv"][l].rearrange("(c p) h e -> p c h e", p=128), q="pool")
        wuq = P.sbuf("wuq", [128, 2, 4, 192], BF16)
        P.dma(wuq[:], I["mla_w_uq"][l].rearrange("(c p) h e -> p c h e", p=128), q="pool")
        wuqs = P.sbuf("wuqs", [128, 2, 4, 64], BF16)
        uq_r = I["mla_w_uq"][l].rearrange("(c p) h e -> p c h e", p=128)
        for a in range(2):
            for hf in range(2):
                for c in range(2):
                    P.dma(wuqs[:, c, :, a * 32 + hf * 16:a * 32 + hf * 16 + 16],
                          uq_r[:, c, :, 128 + a * 32 + (1 - hf) * 16:128 + a * 32 + (1 - hf) * 16 + 16], q="pool")
        gkv = P.sbuf("gkv", [128, 2], F32)
        load_col_vec(P, gkv[:], I["mla_g_kv"][l], 2)
        gq = P.sbuf("gq", [128, 2], F32)
        load_col_vec(P, gq[:], I["mla_g_q"][l], 2)
        ropc_t = [P.sbuf("ropc%d" % i, [64, 512], F32) for i in range(2)]
        rops_t = [P.sbuf("rops%d" % i, [64, 512], F32) for i in range(2)]
        rcnt = [0]

        def rope_tabs(l0, w):
            i = rcnt[0] % 2
            rcnt[0] += 1
            P.dma(ropc_t[i][:, :w], I["rope_c"][:, l0:l0 + w])
            P.dma(rops_t[i][:, :w], I["rope_s"][:, l0:l0 + w])
            return ropc_t[i], rops_t[i]
        nkv = P.sbuf("nkv", [128, 2, T], BF16)
        nq = P.sbuf("nq", [128, 2, T], BF16)
        krope = P.sbuf("krope", [64, T], BF16)
        vall = P.sbuf("vall", [128, 18, 128], BF16)
        aT = P.sbuf("aT", [128, T], BF16)
        raw = P.sbuf("raw", [128, 2, 512], F32)
        rstd = P.sbuf("rstd", [128, 512], F32)
        sq = [P.sbuf("sq%d" % i, [128, 512], BF16) for i in range(2)]
        t1 = P.sbuf("t1", [128, 512], F32)
        t2 = P.sbuf("t2", [128, 512], F32)
        ps = k.ps

        def lowrank_norm(wt, gvec, dst):
            for (c0, w, st) in (subs if dst is nkv else qsubs):
                h0 = hcol(c0)
                for m in range(2):
                    for c in range(KC):
                        P.mm(ps[1 + m][:, :w], wt[:, c, m * 128:(m + 1) * 128], k.hmix[:, c, h0:h0 + w],
                             start=(c == 0), stop=(c == KC - 1))
                    P.copy(raw[:, m, :w], ps[1 + m][:, :w], e="act")
                sumsq_rstd(P, k, lambda c, oo, ww: raw[:, c, oo:oo + ww], 2, [(0, w)], rstd, ps[0], sq, 1.0 / 256)
                for m in range(2):
                    P.stt(dst[:, m, c0:c0 + w], raw[:, m, :w], gvec[:, m:m + 1], rstd[:, :w], ALU.mult, ALU.mult)

        lowrank_norm(wckv, gkv, nkv)
        lowrank_norm(wcq, gq, nq)
        for (c0, w, st) in subs:
            h0 = hcol(c0)
            for hh in range(2):
                for c in range(KC):
                    P.mm(ps[1 + hh][0:64, :w], wkr[:, c, hh * 64:(hh + 1) * 64], k.hmix[:, c, h0:h0 + w],
                         start=(c == 0), stop=(c == KC - 1))
            if st == 1:
                P.copy(krope[:, c0:c0 + w], ps[1][0:64, :w], e="act")
            else:
                l0 = c0 - NCTX
                rc, rs = rope_tabs(l0, w)
                P.tt(t1[0:64, :w], ps[1][0:64, :w], rc[:, :w], ALU.mult)
                P.tt(t2[0:64, :w], ps[2][0:64, :w], rs[:, :w], ALU.mult)
                P.tt(krope[:, c0:c0 + w], t1[0:64, :w], t2[0:64, :w], ALU.add, e="pool")
        knT = P.sbuf("knT", [128, T], BF16)
        qnT = P.sbuf("qnT", [128, T], BF16)
        qrope = P.sbuf("qrope", [64, T], BF16)
        pT = [P.sbuf("pT%d" % i, [128, 512], BF16) for i in range(3)]
        rden = P.sbuf("rden", [128, 512], F32)
        for hd in range(4):
            for tc in range(18):
                pv = ps[1 + tc % 2]
                for c in range(2):
                    P.mm(pv[:, 0:128], nkv[:, c, tc * 128:(tc + 1) * 128], wukv[:, c, hd, 128:256], start=(c == 0), stop=(c == 1))
                P.copy(vall[:, tc, :], pv[:, 0:128], e=("act" if tc % 2 else "dve"))
            for (c0, w, st) in subs:
                for c in range(2):
                    P.mm(ps[1][:, :w], wukv[:, c, hd, 0:128], nkv[:, c, c0:c0 + w], start=(c == 0), stop=(c == 1))
                P.copy(knT[:, c0:c0 + w], ps[1][:, :w], e="act")
            for (c0, w, st) in qsubs:
                for c in range(2):
                    P.mm(ps[1][:, :w], wuq[:, c, hd, 0:128], nq[:, c, c0:c0 + w], start=(c == 0), stop=(c == 1))
                P.copy(qnT[:, c0:c0 + w], ps[1][:, :w], e="act")
                for c in range(2):
                    P.mm(ps[2][0:64, :w], wuq[:, c, hd, 128:192], nq[:, c, c0:c0 + w], start=(c == 0), stop=(c == 1))
                if st == 1:
                    P.copy(qrope[:, c0:c0 + w], ps[2][0:64, :w], e="dve")
                else:
                    for c in range(2):
                        P.mm(ps[3][0:64, :w], wuqs[:, c, hd, :], nq[:, c, c0:c0 + w], start=(c == 0), stop=(c == 1))
                    l0 = c0 - NCTX
                    rc, rs = rope_tabs(l0, w)
                    P.tt(t1[0:64, :w], ps[2][0:64, :w], rc[:, :w], ALU.mult)
                    P.tt(t2[0:64, :w], ps[3][0:64, :w], rs[:, :w], ALU.mult)
                    P.tt(qrope[:, c0:c0 + w], t1[0:64, :w], t2[0:64, :w], ALU.add, e="pool")
            cnt = 0
            for qi, (c0, w, st) in enumerate(qsubs):
                kcs = list(range(2)) if st == 1 else list(range(18))
                pso, psd = (ps[6], ps[7]) if qi % 2 == 0 else (ps[2], ps[3])
                base = cnt
                cnt += len(kcs)

                def emit_s(i):
                    kc = kcs[i]
                    pss = ps[4 + (base + i) % 2]
                    P.mm(pss[:, :w], knT[:, kc * 128:(kc + 1) * 128], qnT[:, c0:c0 + w], start=True, stop=False)
                    P.mm(pss[:, :w], krope[:, kc * 128:(kc + 1) * 128], qrope[:, c0:c0 + w], start=False, stop=True)
                emit_s(0)
                for i, kc in enumerate(kcs):
                    if i + 1 < len(kcs):
                        emit_s(i + 1)
                    pss = ps[4 + (base + i) % 2]
                    p = pT[(base + i) % 3]
                    P.act(p[:, :w], pss[:, :w], AF.Exp, scale=SC)
                    P.mm(pso[:, :w], vall[:, kc, :], p[:, :w], start=(i == 0), stop=(i == len(kcs) - 1))
                    P.mm(psd[:, :w], k.ones_bf[:], p[:, :w], start=(i == 0), stop=(i == len(kcs) - 1))
                P.recip(rden[:, :w], psd[:, :w])
                P.tt(aT[:, c0:c0 + w], pso[:, :w], rden[:, :w], ALU.mult)
            cols0 = 0 if ctx_out else NCTX
            P.dma(k.br[0][hd * 128:(hd + 1) * 128, cols0:T], aT[:, cols0:T], q="sp")


def declare_na_inputs(P, I, nl):
    nc = P.nc

    def inp(name, shape):
        I[name] = nc.dram_tensor(name, list(shape), F32, kind="ExternalInput").ap()
    inp("na_G", [nl, 128, 8 * 16 * 64])
    inp("na_mv", [2, 16 * 128]); inp("na_sel", [2, 128]); inp("na_cm", [128, 64])
    inp("na_m01", [len(na_variants(na_geometry())[0]), 128, 128])


def na_branch(P, k, l, ctx_out, geo):
    nc = P.nc
    I = k.I
    w_in = I["w_in"][l].rearrange("(c p) n -> p c n", p=128)
    ps = k.ps
    with P.scope():
        BP = P.sbuf("na_BP", [128, 8, 16, 64], BF16)
        cm = P.sbuf("na_cm", [128, 64], F32)
        P.dma(cm[:], I["na_cm"])
        gt = [P.sbuf("na_gt%d" % i, [128, 16, 64], F32) for i in range(2)]
        Gr = I["na_G"][l].rearrange("p (h i q) -> p h i q", h=8, i=16)
        for h in range(8):
            P.dma(gt[h % 2][:], Gr[:, h])
            P.tt(BP[:, h], gt[h % 2][:], cm[:].unsqueeze(1).broadcast_to([128, 16, 64]), ALU.add)
            P.act(BP[:, h], BP[:, h], AF.Exp)
        variants = na_variants(geo)[0]
        nv = len(variants)
        m01f = P.sbuf("na_m01f", [128, nv, 128], F32)
        for vi in range(nv):
            P.dma(m01f[:, vi, :], I["na_m01"][vi])
        EBm = P.sbuf("na_EBm", [128, 8, nv, 128], BF16)
        for h in range(8):
            for vi, (idxp_, combo_) in enumerate(variants):
                P.tt(EBm[:, h, vi, :], BP[:, h, idxp_:idxp_ + 2, :], m01f[:, vi, :], ALU.mult, e=("dve" if (h + vi) % 2 else "pool"))
        wk = P.sbuf("na_wk", [128, KC, 128], BF16)
        wq = P.sbuf("na_wq", [128, KC, 128], BF16)
        wv = P.sbuf("na_wv", [128, KC, 128], BF16)
        KT = P.sbuf("na_KT", [128, T], BF16)
        QT = P.sbuf("na_QT", [128, T], BF16)
        V = P.sbuf("na_V", [128, 18, 128], BF16)
        dT = P.sbuf("na_dT", [128, T], BF16)
        pT = [P.sbuf("na_pT%d" % i, [128, 128], BF16) for i in range(4)]
        rden = P.sbuf("na_rden", [128, 128], F32)
        qsubs = ALLSUBS if ctx_out else LATSUBS
        cnt = 0
        for hp in range(4):
            P.dma(wk[:], w_in[:, :, C_NK + hp * 128:C_NK + (hp + 1) * 128], q="pool")
            P.dma(wq[:], w_in[:, :, C_NQ + hp * 128:C_NQ + (hp + 1) * 128], q="pool")
            P.dma(wv[:], w_in[:, :, C_NV + hp * 128:C_NV + (hp + 1) * 128], q="pool")
            for (c0, w, st) in ALLSUBS:
                h0 = hcol(c0)
                for c in range(KC):
                    P.mm(ps[1][:, :w], wk[:, c, :], k.hmix[:, c, h0:h0 + w], start=(c == 0), stop=(c == KC - 1))
                P.copy(KT[:, c0:c0 + w], ps[1][:, :w], e="act")
            for (c0, w, st) in qsubs:
                h0 = hcol(c0)
                for c in range(KC):
                    P.mm(ps[2][:, :w], wq[:, c, :], k.hmix[:, c, h0:h0 + w], start=(c == 0), stop=(c == KC - 1))
                P.ts(QT[:, c0:c0 + w], ps[2][:, :w], 0.125, None, op0=ALU.mult)
            for tc in range(18):
                h0 = hcol(tc * 128)
                pv = ps[1 + tc % 2]
                for c in range(KC):
                    P.mm(pv[:, 0:128], k.hmix[:, c, h0:h0 + 128], wv[:, c, :], start=(c == 0), stop=(c == KC - 1))
                P.copy(V[:, tc, :], pv[:, 0:128], e=("act" if tc % 2 else "dve"))
            qblocks = []
            if ctx_out:
                qblocks += [(0, []), (128, [])]
            for i in range(16):
                qblocks.append((NCTX + i * 128, geo[i]))
            for qi, (q0, loc) in enumerate(qblocks):
                chunks = [(0, None, None), (1, None, None)] + [(2 + j, idxp, combo) for (j, idxp, combo) in loc]
                pso, psd = (ps[6], ps[7]) if qi % 2 == 0 else (ps[2], ps[3])
                work = [(hh, ci) for ci in range(len(chunks)) for hh in range(2)]
                base = cnt
                cnt += len(work)

                sbanks = [ps[4], ps[5], ps[1]]

                def emit_s(wi):
                    hh, ci = work[wi]
                    kc, idxp, combo = chunks[ci]
                    pr = slice(hh * 64, (hh + 1) * 64)
                    pss = sbanks[(base + wi) % 3]
                    P.mm(pss[:, 0:128], KT[pr, kc * 128:(kc + 1) * 128], QT[pr, q0:q0 + 128], start=True, stop=True)
                emit_s(0)
                if len(work) > 1:
                    emit_s(1)
                for wi, (hh, ci) in enumerate(work):
                    if wi + 2 < len(work):
                        emit_s(wi + 2)
                    kc, idxp, combo = chunks[ci]
                    h = 2 * hp + hh
                    pr = slice(hh * 64, (hh + 1) * 64)
                    pss = sbanks[(base + wi) % 3]
                    p = pT[(base + wi) % 4]
                    P.act(p[:], pss[:, 0:128], AF.Exp)
                    if idxp is not None:
                        if combo == 0:
                            tab = BP[:, h, idxp:idxp + 2, :]
                        else:
                            tab = EBm[:, h, variants.index((idxp, combo)), :]
                        P.tt(p[:], p[:], tab, ALU.mult)
                    P.mm(pso[pr, 0:128], V[:, kc, pr], p[:], start=(ci == 0), stop=(ci == len(chunks) - 1))
                    P.mm(psd[pr, 0:128], k.ones_bf[:, 0:64], p[:], start=(ci == 0), stop=(ci == len(chunks) - 1))
                P.recip(rden[:], psd[:, 0:128])
                P.tt(dT[:, q0:q0 + 128], pso[:, 0:128], rden[:], ALU.mult)
            cols0 = 0 if ctx_out else NCTX
            P.dma(k.br[3][hp * 128:(hp + 1) * 128, cols0:T], dT[:, cols0:T], q="sp")


def post_norm_residual(P, k, ob, s, subs_local, rstd, ps_ss, sq, tmp, xb):
    for (o, c0, w, st) in subs_local:
        P.dma(xb[:, :, o:o + w], k.xres[:, :, c0:c0 + w])
        sumsq_rstd(P, k, lambda c, oo, ww: ob[:, c, oo:oo + ww], KC, [(o, w)], rstd, ps_ss, sq, 1.0 / D)
        for c in range(KC):
            t = tmp[c % 2]
            P.stt(t[:, :w], ob[:, c, o:o + w], k.Gg[:, s, c, st:st + 1], rstd[:, o:o + w], ALU.mult, ALU.mult)
            P.tt(xb[:, c, o:o + w], xb[:, c, o:o + w], t[:, :w], ALU.add, e=("pool" if c % 3 == 2 else "dve"))
        P.dma(k.xres[:, :, c0:c0 + w], xb[:, :, o:o + w], q="sp")


def merge_phase(P, k, l, ctx_out):
    I = k.I
    w_in = I["w_in"][l].rearrange("(c p) n -> p c n", p=128)
    ps = k.ps
    subs = ALLSUBS if ctx_out else LATSUBS
    with P.scope():
        wg = P.sbuf("mg_wg", [128, KC, 4 * D], BF16)
        wb = P.sbuf("mg_wb", [128, 4, 4, D], BF16)
        for half in range(2):
            for j in range(half, 8, 2):
                P.dma(wg[:, :, j * 512:(j + 1) * 512], w_in[:, :, C_GT + j * 512:C_GT + (j + 1) * 512], q="pool")
            for n in range(4):
                P.dma(wb[:, n, :, half * 512:(half + 1) * 512],
                      I["w_branch"][l, n].rearrange("(kk p) d -> p kk d", p=128)[:, :, half * 512:(half + 1) * 512], q="pool")
        wo = P.sbuf("mg_wo", [128, KC, D], BF16)
        for j in range(2):
            P.dma(wo[:, :, j * 512:(j + 1) * 512], I["w_out"][l].rearrange("(kk p) d -> p kk d", p=128)[:, :, j * 512:(j + 1) * 512], q="pool")
        brt = [[P.sbuf("mg_br%d_%d" % (n, i), [128, 4, 512], BF16) for n in range(4)] for i in range(1)]
        mt = P.sbuf("mg_mt", [128, KC, 512], BF16)
        ob = P.sbuf("mg_ob", [128, KC, 512], BF16)
        sg = [P.sbuf("mg_sg%d" % i, [128, 512], F32) for i in range(2)]
        acc = P.sbuf("mg_acc", [128, 512], F32)
        tm = [P.sbuf("mg_tm%d" % i, [128, 512], F32) for i in range(2)]
        rstd = P.sbuf("mg_rstd", [128, 512], F32)
        sq = [P.sbuf("mg_sq%d" % i, [128, 512], BF16) for i in range(2)]
        xbm = P.sbuf("mg_x", [128, KC, 512], F32)
        hbt = [P.sbuf("mg_h%d" % i, [128, KC, 512], BF16) for i in range(2)]
        cnt = 0
        for si, (c0, w, st) in enumerate(subs):
            h0 = hcol(c0)
            hb_ = hbt[si % 2]
            P.dma(hb_[:, :, :w], k.hmx[:, :, h0:h0 + w])
            bt = brt[0]
            for n in range(4):
                P.dma(bt[n][:, :, :w], k.br[n].rearrange("(c p) t -> p c t", p=128)[:, :, c0:c0 + w])
            for dc in range(KC):
                for n in range(4):
                    pg = ps[1 + cnt % 2]
                    pp = ps[3 + cnt % 2]
                    s_ = sg[cnt % 2]
                    cnt += 1
                    col = n * D + dc * 128
                    for c in range(KC):
                        P.mm(pg[:, :w], wg[:, c, col:col + 128], hb_[:, c, :w], start=(c == 0), stop=(c == KC - 1))
                    for kk in range(4):
                        P.mm(pp[:, :w], wb[:, n, kk, dc * 128:(dc + 1) * 128], bt[n][:, kk, :w], start=(kk == 0), stop=(kk == 3))
                    P.act(s_[:, :w], pg[:, :w], AF.Sigmoid)
                    if n == 0:
                        P.tt(acc[:, :w], s_[:, :w], pp[:, :w], ALU.mult)
                    else:
                        t = tm[n % 2]
                        P.tt(t[:, :w], s_[:, :w], pp[:, :w], ALU.mult)
                        if n < 3:
                            P.tt(acc[:, :w], acc[:, :w], t[:, :w], ALU.add, e="pool")
                        else:
                            P.tt(mt[:, dc, :w], acc[:, :w], t[:, :w], ALU.add, e="pool")
            for dc in range(KC):
                po = ps[5 + dc % 2]
                for kk in range(KC):
                    P.mm(po[:, :w], wo[:, kk, dc * 128:(dc + 1) * 128], mt[:, kk, :w], start=(kk == 0), stop=(kk == KC - 1))
                P.copy(ob[:, dc, :w], po[:, :w], e="act")
            post_norm_residual(P, k, ob, 1, [(0, c0, w, st)], rstd, ps[0], sq, tm, xbm)


def declare_hyena_inputs(P, I, nl, with_ctx=True):
    nc = P.nc

    def inp(name, shape):
        I[name] = nc.dram_tensor(name, list(shape), F32, kind="ExternalInput").ap()
    inp("hy_conv_w", [nl, 3, 1536]); inp("hy_conv_b", [nl, 1536]); inp("hy_bias", [nl, 512])
    inp("hy_w1", [nl, 33, 64]); inp("hy_b1", [nl, 64]); inp("hy_freq1", [nl, 64])
    inp("hy_w2", [nl, 64, 64]); inp("hy_b2", [nl, 64]); inp("hy_freq2", [nl, 64]); inp("hy_w3", [nl, 64, 1024])
    for nm, N in (("lat", NLAT), ("ctx", NCTX)):
        if nm == "ctx" and not with_ctx:
            continue
        for t in ("c", "s", "ct", "st"):
            I["dft_%s_%s" % (t, nm)] = nc.dram_tensor("dft_%s_%s" % (t, nm), [N // 128, 128, N // 128, 128], BF16, kind="ExternalInput").ap()
        inp("hy_zT_%s" % nm, [33, N]); inp("hy_decay_%s" % nm, [N, 512])


PI = math.pi


def sin_reduced(P, dst, src, shp, tmp):
    P.ts(tmp, src, PI, -2.0 * PI, op0=ALU.is_gt, op1=ALU.mult)
    P.tt(src, src, tmp, ALU.add)
    P.ts(tmp, src, -PI, 2.0 * PI, op0=ALU.is_lt, op1=ALU.mult)
    P.tt(src, src, tmp, ALU.add)
    P.act(dst, src, AF.Sin)


def hyena_branch(P, k, l, seq):
    nc = P.nc
    I = k.I
    nm, N, hbase, xbase = seq
    NT = N // 128
    NF = NT
    CW = min(512, N)
    w_in = I["w_in"][l].rearrange("(c p) n -> p c n", p=128)
    ps = k.ps
    dft_c = I["dft_c_%s" % nm]
    dft_s = I["dft_s_%s" % nm]
    dft_ct = I["dft_ct_%s" % nm]
    dft_st = I["dft_st_%s" % nm]
    with P.scope():
        Kre = P.sbuf("hy_Kre", [128, NF, 512], BF16)
        Kim = P.sbuf("hy_Kim", [128, NF, 512], BF16)
        Ct = [P.sbuf("hy_Ct%d" % i, [128, NT, 128], BF16) for i in range(2)]
        St = [P.sbuf("hy_St%d" % i, [128, NT, 128], BF16) for i in range(2)]
        tA = P.sbuf("hy_tA", [128, 512], F32)
        tB = P.sbuf("hy_tB", [128, 512], F32)
        tC = P.sbuf("hy_tC", [128, 512], F32)
        tD = P.sbuf("hy_tD", [128, 512], F32)
        with P.scope():
            w1 = P.sbuf("hy_w1", [33, 64], F32); P.dma(w1[:], I["hy_w1"][l])
            w2 = P.sbuf("hy_w2", [64, 64], F32); P.dma(w2[:], I["hy_w2"][l])
            w3 = P.sbuf("hy_w3", [64, 1024], F32); P.dma(w3[:], I["hy_w3"][l])
            cols = P.sbuf("hy_cols", [64, 6], F32)
            for j, nm_ in enumerate(["hy_b1", "hy_freq1", "hy_b2", "hy_freq2"]):
                P.dma(cols[:, j:j + 1], I[nm_][l].rearrange("(p o) -> p o", o=1))
            P.tt(cols[:, 4:5], cols[:, 0:1], cols[:, 1:2], ALU.mult)
            P.tt(cols[:, 5:6], cols[:, 2:3], cols[:, 3:4], ALU.mult)
            zT = P.sbuf("hy_zT", [33, N], F32); P.dma(zT[:], I["hy_zT_%s" % nm])
            h1 = P.sbuf("hy_h1", [64, N], F32)
            h2 = P.sbuf("hy_h2", [64, N], F32)
            filt = P.sbuf("hy_filt", [128, NT, 1024], BF16)
            dec = [P.sbuf("hy_dec%d" % i, [128, 512], F32) for i in range(2)]
            for cb in range(N // CW):
                cs = slice(cb * CW, (cb + 1) * CW)
                P.mm(ps[1][0:64, :CW], w1[:, :], zT[:, cs], start=True, stop=True)
                P.act(tA[0:64, :CW], ps[1][0:64, :CW], AF.Identity, bias=cols[:, 4:5], scale=cols[:, 1:2])
                sin_reduced(P, h1[:, cs], tA[0:64, :CW], None, tB[0:64, :CW])
            for cb in range(N // CW):
                cs = slice(cb * CW, (cb + 1) * CW)
                P.mm(ps[1][0:64, :CW], w2[:, :], h1[:, cs], start=True, stop=True)
                P.act(tA[0:64, :CW], ps[1][0:64, :CW], AF.Identity, bias=cols[:, 5:6], scale=cols[:, 3:4])
                sin_reduced(P, h2[:, cs], tA[0:64, :CW], None, tB[0:64, :CW])
            for tc in range(NT):
                d = dec[tc % 2]
                P.dma(d[:], I["hy_decay_%s" % nm][tc * 128:(tc + 1) * 128, :])
                for hf in range(2):
                    pp = ps[1 + hf]
                    P.mm(pp[:, :], h2[:, tc * 128:(tc + 1) * 128], w3[:, hf * 512:(hf + 1) * 512], start=True, stop=True)
                    P.tt(filt[:, tc, hf * 512:(hf + 1) * 512], pp[:, :], d[:], ALU.mult)
            fsd = P.sbuf("hy_fsd", [128, NT, 1024], BF16)
            for tc in range(NT):
                P.tt(fsd[:, tc, 0:512], filt[:, tc, 0:512], filt[:, tc, 512:1024], ALU.add, e=("dve" if tc % 2 else "pool"))
                P.tt(fsd[:, tc, 512:1024], filt[:, tc, 0:512], filt[:, tc, 512:1024], ALU.subtract, e=("pool" if tc % 2 else "dve"))
            for fc in range(NF):
                c_ = Ct[fc % 2]; s_ = St[fc % 2]
                P.dma(c_[:], dft_c[fc])
                P.dma(s_[:], dft_s[fc])
                pa = ps[1 + 2 * (fc % 2)]; pb = ps[2 + 2 * (fc % 2)]
                for tc in range(NT):
                    P.mm(pa[:, :], c_[:, tc, :], fsd[:, tc, 0:512], start=(tc == 0), stop=(tc == NT - 1))
                for tc in range(NT):
                    P.mm(pb[:, :], s_[:, tc, :], fsd[:, tc, 512:1024], start=(tc == 0), stop=(tc == NT - 1))
                P.copy(Kre[:, fc, :], pa[:, :], e="act")
                P.copy(Kim[:, fc, :], pb[:, :], e="dve")
        s_bf = P.sbuf("hy_s", [128, NT, 512], BF16)
        x0_bf = P.sbuf("hy_x0", [128, NT, 512], BF16)
        with P.scope():
            wz = [P.sbuf("hy_wz%d" % i, [128, KC, 128], BF16) for i in range(3)]
            cwc = P.sbuf("hy_cwc", [128, 3, 12], F32)
            for kk in range(3):
                load_col_vec(P, cwc[:, kk, :], I["hy_conv_w"][l, kk], 12)
            cbc = P.sbuf("hy_cbc", [128, 12], F32)
            load_col_vec(P, cbc[:], I["hy_conv_b"][l], 12)
            zs = P.sbuf("hy_zs", [128, N + 2], F32)
            P.memset(zs[:, 0:1], 0.0)
            P.memset(zs[:, N + 1:N + 2], 0.0)
            uu = P.sbuf("hy_uu", [128, N], F32)
            vT = P.sbuf("hy_vT", [128, N], F32)
            sT = P.sbuf("hy_sT", [128, 4, N], BF16)
            x0T = P.sbuf("hy_x0T", [128, 4, N], BF16)
            wcnt = 0
            for m in range(4):
                for blk in range(3):
                    ch = blk * 4 + m
                    w_ = wz[wcnt % 3]
                    wcnt += 1
                    P.dma(w_[:], w_in[:, :, C_HY + ch * 128:C_HY + (ch + 1) * 128], q="pool")
                    for sb in range(N // CW):
                        pp = ps[1 + sb % 2]
                        for c in range(KC):
                            P.mm(pp[:, :CW], w_[:, c, :], k.hmix[:, c, hbase + sb * CW:hbase + (sb + 1) * CW],
                                 start=(c == 0), stop=(c == KC - 1))
                        P.copy(zs[:, 1 + sb * CW:1 + (sb + 1) * CW], pp[:, :CW], e="act")
                    P.act(uu[:], zs[:, 1:N + 1], AF.Identity, bias=cbc[:, ch:ch + 1], scale=cwc[:, 1, ch:ch + 1])
                    P.stt(uu[:], zs[:, 0:N], cwc[:, 0, ch:ch + 1], uu[:], ALU.mult, ALU.add)
                    if blk == 0:
                        P.stt(vT[:], zs[:, 2:N + 2], cwc[:, 2, ch:ch + 1], uu[:], ALU.mult, ALU.add)
                    elif blk == 1:
                        P.stt(uu[:], zs[:, 2:N + 2], cwc[:, 2, ch:ch + 1], uu[:], ALU.mult, ALU.add)
                        P.tt(sT[:, m, :], vT[:], uu[:], ALU.mult, e="pool")
                    else:
                        P.stt(x0T[:, m, :], zs[:, 2:N + 2], cwc[:, 2, ch:ch + 1], uu[:], ALU.mult, ALU.add)
            for tc in range(NT):
                for si_, (src, dst) in enumerate(((sT, s_bf), (x0T, x0_bf))):
                    pt = ps[3 + (2 * tc + si_) % 4][:].bitcast(BF16)
                    for m in range(4):
                        P.transpose(pt[:, m * 128:(m + 1) * 128], src[:, m, tc * 128:(tc + 1) * 128], k.ident_bf[:])
                    P.copy(dst[:, tc, :], pt[:, 0:512], e=("act" if si_ else "dve"))
        Yre = P.sbuf("hy_Yre", [128, NF, 512], BF16)
        Yim = P.sbuf("hy_Yim", [128, NF, 512], BF16)
        for fc in range(NF):
            c_ = Ct[fc % 2]; s_ = St[fc % 2]
            P.dma(c_[:], dft_c[fc])
            P.dma(s_[:], dft_s[fc])
            pa = ps[1 + 2 * (fc % 2)]; pb = ps[2 + 2 * (fc % 2)]
            for tc in range(NT):
                P.mm(pa[:, :], c_[:, tc, :], s_bf[:, tc, :], start=(tc == 0), stop=(tc == NT - 1))
            for tc in range(NT):
                P.mm(pb[:, :], s_[:, tc, :], s_bf[:, tc, :], start=(tc == 0), stop=(tc == NT - 1))
            P.copy(tA[:], pa[:, :], e="act")
            P.copy(tB[:], pb[:, :], e="act")
            P.tt(tC[:], tA[:], Kre[:, fc, :], ALU.mult)
            P.tt(tD[:], tB[:], Kim[:, fc, :], ALU.mult, e="pool")
            P.tt(Yre[:, fc, :], tC[:], tD[:], ALU.subtract)
            P.tt(tC[:], tA[:], Kim[:, fc, :], ALU.mult, e="pool")
            P.tt(tD[:], tB[:], Kre[:, fc, :], ALU.mult)
            P.tt(Yim[:, fc, :], tC[:], tD[:], ALU.add, e="pool")
        bd = P.sbuf("hy_bd", [128, 512], F32)
        P.dma(bd[:], I["hy_bias"][l].partition_broadcast(128))
        bT = P.sbuf("hy_bT", [128, 4, N], BF16)
        otm = [P.sbuf("hy_otm%d" % i, [128, 512], BF16) for i in range(2)]
        for tc in range(NT):
            c_ = Ct[tc % 2]; s_ = St[tc % 2]
            P.dma(c_[:], dft_ct[tc])
            P.dma(s_[:], dft_st[tc])
            pp = ps[1 + tc % 2]
            for fc in range(NF):
                P.mm(pp[:, :], c_[:, fc, :], Yre[:, fc, :], start=(fc == 0), stop=False)
            for fc in range(NF):
                P.mm(pp[:, :], s_[:, fc, :], Yim[:, fc, :], start=False, stop=(fc == NF - 1))
            P.tt(tA[:], s_bf[:, tc, :], bd[:], ALU.mult)
            P.stt(tB[:], pp[:, :], 1.0 / N, tA[:], ALU.mult, ALU.add)
            o = otm[tc % 2]
            P.tt(o[:], tB[:], x0_bf[:, tc, :], ALU.mult, e="pool")
            pt = ps[5 + tc % 2][:].bitcast(BF16)
            for j in range(4):
                P.transpose(pt[:, j * 128:(j + 1) * 128], o[:, j * 128:(j + 1) * 128], k.ident_bf[:])
            P.copy(bT[:, :, tc * 128:(tc + 1) * 128], pt[:, 0:512].rearrange("p (j t) -> p j t", j=4), e="act")
        P.dma(k.br[1].rearrange("(c p) t -> p c t", p=128)[:, :, xbase:xbase + N], bT[:], q="sp")


HY_LAT = ("lat", NLAT, LT0, NCTX)
HY_CTX = ("ctx", NCTX, CT0, 0)


NCH = 144


def declare_s5_inputs(P, I, nl):
    nc = P.nc

    def inp(name, shape):
        I[name] = nc.dram_tensor(name, list(shape), F32, kind="ExternalInput").ap()
    inp("s5_lam_re", [nl, 2, 2048]); inp("s5_lam_im", [nl, 2, 2048]); inp("s5_log_dt", [nl, 2, 32])
    inp("s5_b_re", [nl, 2, 2048, 16]); inp("s5_b_im", [nl, 2, 2048, 16])
    inp("s5_c_re", [nl, 2, 32, 16, 64]); inp("s5_c_im", [nl, 2, 32, 16, 64])
    inp("s5_d", [nl, 512]); inp("s5_w_glu", [nl, 512, 1024]); inp("s5_b_glu", [nl, 1024])
    inp("s5_mask", [2, 2, 128, 256])


def s5_setup(P, k):
    nc = P.nc
    k.u_tm = nc.dram_tensor("s5_u_tm", [T, 512], F32, kind="Internal").ap()
    k.y_tm = nc.dram_tensor("s5_y_tm", [T, 512], F32, kind="Internal").ap()


def s5_part1(P, k, l):
    I = k.I
    w_in = I["w_in"][l].rearrange("(c p) n -> p c n", p=128)
    ps = k.ps
    with P.scope():
        wu = P.sbuf("s5_wu", [128, KC, 512], BF16)
        P.dma(wu[:], w_in[:, :, C_U:C_U + 512], q="pool")
        ut = [P.sbuf("s5_ut%d" % i, [128, 512], F32) for i in range(2)]
        for tc in range(18):
            h0 = hcol(tc * 128)
            pp = ps[1 + tc % 2]
            for c in range(KC):
                P.mm(pp[:, :], k.hmix[:, c, h0:h0 + 128], wu[:, c, :], start=(c == 0), stop=(c == KC - 1))
            P.copy(ut[tc % 2][:], pp[:, :], e=("act" if tc % 2 else "dve"))
            P.dma(k.u_tm[tc * 128:(tc + 1) * 128, :], ut[tc % 2][:], q="sp")


def cmul(P, o_re, o_im, a_re, a_im, b_re, b_im, t1, t2, neg_im=False):
    P.tt(t1, a_re, b_re, ALU.mult)
    P.tt(t2, a_im, b_im, ALU.mult, e="pool")
    P.tt(o_re, t1, t2, ALU.subtract)
    P.tt(t1, a_re, b_im, ALU.mult)
    P.tt(t2, a_im, b_re, ALU.mult, e="pool")
    if neg_im:
        P.stt(o_im, t1, -1.0, t2, ALU.mult, ALU.subtract)
    else:
        P.tt(o_im, t1, t2, ALU.add)


def s5_part2(P, k, l, ctx_out):
    nc = P.nc
    I = k.I
    ps = k.ps
    with P.scope():
        U = P.sbuf("s5_U", [128, 32, 2, NCH], BF16)
        M = P.sbuf("s5_M", [128, 32, 2, 256], BF16)
        Qre = [P.sbuf("s5_Qre%d" % d, [128, 16, 256], BF16) for d in range(2)]
        nQim = [P.sbuf("s5_nQim%d" % d, [128, 16, 256], BF16) for d in range(2)]
        Xre = [P.sbuf("s5_Xre%d" % d, [128, 16, NCH], BF16) for d in range(2)]
        Xim = [P.sbuf("s5_Xim%d" % d, [128, 16, NCH], BF16) for d in range(2)]
        with P.scope():
            uc = P.sbuf("s5_uc", [128, 16 * 512], F32)
            uc2 = P.sbuf("s5_uc2", [128, 16 * 512], F32)
            ucv = uc[:].rearrange("p (j g h) -> p g j h", j=16, g=32)
            uc2v = uc2[:].rearrange("p (g j h) -> p g j h", g=32, j=16)
            for (part, np_, c_lo) in (("lat", 128, 16), ("ctx", 16, 0)):
                rows = k.u_tm[NCTX:T, :] if part == "lat" else k.u_tm[0:NCTX, :]
                P.dma(uc[0:np_, :], rows.rearrange("(c j) n -> c (j n)", j=16))
                for gi, eng in enumerate(("act", "dve", "act", "pool")):
                    gs = slice(gi * 8, (gi + 1) * 8)
                    P.copy(uc2v[0:np_, gs], ucv[0:np_, gs], e=eng)
                cnt = 0
                for g in range(32):
                    for a in range(2):
                        pp = ps[1 + cnt % 4]
                        cnt += 1
                        P.transpose(pp[:, 0:np_], uc2[0:np_, g * 256 + a * 128:g * 256 + (a + 1) * 128], k.ident[0:np_, 0:np_])
                        P.copy(U[:, g, a, c_lo:c_lo + np_], pp[:, 0:np_], e=("act" if cnt % 2 else "dve"))
        for d in range(2):
            with P.scope():
                Ar = P.sbuf("s5_Ar", [128, 8, 16], F32); Ai = P.sbuf("s5_Ai", [128, 8, 16], F32); nAi = P.sbuf("s5_nAi", [128, 8, 16], F32)
                PTre = P.sbuf("s5_PTre", [128, 16, 2, 128], BF16); PTim = P.sbuf("s5_PTim", [128, 16, 2, 128], BF16)
                with P.scope():
                    sm = P.sbuf("s5_sm", [128, 40, 16], F32)
                    slot = [0]

                    def S():
                        i = slot[0]
                        slot[0] += 1
                        return sm[:, i, :]
                    lre = S(); lim = S(); dt = S()
                    praw = P.sbuf("s5_praw", [16, 2, 128], F32)
                    P.dma(praw[:, 0, :], I["s5_lam_re"][l, d].rearrange("(pr q) -> pr q", q=128))
                    P.dma(praw[:, 1, :], I["s5_lam_im"][l, d].rearrange("(pr q) -> pr q", q=128))
                    P.transpose(ps[1][:, 0:16], praw[:, 0, :], k.ident[0:16, 0:16])
                    P.transpose(ps[1][:, 16:32], praw[:, 1, :], k.ident[0:16, 0:16])
                    P.copy(lre, ps[1][:, 0:16])
                    P.copy(lim, ps[1][:, 16:32])
                    ldt2 = P.sbuf("s5_ldt2", [2, 16], F32)
                    P.dma(ldt2[:], I["s5_log_dt"][l, d].rearrange("(pr g2) -> g2 pr", g2=2), allow_slow_non_contiguous=True)
                    self_ = P.sbuf("s5_self", [2, 128], F32)
                    P.dma(self_[:], I["na_sel"])
                    P.mm(ps[2][:, 0:16], self_[:, :], ldt2[:, :], start=True, stop=True)
                    P.copy(dt, ps[2][:, 0:16])
                    P.act(dt, dt, AF.Exp)
                    P.ts(lre, lre, -1e-4, None, op0=ALU.min)
                    a_ = S(); th = S(); t1 = S(); t2 = S()
                    P.tt(a_, lre, dt, ALU.mult)
                    P.tt(th, lim, dt, ALU.mult)
                    mag = S(); imag = S()
                    P.act(mag, a_, AF.Exp)
                    P.act(imag, a_, AF.Exp, scale=-1.0)
                    for _ in range(4):
                        P.ts(t1, th, PI, -2.0 * PI, op0=ALU.is_gt, op1=ALU.mult)
                        P.tt(th, th, t1, ALU.add)
                    thc = S()
                    P.ts(thc, th, PI / 2, None, op0=ALU.add)
                    P.ts(t1, thc, PI, -2.0 * PI, op0=ALU.is_gt, op1=ALU.mult)
                    P.tt(thc, thc, t1, ALU.add)
                    sn = S(); cs = S()
                    P.act(sn, th, AF.Sin)
                    P.act(cs, thc, AF.Sin)
                    lbr = S(); lbi = S(); lir = S(); lii = S()
                    P.tt(lbr, mag, cs, ALU.mult); P.tt(lbi, mag, sn, ALU.mult)
                    P.tt(lir, imag, cs, ALU.mult); P.stt(lii, imag, -1.0, sn, ALU.mult, ALU.mult)
                    den = S(); icr = S(); ici = S()
                    P.tt(den, lre, lre, ALU.mult); P.tt(t1, lim, lim, ALU.mult); P.tt(den, den, t1, ALU.add)
                    P.recip(den, den)
                    P.tt(icr, lre, den, ALU.mult); P.stt(ici, lim, -1.0, den, ALU.mult, ALU.mult)
                    lm1 = S(); cfr = S(); cfi = S()
                    P.ts(lm1, lbr, -1.0, None, op0=ALU.add)
                    cmul(P, cfr, cfi, lm1, lbi, icr, ici, t1, t2)
                    Ppr = P.sbuf("s5_Ppr", [128, 16, 17], F32); Ppi = P.sbuf("s5_Ppi", [128, 16, 17], F32)
                    Pnr = P.sbuf("s5_Pnr", [128, 16, 17], F32); Pni = P.sbuf("s5_Pni", [128, 16, 17], F32)
                    tw1 = P.sbuf("s5_tw1", [128, 16, 8], F32); tw2 = P.sbuf("s5_tw2", [128, 16, 8], F32)
                    for (tr, ti, br_, bi_) in ((Ppr, Ppi, lbr, lbi), (Pnr, Pni, lir, lii)):
                        P.memset(tr[:, :, 0:1], 1.0); P.memset(ti[:, :, 0:1], 0.0)
                        P.copy(tr[:, :, 1], br_); P.copy(ti[:, :, 1], bi_)
                        n = 2
                        while n <= 16:
                            sqr = S() if False else None
                            h = n // 2
                            cmul(P, tr[:, :, n], ti[:, :, n], tr[:, :, h], ti[:, :, h], tr[:, :, h], ti[:, :, h], tw1[:, :, 0], tw2[:, :, 0])
                            cnt_ = min(n, 17 - n) - 1
                            if cnt_ > 0:
                                bre = tr[:, :, n:n + 1].broadcast_to([128, 16, cnt_]) if False else None
                                cmul(P, tr[:, :, n + 1:n + 1 + cnt_], ti[:, :, n + 1:n + 1 + cnt_],
                                     tr[:, :, 1:1 + cnt_], ti[:, :, 1:1 + cnt_],
                                     tr[:, :, n:n + 1].to_broadcast([128, 16, cnt_]), ti[:, :, n:n + 1].to_broadcast([128, 16, cnt_]),
                                     tw1[:, :, 0:cnt_], tw2[:, :, 0:cnt_])
                            n *= 2
                    P.copy(Ar[:, 0, :], Ppr[:, :, 16]); P.copy(Ai[:, 0, :], Ppi[:, :, 16])
                    for kk in range(1, 8):
                        cmul(P, Ar[:, kk, :], Ai[:, kk, :], Ar[:, kk - 1, :], Ai[:, kk - 1, :], Ar[:, kk - 1, :], Ai[:, kk - 1, :], t1, t2)
                    P.ts(nAi[:], Ai[:], -1.0, None, op0=ALU.mult)
                    Bre = P.sbuf("s5_Bre", [128, 16, 16], F32); Bim = P.sbuf("s5_Bim", [128, 16, 16], F32)
                    P.dma(Bre[:], I["s5_b_re"][l, d].rearrange("(pr q) h -> q pr h", q=128))
                    P.dma(Bim[:], I["s5_b_im"][l, d].rearrange("(pr q) h -> q pr h", q=128))
                    Cre = P.sbuf("s5_Cre", [128, 16, 16], F32); Cim = P.sbuf("s5_Cim", [128, 16, 16], F32)
                    craw = P.sbuf("s5_craw", [128, 2, 4, 64], F32)
                    for ri, nm_ in enumerate(("s5_c_re", "s5_c_im")):
                        P.dma(craw[:, ri], I[nm_][l, d].rearrange("(a gl) h p -> (gl h) a p", a=4))
                    for ri, dst in enumerate((Cre, Cim)):
                        for a in range(4):
                            pa_ = ps[3 + a % 2]
                            P.mm(pa_[0:64, 0:128], craw[:, ri, a, :], k.ident[:], start=True, stop=True)
                            P.mm(pa_[64:128, 0:128], craw[:, ri, a, :], k.ident[:], start=True, stop=True)
                            v_ = pa_[:, 0:128].rearrange("p (pl g2 h) -> p pl g2 h", g2=2, h=16)
                            P.copy(dst[0:64, a * 4:(a + 1) * 4, :], v_[0:64, :, 0, :], e="act")
                            P.copy(dst[64:128, a * 4:(a + 1) * 4, :], v_[64:128, :, 1, :], e="dve")
                    tb1 = P.sbuf("s5_tb1", [128, 16, 16], F32); tb2 = P.sbuf("s5_tb2", [128, 16, 16], F32)
                    bbr = P.sbuf("s5_bbr", [128, 16, 16], F32); bbi = P.sbuf("s5_bbi", [128, 16, 16], F32)
                    bc = lambda x: x.unsqueeze(2).to_broadcast([128, 16, 16])
                    cmul(P, bbr[:], bbi[:], Bre[:], Bim[:], bc(cfr), bc(cfi), tb1[:], tb2[:])
                    if d == 1:
                        cmul(P, Bre[:], Bim[:], bbr[:], bbi[:], bc(Pnr[:, :, 15]), bc(Pni[:, :, 15]), tb1[:], tb2[:])
                        vbr, vbi = Bre, Bim
                        cr2 = P.sbuf("s5_cr2", [128, 16, 16], F32); ci2 = P.sbuf("s5_ci2", [128, 16, 16], F32)
                        cmul(P, cr2[:], ci2[:], Cre[:], Cim[:], bc(Ppr[:, :, 15]), bc(Ppi[:, :, 15]), tb1[:], tb2[:])
                        vcr, vci = cr2, ci2
                        tabP, tabQ = (Ppr, Ppi), (Pnr, Pni)
                    else:
                        vbr, vbi = bbr, bbi
                        vcr, vci = Cre, Cim
                        tabP, tabQ = (Pnr, Pni), (Ppr, Ppi)
                    Pre = P.sbuf("s5_Pre", [128, 16, 256], BF16); Pim = P.sbuf("s5_Pim", [128, 16, 256], BF16)
                    to1 = P.sbuf("s5_to1", [128, 4, 256], F32); to2 = P.sbuf("s5_to2", [128, 4, 256], F32)
                    v4 = lambda x: x.rearrange("p a (j h) -> p a j h", j=16)
                    for pg in range(4):
                        sl = slice(pg * 4, pg * 4 + 4)
                        tj = lambda tab: tab[:, sl, 0:16].unsqueeze(3).to_broadcast([128, 4, 16, 16])
                        vh = lambda v: v[:, sl, :].unsqueeze(2).to_broadcast([128, 4, 16, 16])
                        cmul(P, v4(Pre[:, sl, :]), v4(Pim[:, sl, :]), tj(tabP[0]), tj(tabP[1]), vh(vbr), vh(vbi), v4(to1[:]), v4(to2[:]))
                        cmul(P, v4(Qre[d][:, sl, :]), v4(nQim[d][:, sl, :]), tj(tabQ[0]), tj(tabQ[1]), vh(vcr), vh(vci), v4(to1[:]), v4(to2[:]),
                             neg_im=True)
                    msk = [P.sbuf("s5_msk%d" % a, [128, 256], F32) for a in range(2)]
                    for a in range(2):
                        P.dma(msk[a][:], I["s5_mask"][d, a])
                    tm = [P.sbuf("s5_tm%d" % i, [128, 256], BF16) for i in range(2)]
                    cnt = 0
                    for g in range(32):
                        pr_, g2 = g // 2, g % 2
                        rows = slice(g2 * 64, (g2 + 1) * 64)
                        for a in range(2):
                            pp = ps[1 + cnt % 4]
                            cnt += 1
                            P.mm(pp[:, 0:256], Pre[rows, pr_, a * 128:(a + 1) * 128], Qre[d][rows, pr_, :], start=True, stop=False)
                            P.mm(pp[:, 0:256], Pim[rows, pr_, a * 128:(a + 1) * 128], nQim[d][rows, pr_, :], start=False, stop=True)
                            if d == 0:
                                P.tt(M[:, g, a, :], pp[:, 0:256], msk[a][:], ALU.mult)
                            else:
                                t = tm[cnt % 2]
                                P.tt(t[:], pp[:, 0:256], msk[a][:], ALU.mult)
                                P.tt(M[:, g, a, :], M[:, g, a, :], t[:], ALU.add, e="pool")
                    cnt = 0
                    for pr_ in range(16):
                        for a in range(2):
                            for (src, dst) in ((Pre, PTre), (Pim, PTim)):
                                pt = ps[5 + cnt % 2][:].bitcast(BF16)
                                cnt += 1
                                P.transpose(pt[:, 0:128], src[:, pr_, a * 128:(a + 1) * 128], k.ident_bf[:])
                                P.copy(dst[:, pr_, a, :], pt[:, 0:128], e=("act" if cnt % 2 else "dve"))
                SA = [P.sbuf("s5_SAre", [128, 16, NCH], F32), P.sbuf("s5_SAim", [128, 16, NCH], F32)]
                SB = [P.sbuf("s5_SBre", [128, 16, NCH], F32), P.sbuf("s5_SBim", [128, 16, NCH], F32)]
                ts_ = P.sbuf("s5_ts", [128, NCH], F32); ts2 = P.sbuf("s5_ts2", [128, NCH], F32)
                for pr_ in range(16):
                    pre_, pim_ = ps[1 + 2 * (pr_ % 2)], ps[2 + 2 * (pr_ % 2)]
                    for (pt_, PT) in ((pre_, PTre), (pim_, PTim)):
                        for g2 in range(2):
                            g = 2 * pr_ + g2
                            rows = slice(g2 * 64, (g2 + 1) * 64)
                            if d == 0:
                                for a in range(2):
                                    P.mm(pt_[rows, 0:NCH], PT[:, pr_, a, rows], U[:, g, a, :], start=(a == 0), stop=(a == 1))
                            else:
                                for a in range(2):
                                    P.mm(pt_[rows, 0:128], PT[:, pr_, a, rows], U[:, g, a, 16:NCH], start=(a == 0), stop=(a == 1))
                                for a in range(2):
                                    P.mm(pt_[rows, 128:NCH], PT[:, pr_, a, rows], U[:, g, a, 0:16], start=(a == 0), stop=(a == 1))
                    P.ts(ts_[:], pre_[:, 0:NCH], Ar[:, 0, pr_:pr_ + 1], None, op0=ALU.mult)
                    P.stt(SA[0][:, pr_, :], pim_[:, 0:NCH], nAi[:, 0, pr_:pr_ + 1], ts_[:], ALU.mult, ALU.add)
                    P.ts(ts2[:], pim_[:, 0:NCH], Ar[:, 0, pr_:pr_ + 1], None, op0=ALU.mult)
                    P.stt(SA[1][:, pr_, :], pre_[:, 0:NCH], Ai[:, 0, pr_:pr_ + 1], ts2[:], ALU.mult, ALU.add)
                tsa = [P.sbuf("s5_tsa%d" % i, [128, NCH], F32) for i in range(4)]
                tsb = [P.sbuf("s5_tsb%d" % i, [128, NCH], F32) for i in range(4)]
                cur, nxt = SA, SB
                for kk in range(8):
                    sh = 1 << kk
                    n_ = NCH - sh
                    for pg in range(4):
                        prs = list(range(pg * 4, pg * 4 + 4))

                        def views(pr_):
                            if d == 0:
                                return (nxt[0][:, pr_, sh:NCH], nxt[1][:, pr_, sh:NCH],
                                        cur[0][:, pr_, 0:n_], cur[1][:, pr_, 0:n_],
                                        cur[0][:, pr_, sh:NCH], cur[1][:, pr_, sh:NCH])
                            return (nxt[0][:, pr_, 0:n_], nxt[1][:, pr_, 0:n_],
                                    cur[0][:, pr_, sh:NCH], cur[1][:, pr_, sh:NCH],
                                    cur[0][:, pr_, 0:n_], cur[1][:, pr_, 0:n_])
                        V = [views(pr_) for pr_ in prs]
                        for i, pr_ in enumerate(prs):
                            P.stt(tsa[i][:, 0:n_], V[i][2], Ar[:, kk, pr_:pr_ + 1], V[i][4], ALU.mult, ALU.add)
                        for i, pr_ in enumerate(prs):
                            P.stt(tsb[i][:, 0:n_], V[i][3], Ar[:, kk, pr_:pr_ + 1], V[i][5], ALU.mult, ALU.add)
                        for i, pr_ in enumerate(prs):
                            P.stt(V[i][0], V[i][3], nAi[:, kk, pr_:pr_ + 1], tsa[i][:, 0:n_], ALU.mult, ALU.add)
                        for i, pr_ in enumerate(prs):
                            P.stt(V[i][1], V[i][2], Ai[:, kk, pr_:pr_ + 1], tsb[i][:, 0:n_], ALU.mult, ALU.add)
                    for ri in range(2):
                        if d == 0:
                            P.copy(nxt[ri][:, :, 0:sh], cur[ri][:, :, 0:sh], e="pool")
                        else:
                            P.copy(nxt[ri][:, :, n_:NCH], cur[ri][:, :, n_:NCH], e="pool")
                    cur, nxt = nxt, cur
                for ri, X in ((0, Xre[d]), (1, Xim[d])):
                    W = cur[ri]
                    if d == 0:
                        P.memset(X[:, :, 0:1], 0.0)
                        P.copy(X[:, :, 1:NCH], W[:, :, 0:NCH - 1], e=("act" if ri else "dve"))
                    else:
                        P.copy(X[:, :, 0:15], W[:, :, 129:144], e="act")
                        P.memset(X[:, :, 15:16], 0.0)
                        P.copy(X[:, :, 16:143], W[:, :, 1:128], e="dve")
                        P.copy(X[:, :, 143:144], W[:, :, 128:129], e="act")
        with P.scope():
            ycl = P.sbuf("s5_ycl", [128, 16 * 512], F32)
            ycc = P.sbuf("s5_ycc", [16, 16 * 512], F32)
            yclv = ycl[:].rearrange("p (j g h) -> p j g h", j=16, g=32)
            yccv = ycc[:].rearrange("p (j g h) -> p j g h", j=16, g=32)
            ysb = [P.sbuf("s5_ysb%d" % i, [128, NCH], F32) for i in range(2)]
            items = [(g, b) for g in range(32) for b in range(2)]

            def y_mm(i):
                g, b = items[i]
                pr_, g2 = g // 2, g % 2
                rows = slice(g2 * 64, (g2 + 1) * 64)
                pp = ps[1 + i % 2]
                cols = slice(b * 128, (b + 1) * 128)
                P.mm(pp[:, 0:NCH], M[:, g, 0, cols], U[:, g, 0, :], start=True, stop=False)
                P.mm(pp[:, 0:NCH], M[:, g, 1, cols], U[:, g, 1, :], start=False, stop=False)
                for d in range(2):
                    P.mm(pp[:, 0:NCH], Qre[d][rows, pr_, cols], Xre[d][rows, pr_, :], start=False, stop=False)
                    P.mm(pp[:, 0:NCH], nQim[d][rows, pr_, cols], Xim[d][rows, pr_, :], start=False, stop=(d == 1))
                P.copy(ysb[i % 2][:], pp[:, 0:NCH], e="act")

            def y_tr(i):
                g, b = items[i]
                ys = ysb[i % 2]
                pt1 = ps[3 + i % 2]
                pt2 = ps[5 + i % 2]
                P.transpose(pt1[:, 0:128], ys[:, 16:NCH], k.ident[:])
                P.copy(yclv[:, b * 8:(b + 1) * 8, g, :], pt1[:, 0:128].rearrange("p (j h) -> p j h", j=8), e="dve")
                P.transpose(pt2[0:16, 0:128], ys[:, 0:16], k.ident[:])
                P.copy(yccv[:, b * 8:(b + 1) * 8, g, :], pt2[0:16, 0:128].rearrange("p (j h) -> p j h", j=8), e="dve")
            y_mm(0)
            for i in range(len(items)):
                if i + 1 < len(items):
                    y_mm(i + 1)
                y_tr(i)
            P.dma(k.y_tm[NCTX:T, :].rearrange("(c j) n -> c (j n)", j=16), ycl[:], q="sp")
            P.dma(k.y_tm[0:NCTX, :].rearrange("(c j) n -> c (j n)", j=16), ycc[:], q="sp")
        with P.scope():
            dbc = P.sbuf("s5_dbc", [128, 512], F32)
            P.dma(dbc[:], I["s5_d"][l].partition_broadcast(128))
            wgl = P.sbuf("s5_wgl", [128, 4, 1024], BF16)
            P.dma(wgl[:], I["s5_w_glu"][l].rearrange("(kk p) n -> p kk n", p=128), q="pool")
            bgl = P.sbuf("s5_bgl", [128, 8], F32)
            load_col_vec(P, bgl[:], I["s5_b_glu"][l], 8)
            gT = P.sbuf("s5_gT", [128, 4, T], BF16)
            yt = [P.sbuf("s5_yt%d" % i, [128, 512], F32) for i in range(2)]
            ut = [P.sbuf("s5_ut%d" % i, [128, 512], F32) for i in range(2)]
            w1_ = P.sbuf("s5_w1", [128, 512], F32); w2_ = P.sbuf("s5_w2", [128, 512], F32)
            gtm = [P.sbuf("s5_gtm%d" % i, [128, 512], BF16) for i in range(2)]
            tcs = list(range(18)) if ctx_out else list(range(2, 18))
            for tc in tcs:
                y = yt[tc % 2]; u = ut[tc % 2]
                P.dma(y[:], k.y_tm[tc * 128:(tc + 1) * 128, :])
                P.dma(u[:], k.u_tm[tc * 128:(tc + 1) * 128, :])
                P.tt(u[:], u[:], dbc[:], ALU.mult)
                P.tt(y[:], y[:], u[:], ALU.add, e="pool")
                P.act(w1_[:], y[:], AF.Square)
                P.ts(w1_[:], w1_[:], 0.044715, 1.0, op0=ALU.mult, op1=ALU.add)
                P.tt(w1_[:], w1_[:], y[:], ALU.mult)
                P.act(w2_[:], w1_[:], AF.Sigmoid, scale=1.5957691216057308)
                gt_ = gtm[tc % 2]
                P.tt(gt_[:], y[:], w2_[:], ALU.mult, e="pool")
                pt = ps[5 + tc % 2][:].bitcast(BF16)
                for j in range(4):
                    P.transpose(pt[:, j * 128:(j + 1) * 128], gt_[:, j * 128:(j + 1) * 128], k.ident_bf[:])
                P.copy(gT[:, :, tc * 128:(tc + 1) * 128], pt[:, 0:512].rearrange("p (j t) -> p j t", j=4), e="act")
            cT = P.sbuf("s5_cT", [128, T], BF16)
            sg = [P.sbuf("s5_sg%d" % i, [128, 512], F32) for i in range(2)]
            subs = ALLSUBS if ctx_out else LATSUBS
            cnt = 0
            for m in range(4):
                for (c0, w, st) in subs:
                    pa = ps[1 + cnt % 2]; pb = ps[3 + cnt % 2]; s_ = sg[cnt % 2]
                    cnt += 1
                    for kk in range(4):
                        P.mm(pa[:, :w], wgl[:, kk, m * 128:(m + 1) * 128], gT[:, kk, c0:c0 + w], start=(kk == 0), stop=(kk == 3))
                    for kk in range(4):
                        P.mm(pb[:, :w], wgl[:, kk, 512 + m * 128:512 + (m + 1) * 128], gT[:, kk, c0:c0 + w], start=(kk == 0), stop=(kk == 3))
                    P.act(s_[:, :w], pb[:, :w], AF.Sigmoid, bias=bgl[:, 4 + m:5 + m])
                    P.stt(cT[:, c0:c0 + w], pa[:, :w], bgl[:, m:m + 1], s_[:, :w], ALU.add, ALU.mult)
                cols0 = 0 if ctx_out else NCTX
                P.dma(k.br[2][m * 128:(m + 1) * 128, cols0:T], cT[:, cols0:T], q="sp")


def build_full(nl=2, upto=None):
    P = Prog()
    nc = P.nc
    k = K()
    k.I = declare_inputs(P, nl)
    declare_mixer_inputs(P, k.I, nl)
    declare_na_inputs(P, k.I, nl)
    declare_hyena_inputs(P, k.I, nl)
    declare_s5_inputs(P, k.I, nl)
    yT = nc.dram_tensor("yT", [D, NLAT], F32, kind="ExternalOutput").ap()
    setup_common(P, k)
    mixer_setup(P, k)
    s5_setup(P, k)
    geo = na_geometry()
    P.dma(k.xres, k.I["xT"].rearrange("(c p) t -> p c t", p=128))
    for l in range(nl):
        ctx_out = l < nl - 1
        with P.scope():
            layer_mods(P, k, l)
        with P.scope():
            ffn_sublayer(P, k, l, 0, FULL_BLOCKS)
        with P.scope():
            alloc_hmix(P, k)
            mixer_modnorm(P, k)
            mla_branch(P, k, l, ctx_out)
            na_branch(P, k, l, ctx_out, geo)
            hyena_branch(P, k, l, HY_LAT)
            if ctx_out:
                hyena_branch(P, k, l, HY_CTX)
            s5_part1(P, k, l)
        s5_part2(P, k, l, ctx_out)
        merge_phase(P, k, l, ctx_out)
        with P.scope():
            ffn_sublayer(P, k, l, 2, FULL_BLOCKS if ctx_out else LAT_BLOCKS)
    P.dma(yT.rearrange("(c p) t -> p c t", p=128), k.xres[:, :, NCTX:T], q="sp")
    P.finish("sp")
    P.close()
    return P, k


_CACHE = {}


def _host_constants():
    if "c" in _CACHE:
        return _CACHE["c"]
    C, S = rope_tables()
    mv, sel, cm = na_const_tables()
    m = {"ident": np.eye(128, dtype=np.float32), "rope_c": C, "rope_s": S,
         "na_mv": np.ascontiguousarray(mv.reshape(2, -1)), "na_sel": sel, "na_cm": cm, "s5_mask": s5_masks(),
         "na_m01": na_variants(na_geometry())[1]}
    for nm_, N in (("lat", NLAT), ("ctx", NCTX)):
        hc = hyena_consts(N)
        for t in ("c", "s", "ct", "st"):
            m["dft_%s_%s" % (t, nm_)] = hc[t]
        m["hy_zT_%s" % nm_] = hc["zT"]
        m["hy_decay_%s" % nm_] = hc["decay"]
    _CACHE["c"] = m
    return m


def kernel(**inputs):
    nl = 2
    if "prog" not in _CACHE:
        _CACHE["prog"] = build_full(nl)
    P, k = _CACHE["prog"]
    shared = dict(_host_constants())
    shared["na_G"] = np.ascontiguousarray(
        np.stack([na_bias_gather(np.asarray(inputs["na_rpb"][l], np.float32)).reshape(128, -1) for l in range(nl)], 0))
    for nm, ap in k.I.items():
        if nm in shared or nm in ("xT", "cvec"):
            continue
        shared[nm] = np.ascontiguousarray(np.asarray(inputs[nm], np.float32).reshape(ap.shape))
    x = np.asarray(inputs["x"], np.float32)
    ctx = np.asarray(inputs["ctx"], np.float32)
    c = np.asarray(inputs["c"], np.float32)
    c_ctx = np.asarray(inputs["c_ctx"], np.float32)
    B = x.shape[0]
    in_maps = []
    for b in range(B):
        m = dict(shared)
        m["xT"] = np.ascontiguousarray(np.concatenate([ctx[b], x[b]], 0).T)
        m["cvec"] = np.ascontiguousarray(np.stack([c[b], c_ctx], 1))
        in_maps.append(m)
    res = run_bass_kernel_spmd(P.nc, in_maps, core_ids=list(range(B)))
    out = np.stack([np.asarray(res.results[b]["yT"], np.float32).T for b in range(B)], 0)
    return np.ascontiguousarray(out)
```
